# Optimizing a Trainium2 kernel written in Bass

```python
import functools
import jax, jax.numpy as jnp
from jax import lax
import numpy as np

D_MODEL = 1024
BATCH = 2
SEQ = 16384
DEPTH = 1
DEC_BATCH = 32
DEC_SEQ = 16
PAST_LEN = 2048

CHUNK = 64
N_META = 16
GLA_HEADS = 4
GLA_DK = 128
GLA_DV = 256
GLA_RANK = 16
GLA_TAU = 16.0
GLA_BLOCK = 64
DSA_HEADS = 16
DSA_KV_HEADS = 4
DSA_HEAD_DIM = 64
DSA_GROUP = DSA_HEADS // DSA_KV_HEADS
DSA_SCALE = DSA_HEAD_DIM ** -0.5
IDX_HEADS = 8
IDX_DIM = 64
IDX_W_SCALE = (IDX_HEADS ** -0.5) * (IDX_DIM ** -0.5)
TOPK_MAX = 256
Q_BLOCK = 128
NORM_EPS = 1e-5
ALPHA = (2.0 * DEPTH) ** 0.25
BETA = (8.0 * DEPTH) ** -0.25
GLA_QK = GLA_HEADS * GLA_DK
GLA_V = GLA_HEADS * GLA_DV
DSA_Q = DSA_HEADS * DSA_HEAD_DIM
DSA_KV = DSA_KV_HEADS * DSA_HEAD_DIM
IDX_Q = IDX_HEADS * IDX_DIM
SPLITS = (GLA_QK, GLA_QK, GLA_V, GLA_RANK, GLA_V,
          DSA_Q, DSA_KV, DSA_KV, IDX_Q, IDX_DIM, IDX_HEADS, DSA_Q,
          D_MODEL, D_MODEL)
IN_COLS = sum(SPLITS)
SPLIT_OFFSETS = tuple(int(o) for o in np.cumsum(SPLITS)[:-1])

kernel_name = 'hybrid_gla_dsa_streaming_step'


def layer_norm(x, g, b):
    xf = x.astype(jnp.float32)
    mu = jnp.mean(xf, axis=-1, keepdims=True)
    xc = xf - mu
    var = jnp.mean(xc * xc, axis=-1, keepdims=True)
    return (xc * lax.rsqrt(var + NORM_EPS) * g.astype(jnp.float32) + b.astype(jnp.float32)).astype(x.dtype)


def gla_recurrence(q, k, v, log_f, s0):
    bsz, t = q.shape[0], q.shape[1]
    blk = min(GLA_BLOCK, t)
    nb = -(-t // blk)
    pad = nb * blk - t

    def to_blocks(a):
        a = jnp.pad(a.astype(jnp.float32), ((0, 0), (0, pad), (0, 0), (0, 0)))
        return a.reshape(bsz, nb, blk, a.shape[2], a.shape[3]).transpose(1, 0, 3, 2, 4)

    causal = jnp.tril(jnp.ones((blk, blk), dtype=bool))[:, :, None]

    def step(s, xs):
        qc, kc, vc, gc = xs
        b = jnp.cumsum(gc, axis=2)
        rel = jnp.where(causal, b[:, :, :, None, :] - b[:, :, None, :, :], -jnp.inf)
        a = jnp.einsum('bhid,bhjd,bhijd->bhij', qc, kc, jnp.exp(rel))
        o = (jnp.einsum('bhid,bhde->bhie', qc * jnp.exp(b), s)
             + jnp.einsum('bhij,bhje->bhie', a, vc))
        b_end = b[:, :, -1, :]
        s_new = (jnp.exp(b_end)[..., None] * s
                 + jnp.einsum('bhjd,bhje->bhde', kc * jnp.exp(b_end[:, :, None, :] - b), vc))
        return s_new, o

    s_t, o = lax.scan(step, s0.astype(jnp.float32),
                      (to_blocks(q), to_blocks(k), to_blocks(v), to_blocks(log_f)))
    o = o.transpose(1, 0, 3, 2, 4).reshape(bsz, nb * blk, q.shape[2], v.shape[3])[:, :t]
    return o, s_t


def dsa_select_attend(q, q_i, w_i, adm, k_all, v_all, ki_all, topk):
    bsz, nq = q.shape[0], q.shape[1]
    rel = jax.nn.relu(jnp.einsum('bqhd,bsd->bqhs', q_i.astype(jnp.float32), ki_all.astype(jnp.float32)))
    score = jnp.einsum('bqhs,bqh->bqs', rel, w_i.astype(jnp.float32))
    score = jnp.where(adm[None], score, -jnp.inf)
    top_val, top_idx = lax.top_k(score, topk)
    keep = jnp.isfinite(top_val)
    gather = jax.vmap(lambda rows, idx: rows[idx])
    k_sel = gather(k_all, top_idx)
    v_sel = gather(v_all, top_idx)
    qg = q.reshape(bsz, nq, DSA_KV_HEADS, DSA_GROUP, DSA_HEAD_DIM).astype(jnp.float32)
    logits = jnp.einsum('bqgrd,bqkgd->bqgrk', qg, k_sel.astype(jnp.float32)) * DSA_SCALE
    logits = jnp.where(keep[:, :, None, None, :], logits, -jnp.inf)
    p = jax.nn.softmax(logits, axis=-1)
    o = jnp.einsum('bqgrk,bqkgd->bqgrd', p, v_sel.astype(jnp.float32))
    return o.reshape(bsz, nq, DSA_Q).astype(q.dtype)


def chunk_ids(n):
    pos = jnp.arange(n)
    return jnp.where(pos < N_META, -1, (pos - N_META) // CHUNK)


def prompt_attend(q, q_i, w_i, k, v, k_i):
    bsz, t = q.shape[0], q.shape[1]
    nb = -(-t // Q_BLOCK)
    pad = nb * Q_BLOCK - t
    topk = min(TOPK_MAX, (t - N_META) // 4)
    chunk_k = chunk_ids(t)
    chunk_q = chunk_ids(nb * Q_BLOCK).reshape(nb, Q_BLOCK)

    def blocks(a):
        a = jnp.pad(a, [(0, 0), (0, pad)] + [(0, 0)] * (a.ndim - 2))
        return a.reshape((bsz, nb, Q_BLOCK) + a.shape[2:]).swapaxes(0, 1)

    def one_block(xs):
        qb, qib, wib, cq = xs
        adm = chunk_k[None, :] <= cq[:, None]
        return dsa_select_attend(qb, qib, wib, adm, k, v, k_i, topk)

    o = lax.map(one_block, (blocks(q), blocks(q_i), blocks(w_i), chunk_q))
    return o.swapaxes(0, 1).reshape(bsz, nb * Q_BLOCK, DSA_Q)[:, :t]


def sample_attend(cache_k_l, cache_v_l, cache_ik_l, q, q_i, w_i, k, v, k_i):
    k_all = jnp.concatenate([cache_k_l.astype(k.dtype), k], axis=1)
    v_all = jnp.concatenate([cache_v_l.astype(v.dtype), v], axis=1)
    ki_all = jnp.concatenate([cache_ik_l.astype(k_i.dtype), k_i], axis=1)
    n_keys = k_all.shape[1]
    topk = min(TOPK_MAX, n_keys // 4)
    adm = jnp.ones((q.shape[1], n_keys), dtype=bool)
    return dsa_select_attend(q, q_i, w_i, adm, k_all, v_all, ki_all, topk)


def layer_forward(h, s0, attend, w_in, gla_w2, gla_gate_b, gla_norm_g, idx_kn_g, idx_kn_b,
                  w_gla, w_dsa, gate_b, w_out, ln_g, ln_b):
    bsz, t, _ = h.shape
    (g_q, g_k, g_v, g_low, g_r, d_q, d_k, d_v, i_q, i_k, i_w, d_z, m_a, m_b) = jnp.split(
        h @ w_in, SPLIT_OFFSETS, axis=-1)
    q = (g_q * GLA_DK ** -0.5).reshape(bsz, t, GLA_HEADS, GLA_DK)
    k = g_k.reshape(bsz, t, GLA_HEADS, GLA_DK)
    v = g_v.reshape(bsz, t, GLA_HEADS, GLA_DV)
    log_f = (jax.nn.log_sigmoid((g_low @ gla_w2 + gla_gate_b).astype(jnp.float32)) / GLA_TAU
             ).reshape(bsz, t, GLA_HEADS, GLA_DK)
    o_a, s_t = gla_recurrence(q, k, v, log_f, s0)
    o_a = o_a * lax.rsqrt(jnp.mean(o_a * o_a, axis=-1, keepdims=True) + NORM_EPS) * gla_norm_g.astype(jnp.float32)
    y_a = (o_a.reshape(bsz, t, GLA_V).astype(h.dtype) * jax.nn.silu(g_r)) @ w_gla
    q_d = d_q.reshape(bsz, t, DSA_HEADS, DSA_HEAD_DIM)
    k_d = d_k.reshape(bsz, t, DSA_KV_HEADS, DSA_HEAD_DIM)
    v_d = d_v.reshape(bsz, t, DSA_KV_HEADS, DSA_HEAD_DIM)
    q_i = i_q.reshape(bsz, t, IDX_HEADS, IDX_DIM)
    k_i = layer_norm(i_k, idx_kn_g, idx_kn_b)
    w_i = i_w * IDX_W_SCALE
    o_b = attend(q_d, q_i, w_i, k_d, v_d, k_i)
    y_b = (o_b * jax.nn.silu(d_z)) @ w_dsa
    merged = jax.nn.sigmoid(m_a + gate_b[0]) * y_a + jax.nn.sigmoid(m_b + gate_b[1]) * y_b
    h_new = layer_norm(ALPHA * h + merged @ w_out, ln_g, ln_b)
    return h_new, k_d, v_d, k_i, s_t


def setup_inputs(seed: int = 0) -> dict:
    key = jax.random.key(seed)
    ks = jax.random.split(key, 24)
    n = jax.random.normal
    f32 = jnp.float32
    return {
        'x_prompt': n(ks[0], (BATCH, SEQ, D_MODEL), f32),
        'x_sample': n(ks[1], (DEC_BATCH, DEC_SEQ, D_MODEL), f32),
        'cache_k': n(ks[2], (DEPTH, DEC_BATCH, PAST_LEN, DSA_KV_HEADS, DSA_HEAD_DIM), f32),
        'cache_v': n(ks[3], (DEPTH, DEC_BATCH, PAST_LEN, DSA_KV_HEADS, DSA_HEAD_DIM), f32),
        'cache_idx_k': n(ks[4], (DEPTH, DEC_BATCH, PAST_LEN, IDX_DIM), f32),
        'state_gla': n(ks[5], (DEPTH, DEC_BATCH, GLA_HEADS, GLA_DK, GLA_DV), f32),
        'meta': n(ks[6], (N_META, D_MODEL), f32),
        'ln_in_g': 1.0 + 0.02 * n(ks[7], (D_MODEL,), f32),
        'ln_in_b': 0.02 * n(ks[8], (D_MODEL,), f32),
        'w_in': n(ks[9], (DEPTH, D_MODEL, IN_COLS), f32) * D_MODEL ** -0.5,
        'gla_w2': n(ks[10], (DEPTH, GLA_RANK, GLA_QK), f32) * GLA_RANK ** -0.5,
        'gla_gate_b': 0.1 * n(ks[11], (DEPTH, GLA_QK), f32),
        'gla_norm_g': 1.0 + 0.02 * n(ks[12], (DEPTH, GLA_DV), f32),
        'idx_kn_g': 1.0 + 0.02 * n(ks[13], (DEPTH, IDX_DIM), f32),
        'idx_kn_b': 0.02 * n(ks[14], (DEPTH, IDX_DIM), f32),
        'w_gla': n(ks[15], (DEPTH, GLA_V, D_MODEL), f32) * (GLA_V ** -0.5) * BETA,
        'w_dsa': n(ks[16], (DEPTH, DSA_Q, D_MODEL), f32) * (DSA_Q ** -0.5) * BETA,
        'gate_b': 0.02 * n(ks[17], (DEPTH, 2, D_MODEL), f32),
        'w_out': n(ks[18], (DEPTH, D_MODEL, D_MODEL), f32) * (D_MODEL ** -0.5) * BETA,
        'ln_g': 1.0 + 0.02 * n(ks[19], (DEPTH, D_MODEL), f32),
        'ln_b': 0.02 * n(ks[20], (DEPTH, D_MODEL), f32),
    }


def reference(x_prompt, x_sample, cache_k, cache_v, cache_idx_k, state_gla, meta, ln_in_g, ln_in_b,
              w_in, gla_w2, gla_gate_b, gla_norm_g, idx_kn_g, idx_kn_b, w_gla, w_dsa, gate_b,
              w_out, ln_g, ln_b):
    bsz = x_prompt.shape[0]
    meta_rows = jnp.broadcast_to(meta.astype(x_prompt.dtype)[None], (bsz, N_META, meta.shape[-1]))
    h_p = layer_norm(jnp.concatenate([meta_rows, x_prompt], axis=1), ln_in_g, ln_in_b)
    h_s = layer_norm(x_sample, ln_in_g, ln_in_b)
    s0_p = jnp.zeros((bsz, GLA_HEADS, GLA_DK, GLA_DV), jnp.float32)
    kp, vp, ikp, sp = [], [], [], []
    ksm, vsm, iks, ssm = [], [], [], []
    for l in range(DEPTH):
        weights = (w_in[l], gla_w2[l], gla_gate_b[l], gla_norm_g[l], idx_kn_g[l], idx_kn_b[l],
                   w_gla[l], w_dsa[l], gate_b[l], w_out[l], ln_g[l], ln_b[l])
        h_p, k_l, v_l, ik_l, s_l = layer_forward(h_p, s0_p, prompt_attend, *weights)
        kp.append(k_l); vp.append(v_l); ikp.append(ik_l); sp.append(s_l)
        attend_s = functools.partial(sample_attend, cache_k[l], cache_v[l], cache_idx_k[l])
        h_s, k_l, v_l, ik_l, s_l = layer_forward(h_s, state_gla[l], attend_s, *weights)
        ksm.append(k_l); vsm.append(v_l); iks.append(ik_l); ssm.append(s_l)
    y_prompt = h_p[:, N_META:]
    y_sample = h_s
    k_prompt = jnp.stack(kp, axis=0)
    v_prompt = jnp.stack(vp, axis=0)
    idx_k_prompt = jnp.stack(ikp, axis=0)
    gla_prompt = jnp.stack(sp, axis=0)
    k_sample = jnp.stack(ksm, axis=0)
    v_sample = jnp.stack(vsm, axis=0)
    idx_k_sample = jnp.stack(iks, axis=0)
    gla_sample = jnp.stack(ssm, axis=0)
    return (y_prompt, y_sample, k_prompt, v_prompt, idx_k_prompt, gla_prompt,
            k_sample, v_sample, idx_k_sample, gla_sample)
```

```python
from contextlib import ExitStack
import numpy as np
import concourse.bass as bass
import concourse.mybir as mybir
from concourse.bass_utils import run_bass_kernel_spmd

F32 = mybir.dt.float32
BF16 = mybir.dt.bfloat16
AF = mybir.ActivationFunctionType
ALU = mybir.AluOpType
AX = mybir.AxisListType

D = 1024
NEG = -30000.0
KIT = 22
IDX_W_SCALE = (8 ** -0.5) * (64 ** -0.5)
ALPHA = 2.0 ** 0.25
EPS = 1e-5
C_GQ, C_GK, C_GV, C_GLOW, C_GR, C_DQ, C_DK, C_DV, C_IQ, C_IK, C_IW, C_DZ, C_MA, C_MB = (
    0, 512, 1024, 2048, 2064, 3088, 4112, 4368, 4624, 5136, 5200, 5208, 6232, 7256)
IN_COLS = 8280


class Prog:
    def __init__(self):
        self.ops = []

    def add(self, eng, fn, r=(), w=(), dsem=None):
        r = list(r)
        w = list(w)
        for b in list(r):
            if isinstance(b, str) and b.startswith('ps'):
                r.remove(b)
                if b not in w:
                    w.append(b)
        self.ops.append(dict(eng=eng, fn=fn, r=tuple(r), w=tuple(w), dsem=dsem))

    def barrier(self):
        self.ops.append(dict(eng='barrier', fn=None, r=(), w=(), dsem=None))

    def analyze(self):
        ops = self.ops
        last_w, readers = {}, {}
        last_of = {}
        for i, op in enumerate(ops):
            if op['eng'] == 'barrier':
                for e_, j in last_of.items():
                    ops[j]['needed'] = True
                last_w, readers = {}, {}
                op['deps'] = set()
                continue
            deps = set()
            for b in op['r']:
                if b in last_w:
                    deps.add(('raw', last_w[b]))
            for b in op['w']:
                if b in last_w:
                    deps.add(('waw', last_w[b]))
                for rr in readers.get(b, ()):
                    deps.add(('war', rr))
            for b in op['r']:
                readers.setdefault(b, []).append(i)
            for b in op['w']:
                last_w[b] = i
                readers[b] = []
            keep = set()
            for kind, j in deps:
                if j == i:
                    continue
                pj = ops[j]
                if pj['dsem'] is None and op['dsem'] is None and pj['eng'] == op['eng']:
                    if op['eng'] == 'pe' or kind == 'war':
                        continue
                keep.add(j)
            op['deps'] = keep
            for j in keep:
                ops[j]['needed'] = True
            if op['dsem'] is None:
                last_of[op['eng']] = i
        for e_, j in last_of.items():
            ops[j]['needed'] = True
        cnt = {}
        for op in ops:
            if op['eng'] == 'barrier':
                continue
            if op['dsem'] is not None:
                k = 'D:' + op['dsem']
                cnt[k] = cnt.get(k, 0) + 16
                op['sem'] = k
                op['val'] = cnt[k]
            elif op.get('needed'):
                k = 'E:' + op['eng']
                cnt[k] = cnt.get(k, 0) + 1
                op['sem'] = k
                op['val'] = cnt[k]
        waited = {}
        running = {}
        pending = {}
        for op in ops:
            if op['eng'] == 'barrier':
                for e_ in ('pe', 'act', 'dve', 'pool', 'sp'):
                    pending[e_] = dict(running)
                continue
            ws = {}
            if pending.get(op['eng']):
                ws.update(pending[op['eng']])
                pending[op['eng']] = None
            for j in op['deps']:
                pj = ops[j]
                ws[pj['sem']] = max(ws.get(pj['sem'], 0), pj['val'])
            wl = []
            we = waited.setdefault(op['eng'], {})
            for k, v in ws.items():
                if we.get(k, 0) >= v:
                    continue
                we[k] = v
                wl.append((k, v))
            op['waits'] = wl
            if op.get('sem') is not None:
                running[op['sem']] = op['val']
        self.totals = cnt
        return cnt

    def emit(self, nc, es):
        cnt = self.analyze()
        sems = {}
        for k in cnt:
            sems[k] = es.enter_context(nc.semaphore(k.replace(':', '_')))
        block = es.enter_context(nc.Block())
        ops = self.ops

        def run(engname):
            def f(e):
                for op in ops:
                    if op['eng'] != engname:
                        continue
                    for k, v in op['waits']:
                        e.wait_ge(sems[k], v)
                    ins = op['fn'](e)
                    if op.get('sem') is not None:
                        ins.then_inc(sems[op['sem']], 16 if op['dsem'] is not None else 1)
                for k, v in cnt.items():
                    e.wait_ge(sems[k], v)
            return f

        block.sync(run('sp'))
        block.scalar(run('act'))
        block.vector(run('dve'))
        block.gpsimd(run('pool'))
        block.tensor(run('pe'))


class KB:
    def __init__(self, nc):
        self.nc = nc
        self.P = Prog()
        self.rot = 0

    def capture(self):
        self._saved = self.P.ops
        self.P.ops = []

    def end_capture(self):
        l = self.P.ops
        self.P.ops = self._saved
        return l

    def merge(self, A, B):
        out = []
        ia = ib = 0
        na, nb_ = max(len(A), 1), max(len(B), 1)
        while ia < len(A) or ib < len(B):
            if ib >= len(B) or (ia < len(A) and ia * nb_ <= ib * na):
                out.append(A[ia]); ia += 1
            else:
                out.append(B[ib]); ib += 1
        self.P.ops.extend(out)

    def act(self, out, in_, func, r, w, **kw):
        self.P.add('act', lambda e: e.activation(out=out, in_=in_, func=func, **kw), r, w)

    def ts(self, out, in0, s1, s2, op0, op1, r, w, eng='dve', accum=None):
        if accum is None:
            if op1 is None:
                self.P.add(eng, lambda e: e.tensor_scalar(out=out, in0=in0, scalar1=s1, scalar2=None, op0=op0), r, w)
            else:
                self.P.add(eng, lambda e: e.tensor_scalar(out=out, in0=in0, scalar1=s1, scalar2=s2, op0=op0, op1=op1), r, w)
        else:
            self.P.add(eng, lambda e: e.tensor_scalar(out=out, in0=in0, scalar1=s1, scalar2=s2, op0=op0, op1=op1,
                                                      accum_out=accum), r, w)

    def tt(self, out, in0, in1, op, r, w, eng='dve'):
        self.P.add(eng, lambda e: e.tensor_tensor(out=out, in0=in0, in1=in1, op=op), r, w)

    def stt(self, out, in0, scalar, in1, op0, op1, r, w):
        self.P.add('dve', lambda e: e.scalar_tensor_tensor(out=out, in0=in0, scalar=scalar, in1=in1, op0=op0, op1=op1), r, w)

    def cp(self, eng, out, in_, r, w):
        if eng == 'act':
            self.P.add('act', lambda e: e.copy(out=out, in_=in_), r, w)
        else:
            self.P.add(eng, lambda e: e.tensor_copy(out=out, in_=in_), r, w)

    def memset(self, eng, ap, val, w):
        self.P.add(eng, lambda e: e.memset(ap, val), (), w)

    def mm(self, out, lhsT, rhs, start, stop, r, w):
        self.P.add('pe', lambda e: e.matmul(out, lhsT=lhsT, rhs=rhs, start=start, stop=stop), r, w)

    def tr(self, out, in_, ident, r, w):
        self.P.add('pe', lambda e: e.transpose(out=out, in_=in_, identity=ident), r, w)

    def dma(self, q, out, in_, r, w, dsem):
        self.P.add(q, lambda e: e.dma_start(out=out, in_=in_), r, w, dsem=dsem)

    def red(self, out, in_, op, r, w):
        self.P.add('dve', lambda e: e.tensor_reduce(out=out, in_=in_, axis=AX.X, op=op), r, w)

    def recip(self, out, in_, r, w):
        self.P.add('dve', lambda e: e.reciprocal(out=out, in_=in_), r, w)

    def bn_stats(self, out, in_, r, w):
        self.P.add('dve', lambda e: e.bn_stats(out=out, in_=in_), r, w)

    def bn_aggr(self, out, in_, r, w):
        self.P.add('dve', lambda e: e.bn_aggr(out=out, in_=in_), r, w)


def geometry(SEQ):
    T = SEQ + 16
    NB = T // 128 + 1
    assert T == 128 * (NB - 1) + 16 and NB % 4 == 1
    G = (NB + 3) // 4
    return T, NB, G


def build(SEQ, phases="0ABS"):
    T, NB, G = geometry(SEQ)
    NKMAX = NB * 128
    nc = bass.Bass("TRN2", target_bir_lowering=False)

    def din(name, shape, dt=F32):
        return nc.dram_tensor(name, list(shape), dt, kind="ExternalInput").ap()

    def dout(name, shape, dt=F32):
        return nc.dram_tensor(name, list(shape), dt, kind="ExternalOutput").ap()

    def dscr(name, shape, dt):
        return nc.dram_tensor(name, list(shape), dt, kind="Internal").ap()

    xall = din("xall", [NB * 128, D])
    xown = din("xown", [G * 128, D])
    xs = din("xs", [128, D])
    w_in = din("w_in", [D, IN_COLS])
    w3 = din("w3", [3, D, D])
    ln_in_g = din("ln_in_g", [1, D]); ln_in_b = din("ln_in_b", [1, D])
    ln_g = din("ln_g", [1, D]); ln_b = din("ln_b", [1, D])
    gate_b = din("gate_b", [1, 2048])
    gla_gate_b = din("gla_gate_b", [1, 512])
    gla_norm_g = din("gla_norm_g", [1, 256])
    idx_kn_g = din("idx_kn_g", [1, 64]); idx_kn_b = din("idx_kn_b", [1, 64])
    gla_w2 = din("gla_w2", [16, 512])
    cache_k = din("cache_k", [4, 2048, 256]); cache_v = din("cache_v", [4, 2048, 256])
    cache_ik = din("cache_ik", [4, 2048, 64])
    state = din("state", [4, 4, 128, 256])
    c_ident = din("c_ident", [128, 128])
    c_triA = din("c_triA", [128, 128]); c_blkA = din("c_blkA", [128, 2])
    c_triB = din("c_triB", [128, 128]); c_maskB = din("c_maskB", [128, 4, 128])
    c_triS = din("c_triS", [128, 128]); c_maskS = din("c_maskS", [128, 4, 128])
    c_mrevS = din("c_mrevS", [128, 128]); c_bsumS = din("c_bsumS", [128, 4])
    c_cmaskS = din("c_cmaskS", [128, 4, 128]); c_rmaskS = din("c_rmaskS", [128, 4])
    c_idrep = din("c_idrep", [128, 512]); c_idrepS = din("c_idrepS", [128, 4, 64])
    c_selS = din("c_selS", [128, 4, 128])
    c_ctab = din("c_ctab", [128, KIT + 1])
    c_tbias = din("c_tbias", [128, 2, 640])
    c_onehot = din("c_onehot", [128, 4])
    c_tailmask = din("c_tailmask", [128, 1])

    y_own = dout("y_own", [G * 128, D])
    kp = dout("kp", [NB * 128, 256]); vp = dout("vp", [NB * 128, 256]); ikp = dout("ikp", [NB * 128, 64])
    gla_p = dout("gla_p", [128, 1024])
    ys = dout("ys", [128, D]); ks = dout("ks", [128, 256]); vs = dout("vs", [128, 256]); iks = dout("iks", [128, 64])
    gla_s = dout("gla_s", [4, 128, 1024])

    wbf = dscr("wbf", [D, IN_COLS], BF16)
    wbf3 = dscr("wbf3", [3, D, D], BF16)
    kT_d = dscr("kT_d", [128, 2, NKMAX], BF16)
    v_d = dscr("v_d", [128, NB, 260], BF16)
    ki_d = dscr("ki_d", [128, NKMAX], BF16)
    snap = dscr("snap", [NB, 128, 1024], F32)

    k = KB(nc)
    P = k.P
    top = ExitStack()
    with top:
        pst = [top.enter_context(nc.psum_tensor("ps%d" % i, [128, 512], F32)) for i in range(8)]

        def PSF(i):
            return pst[i][:]

        def PSB(i):
            return pst[i][:].bitcast(BF16)

        if "0" in phases:
            with ExitStack() as es:
                def sb(name, shape, dt):
                    return es.enter_context(nc.sbuf_tensor(name, shape, dt))
                wst = [sb("wst%d" % s, [128, 8, 512], F32) for s in range(2)]
                wcb = [sb("wcb%d" % s, [128, 8, 512], BF16) for s in range(2)]
                jobs = []
                for c in range(17):
                    c0 = c * 512
                    n = min(512, IN_COLS - c0)
                    jobs.append((w_in[:, c0:c0 + n], wbf[:, c0:c0 + n], n))
                for m in range(3):
                    for c in range(2):
                        jobs.append((w3[m, :, c * 512:(c + 1) * 512], wbf3[m, :, c * 512:(c + 1) * 512], 512))
                engs = ['dve', 'act', 'pool']
                for idx, (src, dst, n) in enumerate(jobs):
                    s = idx % 2
                    k.dma('sp', wst[s][:, :, :n], src.rearrange("(k p) n -> p k n", p=128), [], ['wst%d' % s], 'wst%d' % s)
                    k.cp(engs[idx % 3], wcb[s][:, :, :n], wst[s][:, :, :n], ['wst%d' % s], ['wcb%d' % s])
                    k.dma('act', dst.rearrange("(k p) n -> p k n", p=128), wcb[s][:, :, :n], ['wcb%d' % s], [], 'wcb%d' % s)
            P.barrier()

        def layernorm(src, srckey, gB, bB, tl, tag, out_f32=None, out_f32_key=None, out_bf=None, out_bf_key=None):
            tk = lambda n: tag + n
            for c in range(2):
                k.bn_stats(tl['st'][:, c, :], src[:, c * 512:(c + 1) * 512], [srckey], [tk('st%d' % c)])
            k.bn_aggr(tl['mv'][:], tl['st'][:].rearrange("p a b -> p (a b)"), [tk('st0'), tk('st1')], [tk('mv')])
            k.act(tl['sd'][:], tl['mv'][:, 1:2], AF.Ln, [tk('mv'), 'eps'], [tk('sd')], bias=tl['eps'][:, 0:1], scale=1.0)
            k.act(tl['rstd'][:], tl['sd'][:], AF.Exp, [tk('sd')], [tk('rstd')], scale=-0.5)
            k.ts(tl['nmr'][:], tl['mv'][:, 0:1], tl['rstd'][:, 0:1], -1.0, ALU.mult, ALU.mult, [tk('mv'), tk('rstd')], [tk('nmr')])
            k.act(tl['xn'][:], src, AF.Identity, [srckey, tk('nmr'), tk('rstd')], [tl['xnkey']],
                  bias=tl['nmr'][:, 0:1], scale=tl['rstd'][:, 0:1])
            k.tt(tl['xn'][:], tl['xn'][:], gB, ALU.mult, [tl['xnkey'], 'lnconst'], [tl['xnkey']])
            if out_f32 is not None:
                k.tt(out_f32, tl['xn'][:], bB, ALU.add, [tl['xnkey'], 'lnconst'], [out_f32_key])
                if out_bf is not None:
                    k.cp('pool', out_bf, out_f32, [out_f32_key], [out_bf_key])
            else:
                k.tt(out_bf, tl['xn'][:], bB, ALU.add, [tl['xnkey'], 'lnconst'], [out_bf_key])

        if "A" in phases:
            with ExitStack() as es:
                def sb(name, shape, dt):
                    return es.enter_context(nc.sbuf_tensor(name, shape, dt))
                gB = sb("a_gB", [128, D], F32); bB = sb("a_bB", [128, D], F32)
                identf = sb("a_idf", [128, 128], F32); identb = sb("a_idb", [128, 128], BF16)
                triA = sb("a_triA", [128, 128], F32); blkA = sb("a_blkA", [128, 2], F32)
                w2 = sb("a_w2", [16, 512], F32); gbias = sb("a_gbias", [1, 512], F32); ones1 = sb("a_ones1", [1, 128], F32)
                gkiB = sb("a_gkiB", [128, 64], F32); bkiB = sb("a_bkiB", [128, 64], F32)
                eps = sb("a_eps", [128, 1], F32); one = sb("a_one", [128, 1], F32)
                tailm = sb("a_tailm", [128, 1], F32)
                wA = sb("a_wA", [128, 8, 2128], BF16)
                SS = [sb("a_S%d" % s, [128, 4, 256], F32) for s in range(3)]
                xa = [sb("a_xa%d" % s, [128, D], F32) for s in range(3)]
                xn = sb("a_xn", [128, D], F32)
                hb = [sb("a_hb%d" % s, [128, D], BF16) for s in range(3)]
                hT = [sb("a_hT%d" % s, [128, 8, 128], BF16) for s in range(2)]
                st = sb("a_st", [128, 2, 6], F32); mv = sb("a_mv", [128, 2], F32)
                sd = sb("a_sd", [128, 1], F32); rstd = sb("a_rstd", [128, 1], F32); nmr = sb("a_nmr", [128, 1], F32)
                st2 = sb("a_st2", [128, 6], F32); mv2 = sb("a_mv2", [128, 2], F32)
                sd2 = sb("a_sd2", [128, 1], F32); rstd2 = sb("a_rstd2", [128, 1], F32); nmr2 = sb("a_nmr2", [128, 1], F32)
                Vt = [sb("a_V%d" % s, [128, 1024], BF16) for s in range(4)]
                kdv = [sb("a_kdv%d" % s, [128, 512], F32) for s in range(3)]
                kdb = [sb("a_kdb%d" % s, [128, 256], BF16) for s in range(2)]
                vext = [sb("a_vext%d" % s, [128, 4, 65], BF16) for s in range(2)]
                kTt = [sb("a_kT%d" % s, [128, 2, 128], BF16) for s in range(2)]
                kin = [sb("a_kin%d" % s, [128, 64], F32) for s in range(3)]
                ksb = [sb("a_ksb%d" % s, [128, 512], F32) for s in range(3)]
                kif = [sb("a_kif%d" % s, [128, 64], F32) for s in range(2)]
                kib = [sb("a_kib%d" % s, [128, 128], BF16) for s in range(2)]
                kiT = [sb("a_kiT%d" % s, [128, 128], BF16) for s in range(2)]
                glb = sb("a_glb", [16, 128], BF16); w2b = sb("a_w2b", [16, 512], BF16); gbB = sb("a_gbB", [128, 512], F32)
                el = [sb("a_el%d" % s, [128, 512], F32) for s in range(3)]
                er = sb("a_er", [128, 512], F32)
                Kt = [sb("a_Kt%d" % s, [128, 512], BF16) for s in range(2)]
                dec = [sb("a_dec%d" % s, [128, 8], F32) for s in range(2)]

                k.dma('sp', gB[:], ln_in_g.partition_broadcast(128), [], ['lnconst0'], 'ca')
                k.dma('sp', bB[:], ln_in_b.partition_broadcast(128), [], ['lnconst1'], 'ca')
                k.dma('sp', identf[:], c_ident, [], ['identf'], 'ca')
                k.dma('sp', triA[:], c_triA, [], ['triA'], 'ca')
                k.dma('sp', blkA[:], c_blkA, [], ['blkA'], 'ca')
                k.dma('sp', w2[:], gla_w2, [], ['w2'], 'ca')
                k.dma('sp', gbias[:], gla_gate_b, [], ['gbias'], 'ca')
                k.dma('sp', gbB[:], gla_gate_b.partition_broadcast(128), [], ['gbB'], 'ca')
                k.dma('sp', gkiB[:], idx_kn_g.partition_broadcast(128), [], ['gkiB'], 'ca')
                k.dma('sp', bkiB[:], idx_kn_b.partition_broadcast(128), [], ['bkiB'], 'ca')
                k.dma('sp', tailm[:], c_tailmask, [], ['tailm'], 'ca')
                wmap = [(C_GK, 512, 0), (C_GV, 1024, 512), (C_DK, 512, 1536), (C_IK, 64, 2048), (C_GLOW, 16, 2112)]
                for (c0, n, o) in wmap:
                    k.dma('sp', wA[:, :, o:o + n], wbf[:, c0:c0 + n].rearrange("(k p) n -> p k n", p=128), [], ['wA%d' % o], 'ca')
                P.barrier()
                k.memset('dve', eps[:], EPS, ['eps'])
                k.memset('dve', one[:], 1.0, ['one'])
                k.memset('dve', ones1[:], 1.0, ['ones1'])
                k.memset('dve', SS[0][:], 0.0, ['S0'])
                for s in range(2):
                    k.memset('pool', vext[s][:], 1.0, ['vext%d' % s])
                k.cp('dve', identb[:], identf[:], [], ['identb'])
                k.cp('dve', w2b[:], w2[:], [], ['w2b'])
                P.barrier()
                tl = dict(st=st, mv=mv, sd=sd, rstd=rstd, nmr=nmr, xn=xn, eps=eps, xnkey='xn')

                def loadx(i):
                    s = i % 3
                    k.dma('sp', xa[s][:], xall[i * 128:(i + 1) * 128, :], [], ['xa%d' % s], 'xa%d' % s)

                loadx(0)

                def fa(i):
                    s = i % 3
                    if i + 1 < NB:
                        loadx(i + 1)
                    layernorm(xa[s][:], 'xa%d' % s, gB[:], bB[:], tl, 'a', out_bf=hb[s][:], out_bf_key='hb%d' % s)

                def fb_a(i):
                    s = i % 3
                    s2 = i % 2
                    for kc in range(8):
                        k.tr(PSB(0)[:, kc * 128:(kc + 1) * 128], hb[s][:, kc * 128:(kc + 1) * 128], identb[:], ['hb%d' % s], ['ps0'])
                    k.cp('act', hT[s2][:].rearrange("p a b -> p (a b)"), PSB(0), ['ps0'], ['hT%d' % s2])

                def fb_b(i):
                    s = i % 3
                    s2 = i % 2
                    last = (i == NB - 1)
                    hk = 'hT%d' % s2
                    for kc in range(8):
                        k.mm(PSF(1), hT[s2][:, kc, :], wA[:, kc, 0:512], kc == 0, kc == 7, [hk], ['ps1'])
                    k.cp('act', ksb[s][:], PSF(1), ['ps1'], ['ksb%d' % s])
                    for half in range(2):
                        for kc in range(8):
                            k.mm(PSF(2 + half), hT[s2][:, kc, :], wA[:, kc, 512 + half * 512:1024 + half * 512], kc == 0, kc == 7, [hk], ['ps%d' % (2 + half)])
                        k.cp('act' if half == 0 else 'dve', Vt[i % 4][:, half * 512:(half + 1) * 512], PSF(2 + half), ['ps%d' % (2 + half)], ['V%d' % (i % 4)])
                    for kc in range(8):
                        k.mm(PSF(4), hT[s2][:, kc, :], wA[:, kc, 1536:2048], kc == 0, kc == 7, [hk], ['ps4'])
                    k.cp('act', kdv[s][:], PSF(4), ['ps4'], ['kdv%d' % s])
                    for kc in range(8):
                        k.mm(PSF(5)[:, 0:64], hT[s2][:, kc, :], wA[:, kc, 2048:2112], kc == 0, kc == 7, [hk], ['ps5'])
                    k.cp('dve', kin[s][:], PSF(5)[:, 0:64], ['ps5'], ['kin%d' % s])

                    for kc in range(8):
                        k.mm(PSF(5)[0:16, 64:192], wA[:, kc, 2112:2128], hT[s2][:, kc, :], kc == 0, kc == 7, [hk], ['ps5'])
                    k.cp('dve', glb[:], PSF(5)[0:16, 64:192], ['ps5'], ['gl'])
                    k.mm(PSF(4), glb[:], w2b[:], True, True, ['gl'], ['ps4'])
                    k.tt(el[s][:], PSF(4), gbB[:], ALU.add, ['ps4'], ['el%d' % s])
                    k.act(el[s][:], el[s][:], AF.Exp, ['el%d' % s], ['el%d' % s], scale=-1.0)
                    k.act(el[s][:], el[s][:], AF.Ln, ['el%d' % s], ['el%d' % s], bias=one[:, 0:1], scale=1.0)
                    if last:
                        k.ts(el[s][:], el[s][:], tailm[:, 0:1], None, ALU.mult, None, ['el%d' % s], ['el%d' % s])
                def back1(i):
                    s3 = i % 3
                    s = i % 2
                    last = (i == NB - 1)
                    k.mm(PSF(6), triA[:], el[s3][:], True, True, ['el%d' % s3], ['ps6'])
                    for h in range(4):
                        k.mm(PSF(7)[:, 448 + 2 * h:450 + 2 * h], el[s3][:, h * 128:(h + 1) * 128], blkA[:], True, True, ['el%d' % s3], ['ps7'])
                    k.act(er[:], PSF(6), AF.Exp, ['ps6'], ['er'])
                    k.act(dec[s][:], PSF(7)[:, 448:456], AF.Exp, ['ps7'], ['dec%d' % s])
                    if last:
                        k.stt(Kt[s][:], ksb[s3][:], tailm[:, 0:1], er[:], ALU.mult, ALU.mult, ['ksb%d' % s3, 'er'], ['Kt%d' % s])
                    else:
                        k.tt(Kt[s][:], ksb[s3][:], er[:], ALU.mult, ['ksb%d' % s3, 'er'], ['Kt%d' % s])
                    k.dma('act', kp[i * 128:(i + 1) * 128, :], kdv[s3][:, 0:256], ['kdv%d' % s3], [], 'kdvo%d' % s3)
                    k.dma('act', vp[i * 128:(i + 1) * 128, :], kdv[s3][:, 256:512], ['kdv%d' % s3], [], 'kdvo%d' % s3)
                    k.cp('pool', kdb[s][:], kdv[s3][:, 0:256], ['kdv%d' % s3], ['kdb%d' % s])
                    k.cp('pool', vext[s][:, :, 0:64], kdv[s3][:, 256:512].rearrange("p (g d) -> p g d", g=4), ['kdv%d' % s3], ['vext%d' % s])
                    for c in range(2):
                        k.tr(PSB(7)[:, c * 128:(c + 1) * 128], kdb[s][:, c * 128:(c + 1) * 128], identb[:], ['kdb%d' % s], ['ps7'])
                    k.cp('dve', kTt[s][:].rearrange("p a b -> p (a b)"), PSB(7)[:, 0:256], ['ps7'], ['kT%d' % s])
                    k.dma('pool', kT_d[:, :, i * 128:(i + 1) * 128], kTt[s][:], ['kT%d' % s], [], 'kTo%d' % s)
                    k.dma('pool', v_d[:, i, :], vext[s][:].rearrange("p g d -> p (g d)"), ['vext%d' % s], [], 'vexto%d' % s)
                    kn = kin[s3]
                    knk = 'kin%d' % s3
                    k.bn_stats(st2[:], kn[:], [knk], ['st2'])
                    k.bn_aggr(mv2[:], st2[:], ['st2'], ['mv2'])
                    k.act(sd2[:], mv2[:, 1:2], AF.Ln, ['mv2'], ['sd2'], bias=eps[:, 0:1], scale=1.0)
                    k.act(rstd2[:], sd2[:], AF.Exp, ['sd2'], ['rstd2'], scale=-0.5)
                    k.ts(nmr2[:], mv2[:, 0:1], rstd2[:, 0:1], -1.0, ALU.mult, ALU.mult, ['mv2', 'rstd2'], ['nmr2'])
                    k.act(kn[:], kn[:], AF.Identity, [knk, 'nmr2', 'rstd2'], [knk], bias=nmr2[:, 0:1], scale=rstd2[:, 0:1])
                    k.tt(kn[:], kn[:], gkiB[:], ALU.mult, [knk], [knk])
                    k.tt(kif[s][:], kn[:], bkiB[:], ALU.add, [knk], ['kif%d' % s])
                    k.dma('act', ikp[i * 128:(i + 1) * 128, :], kif[s][:], ['kif%d' % s], [], 'kifo%d' % s)
                    k.cp('pool', kib[s][:, 0:64], kif[s][:], ['kif%d' % s], ['kib%d' % s])
                    k.cp('pool', kib[s][:, 64:128], kif[s][:], ['kif%d' % s], ['kib%d' % s])
                    k.tr(PSB(7)[:, 256:384], kib[s][:], identb[:], ['kib%d' % s], ['ps7'])
                    k.cp('dve', kiT[s][:], PSB(7)[:, 256:384], ['ps7'], ['kiT%d' % s])
                    k.dma('pool', ki_d[:, i * 128:(i + 1) * 128], kiT[s][:], ['kiT%d' % s], [], 'kiTo%d' % s)

                def back2(i):
                    s3 = i % 3
                    s = i % 2
                    cur = (2 * i) % 3
                    k.dma('sp', snap[i], SS[cur][:].rearrange("p h e -> p (h e)"), ['S%d' % cur], [], 'Ssto%d' % cur)
                    sbanks = [[6, 7], [6, 7]]
                    for c in range(2):
                        for hp in range(2):
                            bk = sbanks[c][hp]
                            for hh in range(2):
                                h = hp * 2 + hh
                                k.mm(PSF(bk)[:, hh * 256:(hh + 1) * 256], Kt[s][c * 64:(c + 1) * 64, h * 128:(h + 1) * 128],
                                     Vt[i % 4][c * 64:(c + 1) * 64, h * 256:(h + 1) * 256], True, True, ['Kt%d' % s, 'V%d' % (i % 4)], ['ps%d' % bk])
                            src_, dst_ = (2 * i + c) % 3, (2 * i + c + 1) % 3
                            for hh in range(2):
                                h = hp * 2 + hh
                                k.stt(SS[dst_][:, h, :], SS[src_][:, h, :], dec[s][:, 2 * h + c:2 * h + c + 1], PSF(bk)[:, hh * 256:(hh + 1) * 256],
                                      ALU.mult, ALU.add, ['S%d' % src_, 'dec%d' % s, 'ps%d' % bk], ['S%d' % dst_])

                for i0 in range(min(3, NB)):
                    fa(i0)
                for i0 in range(3):
                    if i0 < NB:
                        fb_a(i0)
                        fb_b(i0)
                    if i0 + 3 < NB and i0 < 2:
                        fa(i0 + 3)
                back1(0)
                for i in range(NB):
                    if i + 3 < NB:
                        fb_a(i + 3)
                    back2(i)
                    if i + 1 < NB:
                        back1(i + 1)
                    if i + 3 < NB:
                        fb_b(i + 3)
                    if i + 5 < NB:
                        fa(i + 5)
                fin = (2 * NB) % 3
                k.dma('sp', gla_p, SS[fin][:].rearrange("p h e -> p (h e)"), ['S%d' % fin], [], 'glap')
            P.barrier()


        def phase_own(mode):
            PR = (mode == 'P')
            NK = NKMAX if PR else 2176
            with ExitStack() as es:
                def sb(name, shape, dt):
                    return es.enter_context(nc.sbuf_tensor(mode + name, shape, dt))
                gB = sb("gB", [128, D], F32); bB = sb("bB", [128, D], F32)
                g2B = sb("g2B", [128, D], F32); b2B = sb("b2B", [128, D], F32)
                gtb = [sb("gtb%d" % s_, [128, 512], F32) for s_ in range(2)]
                gnB = sb("gnB", [128, 256], F32)
                identf = sb("idf", [128, 128], F32); identb = sb("idb", [128, 128], BF16)
                tri = sb("tri", [128, 128], F32)
                maskf = sb("maskf", [128, 512], F32)
                cst = sb("cst", [128, 512], F32); idrepb = sb("idrepb", [128, 512], BF16)
                w2 = sb("w2", [16, 512], F32); gbias = sb("gbias", [1, 512], F32); ones1 = sb("ones1", [1, 128], F32)
                eps = sb("eps", [128, 1], F32); one = sb("one", [128, 1], F32)
                ctab = sb("ctab", [128, KIT + 1], F32)
                tbias = sb("tbias", [128, 2, 640 if PR else 128], F32)
                onehot = sb("onehot", [128, 4], F32)
                xo = sb("xo", [128, D], F32); h = sb("h", [128, D], F32); hb = sb("hb", [128, D], BF16)
                hT = sb("hT", [128, 8, 128], BF16)
                tmp = sb("tmp", [128, D], F32)
                st = sb("st", [128, 2, 6], F32); mv = sb("mv", [128, 2], F32)
                sd = sb("sd", [128, 1], F32); rstd = sb("rstd", [128, 1], F32); nmr = sb("nmr", [128, 1], F32)
                wch = [sb("wch%d" % s_, [128, 8, 512], BF16) for s_ in range(2)]
                gl = sb("gl", [16, 128], F32)
                el = sb("el", [128, 512], F32); eb = sb("eb", [128, 512], F32); enb = sb("enb", [128, 512], F32)
                qT = sb("qT", [128, 4, 128], BF16); kTh = sb("kTh", [128, 4, 128], BF16)
                V = sb("V", [128, 1024], BF16)
                sg = [sb("sg%d" % s_, [128, 512], F32) for s_ in range(2)]
                QTz = sb("QTz", [128, 4, 512], BF16); qiTz = sb("qiTz", [128, 8, 128], BF16)
                wabs = sb("wabs", [128, 8], F32); wsgn = sb("wsgn", [128, 8], F32)
                AT = sb("AT", [128, 4, 128], BF16)
                ss = sb("ss", [128, 4], F32); rs = sb("rs", [128, 4], F32)
                yain = sb("yain", [128, D], BF16); yT = sb("yT", [128, 8, 128], BF16)
                mrg = sb("mrg", [128, D], F32)
                sc = sb("sc", [128, NK], F32)
                junk = None if PR else sb("junk", [128, 2176], BF16)
                rl = [sb("rl%d" % s_, [128, 512], F32) for s_ in range(2)]
                rd = sb("rd", [128, 16], F32)
                yout = xo
                rmax = sb("rmax", [128, 1], F32); rmin = sb("rmin", [128, 1], F32); Wd = sb("Wd", [128, 1], F32)
                wtab = sb("wtab", [128, KIT + 1], F32); mids = sb("mids", [128, KIT + 1], F32)
                cnts = sb("cnts", [128, KIT], F32); us = sb("us", [128, KIT], F32); thr = sb("thr", [128, 1], F32)
                sAs = sb("sAs", [128, KIT], F32); vvs = sb("vvs", [128, KIT], F32)
                jd = sb("jd", [128, 8], BF16); ja = sb("ja", [128, 8], BF16); jq = sb("jq", [128, 8], BF16)
                if PR:
                    Sc = [sb("Sc%d" % s_, [128, 1024], F32) for s_ in range(2)]
                    Sown = sb("Sown", [128, 1024], F32); Sb = sb("Sb", [128, 4, 256], BF16)
                    kich = [sb("kich%d" % s_, [128, 512], BF16) for s_ in range(2)]
                    kTch = [sb("kTch%d" % s_, [128, 2, 512], BF16) for s_ in range(2)]
                    vch = [sb("vch%d" % s_, [128, 4, 260], BF16) for s_ in range(2)]
                    mbt = [sb("mbt%d" % s_, [128, 128], BF16) for s_ in range(3)]
                    pT = [sb("pT%d" % s_, [128, 512], BF16) for s_ in range(3)]
                    oT = [sb("oT%d" % s_, [65, 512], F32) for s_ in range(2)]
                else:
                    cmaskS = sb("cmaskS", [128, 4, 128], F32); rmaskS = sb("rmaskS", [128, 4], F32)
                    mrevS = sb("mrevS", [128, 128], F32); bsumS = sb("bsumS", [128, 4], F32)
                    idrepSb = sb("idrepSb", [128, 4, 64], BF16)
                    selSb = sb("selSb", [128, 4, 128], BF16)
                    gkiB = sb("gkiB", [128, 64], F32); bkiB = sb("bkiB", [128, 64], F32)
                    S0f = [sb("S0f%d" % b_, [128, 4, 256], F32) for b_ in range(2)]
                    S0b = [sb("S0b%d" % b_, [128, 4, 256], BF16) for b_ in range(4)]
                    qTb = [sb("qTb%d" % b_, [128, 4, 128], BF16) for b_ in range(4)]
                    wabsb = sb("wabsb", [128, 4, 8], F32)
                    QTs = sb("QTs", [128, 4, 4, 64], BF16)
                    kTs1 = sb("kTs", [128, 2, 2176], BF16)
                    kiTs = [sb("kiTs%d" % b_, [128, 2176], BF16) for b_ in range(4)]
                    vexts1 = sb("vexts", [128, 17, 260], BF16)
                    ckf = sb("ckf", [128, 8, 256], F32); ckb = sb("ckb", [128, 8, 256], BF16)
                    cif = sb("cif", [128, 8, 64], F32); cib = sb("cib", [128, 8, 128], BF16)
                    kdv = sb("kdv", [128, 512], F32); kdb = sb("kdb", [128, 256], BF16); vnb = sb("vnb", [128, 256], BF16)
                    kin = sb("kin", [128, 64], F32); kif = sb("kif", [128, 64], F32); kib = sb("kib", [128, 128], BF16)
                    st2 = sb("st2", [128, 6], F32); mv2 = sb("mv2", [128, 2], F32)
                    sd2 = sb("sd2", [128, 1], F32); rstd2 = sb("rstd2", [128, 1], F32); nmr2 = sb("nmr2", [128, 1], F32)
                    kTnew = sb("kTnew", [128, 2, 128], BF16); kiTnew = sb("kiTnew", [128, 128], BF16)
                    Kt = sb("Kt", [128, 512], BF16); Ktb = [sb("Ktb%d" % s_, [128, 512], BF16) for s_ in range(2)]
                    decs = sb("decs", [128, 16], F32)
                    pTs = [sb("pTs%d" % s_, [128, 64], BF16) for s_ in range(3)]

                cl = 'c' + mode
                k.dma('sp', gB[:], ln_in_g.partition_broadcast(128), [], [], cl)
                k.dma('sp', bB[:], ln_in_b.partition_broadcast(128), [], [], cl)
                k.dma('sp', g2B[:], ln_g.partition_broadcast(128), [], [], cl)
                k.dma('sp', b2B[:], ln_b.partition_broadcast(128), [], [], cl)
                k.dma('sp', gnB[:], gla_norm_g.partition_broadcast(128), [], [], cl)
                k.dma('sp', identf[:], c_ident, [], [], cl)
                k.dma('sp', tri[:], c_triB if PR else c_triS, [], [], cl)
                k.dma('sp', maskf[:], (c_maskB if PR else c_maskS).rearrange("p a b -> p (a b)"), [], [], cl)
                k.dma('sp', w2[:], gla_w2, [], [], cl)
                k.dma('sp', gbias[:], gla_gate_b, [], [], cl)
                k.dma('sp', ctab[:], c_ctab, [], [], cl)
                k.dma('sp', tbias[:], c_tbias if PR else c_tbias[:, :, 0:128], [], [], cl)
                k.dma('sp', onehot[:], c_onehot, [], [], cl)
                if not PR:
                    k.dma('sp', cmaskS[:], c_cmaskS, [], [], cl)
                    k.dma('sp', rmaskS[:], c_rmaskS, [], [], cl)
                    k.dma('sp', mrevS[:], c_mrevS, [], [], cl)
                    k.dma('sp', bsumS[:], c_bsumS, [], [], cl)
                    k.dma('sp', gkiB[:], idx_kn_g.partition_broadcast(128), [], [], cl)
                    k.dma('sp', bkiB[:], idx_kn_b.partition_broadcast(128), [], [], cl)
                P.barrier()
                k.memset('dve', eps[:], EPS, [])
                k.memset('dve', one[:], 1.0, [])
                k.memset('dve', ones1[:], 1.0, [])
                k.memset('pool', QTz[:], 0.0, [])
                k.memset('pool', qiTz[:], 0.0, [])
                k.memset('pool', yain[:], 0.0, [])
                k.memset('pool', tmp[:], 0.0, [])
                k.cp('dve', identb[:], identf[:], [], [])
                k.dma('sp', cst[:], c_idrep, [], ['cst'], 'cst')
                k.cp('dve', idrepb[:], cst[:], ['cst'], ['idrepb'])
                if not PR:
                    k.dma('sp', cst[:, 0:256], c_idrepS.rearrange("p a b -> p (a b)"), ['idrepb'], ['cst'], 'cst')
                    k.cp('dve', idrepSb[:].rearrange("p a b -> p (a b)"), cst[:, 0:256], ['cst'], ['idrepSb'])
                    k.dma('sp', cst[:], c_selS.rearrange("p a b -> p (a b)"), ['idrepSb'], ['cst'], 'cst')
                    k.cp('dve', selSb[:].rearrange("p a b -> p (a b)"), cst[:], ['cst'], ['selSb'])
                    for b_ in range(4):
                        s_ = b_ % 2
                        k.dma('sp', S0f[s_][:], state[b_].rearrange("h p e -> p h e"), [], ['S0f%d' % s_], 'S0f%d' % s_)
                        k.cp('pool', S0b[b_][:], S0f[s_][:], ['S0f%d' % s_], ['S0b%d' % b_])
                        k.memset('pool', kiTs[b_][:, 2048:2176], 0.0, [])
                    k.memset('pool', vexts1[:], 1.0, [])
                    k.memset('pool', kTs1[:, :, 2048:2176], 0.0, [])
                P.barrier()
                tl = dict(st=st, mv=mv, sd=sd, rstd=rstd, nmr=nmr, xn=tmp, eps=eps, xnkey='tmp')
                bank = [0]

                def nb():
                    bank[0] = (bank[0] + 1) % 8
                    return bank[0]

                def own_block(g):
                    jobs = []

                    def J(src, n):
                        jobs.append((src, n))
                    J(wbf[:, C_IW:C_IW + 8], 8)
                    J(wbf[:, C_IQ:C_IQ + 512], 512)
                    J(wbf[:, C_DQ:C_DQ + 512], 512); J(wbf[:, C_DQ + 512:C_DQ + 1024], 512)
                    if not PR:
                        J(wbf[:, C_DK:C_DK + 512], 512)
                        J(wbf[:, C_IK:C_IK + 64], 64)
                    J(wbf[:, C_GLOW:C_GLOW + 16], 16)
                    J(wbf[:, C_GQ:C_GQ + 512], 512)
                    J(wbf[:, C_GK:C_GK + 512], 512)
                    J(wbf[:, C_GV:C_GV + 512], 512); J(wbf[:, C_GV + 512:C_GV + 1024], 512)
                    J(wbf[:, C_GR:C_GR + 512], 512); J(wbf[:, C_GR + 512:C_GR + 1024], 512)
                    for cc in range(2):
                        J(wbf3[0, :, cc * 512:(cc + 1) * 512], 512)
                        J(wbf[:, C_MA + cc * 512:C_MA + (cc + 1) * 512], 512)
                    J(wbf[:, C_DZ:C_DZ + 512], 512); J(wbf[:, C_DZ + 512:C_DZ + 1024], 512)
                    for cc in range(2):
                        J(wbf3[1, :, cc * 512:(cc + 1) * 512], 512)
                        J(wbf[:, C_MB + cc * 512:C_MB + (cc + 1) * 512], 512)
                    for cc in range(2):
                        J(wbf3[2, :, cc * 512:(cc + 1) * 512], 512)
                    jpos = [0]

                    def wissue(idx):
                        src, n = jobs[idx]
                        s_ = idx % 2
                        k.dma('sp', wch[s_][:, :, :n], src.rearrange("(k p) n -> p k n", p=128), [], ['wch%d' % s_], 'wch%d' % s_)

                    def next_w():
                        idx = jpos[0]
                        if idx == 0:
                            wissue(0)
                        if idx + 1 < len(jobs):
                            wissue(idx + 1)
                        jpos[0] += 1
                        return wch[idx % 2], 'wch%d' % (idx % 2)

                    def projT(wt, wk, n, psap, pskey):
                        for kc in range(8):
                            k.mm(psap, hT[:, kc, :], wt[:, kc, :n], kc == 0, kc == 7, ['hT', wk], [pskey])

                    def projF(wt, wk, bk):
                        for sub in range(4):
                            for kc in range(8):
                                k.mm(PSF(bk)[:, sub * 128:(sub + 1) * 128], wt[:, kc, sub * 128:(sub + 1) * 128], hT[:, kc, :],
                                     kc == 0, kc == 7, ['hT', wk], ['ps%d' % bk])

                    def transpose8(src, srckey, dst, dstkey):
                        bk = nb()
                        for kc in range(8):
                            k.tr(PSB(bk)[:, kc * 128:(kc + 1) * 128], src[:, kc * 128:(kc + 1) * 128], identb[:], [srckey], ['ps%d' % bk])
                        k.cp('act', dst[:].rearrange("p a b -> p (a b)"), PSB(bk), ['ps%d' % bk], [dstkey])

                    xsrc = xown[g * 128:(g + 1) * 128, :] if PR else xs
                    k.dma('act', xo[:], xsrc, [], ['xo'], 'xo')
                    layernorm(xo[:], 'xo', gB[:], bB[:], tl, 'b', out_f32=h[:], out_f32_key='h', out_bf=hb[:], out_bf_key='hb')
                    transpose8(hb, 'hb', hT, 'hT')
                    wt, wk = next_w()
                    bw = nb()
                    projT(wt, wk, 8, PSF(bw)[:, 0:8], 'ps%d' % bw)
                    k.ts(wabs[:], PSF(bw)[:, 0:8], -IDX_W_SCALE, None, ALU.mult, None, ['ps%d' % bw], ['wabs'])
                    k.stt(wabs[:], PSF(bw)[:, 0:8], IDX_W_SCALE, wabs[:], ALU.mult, ALU.max, ['ps%d' % bw, 'wabs'], ['wabs'])
                    k.ts(wsgn[:], PSF(bw)[:, 0:8], 0.0, 2.0, ALU.is_ge, ALU.mult, ['ps%d' % bw], ['wsgn'])
                    k.ts(wsgn[:], wsgn[:], -1.0, None, ALU.add, None, ['wsgn'], ['wsgn'])
                    wt, wk = next_w()
                    bi = nb()
                    projF(wt, wk, bi)
                    qv = qiTz[:].rearrange("p (s two) t -> p s two t", two=2)
                    pv = PSF(bi).rearrange("p (s t) -> p s t", s=4)
                    k.cp('act', qv[0:64, :, 0, :], pv[0:64, :, :], ['ps%d' % bi], ['qiTz'])
                    k.cp('act', qv[64:128, :, 1, :], pv[64:128, :, :], ['ps%d' % bi], ['qiTz'])
                    for m in range(2):
                        wt, wk = next_w()
                        bdq = nb()
                        projF(wt, wk, bdq)
                        k.cp('act', QTz[0:64, 2 * m, :], PSF(bdq)[0:64, :], ['ps%d' % bdq], ['QTz'])
                        k.cp('act', QTz[64:128, 2 * m + 1, :], PSF(bdq)[64:128, :], ['ps%d' % bdq], ['QTz'])
                    if not PR:
                        wt, wk = next_w()
                        bkv = nb()
                        projT(wt, wk, 512, PSF(bkv), 'ps%d' % bkv)
                        k.cp('act', kdv[:], PSF(bkv), ['ps%d' % bkv], ['kdv'])
                        k.dma('act', ks, kdv[:, 0:256], ['kdv'], [], 'so1')
                        k.dma('act', vs, kdv[:, 256:512], ['kdv'], [], 'so1')
                        k.cp('pool', kdb[:], kdv[:, 0:256], ['kdv'], ['kdb'])
                        k.cp('pool', vnb[:], kdv[:, 256:512], ['kdv'], ['vnb'])
                        wt, wk = next_w()
                        bik = nb()
                        projT(wt, wk, 64, PSF(bik)[:, 0:64], 'ps%d' % bik)
                        k.cp('dve', kin[:], PSF(bik)[:, 0:64], ['ps%d' % bik], ['kin'])
                        k.bn_stats(st2[:], kin[:], ['kin'], ['st2'])
                        k.bn_aggr(mv2[:], st2[:], ['st2'], ['mv2'])
                        k.act(sd2[:], mv2[:, 1:2], AF.Ln, ['mv2'], ['sd2'], bias=eps[:, 0:1], scale=1.0)
                        k.act(rstd2[:], sd2[:], AF.Exp, ['sd2'], ['rstd2'], scale=-0.5)
                        k.ts(nmr2[:], mv2[:, 0:1], rstd2[:, 0:1], -1.0, ALU.mult, ALU.mult, ['mv2', 'rstd2'], ['nmr2'])
                        k.act(kin[:], kin[:], AF.Identity, ['kin', 'nmr2', 'rstd2'], ['kin'], bias=nmr2[:, 0:1], scale=rstd2[:, 0:1])
                        k.tt(kin[:], kin[:], gkiB[:], ALU.mult, ['kin'], ['kin'])
                        k.tt(kif[:], kin[:], bkiB[:], ALU.add, ['kin'], ['kif'])
                        k.dma('act', iks, kif[:], ['kif'], [], 'so1')
                        k.cp('pool', kib[:, 0:64], kif[:], ['kif'], ['kib'])
                        k.cp('pool', kib[:, 64:128], kif[:], ['kif'], ['kib'])
                        bt_ = nb()
                        for c_ in range(2):
                            k.tr(PSB(bt_)[:, c_ * 128:(c_ + 1) * 128], kdb[:, c_ * 128:(c_ + 1) * 128], identb[:], ['kdb'], ['ps%d' % bt_])
                        k.tr(PSB(bt_)[:, 256:384], kib[:], identb[:], ['kib'], ['ps%d' % bt_])
                        k.cp('dve', kTnew[:].rearrange("p a b -> p (a b)"), PSB(bt_)[:, 0:256], ['ps%d' % bt_], ['kTnew'])
                        k.cp('dve', kiTnew[:], PSB(bt_)[:, 256:384], ['ps%d' % bt_], ['kiTnew'])
                        for b_ in range(4):
                            k.cp('pool', kiTs[b_][:, 2048:2064], kiTnew[:, 16 * b_:16 * b_ + 16], ['kiTnew'], ['kiTs%d' % b_])
                            for t8 in range(2):
                                k.dma('sp', cif[:], cache_ik[b_, t8 * 1024:(t8 + 1) * 1024, :].rearrange("(t p) c -> p t c", p=128), [], ['cif'], 'cif')
                                k.cp('pool', cib[:, :, 0:64], cif[:], ['cif'], ['cib'])
                                k.cp('pool', cib[:, :, 64:128], cif[:], ['cif'], ['cib'])
                                bt_ = nb()
                                for tt_ in range(8):
                                    k.tr(PSB(bt_)[:, tt_ * 128:(tt_ + 1) * 128], cib[:, tt_, :], identb[:], ['cib'], ['ps%d' % bt_])
                                k.cp('dve', kiTs[b_][:, t8 * 1024:(t8 + 1) * 1024], PSB(bt_), ['ps%d' % bt_], ['kiTs%d' % b_])
                    if PR:
                        n_tiles = min(4 * g + 5, NB)
                        tail0 = 4 * g
                        tidx = 1 if g == G - 1 else 0
                    else:
                        n_tiles = 17
                        tail0 = 16
                        tidx = 1
                    n_keys = n_tiles * 128
                    nch = (n_tiles + 3) // 4

                    def kiload(ci):
                        k0 = ci * 512
                        w_ = min(512, n_keys - k0)
                        s_ = ci % 2
                        k.dma('sp', kich[s_][:, :w_], ki_d[:, k0:k0 + w_], [], ['kich%d' % s_], 'kich%d' % s_)

                    if PR:
                        kiload(0)
                    if not PR:
                        for b_ in range(4):
                            k.ts(wabsb[:, b_, :], wabs[:], rmaskS[:, b_:b_ + 1], None, ALU.mult, None, ['wabs'], ['wabsb'])
                    rli = 0
                    for ci in range(nch):
                        k0 = ci * 512
                        w_ = min(512, n_keys - k0)
                        if PR and ci + 1 < nch:
                            kiload(ci + 1)
                        first = True
                        for hh in range(8):
                            for b_ in (range(1) if PR else range(4)):
                                bx = nb()
                                if PR:
                                    k.mm(PSF(bx)[:, :w_], qiTz[:, hh, :], kich[ci % 2][:, :w_], True, True, ['qiTz', 'kich%d' % (ci % 2)], ['ps%d' % bx])
                                    scl = wabs[:, hh:hh + 1]
                                    sck = 'wabs'
                                else:
                                    k.mm(PSF(bx)[:, :w_], qiTz[:, hh, :], kiTs[b_][:, k0:k0 + w_], True, True, ['qiTz', 'kiTs%d' % b_], ['ps%d' % bx])
                                    scl = wabsb[:, b_, hh:hh + 1]
                                    sck = 'wabsb'
                                r_ = rl[rli % 2]
                                rk = 'rl%d' % (rli % 2)
                                rli += 1
                                k.act(r_[:, :w_], PSF(bx)[:, :w_], AF.Relu, ['ps%d' % bx, sck], [rk], scale=scl)
                                if first:
                                    k.ts(sc[:, k0:k0 + w_], r_[:, :w_], wsgn[:, hh:hh + 1], None, ALU.mult, None, [rk, 'wsgn'], ['sc'])
                                    first = False
                                else:
                                    k.stt(sc[:, k0:k0 + w_], r_[:, :w_], wsgn[:, hh:hh + 1], sc[:, k0:k0 + w_], ALU.mult, ALU.add, [rk, 'wsgn', 'sc'], ['sc'])
                    k.capture()
                    k.red(rmax[:], sc[:, 0:n_keys], ALU.max, ['sc'], ['rmax'])
                    k.red(rmin[:], sc[:, 0:n_keys], ALU.min, ['sc'], ['rmin'])
                    tw = (n_tiles - tail0) * 128
                    k.tt(sc[:, tail0 * 128:tail0 * 128 + tw], sc[:, tail0 * 128:tail0 * 128 + tw], tbias[:, tidx, 0:tw], ALU.add, ['sc'], ['sc'])
                    k.tt(Wd[:], rmax[:], rmin[:], ALU.subtract, ['rmax', 'rmin'], ['Wd'])
                    k.ts(wtab[:], ctab[:], Wd[:, 0:1], None, ALU.mult, None, ['Wd'], ['wtab'])
                    k.tt(mids[:, 0:1], rmin[:], wtab[:, 0:1], ALU.add, ['rmin', 'wtab'], ['mid0'])
                    nD = (int(n_keys * 0.5) // 128) * 128
                    if nD < 256:
                        nD = n_keys
                    nA = n_keys - nD
                    for it in range(1, KIT + 1):
                        mid = mids[:, it - 1:it]
                        mk = 'mid%d' % (it - 1)
                        cn = cnts[:, it - 1:it]
                        ck_ = 'cnt%d' % it
                        k.ts(jd[:, 0:1].to_broadcast([128, nD]), sc[:, 0:nD], mid, None, ALU.is_ge, ALU.add, ['sc', mk], ['jd', ck_], accum=cn)
                        if nA > 0:
                            k.act(ja[:, 0:1].to_broadcast([128, nA]), sc[:, nD:n_keys], AF.Sign, ['sc', mk], ['ja', 'sa%d' % it],
                                  bias=mid, scale=-1.0, accum_out=sAs[:, it - 1:it])
                            k.stt(vvs[:, it - 1:it], cn, 2.0, sAs[:, it - 1:it], ALU.mult, ALU.subtract, [ck_, 'sa%d' % it], ['vv%d' % it])
                            vsrc, vkey, vthr = vvs[:, it - 1:it], 'vv%d' % it, 511.5 - nA
                        else:
                            vsrc, vkey, vthr = cn, ck_, 255.5
                        u_ = us[:, it - 1:it]
                        k.ts(u_, vsrc, vthr, wtab[:, it - 1:it], ALU.is_ge, ALU.mult, [vkey, 'wtab'], ['u%d' % it])
                        if it < KIT:
                            k.stt(mids[:, it:it + 1], u_, wtab[:, it:it + 1], mid, ALU.subtract, ALU.add, ['u%d' % it, 'wtab', mk], ['mid%d' % it])
                        else:
                            k.stt(thr[:], u_, wtab[:, it - 1:it], mid, ALU.subtract, ALU.add, ['u%d' % it, 'wtab', mk], ['thr'])
                    bisA = k.end_capture()
                    k.capture()
                    wt, wk = next_w()
                    b1 = nb()
                    for kc in range(8):
                        k.mm(PSF(b1)[0:16, 0:128], wt[:, kc, 0:16], hT[:, kc, :], kc == 0, kc == 7, ['hT', wk], ['ps%d' % b1])
                    k.cp('dve', gl[:], PSF(b1)[0:16, 0:128], ['ps%d' % b1], ['gl'])
                    bz = nb()
                    k.mm(PSF(bz), gl[:], w2[:], True, False, ['gl'], ['ps%d' % bz])
                    k.mm(PSF(bz), ones1[:], gbias[:], False, True, [], ['ps%d' % bz])
                    k.act(el[:], PSF(bz), AF.Exp, ['ps%d' % bz], ['el'], scale=-1.0)
                    k.act(el[:], el[:], AF.Ln, ['el'], ['el'], bias=one[:, 0:1], scale=1.0)
                    bb = nb()
                    for hh in range(4):
                        k.mm(PSF(bb)[:, hh * 128:(hh + 1) * 128], el[:, hh * 128:(hh + 1) * 128], tri[:], True, True, ['el'], ['ps%d' % bb])
                    k.act(eb[:], PSF(bb), AF.Exp, ['ps%d' % bb], ['eb'])
                    k.act(enb[:], PSF(bb), AF.Exp, ['ps%d' % bb], ['enb'], scale=-1.0)
                    wt, wk = next_w()
                    bq = nb()
                    projF(wt, wk, bq)
                    k.stt(qT[:].rearrange("p a b -> p (a b)"), PSF(bq), 128.0 ** -0.5, eb[:], ALU.mult, ALU.mult, ['ps%d' % bq, 'eb'], ['qT'])
                    wt, wk = next_w()
                    bk_ = nb()
                    projF(wt, wk, bk_)
                    k.tt(kTh[:].rearrange("p a b -> p (a b)"), PSF(bk_), enb[:], ALU.mult, ['ps%d' % bk_, 'enb'], ['kTh'])
                    if not PR:
                        bkt = nb()
                        projT(wt, wk, 512, PSF(bkt), 'ps%d' % bkt)
                        br_ = nb()
                        k.mm(PSF(br_), mrevS[:], el[:], True, True, ['el'], ['ps%d' % br_])
                        k.act(sg[1][:], PSF(br_), AF.Exp, ['ps%d' % br_], ['sg1'])
                        k.tt(Kt[:], PSF(bkt), sg[1][:], ALU.mult, ['ps%d' % bkt, 'sg1'], ['Kt'])
                        bd_ = nb()
                        for hh in range(4):
                            k.mm(PSF(bd_)[:, hh * 4:(hh + 1) * 4], el[:, hh * 128:(hh + 1) * 128], bsumS[:], True, True, ['el'], ['ps%d' % bd_])
                        k.act(decs[:], PSF(bd_)[:, 0:16], AF.Exp, ['ps%d' % bd_], ['decs'])
                    for half in range(2):
                        wt, wk = next_w()
                        bv = nb()
                        projT(wt, wk, 512, PSF(bv), 'ps%d' % bv)
                        k.cp('act', V[:, half * 512:(half + 1) * 512], PSF(bv), ['ps%d' % bv], ['V'])
                    if PR:
                        for m in range(4):
                            sidx = min(4 * g + m, NB - 1)
                            s_ = m % 2
                            k.dma('act', Sc[s_][:], snap[sidx], [], ['Sc%d' % s_], 'Sc%d' % s_)
                            if m == 0:
                                k.ts(Sown[:], Sc[s_][:], onehot[:, 0:1], None, ALU.mult, None, ['Sc%d' % s_], ['Sown'])
                            else:
                                k.stt(Sown[:], Sc[s_][:], onehot[:, m:m + 1], Sown[:], ALU.mult, ALU.add, ['Sc%d' % s_, 'Sown'], ['Sown'])
                        k.cp('pool', Sb[:].rearrange("p a b -> p (a b)"), Sown[:], ['Sown'], ['Sb'])
                    else:
                        for b_ in range(4):
                            for hh in range(4):
                                k.tt(qTb[b_][:, hh, :], qT[:, hh, :], cmaskS[:, b_, :], ALU.mult, ['qT'], ['qTb%d' % b_])
                    ba = nb()
                    for hh in range(4):
                        k.mm(PSF(ba)[:, hh * 128:(hh + 1) * 128], kTh[:, hh, :], qT[:, hh, :], True, True, ['kTh', 'qT'], ['ps%d' % ba])
                    k.tt(AT[:].rearrange("p a b -> p (a b)"), PSF(ba), maskf[:], ALU.mult, ['ps%d' % ba], ['AT'])
                    bo = [nb(), nb()]
                    for hh in range(4):
                        oap = PSF(bo[hh // 2])[:, (hh % 2) * 256:(hh % 2 + 1) * 256]
                        okey = 'ps%d' % bo[hh // 2]
                        if PR:
                            k.mm(oap, qT[:, hh, :], Sb[:, hh, :], True, False, ['qT', 'Sb'], [okey])
                        else:
                            for b_ in range(4):
                                k.mm(oap, qTb[b_][:, hh, :], S0b[b_][:, hh, :], b_ == 0, False, ['qTb%d' % b_], [okey])
                        k.mm(oap, AT[:, hh, :], V[:, hh * 256:(hh + 1) * 256], False, True, ['AT', 'V'], [okey])
                    for hh in range(4):
                        oap = PSF(bo[hh // 2])[:, (hh % 2) * 256:(hh % 2 + 1) * 256]
                        k.act(jq[:, 0:1].to_broadcast([128, 256]), oap, AF.Square, ['ps%d' % bo[hh // 2]], ['jq', 'ss%d' % hh], accum_out=ss[:, hh:hh + 1])
                    k.act(rs[:], ss[:], AF.Ln, ['ss0', 'ss1', 'ss2', 'ss3'], ['rs'], bias=eps[:, 0:1], scale=1.0 / 256)
                    k.act(rs[:], rs[:], AF.Exp, ['rs'], ['rs'], scale=-0.5)
                    for cc in range(2):
                        wt, wk = next_w()
                        bg = nb()
                        projT(wt, wk, 512, PSF(bg), 'ps%d' % bg)
                        k.act(sg[cc][:], PSF(bg), AF.Silu, ['ps%d' % bg], ['sg%d' % cc])
                        for hh in range(2):
                            hd = 2 * cc + hh
                            oap = PSF(bo[hd // 2])[:, (hd % 2) * 256:(hd % 2 + 1) * 256]
                            k.stt(tmp[:, hd * 256:(hd + 1) * 256], oap, rs[:, hd:hd + 1], gnB[:], ALU.mult, ALU.mult,
                                  ['ps%d' % bo[hd // 2], 'rs'], ['tmp'])
                        k.tt(yain[:, cc * 512:(cc + 1) * 512], tmp[:, cc * 512:(cc + 1) * 512], sg[cc][:], ALU.mult, ['tmp', 'sg%d' % cc], ['yain'])
                    transpose8(yain, 'yain', yT, 'yT')
                    for cc in range(2):
                        wt, wk = next_w()
                        by = nb()
                        for kc in range(8):
                            k.mm(PSF(by), yT[:, kc, :], wt[:, kc, :], kc == 0, kc == 7, ['yT', wk], ['ps%d' % by])
                        wt, wk = next_w()
                        bm = nb()
                        projT(wt, wk, 512, PSF(bm), 'ps%d' % bm)
                        k.dma('act', gtb[cc][:], gate_b[0:1, cc * 512:(cc + 1) * 512].partition_broadcast(128), [], ['gtb%d' % cc], 'gtb%d' % cc)
                        k.tt(sg[cc][:], PSF(bm), gtb[cc][:], ALU.add, ['ps%d' % bm, 'gtb%d' % cc], ['sg%d' % cc])
                        k.act(sg[cc][:], sg[cc][:], AF.Sigmoid, ['sg%d' % cc], ['sg%d' % cc])
                        k.tt(mrg[:, cc * 512:(cc + 1) * 512], PSF(by), sg[cc][:], ALU.mult, ['ps%d' % by, 'sg%d' % cc], ['mrg'])
                    if not PR:
                        for b_ in range(4):
                            s_ = b_ % 2
                            k.dma('sp', S0f[s_][:], state[b_].rearrange("h p e -> p h e"), [], ['S0f%d' % s_], 'S0f%d' % s_)
                            k.ts(Ktb[s_][:], Kt[:], rmaskS[:, b_:b_ + 1], None, ALU.mult, None, ['Kt'], ['Ktb%d' % s_])
                            for hp in range(2):
                                bs_ = nb()
                                for hh in range(2):
                                    hd = hp * 2 + hh
                                    k.mm(PSF(bs_)[:, hh * 256:(hh + 1) * 256], Ktb[s_][:, hd * 128:(hd + 1) * 128], V[:, hd * 256:(hd + 1) * 256],
                                         True, True, ['Ktb%d' % s_, 'V'], ['ps%d' % bs_])
                                for hh in range(2):
                                    hd = hp * 2 + hh
                                    k.stt(S0f[s_][:, hd, :], S0f[s_][:, hd, :], decs[:, hd * 4 + b_:hd * 4 + b_ + 1], PSF(bs_)[:, hh * 256:(hh + 1) * 256],
                                          ALU.mult, ALU.add, ['decs', 'ps%d' % bs_, 'S0f%d' % s_], ['S0f%d' % s_])
                            k.dma('act', gla_s[b_], S0f[s_][:].rearrange("p a b -> p (a b)"), ['S0f%d' % s_], [], 'Sno%d' % s_)

                    glaB = k.end_capture()
                    k.merge(bisA, glaB)

                    if PR:
                        def kvload(ci):
                            k0 = ci * 512
                            nt = min(4, n_tiles - ci * 4)
                            s_ = ci % 2
                            k.dma('sp', kTch[s_][:, :, :nt * 128], kT_d[:, :, k0:k0 + nt * 128], [], ['kTch%d' % s_], 'kTch%d' % s_)
                            k.dma('sp', vch[s_][:, :nt, :], v_d[:, ci * 4:ci * 4 + nt, :], [], ['vch%d' % s_], 'vch%d' % s_)
                        kvload(0)
                        li = 0
                        groups = []
                        for kt in range(n_tiles):
                            ci, tl_ = kt // 4, kt % 4
                            mb_ = mbt[kt % 3]
                            mbk = 'mbt%d' % (kt % 3)
                            for gg in range(4):
                                bl = 4 + (li % 4)
                                p_ = pT[li % 3]
                                pk = 'pT%d' % (li % 3)
                                li += 1
                                k.capture()
                                if gg == 0:
                                    if tl_ == 1 and ci + 1 < nch:
                                        kvload(ci + 1)
                                    k.ts(mb_[:], sc[:, kt * 128:(kt + 1) * 128], thr[:, 0:1], NEG, ALU.is_lt, ALU.mult, ['sc', 'thr'], [mbk])
                                k.mm(PSF(bl), kTch[ci % 2][:, gg // 2, tl_ * 128:(tl_ + 1) * 128], QTz[:, gg, :], True, False,
                                     ['kTch%d' % (ci % 2), 'QTz'], ['ps%d' % bl])
                                k.mm(PSF(bl), mb_[:], idrepb[:], False, True, [mbk], ['ps%d' % bl])
                                k.act(p_[:], PSF(bl), AF.Exp, ['ps%d' % bl], [pk], scale=0.125)
                                s1 = k.end_capture()
                                k.capture()
                                k.mm(PSF(gg)[0:65, :], vch[ci % 2][:, tl_, gg * 65:(gg + 1) * 65], p_[:], kt == 0, kt == n_tiles - 1,
                                     ['vch%d' % (ci % 2), pk], ['ps%d' % gg])
                                s2 = k.end_capture()
                                groups.append((s1, s2))
                        SK = 2
                        for idx in range(len(groups) + SK):
                            if idx < len(groups):
                                P.ops.extend(groups[idx][0])
                            if idx >= SK:
                                P.ops.extend(groups[idx - SK][1])
                        for gg in range(4):
                            o_ = oT[gg % 2]
                            ok_ = 'oT%d' % (gg % 2)
                            k.cp('act', o_[:], PSF(gg)[0:65, :], ['ps%d' % gg], [ok_])
                            for r_ in range(4):
                                k.tr(PSF(4 + gg)[:, r_ * 65:(r_ + 1) * 65], o_[0:65, r_ * 128:(r_ + 1) * 128], identf[0:65, 0:65], [ok_], ['ps%d' % (4 + gg)])
                        NPT = 128
                    else:
                        mball = junk
                        for b_ in range(4):
                            k.cp('pool', QTs[:, b_, :, :].rearrange("p g (r t) -> p g r t", r=4),
                                 QTz[:].rearrange("p g (r t) -> p g r t", r=4)[:, :, :, 16 * b_:16 * b_ + 16], ['QTz'], ['QTs'])
                        k.ts(mball[:], sc[:, 0:2176], thr[:, 0:1], NEG, ALU.is_lt, ALU.mult, ['sc', 'thr'], ['junk'])
                        li = 0
                        for b_ in range(4):
                            k.cp('pool', kTs1[:, :, 2048:2064], kTnew[:, :, 16 * b_:16 * b_ + 16], ['kTnew'], ['kTs'])
                            bs_ = nb() % 4 + 4
                            k.mm(PSF(bs_)[:, 0:256], selSb[:, b_, :], vnb[:], True, True, ['vnb'], ['ps%d' % bs_])
                            k.cp('act', vexts1[:, 16, :].rearrange("p (g d) -> p g d", g=4)[:, :, 0:64],
                                 PSF(bs_)[:, 0:256].rearrange("p (g d) -> p g d", g=4), ['ps%d' % bs_], ['vexts'])
                            for t8 in range(2):
                                k.dma('sp', ckf[:], cache_k[b_, t8 * 1024:(t8 + 1) * 1024, :].rearrange("(t p) c -> p t c", p=128), [], ['ckf'], 'ckf')
                                k.cp('pool', ckb[:], ckf[:], ['ckf'], ['ckb'])
                                for t4 in range(2):
                                    bt_ = nb() % 4 + 4
                                    for tt_ in range(4):
                                        for c_ in range(2):
                                            col = (tt_ * 2 + c_) * 128
                                            k.tr(PSB(bt_)[:, col:col + 128], ckb[:, t4 * 4 + tt_, c_ * 128:(c_ + 1) * 128], identb[:], ['ckb'], ['ps%d' % bt_])
                                    c0_ = t8 * 1024 + t4 * 512
                                    k.cp('dve', kTs1[:, :, c0_:c0_ + 512].rearrange("p c (t s) -> p c t s", t=4),
                                         PSB(bt_).rearrange("p (t c s) -> p c t s", t=4, c=2), ['ps%d' % bt_], ['kTs'])
                                k.dma('sp', ckf[:], cache_v[b_, t8 * 1024:(t8 + 1) * 1024, :].rearrange("(t p) c -> p t c", p=128), [], ['ckf'], 'ckf')
                                k.cp('pool', vexts1[:, t8 * 8:(t8 + 1) * 8, :].rearrange("p t (g d) -> p t g d", g=4)[:, :, :, 0:64],
                                     ckf[:].rearrange("p t (g d) -> p t g d", g=4), ['ckf'], ['vexts'])
                            for gg in range(4):
                                ob = b_ // 2
                                ocol = ((b_ % 2) * 4 + gg) * 64
                                qsel = QTs[:, b_, gg, :]
                                for kt in range(17):
                                    bl = 4 + (li % 4)
                                    p_ = pTs[li % 3]
                                    pk = 'pTs%d' % (li % 3)
                                    li += 1
                                    k.mm(PSF(bl)[:, 0:64], kTs1[:, gg // 2, kt * 128:(kt + 1) * 128], qsel, True, False,
                                         ['kTs', 'QTs'], ['ps%d' % bl])
                                    k.mm(PSF(bl)[:, 0:64], mball[:, kt * 128:(kt + 1) * 128], idrepSb[:, b_, :], False, True, ['junk'], ['ps%d' % bl])
                                    k.act(p_[:], PSF(bl)[:, 0:64], AF.Exp, ['ps%d' % bl], [pk], scale=0.125)
                                    k.mm(PSF(ob)[0:65, ocol:ocol + 64], vexts1[:, kt, gg * 65:(gg + 1) * 65], p_[:], kt == 0, kt == 16,
                                         ['vexts', pk], ['ps%d' % ob])
                        oTs = sc[0:65, 0:1024]
                        ov = oTs.rearrange("p (g r b t) -> p b g r t", g=4, r=4, b=4)
                        for ob in range(2):
                            k.cp('act', ov[:, 2 * ob:2 * ob + 2], PSF(ob)[0:65, :].rearrange("p (b g r t) -> p b g r t", b=2, g=4, r=4),
                                 ['ps%d' % ob, 'junk'], ['sc'])
                        for gg in range(4):
                            for r_ in range(4):
                                c0_ = (gg * 4 + r_) * 64
                                k.tr(PSF(4 + gg)[0:64, r_ * 65:(r_ + 1) * 65], oTs[:, c0_:c0_ + 64], identf[0:65, 0:65], ['sc'], ['ps%d' % (4 + gg)])
                        NPT = 64
                    for gg in range(4):
                        pv4 = PSF(4 + gg)[0:NPT, 0:260].rearrange("p (r c) -> p r c", c=65)
                        k.recip(rd[0:NPT, 4 * gg:4 * gg + 4], pv4[:, :, 64], ['ps%d' % (4 + gg)], ['rd'])
                        for r_ in range(4):
                            hd = 4 * gg + r_
                            k.ts(tmp[0:NPT, hd * 64:(hd + 1) * 64], pv4[:, r_, 0:64], rd[0:NPT, hd:hd + 1], None, ALU.mult, None,
                                 ['ps%d' % (4 + gg), 'rd'], ['tmp'])
                    for cc in range(2):
                        wt, wk = next_w()
                        bg = nb()
                        projT(wt, wk, 512, PSF(bg), 'ps%d' % bg)
                        k.act(sg[cc][:], PSF(bg), AF.Silu, ['ps%d' % bg], ['sg%d' % cc])
                        k.tt(yain[0:NPT, cc * 512:(cc + 1) * 512], tmp[0:NPT, cc * 512:(cc + 1) * 512], sg[cc][0:NPT, :], ALU.mult,
                             ['tmp', 'sg%d' % cc], ['yain'])
                    transpose8(yain, 'yain', yT, 'yT')
                    for cc in range(2):
                        wt, wk = next_w()
                        by = nb()
                        for kc in range(8):
                            k.mm(PSF(by), yT[:, kc, :], wt[:, kc, :], kc == 0, kc == 7, ['yT', wk], ['ps%d' % by])
                        wt, wk = next_w()
                        bm = nb()
                        projT(wt, wk, 512, PSF(bm), 'ps%d' % bm)
                        k.dma('act', gtb[cc][:], gate_b[0:1, 1024 + cc * 512:1024 + (cc + 1) * 512].partition_broadcast(128), [], ['gtb%d' % cc], 'gtb%d' % cc)
                        k.tt(sg[cc][:], PSF(bm), gtb[cc][:], ALU.add, ['ps%d' % bm, 'gtb%d' % cc], ['sg%d' % cc])
                        k.act(sg[cc][:], sg[cc][:], AF.Sigmoid, ['sg%d' % cc], ['sg%d' % cc])
                        k.tt(sg[cc][:], PSF(by), sg[cc][:], ALU.mult, ['ps%d' % by, 'sg%d' % cc], ['sg%d' % cc])
                        k.tt(mrg[:, cc * 512:(cc + 1) * 512], mrg[:, cc * 512:(cc + 1) * 512], sg[cc][:], ALU.add, ['mrg', 'sg%d' % cc], ['mrg'])
                    k.cp('pool', yain[:], mrg[:], ['mrg'], ['yain'])
                    transpose8(yain, 'yain', yT, 'yT')
                    for cc in range(2):
                        wt, wk = next_w()
                        by = nb()
                        for kc in range(8):
                            k.mm(PSF(by), yT[:, kc, :], wt[:, kc, :], kc == 0, kc == 7, ['yT', wk], ['ps%d' % by])
                        k.stt(mrg[:, cc * 512:(cc + 1) * 512], h[:, cc * 512:(cc + 1) * 512], ALPHA, PSF(by), ALU.mult, ALU.add, ['h', 'ps%d' % by], ['mrg'])
                    layernorm(mrg[:], 'mrg', g2B[:], b2B[:], tl, 'c', out_f32=yout[:], out_f32_key='xo')
                    ydst = y_own[g * 128:(g + 1) * 128, :] if PR else ys
                    k.dma('act', ydst, yout[:], ['xo'], [], 'youto')

                for g in (range(G) if PR else [0]):
                    own_block(g)
            P.barrier()

        if "B" in phases:
            phase_own('P')
        if "S" in phases:
            phase_own('S')

        P.emit(nc, top)
    return nc


def _consts(j, G):
    c = {}
    p = np.arange(128)
    c["c_ident"] = np.eye(128, dtype=np.float32)
    jj, ii = np.meshgrid(p, p, indexing="ij")
    c["c_triA"] = np.where((jj > ii) & (jj // 64 == ii // 64), -1.0 / 16, 0.0).astype(np.float32)
    c["c_blkA"] = np.where(p[:, None] // 64 == np.arange(2)[None, :], -1.0 / 16, 0.0).astype(np.float32)
    c["c_triB"] = np.where(jj <= ii, -1.0 / 16, 0.0).astype(np.float32)
    mB = (jj <= ii).astype(np.float32)
    c["c_maskB"] = np.ascontiguousarray(np.broadcast_to(mB[:, None, :], (128, 4, 128)))
    same = (jj // 16 == ii // 16)
    c["c_triS"] = np.where((jj <= ii) & same, -1.0 / 16, 0.0).astype(np.float32)
    mS = ((jj <= ii) & same).astype(np.float32)
    c["c_maskS"] = np.ascontiguousarray(np.broadcast_to(mS[:, None, :], (128, 4, 128)))
    c["c_mrevS"] = np.where((jj > ii) & same, -1.0 / 16, 0.0).astype(np.float32)
    c["c_bsumS"] = np.where(p[:, None] // 16 == np.arange(4)[None, :], -1.0 / 16, 0.0).astype(np.float32)
    cm = (p[None, :] // 16 == np.arange(4)[:, None]).astype(np.float32)
    c["c_cmaskS"] = np.ascontiguousarray(np.broadcast_to(cm[None], (128, 4, 128)))
    c["c_rmaskS"] = (p[:, None] // 16 == np.arange(4)[None, :]).astype(np.float32)
    c["c_idrep"] = np.ascontiguousarray(np.tile(np.eye(128, dtype=np.float32), (1, 4)))
    ids = np.zeros((128, 4, 64), np.float32)
    sel = np.zeros((128, 4, 128), np.float32)
    for b in range(4):
        for t in range(16):
            for r in range(4):
                ids[16 * b + t, b, r * 16 + t] = 1.0
            sel[16 * b + t, b, t] = 1.0
    c["c_idrepS"] = ids
    c["c_selS"] = sel
    c["c_ctab"] = np.ascontiguousarray(np.broadcast_to((0.5 ** np.arange(1, KIT + 2))[None, :], (128, KIT + 1))).astype(np.float32)
    tb = np.zeros((128, 2, 640), np.float32)
    kk = np.arange(640)
    for r in range(128):
        lim = 128 * j + (16 if r < 16 else (80 if r < 80 else 144))
        tb[r, 0, :] = np.where(kk < lim, 0.0, -1e30)
    tb[:, 1, :] = np.where(kk < 16, 0.0, -1e30)[None, :]
    c["c_tbias"] = tb
    oh = np.zeros((128, 4), np.float32)
    oh[:, j] = 1.0
    c["c_onehot"] = oh
    c["c_tailmask"] = (p < 16).astype(np.float32)[:, None]
    return c


def _dq_perm():
    perm = np.zeros(1024, np.int64)
    n = 0
    for m in range(2):
        for r in range(4):
            for half in range(2):
                g = 2 * m + half
                for d in range(64):
                    perm[n] = (g * 4 + r) * 64 + d
                    n += 1
    return perm


def prep(inp, SEQ):
    T, NB, G = geometry(SEQ)
    f = lambda a: np.ascontiguousarray(np.asarray(a, dtype=np.float32))
    w_in = f(inp["w_in"])[0].copy()
    w_in[:, C_DQ:C_DQ + 1024] = w_in[:, C_DQ:C_DQ + 1024][:, _dq_perm()]
    w3 = np.ascontiguousarray(np.stack([f(inp["w_gla"])[0], f(inp["w_dsa"])[0], f(inp["w_out"])[0]], 0))
    shared = dict(
        w_in=np.ascontiguousarray(w_in), w3=w3,
        ln_in_g=f(inp["ln_in_g"]).reshape(1, D), ln_in_b=f(inp["ln_in_b"]).reshape(1, D),
        ln_g=f(inp["ln_g"]).reshape(1, D), ln_b=f(inp["ln_b"]).reshape(1, D),
        gate_b=f(inp["gate_b"]).reshape(1, 2048), gla_gate_b=f(inp["gla_gate_b"]).reshape(1, 512),
        gla_norm_g=f(inp["gla_norm_g"]).reshape(1, 256),
        idx_kn_g=f(inp["idx_kn_g"]).reshape(1, 64), idx_kn_b=f(inp["idx_kn_b"]).reshape(1, 64),
        gla_w2=f(inp["gla_w2"]).reshape(16, 512))
    xp = f(inp["x_prompt"]); meta = f(inp["meta"]); xsm = f(inp["x_sample"])
    ck = f(inp["cache_k"])[0]; cv = f(inp["cache_v"])[0]; cik = f(inp["cache_idx_k"])[0]; stt = f(inp["state_gla"])[0]
    maps = []
    for c in range(8):
        b, j = c // 4, c % 4
        xall = np.zeros((4 * G * 128, D), np.float32)
        xall[:16] = meta
        xall[16:T] = xp[b]
        xown = np.ascontiguousarray(xall.reshape(G, 4, 128, D)[:, j].reshape(G * 128, D))
        xs_ = np.zeros((128, D), np.float32)
        xs_[:64] = xsm[4 * c:4 * c + 4].reshape(64, D)
        m = dict(shared)
        m.update(_consts(j, G))
        m.update(xall=np.ascontiguousarray(xall[:NB * 128]), xown=xown, xs=xs_,
                 cache_k=np.ascontiguousarray(ck[4 * c:4 * c + 4].reshape(4, 2048, 256)),
                 cache_v=np.ascontiguousarray(cv[4 * c:4 * c + 4].reshape(4, 2048, 256)),
                 cache_ik=np.ascontiguousarray(cik[4 * c:4 * c + 4]),
                 state=np.ascontiguousarray(stt[4 * c:4 * c + 4]))
        maps.append(m)
    return maps


def gather(res, SEQ):
    T, NB, G = geometry(SEQ)
    R = res.results
    yp = np.zeros((2, 4 * G * 128, D), np.float32)
    for c in range(8):
        b, j = c // 4, c % 4
        yp[b].reshape(G, 4, 128, D)[:, j] = R[c]["y_own"].reshape(G, 128, D)
    y_prompt = np.ascontiguousarray(yp[:, 16:T])
    y_sample = np.concatenate([R[c]["ys"][:64].reshape(4, 16, D) for c in range(8)], 0)
    k_prompt = np.stack([R[4 * b]["kp"][:T].reshape(T, 4, 64) for b in range(2)], 0)[None]
    v_prompt = np.stack([R[4 * b]["vp"][:T].reshape(T, 4, 64) for b in range(2)], 0)[None]
    ik_prompt = np.stack([R[4 * b]["ikp"][:T] for b in range(2)], 0)[None]
    gla_prompt = np.stack([R[4 * b]["gla_p"].reshape(128, 4, 256).transpose(1, 0, 2) for b in range(2)], 0)[None]
    k_sample = np.concatenate([R[c]["ks"][:64].reshape(4, 16, 4, 64) for c in range(8)], 0)[None]
    v_sample = np.concatenate([R[c]["vs"][:64].reshape(4, 16, 4, 64) for c in range(8)], 0)[None]
    ik_sample = np.concatenate([R[c]["iks"][:64].reshape(4, 16, 64) for c in range(8)], 0)[None]
    gla_sample = np.concatenate([R[c]["gla_s"].reshape(4, 128, 4, 256).transpose(0, 2, 1, 3) for c in range(8)], 0)[None]
    outs = (y_prompt, y_sample, k_prompt, v_prompt, ik_prompt, gla_prompt, k_sample, v_sample, ik_sample, gla_sample)
    return tuple(np.ascontiguousarray(o, dtype=np.float32) for o in outs)


_NC_CACHE = {}


def run(inputs, SEQ, phases="0ABS"):
    key = (SEQ, phases)
    if key not in _NC_CACHE:
        _NC_CACHE[key] = build(SEQ, phases)
    nc = _NC_CACHE[key]
    maps = prep(inputs, SEQ)
    res = run_bass_kernel_spmd(nc, maps, core_ids=list(range(8)))
    return gather(res, SEQ)


def kernel(**inputs):
    SEQ = int(np.asarray(inputs["x_prompt"]).shape[1])
    return run(inputs, SEQ)
```

```python
from contextlib import ExitStack
import numpy as np
import concourse.bass as bass
import concourse.mybir as mybir
from concourse.bass_utils import run_bass_kernel_spmd

F32 = mybir.dt.float32
BF16 = mybir.dt.bfloat16
AF = mybir.ActivationFunctionType
ALU = mybir.AluOpType
AX = mybir.AxisListType

D = 1024
NEG = -30000.0
KIT = 22
IDX_W_SCALE = (8 ** -0.5) * (64 ** -0.5)
ALPHA = 2.0 ** 0.25
EPS = 1e-5
C_GQ, C_GK, C_GV, C_GLOW, C_GR, C_DQ, C_DK, C_DV, C_IQ, C_IK, C_IW, C_DZ, C_MA, C_MB = (
    0, 512, 1024, 2048, 2064, 3088, 4112, 4368, 4624, 5136, 5200, 5208, 6232, 7256)
IN_COLS = 8280


class Prog:
    def __init__(self):
        self.ops = []

    def add(self, eng, fn, r=(), w=(), dsem=None):
        r = list(r)
        w = list(w)
        for b in list(r):
            if isinstance(b, str) and b.startswith('ps'):
                r.remove(b)
                if b not in w:
                    w.append(b)
        self.ops.append(dict(eng=eng, fn=fn, r=tuple(r), w=tuple(w), dsem=dsem))

    def barrier(self):
        self.ops.append(dict(eng='barrier', fn=None, r=(), w=(), dsem=None))

    def analyze(self):
        ops = self.ops
        last_w, readers = {}, {}
        last_of = {}
        for i, op in enumerate(ops):
            if op['eng'] == 'barrier':
                for e_, j in last_of.items():
                    ops[j]['needed'] = True
                last_w, readers = {}, {}
                op['deps'] = set()
                continue
            deps = set()
            for b in op['r']:
                if b in last_w:
                    deps.add(('raw', last_w[b]))
            for b in op['w']:
                if b in last_w:
                    deps.add(('waw', last_w[b]))
                for rr in readers.get(b, ()):
                    deps.add(('war', rr))
            for b in op['r']:
                readers.setdefault(b, []).append(i)
            for b in op['w']:
                last_w[b] = i
                readers[b] = []
            keep = set()
            for kind, j in deps:
                if j == i:
                    continue
                pj = ops[j]
                if pj['dsem'] is None and op['dsem'] is None and pj['eng'] == op['eng']:
                    if op['eng'] == 'pe' or kind == 'war':
                        continue
                keep.add(j)
            op['deps'] = keep
            for j in keep:
                ops[j]['needed'] = True
            if op['dsem'] is None:
                last_of[op['eng']] = i
        for e_, j in last_of.items():
            ops[j]['needed'] = True
        cnt = {}
        for op in ops:
            if op['eng'] == 'barrier':
                continue
            if op['dsem'] is not None:
                k = 'D:' + op['dsem']
                cnt[k] = cnt.get(k, 0) + 16
                op['sem'] = k
                op['val'] = cnt[k]
            elif op.get('needed'):
                k = 'E:' + op['eng']
                cnt[k] = cnt.get(k, 0) + 1
                op['sem'] = k
                op['val'] = cnt[k]
        waited = {}
        running = {}
        pending = {}
        for op in ops:
            if op['eng'] == 'barrier':
                for e_ in ('pe', 'act', 'dve', 'pool', 'sp'):
                    pending[e_] = dict(running)
                continue
            ws = {}
            if pending.get(op['eng']):
                ws.update(pending[op['eng']])
                pending[op['eng']] = None
            for j in op['deps']:
                pj = ops[j]
                ws[pj['sem']] = max(ws.get(pj['sem'], 0), pj['val'])
            wl = []
            we = waited.setdefault(op['eng'], {})
            for k, v in ws.items():
                if we.get(k, 0) >= v:
                    continue
                we[k] = v
                wl.append((k, v))
            op['waits'] = wl
            if op.get('sem') is not None:
                running[op['sem']] = op['val']
        self.totals = cnt
        return cnt

    def emit(self, nc, es):
        cnt = self.analyze()
        sems = {}
        for k in cnt:
            sems[k] = es.enter_context(nc.semaphore(k.replace(':', '_')))
        block = es.enter_context(nc.Block())
        ops = self.ops

        def run(engname):
            def f(e):
                for op in ops:
                    if op['eng'] != engname:
                        continue
                    for k, v in op['waits']:
                        e.wait_ge(sems[k], v)
                    ins = op['fn'](e)
                    if op.get('sem') is not None:
                        ins.then_inc(sems[op['sem']], 16 if op['dsem'] is not None else 1)
                for k, v in cnt.items():
                    e.wait_ge(sems[k], v)
            return f

        block.sync(run('sp'))
        block.scalar(run('act'))
        block.vector(run('dve'))
        block.gpsimd(run('pool'))
        block.tensor(run('pe'))


class KB:
    def __init__(self, nc):
        self.nc = nc
        self.P = Prog()
        self.rot = 0

    def capture(self):
        self._saved = self.P.ops
        self.P.ops = []

    def end_capture(self):
        l = self.P.ops
        self.P.ops = self._saved
        return l

    def merge(self, A, B):
        out = []
        ia = ib = 0
        na, nb_ = max(len(A), 1), max(len(B), 1)
        while ia < len(A) or ib < len(B):
            if ib >= len(B) or (ia < len(A) and ia * nb_ <= ib * na):
                out.append(A[ia]); ia += 1
            else:
                out.append(B[ib]); ib += 1
        self.P.ops.extend(out)

    def act(self, out, in_, func, r, w, **kw):
        self.P.add('act', lambda e: e.activation(out=out, in_=in_, func=func, **kw), r, w)

    def ts(self, out, in0, s1, s2, op0, op1, r, w, eng='dve', accum=None):
        if accum is None:
            if op1 is None:
                self.P.add(eng, lambda e: e.tensor_scalar(out=out, in0=in0, scalar1=s1, scalar2=None, op0=op0), r, w)
            else:
                self.P.add(eng, lambda e: e.tensor_scalar(out=out, in0=in0, scalar1=s1, scalar2=s2, op0=op0, op1=op1), r, w)
        else:
            self.P.add(eng, lambda e: e.tensor_scalar(out=out, in0=in0, scalar1=s1, scalar2=s2, op0=op0, op1=op1,
                                                      accum_out=accum), r, w)

    def tt(self, out, in0, in1, op, r, w, eng='dve'):
        self.P.add(eng, lambda e: e.tensor_tensor(out=out, in0=in0, in1=in1, op=op), r, w)

    def stt(self, out, in0, scalar, in1, op0, op1, r, w):
        self.P.add('dve', lambda e: e.scalar_tensor_tensor(out=out, in0=in0, scalar=scalar, in1=in1, op0=op0, op1=op1), r, w)

    def cp(self, eng, out, in_, r, w):
        if eng == 'act':
            self.P.add('act', lambda e: e.copy(out=out, in_=in_), r, w)
        else:
            self.P.add(eng, lambda e: e.tensor_copy(out=out, in_=in_), r, w)

    def memset(self, eng, ap, val, w):
        self.P.add(eng, lambda e: e.memset(ap, val), (), w)

    def mm(self, out, lhsT, rhs, start, stop, r, w):
        self.P.add('pe', lambda e: e.matmul(out, lhsT=lhsT, rhs=rhs, start=start, stop=stop), r, w)

    def tr(self, out, in_, ident, r, w):
        self.P.add('pe', lambda e: e.transpose(out=out, in_=in_, identity=ident), r, w)

    def dma(self, q, out, in_, r, w, dsem):
        self.P.add(q, lambda e: e.dma_start(out=out, in_=in_), r, w, dsem=dsem)

    def red(self, out, in_, op, r, w):
        self.P.add('dve', lambda e: e.tensor_reduce(out=out, in_=in_, axis=AX.X, op=op), r, w)

    def recip(self, out, in_, r, w):
        self.P.add('dve', lambda e: e.reciprocal(out=out, in_=in_), r, w)

    def bn_stats(self, out, in_, r, w):
        self.P.add('dve', lambda e: e.bn_stats(out=out, in_=in_), r, w)

    def bn_aggr(self, out, in_, r, w):
        self.P.add('dve', lambda e: e.bn_aggr(out=out, in_=in_), r, w)


def geometry(SEQ):
    T = SEQ + 16
    NB = T // 128 + 1
    assert T == 128 * (NB - 1) + 16 and NB % 4 == 1
    G = (NB + 3) // 4
    return T, NB, G


def build(SEQ, phases="0ABS"):
    T, NB, G = geometry(SEQ)
    NKMAX = NB * 128
    nc = bass.Bass("TRN2", target_bir_lowering=False)

    def din(name, shape, dt=F32):
        return nc.dram_tensor(name, list(shape), dt, kind="ExternalInput").ap()

    def dout(name, shape, dt=F32):
        return nc.dram_tensor(name, list(shape), dt, kind="ExternalOutput").ap()

    def dscr(name, shape, dt):
        return nc.dram_tensor(name, list(shape), dt, kind="Internal").ap()

    xall = din("xall", [NB * 128, D])
    xown = din("xown", [G * 128, D])
    xs = din("xs", [128, D])
    w_in = din("w_in", [D, IN_COLS])
    w3 = din("w3", [3, D, D])
    ln_in_g = din("ln_in_g", [1, D]); ln_in_b = din("ln_in_b", [1, D])
    ln_g = din("ln_g", [1, D]); ln_b = din("ln_b", [1, D])
    gate_b = din("gate_b", [1, 2048])
    gla_gate_b = din("gla_gate_b", [1, 512])
    gla_norm_g = din("gla_norm_g", [1, 256])
    idx_kn_g = din("idx_kn_g", [1, 64]); idx_kn_b = din("idx_kn_b", [1, 64])
    gla_w2 = din("gla_w2", [16, 512])
    cache_k = din("cache_k", [4, 2048, 256]); cache_v = din("cache_v", [4, 2048, 256])
    cache_ik = din("cache_ik", [4, 2048, 64])
    state = din("state", [4, 4, 128, 256])
    c_ident = din("c_ident", [128, 128])
    c_triA = din("c_triA", [128, 128]); c_blkA = din("c_blkA", [128, 2])
    c_triB = din("c_triB", [128, 128]); c_maskB = din("c_maskB", [128, 4, 128])
    c_triS = din("c_triS", [128, 128]); c_maskS = din("c_maskS", [128, 4, 128])
    c_mrevS = din("c_mrevS", [128, 128]); c_bsumS = din("c_bsumS", [128, 4])
    c_cmaskS = din("c_cmaskS", [128, 4, 128]); c_rmaskS = din("c_rmaskS", [128, 4])
    c_idrep = din("c_idrep", [128, 512]); c_idrepS = din("c_idrepS", [128, 4, 64])
    c_selS = din("c_selS", [128, 4, 128])
    c_ctab = din("c_ctab", [128, KIT + 1])
    c_tbias = din("c_tbias", [128, 2, 640])
    c_onehot = din("c_onehot", [128, 4])
    c_tailmask = din("c_tailmask", [128, 1])

    y_own = dout("y_own", [G * 128, D])
    kp = dout("kp", [NB * 128, 256]); vp = dout("vp", [NB * 128, 256]); ikp = dout("ikp", [NB * 128, 64])
    gla_p = dout("gla_p", [128, 1024])
    ys = dout("ys", [128, D]); ks = dout("ks", [128, 256]); vs = dout("vs", [128, 256]); iks = dout("iks", [128, 64])
    gla_s = dout("gla_s", [4, 128, 1024])

    wbf = dscr("wbf", [D, IN_COLS], BF16)
    wbf3 = dscr("wbf3", [3, D, D], BF16)
    kT_d = dscr("kT_d", [128, 2, NKMAX], BF16)
    v_d = dscr("v_d", [128, NB, 260], BF16)
    ki_d = dscr("ki_d", [128, NKMAX], BF16)
    snap = dscr("snap", [NB, 128, 1024], F32)

    k = KB(nc)
    P = k.P
    top = ExitStack()
    with top:
        pst = [top.enter_context(nc.psum_tensor("ps%d" % i, [128, 512], F32)) for i in range(8)]

        def PSF(i):
            return pst[i][:]

        def PSB(i):
            return pst[i][:].bitcast(BF16)

        if "0" in phases:
            with ExitStack() as es:
                def sb(name, shape, dt):
                    return es.enter_context(nc.sbuf_tensor(name, shape, dt))
                wst = [sb("wst%d" % s, [128, 8, 512], F32) for s in range(2)]
                wcb = [sb("wcb%d" % s, [128, 8, 512], BF16) for s in range(2)]
                jobs = []
                for c in range(17):
                    c0 = c * 512
                    n = min(512, IN_COLS - c0)
                    jobs.append((w_in[:, c0:c0 + n], wbf[:, c0:c0 + n], n))
                for m in range(3):
                    for c in range(2):
                        jobs.append((w3[m, :, c * 512:(c + 1) * 512], wbf3[m, :, c * 512:(c + 1) * 512], 512))
                engs = ['dve', 'act', 'pool']
                for idx, (src, dst, n) in enumerate(jobs):
                    s = idx % 2
                    k.dma('sp', wst[s][:, :, :n], src.rearrange("(k p) n -> p k n", p=128), [], ['wst%d' % s], 'wst%d' % s)
                    k.cp(engs[idx % 3], wcb[s][:, :, :n], wst[s][:, :, :n], ['wst%d' % s], ['wcb%d' % s])
                    k.dma('act', dst.rearrange("(k p) n -> p k n", p=128), wcb[s][:, :, :n], ['wcb%d' % s], [], 'wcb%d' % s)
            P.barrier()

        def layernorm(src, srckey, gB, bB, tl, tag, out_f32=None, out_f32_key=None, out_bf=None, out_bf_key=None):
            tk = lambda n: tag + n
            for c in range(2):
                k.bn_stats(tl['st'][:, c, :], src[:, c * 512:(c + 1) * 512], [srckey], [tk('st%d' % c)])
            k.bn_aggr(tl['mv'][:], tl['st'][:].rearrange("p a b -> p (a b)"), [tk('st0'), tk('st1')], [tk('mv')])
            k.act(tl['sd'][:], tl['mv'][:, 1:2], AF.Ln, [tk('mv'), 'eps'], [tk('sd')], bias=tl['eps'][:, 0:1], scale=1.0)
            k.act(tl['rstd'][:], tl['sd'][:], AF.Exp, [tk('sd')], [tk('rstd')], scale=-0.5)
            k.ts(tl['nmr'][:], tl['mv'][:, 0:1], tl['rstd'][:, 0:1], -1.0, ALU.mult, ALU.mult, [tk('mv'), tk('rstd')], [tk('nmr')])
            k.act(tl['xn'][:], src, AF.Identity, [srckey, tk('nmr'), tk('rstd')], [tl['xnkey']],
                  bias=tl['nmr'][:, 0:1], scale=tl['rstd'][:, 0:1])
            k.tt(tl['xn'][:], tl['xn'][:], gB, ALU.mult, [tl['xnkey'], 'lnconst'], [tl['xnkey']])
            if out_f32 is not None:
                k.tt(out_f32, tl['xn'][:], bB, ALU.add, [tl['xnkey'], 'lnconst'], [out_f32_key])
                if out_bf is not None:
                    k.cp('pool', out_bf, out_f32, [out_f32_key], [out_bf_key])
            else:
                k.tt(out_bf, tl['xn'][:], bB, ALU.add, [tl['xnkey'], 'lnconst'], [out_bf_key])

        if "A" in phases:
            with ExitStack() as es:
                def sb(name, shape, dt):
                    return es.enter_context(nc.sbuf_tensor(name, shape, dt))
                gB = sb("a_gB", [128, D], F32); bB = sb("a_bB", [128, D], F32)
                identf = sb("a_idf", [128, 128], F32); identb = sb("a_idb", [128, 128], BF16)
                triA = sb("a_triA", [128, 128], F32); blkA = sb("a_blkA", [128, 2], F32)
                w2 = sb("a_w2", [16, 512], F32); gbias = sb("a_gbias", [1, 512], F32); ones1 = sb("a_ones1", [1, 128], F32)
                gkiB = sb("a_gkiB", [128, 64], F32); bkiB = sb("a_bkiB", [128, 64], F32)
                eps = sb("a_eps", [128, 1], F32); one = sb("a_one", [128, 1], F32)
                tailm = sb("a_tailm", [128, 1], F32)
                wA = sb("a_wA", [128, 8, 2128], BF16)
                SS = [sb("a_S%d" % s, [128, 4, 256], F32) for s in range(3)]
                xa = [sb("a_xa%d" % s, [128, D], F32) for s in range(3)]
                xn = sb("a_xn", [128, D], F32)
                hb = [sb("a_hb%d" % s, [128, D], BF16) for s in range(3)]
                hT = [sb("a_hT%d" % s, [128, 8, 128], BF16) for s in range(2)]
                st = sb("a_st", [128, 2, 6], F32); mv = sb("a_mv", [128, 2], F32)
                sd = sb("a_sd", [128, 1], F32); rstd = sb("a_rstd", [128, 1], F32); nmr = sb("a_nmr", [128, 1], F32)
                st2 = sb("a_st2", [128, 6], F32); mv2 = sb("a_mv2", [128, 2], F32)
                sd2 = sb("a_sd2", [128, 1], F32); rstd2 = sb("a_rstd2", [128, 1], F32); nmr2 = sb("a_nmr2", [128, 1], F32)
                Vt = [sb("a_V%d" % s, [128, 1024], BF16) for s in range(4)]
                kdv = [sb("a_kdv%d" % s, [128, 512], F32) for s in range(3)]
                kdb = [sb("a_kdb%d" % s, [128, 256], BF16) for s in range(2)]
                vext = [sb("a_vext%d" % s, [128, 4, 65], BF16) for s in range(2)]
                kTt = [sb("a_kT%d" % s, [128, 2, 128], BF16) for s in range(2)]
                kin = [sb("a_kin%d" % s, [128, 64], F32) for s in range(3)]
                ksb = [sb("a_ksb%d" % s, [128, 512], F32) for s in range(3)]
                kif = [sb("a_kif%d" % s, [128, 64], F32) for s in range(2)]
                kib = [sb("a_kib%d" % s, [128, 128], BF16) for s in range(2)]
                kiT = [sb("a_kiT%d" % s, [128, 128], BF16) for s in range(2)]
                glb = sb("a_glb", [16, 128], BF16); w2b = sb("a_w2b", [16, 512], BF16); gbB = sb("a_gbB", [128, 512], F32)
                el = [sb("a_el%d" % s, [128, 512], F32) for s in range(3)]
                er = sb("a_er", [128, 512], F32)
                Kt = [sb("a_Kt%d" % s, [128, 512], BF16) for s in range(2)]
                dec = [sb("a_dec%d" % s, [128, 8], F32) for s in range(2)]

                k.dma('sp', gB[:], ln_in_g.partition_broadcast(128), [], ['lnconst0'], 'ca')
                k.dma('sp', bB[:], ln_in_b.partition_broadcast(128), [], ['lnconst1'], 'ca')
                k.dma('sp', identf[:], c_ident, [], ['identf'], 'ca')
                k.dma('sp', triA[:], c_triA, [], ['triA'], 'ca')
                k.dma('sp', blkA[:], c_blkA, [], ['blkA'], 'ca')
                k.dma('sp', w2[:], gla_w2, [], ['w2'], 'ca')
                k.dma('sp', gbias[:], gla_gate_b, [], ['gbias'], 'ca')
                k.dma('sp', gbB[:], gla_gate_b.partition_broadcast(128), [], ['gbB'], 'ca')
                k.dma('sp', gkiB[:], idx_kn_g.partition_broadcast(128), [], ['gkiB'], 'ca')
                k.dma('sp', bkiB[:], idx_kn_b.partition_broadcast(128), [], ['bkiB'], 'ca')
                k.dma('sp', tailm[:], c_tailmask, [], ['tailm'], 'ca')
                wmap = [(C_GK, 512, 0), (C_GV, 1024, 512), (C_DK, 512, 1536), (C_IK, 64, 2048), (C_GLOW, 16, 2112)]
                for (c0, n, o) in wmap:
                    k.dma('sp', wA[:, :, o:o + n], wbf[:, c0:c0 + n].rearrange("(k p) n -> p k n", p=128), [], ['wA%d' % o], 'ca')
                P.barrier()
                k.memset('dve', eps[:], EPS, ['eps'])
                k.memset('dve', one[:], 1.0, ['one'])
                k.memset('dve', ones1[:], 1.0, ['ones1'])
                k.memset('dve', SS[0][:], 0.0, ['S0'])
                for s in range(2):
                    k.memset('pool', vext[s][:], 1.0, ['vext%d' % s])
                k.cp('dve', identb[:], identf[:], [], ['identb'])
                k.cp('dve', w2b[:], w2[:], [], ['w2b'])
                P.barrier()
                tl = dict(st=st, mv=mv, sd=sd, rstd=rstd, nmr=nmr, xn=xn, eps=eps, xnkey='xn')

                def loadx(i):
                    s = i % 3
                    k.dma('sp', xa[s][:], xall[i * 128:(i + 1) * 128, :], [], ['xa%d' % s], 'xa%d' % s)

                loadx(0)

                def fa(i):
                    s = i % 3
                    if i + 1 < NB:
                        loadx(i + 1)
                    layernorm(xa[s][:], 'xa%d' % s, gB[:], bB[:], tl, 'a', out_bf=hb[s][:], out_bf_key='hb%d' % s)

                def fb_a(i):
                    s = i % 3
                    s2 = i % 2
                    for kc in range(8):
                        k.tr(PSB(0)[:, kc * 128:(kc + 1) * 128], hb[s][:, kc * 128:(kc + 1) * 128], identb[:], ['hb%d' % s], ['ps0'])
                    k.cp('act', hT[s2][:].rearrange("p a b -> p (a b)"), PSB(0), ['ps0'], ['hT%d' % s2])

                def fb_b(i):
                    s = i % 3
                    s2 = i % 2
                    last = (i == NB - 1)
                    hk = 'hT%d' % s2
                    for kc in range(8):
                        k.mm(PSF(1), hT[s2][:, kc, :], wA[:, kc, 0:512], kc == 0, kc == 7, [hk], ['ps1'])
                    k.cp('act', ksb[s][:], PSF(1), ['ps1'], ['ksb%d' % s])
                    for half in range(2):
                        for kc in range(8):
                            k.mm(PSF(2 + half), hT[s2][:, kc, :], wA[:, kc, 512 + half * 512:1024 + half * 512], kc == 0, kc == 7, [hk], ['ps%d' % (2 + half)])
                        k.cp('act' if half == 0 else 'dve', Vt[i % 4][:, half * 512:(half + 1) * 512], PSF(2 + half), ['ps%d' % (2 + half)], ['V%d' % (i % 4)])
                    for kc in range(8):
                        k.mm(PSF(4), hT[s2][:, kc, :], wA[:, kc, 1536:2048], kc == 0, kc == 7, [hk], ['ps4'])
                    k.cp('act', kdv[s][:], PSF(4), ['ps4'], ['kdv%d' % s])
                    for kc in range(8):
                        k.mm(PSF(5)[:, 0:64], hT[s2][:, kc, :], wA[:, kc, 2048:2112], kc == 0, kc == 7, [hk], ['ps5'])
                    k.cp('dve', kin[s][:], PSF(5)[:, 0:64], ['ps5'], ['kin%d' % s])

                    for kc in range(8):
                        k.mm(PSF(5)[0:16, 64:192], wA[:, kc, 2112:2128], hT[s2][:, kc, :], kc == 0, kc == 7, [hk], ['ps5'])
                    k.cp('dve', glb[:], PSF(5)[0:16, 64:192], ['ps5'], ['gl'])
                    k.mm(PSF(4), glb[:], w2b[:], True, True, ['gl'], ['ps4'])
                    k.tt(el[s][:], PSF(4), gbB[:], ALU.add, ['ps4'], ['el%d' % s])
                    k.act(el[s][:], el[s][:], AF.Exp, ['el%d' % s], ['el%d' % s], scale=-1.0)
                    k.act(el[s][:], el[s][:], AF.Ln, ['el%d' % s], ['el%d' % s], bias=one[:, 0:1], scale=1.0)
                    if last:
                        k.ts(el[s][:], el[s][:], tailm[:, 0:1], None, ALU.mult, None, ['el%d' % s], ['el%d' % s])
                def back1(i):
                    s3 = i % 3
                    s = i % 2
                    last = (i == NB - 1)
                    k.mm(PSF(6), triA[:], el[s3][:], True, True, ['el%d' % s3], ['ps6'])
                    for h in range(4):
                        k.mm(PSF(7)[:, 448 + 2 * h:450 + 2 * h], el[s3][:, h * 128:(h + 1) * 128], blkA[:], True, True, ['el%d' % s3], ['ps7'])
                    k.act(er[:], PSF(6), AF.Exp, ['ps6'], ['er'])
                    k.act(dec[s][:], PSF(7)[:, 448:456], AF.Exp, ['ps7'], ['dec%d' % s])
                    if last:
                        k.stt(Kt[s][:], ksb[s3][:], tailm[:, 0:1], er[:], ALU.mult, ALU.mult, ['ksb%d' % s3, 'er'], ['Kt%d' % s])
                    else:
                        k.tt(Kt[s][:], ksb[s3][:], er[:], ALU.mult, ['ksb%d' % s3, 'er'], ['Kt%d' % s])
                    k.dma('act', kp[i * 128:(i + 1) * 128, :], kdv[s3][:, 0:256], ['kdv%d' % s3], [], 'kdvo%d' % s3)
                    k.dma('act', vp[i * 128:(i + 1) * 128, :], kdv[s3][:, 256:512], ['kdv%d' % s3], [], 'kdvo%d' % s3)
                    k.cp('pool', kdb[s][:], kdv[s3][:, 0:256], ['kdv%d' % s3], ['kdb%d' % s])
                    k.cp('pool', vext[s][:, :, 0:64], kdv[s3][:, 256:512].rearrange("p (g d) -> p g d", g=4), ['kdv%d' % s3], ['vext%d' % s])
                    for c in range(2):
                        k.tr(PSB(7)[:, c * 128:(c + 1) * 128], kdb[s][:, c * 128:(c + 1) * 128], identb[:], ['kdb%d' % s], ['ps7'])
                    k.cp('dve', kTt[s][:].rearrange("p a b -> p (a b)"), PSB(7)[:, 0:256], ['ps7'], ['kT%d' % s])
                    k.dma('pool', kT_d[:, :, i * 128:(i + 1) * 128], kTt[s][:], ['kT%d' % s], [], 'kTo%d' % s)
                    k.dma('pool', v_d[:, i, :], vext[s][:].rearrange("p g d -> p (g d)"), ['vext%d' % s], [], 'vexto%d' % s)
                    kn = kin[s3]
                    knk = 'kin%d' % s3
                    k.bn_stats(st2[:], kn[:], [knk], ['st2'])
                    k.bn_aggr(mv2[:], st2[:], ['st2'], ['mv2'])
                    k.act(sd2[:], mv2[:, 1:2], AF.Ln, ['mv2'], ['sd2'], bias=eps[:, 0:1], scale=1.0)
                    k.act(rstd2[:], sd2[:], AF.Exp, ['sd2'], ['rstd2'], scale=-0.5)
                    k.ts(nmr2[:], mv2[:, 0:1], rstd2[:, 0:1], -1.0, ALU.mult, ALU.mult, ['mv2', 'rstd2'], ['nmr2'])
                    k.act(kn[:], kn[:], AF.Identity, [knk, 'nmr2', 'rstd2'], [knk], bias=nmr2[:, 0:1], scale=rstd2[:, 0:1])
                    k.tt(kn[:], kn[:], gkiB[:], ALU.mult, [knk], [knk])
                    k.tt(kif[s][:], kn[:], bkiB[:], ALU.add, [knk], ['kif%d' % s])
                    k.dma('act', ikp[i * 128:(i + 1) * 128, :], kif[s][:], ['kif%d' % s], [], 'kifo%d' % s)
                    k.cp('pool', kib[s][:, 0:64], kif[s][:], ['kif%d' % s], ['kib%d' % s])
                    k.cp('pool', kib[s][:, 64:128], kif[s][:], ['kif%d' % s], ['kib%d' % s])
                    k.tr(PSB(7)[:, 256:384], kib[s][:], identb[:], ['kib%d' % s], ['ps7'])
                    k.cp('dve', kiT[s][:], PSB(7)[:, 256:384], ['ps7'], ['kiT%d' % s])
                    k.dma('pool', ki_d[:, i * 128:(i + 1) * 128], kiT[s][:], ['kiT%d' % s], [], 'kiTo%d' % s)

                def back2(i):
                    s3 = i % 3
                    s = i % 2
                    cur = (2 * i) % 3
                    k.dma('sp', snap[i], SS[cur][:].rearrange("p h e -> p (h e)"), ['S%d' % cur], [], 'Ssto%d' % cur)
                    sbanks = [[6, 7], [2, 3]]
                    for c in range(2):
                        for hp in range(2):
                            bk = sbanks[c][hp]
                            for hh in range(2):
                                h = hp * 2 + hh
                                k.mm(PSF(bk)[:, hh * 256:(hh + 1) * 256], Kt[s][c * 64:(c + 1) * 64, h * 128:(h + 1) * 128],
                                     Vt[i % 4][c * 64:(c + 1) * 64, h * 256:(h + 1) * 256], True, True, ['Kt%d' % s, 'V%d' % (i % 4)], ['ps%d' % bk])
                            src_, dst_ = (2 * i + c) % 3, (2 * i + c + 1) % 3
                            for hh in range(2):
                                h = hp * 2 + hh
                                k.stt(SS[dst_][:, h, :], SS[src_][:, h, :], dec[s][:, 2 * h + c:2 * h + c + 1], PSF(bk)[:, hh * 256:(hh + 1) * 256],
                                      ALU.mult, ALU.add, ['S%d' % src_, 'dec%d' % s, 'ps%d' % bk], ['S%d' % dst_])

                for i0 in range(min(3, NB)):
                    fa(i0)
                for i0 in range(3):
                    if i0 < NB:
                        fb_a(i0)
                        fb_b(i0)
                    if i0 + 3 < NB and i0 < 2:
                        fa(i0 + 3)
                back1(0)
                for i in range(NB):
                    if i + 3 < NB:
                        fb_a(i + 3)
                    back2(i)
                    if i + 1 < NB:
                        back1(i + 1)
                    if i + 5 < NB:
                        fa(i + 5)
                    if i + 3 < NB:
                        fb_b(i + 3)
                fin = (2 * NB) % 3
                k.dma('sp', gla_p, SS[fin][:].rearrange("p h e -> p (h e)"), ['S%d' % fin], [], 'glap')
            P.barrier()


        def phase_own(mode):
            PR = (mode == 'P')
            NK = NKMAX if PR else 2176
            with ExitStack() as es:
                def sb(name, shape, dt):
                    return es.enter_context(nc.sbuf_tensor(mode + name, shape, dt))
                gB = sb("gB", [128, D], F32); bB = sb("bB", [128, D], F32)
                g2B = sb("g2B", [128, D], F32); b2B = sb("b2B", [128, D], F32)
                gtb = [sb("gtb%d" % s_, [128, 512], F32) for s_ in range(2)]
                gnB = sb("gnB", [128, 256], F32)
                identf = sb("idf", [128, 128], F32); identb = sb("idb", [128, 128], BF16)
                tri = sb("tri", [128, 128], F32)
                maskf = sb("maskf", [128, 512], F32)
                cst = sb("cst", [128, 512], F32); idrepb = sb("idrepb", [128, 512], BF16)
                w2 = sb("w2", [16, 512], F32); gbias = sb("gbias", [1, 512], F32); ones1 = sb("ones1", [1, 128], F32)
                eps = sb("eps", [128, 1], F32); one = sb("one", [128, 1], F32)
                ctab = sb("ctab", [128, KIT + 1], F32)
                tbias = sb("tbias", [128, 2, 640 if PR else 128], F32)
                onehot = sb("onehot", [128, 4], F32)
                xo = sb("xo", [128, D], F32); h = sb("h", [128, D], F32); hb = sb("hb", [128, D], BF16)
                hT = sb("hT", [128, 8, 128], BF16)
                tmp = sb("tmp", [128, D], F32)
                st = sb("st", [128, 2, 6], F32); mv = sb("mv", [128, 2], F32)
                sd = sb("sd", [128, 1], F32); rstd = sb("rstd", [128, 1], F32); nmr = sb("nmr", [128, 1], F32)
                wch = [sb("wch%d" % s_, [128, 8, 512], BF16) for s_ in range(2)]
                gl = sb("gl", [16, 128], F32)
                el = sb("el", [128, 512], F32); eb = sb("eb", [128, 512], F32); enb = sb("enb", [128, 512], F32)
                qT = sb("qT", [128, 4, 128], BF16); kTh = sb("kTh", [128, 4, 128], BF16)
                V = sb("V", [128, 1024], BF16)
                sg = [sb("sg%d" % s_, [128, 512], F32) for s_ in range(2)]
                QTz = sb("QTz", [128, 4, 512], BF16); qiTz = sb("qiTz", [128, 8, 128], BF16)
                wabs = sb("wabs", [128, 8], F32); wsgn = sb("wsgn", [128, 8], F32)
                AT = sb("AT", [128, 4, 128], BF16)
                ss = sb("ss", [128, 4], F32); rs = sb("rs", [128, 4], F32)
                yain = sb("yain", [128, D], BF16); yT = sb("yT", [128, 8, 128], BF16)
                mrg = sb("mrg", [128, D], F32)
                sc = sb("sc", [128, NK], F32)
                junk = None if PR else sb("junk", [128, 2176], BF16)
                rl = [sb("rl%d" % s_, [128, 512], F32) for s_ in range(2)]
                rd = sb("rd", [128, 16], F32)
                yout = xo
                rmax = sb("rmax", [128, 1], F32); rmin = sb("rmin", [128, 1], F32); Wd = sb("Wd", [128, 1], F32)
                wtab = sb("wtab", [128, KIT + 1], F32); mids = sb("mids", [128, KIT + 1], F32)
                cnts = sb("cnts", [128, KIT], F32); us = sb("us", [128, KIT], F32); thr = sb("thr", [128, 1], F32)
                sAs = sb("sAs", [128, KIT], F32); vvs = sb("vvs", [128, KIT], F32)
                jd = sb("jd", [128, 8], BF16); ja = sb("ja", [128, 8], BF16); jq = sb("jq", [128, 8], BF16)
                if PR:
                    Sc = [sb("Sc%d" % s_, [128, 1024], F32) for s_ in range(2)]
                    Sown = sb("Sown", [128, 1024], F32); Sb = sb("Sb", [128, 4, 256], BF16)
                    kich = [sb("kich%d" % s_, [128, 512], BF16) for s_ in range(2)]
                    kTch = [sb("kTch%d" % s_, [128, 2, 512], BF16) for s_ in range(2)]
                    vch = [sb("vch%d" % s_, [128, 4, 260], BF16) for s_ in range(2)]
                    mbt = [sb("mbt%d" % s_, [128, 128], BF16) for s_ in range(3)]
                    pT = [sb("pT%d" % s_, [128, 512], BF16) for s_ in range(3)]
                    oT = [sb("oT%d" % s_, [65, 512], F32) for s_ in range(2)]
                else:
                    cmaskS = sb("cmaskS", [128, 4, 128], F32); rmaskS = sb("rmaskS", [128, 4], F32)
                    mrevS = sb("mrevS", [128, 128], F32); bsumS = sb("bsumS", [128, 4], F32)
                    idrepSb = sb("idrepSb", [128, 4, 64], BF16)
                    selSb = sb("selSb", [128, 4, 128], BF16)
                    gkiB = sb("gkiB", [128, 64], F32); bkiB = sb("bkiB", [128, 64], F32)
                    S0f = [sb("S0f%d" % b_, [128, 4, 256], F32) for b_ in range(2)]
                    S0b = [sb("S0b%d" % b_, [128, 4, 256], BF16) for b_ in range(4)]
                    qTb = [sb("qTb%d" % b_, [128, 4, 128], BF16) for b_ in range(4)]
                    wabsb = sb("wabsb", [128, 4, 8], F32)
                    QTs = sb("QTs", [128, 4, 4, 64], BF16)
                    kTs1 = sb("kTs", [128, 2, 2176], BF16)
                    kiTs = [sb("kiTs%d" % b_, [128, 2176], BF16) for b_ in range(4)]
                    vexts1 = sb("vexts", [128, 17, 260], BF16)
                    ckf = sb("ckf", [128, 8, 256], F32); ckb = sb("ckb", [128, 8, 256], BF16)
                    cif = sb("cif", [128, 8, 64], F32); cib = sb("cib", [128, 8, 128], BF16)
                    kdv = sb("kdv", [128, 512], F32); kdb = sb("kdb", [128, 256], BF16); vnb = sb("vnb", [128, 256], BF16)
                    kin = sb("kin", [128, 64], F32); kif = sb("kif", [128, 64], F32); kib = sb("kib", [128, 128], BF16)
                    st2 = sb("st2", [128, 6], F32); mv2 = sb("mv2", [128, 2], F32)
                    sd2 = sb("sd2", [128, 1], F32); rstd2 = sb("rstd2", [128, 1], F32); nmr2 = sb("nmr2", [128, 1], F32)
                    kTnew = sb("kTnew", [128, 2, 128], BF16); kiTnew = sb("kiTnew", [128, 128], BF16)
                    Kt = sb("Kt", [128, 512], BF16); Ktb = [sb("Ktb%d" % s_, [128, 512], BF16) for s_ in range(2)]
                    decs = sb("decs", [128, 16], F32)
                    pTs = [sb("pTs%d" % s_, [128, 64], BF16) for s_ in range(3)]

                cl = 'c' + mode
                k.dma('sp', gB[:], ln_in_g.partition_broadcast(128), [], [], cl)
                k.dma('sp', bB[:], ln_in_b.partition_broadcast(128), [], [], cl)
                k.dma('sp', g2B[:], ln_g.partition_broadcast(128), [], [], cl)
                k.dma('sp', b2B[:], ln_b.partition_broadcast(128), [], [], cl)
                k.dma('sp', gnB[:], gla_norm_g.partition_broadcast(128), [], [], cl)
                k.dma('sp', identf[:], c_ident, [], [], cl)
                k.dma('sp', tri[:], c_triB if PR else c_triS, [], [], cl)
                k.dma('sp', maskf[:], (c_maskB if PR else c_maskS).rearrange("p a b -> p (a b)"), [], [], cl)
                k.dma('sp', w2[:], gla_w2, [], [], cl)
                k.dma('sp', gbias[:], gla_gate_b, [], [], cl)
                k.dma('sp', ctab[:], c_ctab, [], [], cl)
                k.dma('sp', tbias[:], c_tbias if PR else c_tbias[:, :, 0:128], [], [], cl)
                k.dma('sp', onehot[:], c_onehot, [], [], cl)
                if not PR:
                    k.dma('sp', cmaskS[:], c_cmaskS, [], [], cl)
                    k.dma('sp', rmaskS[:], c_rmaskS, [], [], cl)
                    k.dma('sp', mrevS[:], c_mrevS, [], [], cl)
                    k.dma('sp', bsumS[:], c_bsumS, [], [], cl)
                    k.dma('sp', gkiB[:], idx_kn_g.partition_broadcast(128), [], [], cl)
                    k.dma('sp', bkiB[:], idx_kn_b.partition_broadcast(128), [], [], cl)
                P.barrier()
                k.memset('dve', eps[:], EPS, [])
                k.memset('dve', one[:], 1.0, [])
                k.memset('dve', ones1[:], 1.0, [])
                k.memset('pool', QTz[:], 0.0, [])
                k.memset('pool', qiTz[:], 0.0, [])
                k.memset('pool', yain[:], 0.0, [])
                k.memset('pool', tmp[:], 0.0, [])
                k.cp('dve', identb[:], identf[:], [], [])
                k.dma('sp', cst[:], c_idrep, [], ['cst'], 'cst')
                k.cp('dve', idrepb[:], cst[:], ['cst'], ['idrepb'])
                if not PR:
                    k.dma('sp', cst[:, 0:256], c_idrepS.rearrange("p a b -> p (a b)"), ['idrepb'], ['cst'], 'cst')
                    k.cp('dve', idrepSb[:].rearrange("p a b -> p (a b)"), cst[:, 0:256], ['cst'], ['idrepSb'])
                    k.dma('sp', cst[:], c_selS.rearrange("p a b -> p (a b)"), ['idrepSb'], ['cst'], 'cst')
                    k.cp('dve', selSb[:].rearrange("p a b -> p (a b)"), cst[:], ['cst'], ['selSb'])
                    for b_ in range(4):
                        s_ = b_ % 2
                        k.dma('sp', S0f[s_][:], state[b_].rearrange("h p e -> p h e"), [], ['S0f%d' % s_], 'S0f%d' % s_)
                        k.cp('pool', S0b[b_][:], S0f[s_][:], ['S0f%d' % s_], ['S0b%d' % b_])
                        k.memset('pool', kiTs[b_][:, 2048:2176], 0.0, [])
                    k.memset('pool', vexts1[:], 1.0, [])
                    k.memset('pool', kTs1[:, :, 2048:2176], 0.0, [])
                P.barrier()
                tl = dict(st=st, mv=mv, sd=sd, rstd=rstd, nmr=nmr, xn=tmp, eps=eps, xnkey='tmp')
                bank = [0]

                def nb():
                    bank[0] = (bank[0] + 1) % 8
                    return bank[0]

                def own_block(g):
                    jobs = []

                    def J(src, n):
                        jobs.append((src, n))
                    J(wbf[:, C_IW:C_IW + 8], 8)
                    J(wbf[:, C_IQ:C_IQ + 512], 512)
                    J(wbf[:, C_DQ:C_DQ + 512], 512); J(wbf[:, C_DQ + 512:C_DQ + 1024], 512)
                    if not PR:
                        J(wbf[:, C_DK:C_DK + 512], 512)
                        J(wbf[:, C_IK:C_IK + 64], 64)
                    J(wbf[:, C_GLOW:C_GLOW + 16], 16)
                    J(wbf[:, C_GQ:C_GQ + 512], 512)
                    J(wbf[:, C_GK:C_GK + 512], 512)
                    J(wbf[:, C_GV:C_GV + 512], 512); J(wbf[:, C_GV + 512:C_GV + 1024], 512)
                    J(wbf[:, C_GR:C_GR + 512], 512); J(wbf[:, C_GR + 512:C_GR + 1024], 512)
                    for cc in range(2):
                        J(wbf3[0, :, cc * 512:(cc + 1) * 512], 512)
                        J(wbf[:, C_MA + cc * 512:C_MA + (cc + 1) * 512], 512)
                    J(wbf[:, C_DZ:C_DZ + 512], 512); J(wbf[:, C_DZ + 512:C_DZ + 1024], 512)
                    for cc in range(2):
                        J(wbf3[1, :, cc * 512:(cc + 1) * 512], 512)
                        J(wbf[:, C_MB + cc * 512:C_MB + (cc + 1) * 512], 512)
                    for cc in range(2):
                        J(wbf3[2, :, cc * 512:(cc + 1) * 512], 512)
                    jpos = [0]

                    def wissue(idx):
                        src, n = jobs[idx]
                        s_ = idx % 2
                        k.dma('sp', wch[s_][:, :, :n], src.rearrange("(k p) n -> p k n", p=128), [], ['wch%d' % s_], 'wch%d' % s_)

                    def next_w():
                        idx = jpos[0]
                        if idx == 0:
                            wissue(0)
                        if idx + 1 < len(jobs):
                            wissue(idx + 1)
                        jpos[0] += 1
                        return wch[idx % 2], 'wch%d' % (idx % 2)

                    def projT(wt, wk, n, psap, pskey):
                        for kc in range(8):
                            k.mm(psap, hT[:, kc, :], wt[:, kc, :n], kc == 0, kc == 7, ['hT', wk], [pskey])

                    def projF(wt, wk, bk):
                        for sub in range(4):
                            for kc in range(8):
                                k.mm(PSF(bk)[:, sub * 128:(sub + 1) * 128], wt[:, kc, sub * 128:(sub + 1) * 128], hT[:, kc, :],
                                     kc == 0, kc == 7, ['hT', wk], ['ps%d' % bk])

                    def transpose8(src, srckey, dst, dstkey):
                        bk = nb()
                        for kc in range(8):
                            k.tr(PSB(bk)[:, kc * 128:(kc + 1) * 128], src[:, kc * 128:(kc + 1) * 128], identb[:], [srckey], ['ps%d' % bk])
                        k.cp('act', dst[:].rearrange("p a b -> p (a b)"), PSB(bk), ['ps%d' % bk], [dstkey])

                    xsrc = xown[g * 128:(g + 1) * 128, :] if PR else xs
                    k.dma('act', xo[:], xsrc, [], ['xo'], 'xo')
                    layernorm(xo[:], 'xo', gB[:], bB[:], tl, 'b', out_f32=h[:], out_f32_key='h', out_bf=hb[:], out_bf_key='hb')
                    transpose8(hb, 'hb', hT, 'hT')
                    wt, wk = next_w()
                    bw = nb()
                    projT(wt, wk, 8, PSF(bw)[:, 0:8], 'ps%d' % bw)
                    k.ts(wabs[:], PSF(bw)[:, 0:8], -IDX_W_SCALE, None, ALU.mult, None, ['ps%d' % bw], ['wabs'])
                    k.stt(wabs[:], PSF(bw)[:, 0:8], IDX_W_SCALE, wabs[:], ALU.mult, ALU.max, ['ps%d' % bw, 'wabs'], ['wabs'])
                    k.ts(wsgn[:], PSF(bw)[:, 0:8], 0.0, 2.0, ALU.is_ge, ALU.mult, ['ps%d' % bw], ['wsgn'])
                    k.ts(wsgn[:], wsgn[:], -1.0, None, ALU.add, None, ['wsgn'], ['wsgn'])
                    wt, wk = next_w()
                    bi = nb()
                    projF(wt, wk, bi)
                    qv = qiTz[:].rearrange("p (s two) t -> p s two t", two=2)
                    pv = PSF(bi).rearrange("p (s t) -> p s t", s=4)
                    k.cp('act', qv[0:64, :, 0, :], pv[0:64, :, :], ['ps%d' % bi], ['qiTz'])
                    k.cp('act', qv[64:128, :, 1, :], pv[64:128, :, :], ['ps%d' % bi], ['qiTz'])
                    for m in range(2):
                        wt, wk = next_w()
                        bdq = nb()
                        projF(wt, wk, bdq)
                        k.cp('act', QTz[0:64, 2 * m, :], PSF(bdq)[0:64, :], ['ps%d' % bdq], ['QTz'])
                        k.cp('act', QTz[64:128, 2 * m + 1, :], PSF(bdq)[64:128, :], ['ps%d' % bdq], ['QTz'])
                    if not PR:
                        wt, wk = next_w()
                        bkv = nb()
                        projT(wt, wk, 512, PSF(bkv), 'ps%d' % bkv)
                        k.cp('act', kdv[:], PSF(bkv), ['ps%d' % bkv], ['kdv'])
                        k.dma('act', ks, kdv[:, 0:256], ['kdv'], [], 'so1')
                        k.dma('act', vs, kdv[:, 256:512], ['kdv'], [], 'so1')
                        k.cp('pool', kdb[:], kdv[:, 0:256], ['kdv'], ['kdb'])
                        k.cp('pool', vnb[:], kdv[:, 256:512], ['kdv'], ['vnb'])
                        wt, wk = next_w()
                        bik = nb()
                        projT(wt, wk, 64, PSF(bik)[:, 0:64], 'ps%d' % bik)
                        k.cp('dve', kin[:], PSF(bik)[:, 0:64], ['ps%d' % bik], ['kin'])
                        k.bn_stats(st2[:], kin[:], ['kin'], ['st2'])
                        k.bn_aggr(mv2[:], st2[:], ['st2'], ['mv2'])
                        k.act(sd2[:], mv2[:, 1:2], AF.Ln, ['mv2'], ['sd2'], bias=eps[:, 0:1], scale=1.0)
                        k.act(rstd2[:], sd2[:], AF.Exp, ['sd2'], ['rstd2'], scale=-0.5)
                        k.ts(nmr2[:], mv2[:, 0:1], rstd2[:, 0:1], -1.0, ALU.mult, ALU.mult, ['mv2', 'rstd2'], ['nmr2'])
                        k.act(kin[:], kin[:], AF.Identity, ['kin', 'nmr2', 'rstd2'], ['kin'], bias=nmr2[:, 0:1], scale=rstd2[:, 0:1])
                        k.tt(kin[:], kin[:], gkiB[:], ALU.mult, ['kin'], ['kin'])
                        k.tt(kif[:], kin[:], bkiB[:], ALU.add, ['kin'], ['kif'])
                        k.dma('act', iks, kif[:], ['kif'], [], 'so1')
                        k.cp('pool', kib[:, 0:64], kif[:], ['kif'], ['kib'])
                        k.cp('pool', kib[:, 64:128], kif[:], ['kif'], ['kib'])
                        bt_ = nb()
                        for c_ in range(2):
                            k.tr(PSB(bt_)[:, c_ * 128:(c_ + 1) * 128], kdb[:, c_ * 128:(c_ + 1) * 128], identb[:], ['kdb'], ['ps%d' % bt_])
                        k.tr(PSB(bt_)[:, 256:384], kib[:], identb[:], ['kib'], ['ps%d' % bt_])
                        k.cp('dve', kTnew[:].rearrange("p a b -> p (a b)"), PSB(bt_)[:, 0:256], ['ps%d' % bt_], ['kTnew'])
                        k.cp('dve', kiTnew[:], PSB(bt_)[:, 256:384], ['ps%d' % bt_], ['kiTnew'])
                        for b_ in range(4):
                            k.cp('pool', kiTs[b_][:, 2048:2064], kiTnew[:, 16 * b_:16 * b_ + 16], ['kiTnew'], ['kiTs%d' % b_])
                            for t8 in range(2):
                                k.dma('sp', cif[:], cache_ik[b_, t8 * 1024:(t8 + 1) * 1024, :].rearrange("(t p) c -> p t c", p=128), [], ['cif'], 'cif')
                                k.cp('pool', cib[:, :, 0:64], cif[:], ['cif'], ['cib'])
                                k.cp('pool', cib[:, :, 64:128], cif[:], ['cif'], ['cib'])
                                bt_ = nb()
                                for tt_ in range(8):
                                    k.tr(PSB(bt_)[:, tt_ * 128:(tt_ + 1) * 128], cib[:, tt_, :], identb[:], ['cib'], ['ps%d' % bt_])
                                k.cp('dve', kiTs[b_][:, t8 * 1024:(t8 + 1) * 1024], PSB(bt_), ['ps%d' % bt_], ['kiTs%d' % b_])
                    if PR:
                        n_tiles = min(4 * g + 5, NB)
                        tail0 = 4 * g
                        tidx = 1 if g == G - 1 else 0
                    else:
                        n_tiles = 17
                        tail0 = 16
                        tidx = 1
                    n_keys = n_tiles * 128
                    nch = (n_tiles + 3) // 4

                    def kiload(ci):
                        k0 = ci * 512
                        w_ = min(512, n_keys - k0)
                        s_ = ci % 2
                        k.dma('sp', kich[s_][:, :w_], ki_d[:, k0:k0 + w_], [], ['kich%d' % s_], 'kich%d' % s_)

                    if PR:
                        kiload(0)
                    if not PR:
                        for b_ in range(4):
                            k.ts(wabsb[:, b_, :], wabs[:], rmaskS[:, b_:b_ + 1], None, ALU.mult, None, ['wabs'], ['wabsb'])
                    rli = 0
                    for ci in range(nch):
                        k0 = ci * 512
                        w_ = min(512, n_keys - k0)
                        if PR and ci + 1 < nch:
                            kiload(ci + 1)
                        first = True
                        for hh in range(8):
                            for b_ in (range(1) if PR else range(4)):
                                bx = nb()
                                if PR:
                                    k.mm(PSF(bx)[:, :w_], qiTz[:, hh, :], kich[ci % 2][:, :w_], True, True, ['qiTz', 'kich%d' % (ci % 2)], ['ps%d' % bx])
                                    scl = wabs[:, hh:hh + 1]
                                    sck = 'wabs'
                                else:
                                    k.mm(PSF(bx)[:, :w_], qiTz[:, hh, :], kiTs[b_][:, k0:k0 + w_], True, True, ['qiTz', 'kiTs%d' % b_], ['ps%d' % bx])
                                    scl = wabsb[:, b_, hh:hh + 1]
                                    sck = 'wabsb'
                                r_ = rl[rli % 2]
                                rk = 'rl%d' % (rli % 2)
                                rli += 1
                                k.act(r_[:, :w_], PSF(bx)[:, :w_], AF.Relu, ['ps%d' % bx, sck], [rk], scale=scl)
                                if first:
                                    k.ts(sc[:, k0:k0 + w_], r_[:, :w_], wsgn[:, hh:hh + 1], None, ALU.mult, None, [rk, 'wsgn'], ['sc'])
                                    first = False
                                else:
                                    k.stt(sc[:, k0:k0 + w_], r_[:, :w_], wsgn[:, hh:hh + 1], sc[:, k0:k0 + w_], ALU.mult, ALU.add, [rk, 'wsgn', 'sc'], ['sc'])
                    k.capture()
                    k.red(rmax[:], sc[:, 0:n_keys], ALU.max, ['sc'], ['rmax'])
                    k.red(rmin[:], sc[:, 0:n_keys], ALU.min, ['sc'], ['rmin'])
                    tw = (n_tiles - tail0) * 128
                    k.tt(sc[:, tail0 * 128:tail0 * 128 + tw], sc[:, tail0 * 128:tail0 * 128 + tw], tbias[:, tidx, 0:tw], ALU.add, ['sc'], ['sc'])
                    k.tt(Wd[:], rmax[:], rmin[:], ALU.subtract, ['rmax', 'rmin'], ['Wd'])
                    k.ts(wtab[:], ctab[:], Wd[:, 0:1], None, ALU.mult, None, ['Wd'], ['wtab'])
                    k.tt(mids[:, 0:1], rmin[:], wtab[:, 0:1], ALU.add, ['rmin', 'wtab'], ['mid0'])
                    nD = (int(n_keys * 0.5) // 128) * 128
                    if nD < 256:
                        nD = n_keys
                    nA = n_keys - nD
                    for it in range(1, KIT + 1):
                        mid = mids[:, it - 1:it]
                        mk = 'mid%d' % (it - 1)
                        cn = cnts[:, it - 1:it]
                        ck_ = 'cnt%d' % it
                        k.ts(jd[:, 0:1].to_broadcast([128, nD]), sc[:, 0:nD], mid, None, ALU.is_ge, ALU.add, ['sc', mk], ['jd', ck_], accum=cn)
                        if nA > 0:
                            k.act(ja[:, 0:1].to_broadcast([128, nA]), sc[:, nD:n_keys], AF.Sign, ['sc', mk], ['ja', 'sa%d' % it],
                                  bias=mid, scale=-1.0, accum_out=sAs[:, it - 1:it])
                            k.stt(vvs[:, it - 1:it], cn, 2.0, sAs[:, it - 1:it], ALU.mult, ALU.subtract, [ck_, 'sa%d' % it], ['vv%d' % it])
                            vsrc, vkey, vthr = vvs[:, it - 1:it], 'vv%d' % it, 511.5 - nA
                        else:
                            vsrc, vkey, vthr = cn, ck_, 255.5
                        u_ = us[:, it - 1:it]
                        k.ts(u_, vsrc, vthr, wtab[:, it - 1:it], ALU.is_ge, ALU.mult, [vkey, 'wtab'], ['u%d' % it])
                        if it < KIT:
                            k.stt(mids[:, it:it + 1], u_, wtab[:, it:it + 1], mid, ALU.subtract, ALU.add, ['u%d' % it, 'wtab', mk], ['mid%d' % it])
                        else:
                            k.stt(thr[:], u_, wtab[:, it - 1:it], mid, ALU.subtract, ALU.add, ['u%d' % it, 'wtab', mk], ['thr'])
                    bisA = k.end_capture()
                    k.capture()
                    wt, wk = next_w()
                    b1 = nb()
                    for kc in range(8):
                        k.mm(PSF(b1)[0:16, 0:128], wt[:, kc, 0:16], hT[:, kc, :], kc == 0, kc == 7, ['hT', wk], ['ps%d' % b1])
                    k.cp('dve', gl[:], PSF(b1)[0:16, 0:128], ['ps%d' % b1], ['gl'])
                    bz = nb()
                    k.mm(PSF(bz), gl[:], w2[:], True, False, ['gl'], ['ps%d' % bz])
                    k.mm(PSF(bz), ones1[:], gbias[:], False, True, [], ['ps%d' % bz])
                    k.act(el[:], PSF(bz), AF.Exp, ['ps%d' % bz], ['el'], scale=-1.0)
                    k.act(el[:], el[:], AF.Ln, ['el'], ['el'], bias=one[:, 0:1], scale=1.0)
                    bb = nb()
                    for hh in range(4):
                        k.mm(PSF(bb)[:, hh * 128:(hh + 1) * 128], el[:, hh * 128:(hh + 1) * 128], tri[:], True, True, ['el'], ['ps%d' % bb])
                    k.act(eb[:], PSF(bb), AF.Exp, ['ps%d' % bb], ['eb'])
                    k.act(enb[:], PSF(bb), AF.Exp, ['ps%d' % bb], ['enb'], scale=-1.0)
                    wt, wk = next_w()
                    bq = nb()
                    projF(wt, wk, bq)
                    k.stt(qT[:].rearrange("p a b -> p (a b)"), PSF(bq), 128.0 ** -0.5, eb[:], ALU.mult, ALU.mult, ['ps%d' % bq, 'eb'], ['qT'])
                    wt, wk = next_w()
                    bk_ = nb()
                    projF(wt, wk, bk_)
                    k.tt(kTh[:].rearrange("p a b -> p (a b)"), PSF(bk_), enb[:], ALU.mult, ['ps%d' % bk_, 'enb'], ['kTh'])
                    if not PR:
                        bkt = nb()
                        projT(wt, wk, 512, PSF(bkt), 'ps%d' % bkt)
                        br_ = nb()
                        k.mm(PSF(br_), mrevS[:], el[:], True, True, ['el'], ['ps%d' % br_])
                        k.act(sg[1][:], PSF(br_), AF.Exp, ['ps%d' % br_], ['sg1'])
                        k.tt(Kt[:], PSF(bkt), sg[1][:], ALU.mult, ['ps%d' % bkt, 'sg1'], ['Kt'])
                        bd_ = nb()
                        for hh in range(4):
                            k.mm(PSF(bd_)[:, hh * 4:(hh + 1) * 4], el[:, hh * 128:(hh + 1) * 128], bsumS[:], True, True, ['el'], ['ps%d' % bd_])
                        k.act(decs[:], PSF(bd_)[:, 0:16], AF.Exp, ['ps%d' % bd_], ['decs'])
                    for half in range(2):
                        wt, wk = next_w()
                        bv = nb()
                        projT(wt, wk, 512, PSF(bv), 'ps%d' % bv)
                        k.cp('act', V[:, half * 512:(half + 1) * 512], PSF(bv), ['ps%d' % bv], ['V'])
                    if PR:
                        for m in range(4):
                            sidx = min(4 * g + m, NB - 1)
                            s_ = m % 2
                            k.dma('act', Sc[s_][:], snap[sidx], [], ['Sc%d' % s_], 'Sc%d' % s_)
                            if m == 0:
                                k.ts(Sown[:], Sc[s_][:], onehot[:, 0:1], None, ALU.mult, None, ['Sc%d' % s_], ['Sown'])
                            else:
                                k.stt(Sown[:], Sc[s_][:], onehot[:, m:m + 1], Sown[:], ALU.mult, ALU.add, ['Sc%d' % s_, 'Sown'], ['Sown'])
                        k.cp('pool', Sb[:].rearrange("p a b -> p (a b)"), Sown[:], ['Sown'], ['Sb'])
                    else:
                        for b_ in range(4):
                            for hh in range(4):
                                k.tt(qTb[b_][:, hh, :], qT[:, hh, :], cmaskS[:, b_, :], ALU.mult, ['qT'], ['qTb%d' % b_])
                    ba = nb()
                    for hh in range(4):
                        k.mm(PSF(ba)[:, hh * 128:(hh + 1) * 128], kTh[:, hh, :], qT[:, hh, :], True, True, ['kTh', 'qT'], ['ps%d' % ba])
                    k.tt(AT[:].rearrange("p a b -> p (a b)"), PSF(ba), maskf[:], ALU.mult, ['ps%d' % ba], ['AT'])
                    bo = [nb(), nb()]
                    for hh in range(4):
                        oap = PSF(bo[hh // 2])[:, (hh % 2) * 256:(hh % 2 + 1) * 256]
                        okey = 'ps%d' % bo[hh // 2]
                        if PR:
                            k.mm(oap, qT[:, hh, :], Sb[:, hh, :], True, False, ['qT', 'Sb'], [okey])
                        else:
                            for b_ in range(4):
                                k.mm(oap, qTb[b_][:, hh, :], S0b[b_][:, hh, :], b_ == 0, False, ['qTb%d' % b_], [okey])
                        k.mm(oap, AT[:, hh, :], V[:, hh * 256:(hh + 1) * 256], False, True, ['AT', 'V'], [okey])
                    for hh in range(4):
                        oap = PSF(bo[hh // 2])[:, (hh % 2) * 256:(hh % 2 + 1) * 256]
                        k.act(jq[:, 0:1].to_broadcast([128, 256]), oap, AF.Square, ['ps%d' % bo[hh // 2]], ['jq', 'ss%d' % hh], accum_out=ss[:, hh:hh + 1])
                    k.act(rs[:], ss[:], AF.Ln, ['ss0', 'ss1', 'ss2', 'ss3'], ['rs'], bias=eps[:, 0:1], scale=1.0 / 256)
                    k.act(rs[:], rs[:], AF.Exp, ['rs'], ['rs'], scale=-0.5)
                    for cc in range(2):
                        wt, wk = next_w()
                        bg = nb()
                        projT(wt, wk, 512, PSF(bg), 'ps%d' % bg)
                        k.act(sg[cc][:], PSF(bg), AF.Silu, ['ps%d' % bg], ['sg%d' % cc])
                        for hh in range(2):
                            hd = 2 * cc + hh
                            oap = PSF(bo[hd // 2])[:, (hd % 2) * 256:(hd % 2 + 1) * 256]
                            k.stt(tmp[:, hd * 256:(hd + 1) * 256], oap, rs[:, hd:hd + 1], gnB[:], ALU.mult, ALU.mult,
                                  ['ps%d' % bo[hd // 2], 'rs'], ['tmp'])
                        k.tt(yain[:, cc * 512:(cc + 1) * 512], tmp[:, cc * 512:(cc + 1) * 512], sg[cc][:], ALU.mult, ['tmp', 'sg%d' % cc], ['yain'])
                    transpose8(yain, 'yain', yT, 'yT')
                    for cc in range(2):
                        wt, wk = next_w()
                        by = nb()
                        for kc in range(8):
                            k.mm(PSF(by), yT[:, kc, :], wt[:, kc, :], kc == 0, kc == 7, ['yT', wk], ['ps%d' % by])
                        wt, wk = next_w()
                        bm = nb()
                        projT(wt, wk, 512, PSF(bm), 'ps%d' % bm)
                        k.dma('act', gtb[cc][:], gate_b[0:1, cc * 512:(cc + 1) * 512].partition_broadcast(128), [], ['gtb%d' % cc], 'gtb%d' % cc)
                        k.tt(sg[cc][:], PSF(bm), gtb[cc][:], ALU.add, ['ps%d' % bm, 'gtb%d' % cc], ['sg%d' % cc])
                        k.act(sg[cc][:], sg[cc][:], AF.Sigmoid, ['sg%d' % cc], ['sg%d' % cc])
                        k.tt(mrg[:, cc * 512:(cc + 1) * 512], PSF(by), sg[cc][:], ALU.mult, ['ps%d' % by, 'sg%d' % cc], ['mrg'])
                    if not PR:
                        for b_ in range(4):
                            s_ = b_ % 2
                            k.dma('sp', S0f[s_][:], state[b_].rearrange("h p e -> p h e"), [], ['S0f%d' % s_], 'S0f%d' % s_)
                            k.ts(Ktb[s_][:], Kt[:], rmaskS[:, b_:b_ + 1], None, ALU.mult, None, ['Kt'], ['Ktb%d' % s_])
                            for hp in range(2):
                                bs_ = nb()
                                for hh in range(2):
                                    hd = hp * 2 + hh
                                    k.mm(PSF(bs_)[:, hh * 256:(hh + 1) * 256], Ktb[s_][:, hd * 128:(hd + 1) * 128], V[:, hd * 256:(hd + 1) * 256],
                                         True, True, ['Ktb%d' % s_, 'V'], ['ps%d' % bs_])
                                for hh in range(2):
                                    hd = hp * 2 + hh
                                    k.stt(S0f[s_][:, hd, :], S0f[s_][:, hd, :], decs[:, hd * 4 + b_:hd * 4 + b_ + 1], PSF(bs_)[:, hh * 256:(hh + 1) * 256],
                                          ALU.mult, ALU.add, ['decs', 'ps%d' % bs_, 'S0f%d' % s_], ['S0f%d' % s_])
                            k.dma('act', gla_s[b_], S0f[s_][:].rearrange("p a b -> p (a b)"), ['S0f%d' % s_], [], 'Sno%d' % s_)

                    glaB = k.end_capture()
                    k.merge(bisA, glaB)

                    if PR:
                        def kvload(ci):
                            k0 = ci * 512
                            nt = min(4, n_tiles - ci * 4)
                            s_ = ci % 2
                            k.dma('sp', kTch[s_][:, :, :nt * 128], kT_d[:, :, k0:k0 + nt * 128], [], ['kTch%d' % s_], 'kTch%d' % s_)
                            k.dma('sp', vch[s_][:, :nt, :], v_d[:, ci * 4:ci * 4 + nt, :], [], ['vch%d' % s_], 'vch%d' % s_)
                        kvload(0)
                        li = 0
                        groups = []
                        for kt in range(n_tiles):
                            ci, tl_ = kt // 4, kt % 4
                            mb_ = mbt[kt % 3]
                            mbk = 'mbt%d' % (kt % 3)
                            for gg in range(4):
                                bl = 4 + (li % 4)
                                p_ = pT[li % 3]
                                pk = 'pT%d' % (li % 3)
                                li += 1
                                k.capture()
                                if gg == 0:
                                    if tl_ == 1 and ci + 1 < nch:
                                        kvload(ci + 1)
                                    k.ts(mb_[:], sc[:, kt * 128:(kt + 1) * 128], thr[:, 0:1], NEG, ALU.is_lt, ALU.mult, ['sc', 'thr'], [mbk])
                                k.mm(PSF(bl), kTch[ci % 2][:, gg // 2, tl_ * 128:(tl_ + 1) * 128], QTz[:, gg, :], True, False,
                                     ['kTch%d' % (ci % 2), 'QTz'], ['ps%d' % bl])
                                k.mm(PSF(bl), mb_[:], idrepb[:], False, True, [mbk], ['ps%d' % bl])
                                k.act(p_[:], PSF(bl), AF.Exp, ['ps%d' % bl], [pk], scale=0.125)
                                s1 = k.end_capture()
                                k.capture()
                                k.mm(PSF(gg)[0:65, :], vch[ci % 2][:, tl_, gg * 65:(gg + 1) * 65], p_[:], kt == 0, kt == n_tiles - 1,
                                     ['vch%d' % (ci % 2), pk], ['ps%d' % gg])
                                s2 = k.end_capture()
                                groups.append((s1, s2))
                        SK = 2
                        for idx in range(len(groups) + SK):
                            if idx < len(groups):
                                P.ops.extend(groups[idx][0])
                            if idx >= SK:
                                P.ops.extend(groups[idx - SK][1])
                        for gg in range(4):
                            o_ = oT[gg % 2]
                            ok_ = 'oT%d' % (gg % 2)
                            k.cp('act', o_[:], PSF(gg)[0:65, :], ['ps%d' % gg], [ok_])
                            for r_ in range(4):
                                k.tr(PSF(4 + gg)[:, r_ * 65:(r_ + 1) * 65], o_[0:65, r_ * 128:(r_ + 1) * 128], identf[0:65, 0:65], [ok_], ['ps%d' % (4 + gg)])
                        NPT = 128
                    else:
                        mball = junk
                        for b_ in range(4):
                            k.cp('pool', QTs[:, b_, :, :].rearrange("p g (r t) -> p g r t", r=4),
                                 QTz[:].rearrange("p g (r t) -> p g r t", r=4)[:, :, :, 16 * b_:16 * b_ + 16], ['QTz'], ['QTs'])
                        k.ts(mball[:], sc[:, 0:2176], thr[:, 0:1], NEG, ALU.is_lt, ALU.mult, ['sc', 'thr'], ['junk'])
                        li = 0
                        for b_ in range(4):
                            k.cp('pool', kTs1[:, :, 2048:2064], kTnew[:, :, 16 * b_:16 * b_ + 16], ['kTnew'], ['kTs'])
                            bs_ = nb() % 4 + 4
                            k.mm(PSF(bs_)[:, 0:256], selSb[:, b_, :], vnb[:], True, True, ['vnb'], ['ps%d' % bs_])
                            k.cp('act', vexts1[:, 16, :].rearrange("p (g d) -> p g d", g=4)[:, :, 0:64],
                                 PSF(bs_)[:, 0:256].rearrange("p (g d) -> p g d", g=4), ['ps%d' % bs_], ['vexts'])
                            for t8 in range(2):
                                k.dma('sp', ckf[:], cache_k[b_, t8 * 1024:(t8 + 1) * 1024, :].rearrange("(t p) c -> p t c", p=128), [], ['ckf'], 'ckf')
                                k.cp('pool', ckb[:], ckf[:], ['ckf'], ['ckb'])
                                for t4 in range(2):
                                    bt_ = nb() % 4 + 4
                                    for tt_ in range(4):
                                        for c_ in range(2):
                                            col = (tt_ * 2 + c_) * 128
                                            k.tr(PSB(bt_)[:, col:col + 128], ckb[:, t4 * 4 + tt_, c_ * 128:(c_ + 1) * 128], identb[:], ['ckb'], ['ps%d' % bt_])
                                    c0_ = t8 * 1024 + t4 * 512
                                    k.cp('dve', kTs1[:, :, c0_:c0_ + 512].rearrange("p c (t s) -> p c t s", t=4),
                                         PSB(bt_).rearrange("p (t c s) -> p c t s", t=4, c=2), ['ps%d' % bt_], ['kTs'])
                                k.dma('sp', ckf[:], cache_v[b_, t8 * 1024:(t8 + 1) * 1024, :].rearrange("(t p) c -> p t c", p=128), [], ['ckf'], 'ckf')
                                k.cp('pool', vexts1[:, t8 * 8:(t8 + 1) * 8, :].rearrange("p t (g d) -> p t g d", g=4)[:, :, :, 0:64],
                                     ckf[:].rearrange("p t (g d) -> p t g d", g=4), ['ckf'], ['vexts'])
                            for gg in range(4):
                                ob = b_ // 2
                                ocol = ((b_ % 2) * 4 + gg) * 64
                                qsel = QTs[:, b_, gg, :]
                                for kt in range(17):
                                    bl = 4 + (li % 4)
                                    p_ = pTs[li % 3]
                                    pk = 'pTs%d' % (li % 3)
                                    li += 1
                                    k.mm(PSF(bl)[:, 0:64], kTs1[:, gg // 2, kt * 128:(kt + 1) * 128], qsel, True, False,
                                         ['kTs', 'QTs'], ['ps%d' % bl])
                                    k.mm(PSF(bl)[:, 0:64], mball[:, kt * 128:(kt + 1) * 128], idrepSb[:, b_, :], False, True, ['junk'], ['ps%d' % bl])
                                    k.act(p_[:], PSF(bl)[:, 0:64], AF.Exp, ['ps%d' % bl], [pk], scale=0.125)
                                    k.mm(PSF(ob)[0:65, ocol:ocol + 64], vexts1[:, kt, gg * 65:(gg + 1) * 65], p_[:], kt == 0, kt == 16,
                                         ['vexts', pk], ['ps%d' % ob])
                        oTs = sc[0:65, 0:1024]
                        ov = oTs.rearrange("p (g r b t) -> p b g r t", g=4, r=4, b=4)
                        for ob in range(2):
                            k.cp('act', ov[:, 2 * ob:2 * ob + 2], PSF(ob)[0:65, :].rearrange("p (b g r t) -> p b g r t", b=2, g=4, r=4),
                                 ['ps%d' % ob, 'junk'], ['sc'])
                        for gg in range(4):
                            for r_ in range(4):
                                c0_ = (gg * 4 + r_) * 64
                                k.tr(PSF(4 + gg)[0:64, r_ * 65:(r_ + 1) * 65], oTs[:, c0_:c0_ + 64], identf[0:65, 0:65], ['sc'], ['ps%d' % (4 + gg)])
                        NPT = 64
                    for gg in range(4):
                        pv4 = PSF(4 + gg)[0:NPT, 0:260].rearrange("p (r c) -> p r c", c=65)
                        k.recip(rd[0:NPT, 4 * gg:4 * gg + 4], pv4[:, :, 64], ['ps%d' % (4 + gg)], ['rd'])
                        for r_ in range(4):
                            hd = 4 * gg + r_
                            k.ts(tmp[0:NPT, hd * 64:(hd + 1) * 64], pv4[:, r_, 0:64], rd[0:NPT, hd:hd + 1], None, ALU.mult, None,
                                 ['ps%d' % (4 + gg), 'rd'], ['tmp'])
                    for cc in range(2):
                        wt, wk = next_w()
                        bg = nb()
                        projT(wt, wk, 512, PSF(bg), 'ps%d' % bg)
                        k.act(sg[cc][:], PSF(bg), AF.Silu, ['ps%d' % bg], ['sg%d' % cc])
                        k.tt(yain[0:NPT, cc * 512:(cc + 1) * 512], tmp[0:NPT, cc * 512:(cc + 1) * 512], sg[cc][0:NPT, :], ALU.mult,
                             ['tmp', 'sg%d' % cc], ['yain'])
                    transpose8(yain, 'yain', yT, 'yT')
                    for cc in range(2):
                        wt, wk = next_w()
                        by = nb()
                        for kc in range(8):
                            k.mm(PSF(by), yT[:, kc, :], wt[:, kc, :], kc == 0, kc == 7, ['yT', wk], ['ps%d' % by])
                        wt, wk = next_w()
                        bm = nb()
                        projT(wt, wk, 512, PSF(bm), 'ps%d' % bm)
                        k.dma('act', gtb[cc][:], gate_b[0:1, 1024 + cc * 512:1024 + (cc + 1) * 512].partition_broadcast(128), [], ['gtb%d' % cc], 'gtb%d' % cc)
                        k.tt(sg[cc][:], PSF(bm), gtb[cc][:], ALU.add, ['ps%d' % bm, 'gtb%d' % cc], ['sg%d' % cc])
                        k.act(sg[cc][:], sg[cc][:], AF.Sigmoid, ['sg%d' % cc], ['sg%d' % cc])
                        k.tt(sg[cc][:], PSF(by), sg[cc][:], ALU.mult, ['ps%d' % by, 'sg%d' % cc], ['sg%d' % cc])
                        k.tt(mrg[:, cc * 512:(cc + 1) * 512], mrg[:, cc * 512:(cc + 1) * 512], sg[cc][:], ALU.add, ['mrg', 'sg%d' % cc], ['mrg'])
                    k.cp('pool', yain[:], mrg[:], ['mrg'], ['yain'])
                    transpose8(yain, 'yain', yT, 'yT')
                    for cc in range(2):
                        wt, wk = next_w()
                        by = nb()
                        for kc in range(8):
                            k.mm(PSF(by), yT[:, kc, :], wt[:, kc, :], kc == 0, kc == 7, ['yT', wk], ['ps%d' % by])
                        k.stt(mrg[:, cc * 512:(cc + 1) * 512], h[:, cc * 512:(cc + 1) * 512], ALPHA, PSF(by), ALU.mult, ALU.add, ['h', 'ps%d' % by], ['mrg'])
                    layernorm(mrg[:], 'mrg', g2B[:], b2B[:], tl, 'c', out_f32=yout[:], out_f32_key='xo')
                    ydst = y_own[g * 128:(g + 1) * 128, :] if PR else ys
                    k.dma('act', ydst, yout[:], ['xo'], [], 'youto')

                for g in (range(G) if PR else [0]):
                    own_block(g)
            P.barrier()

        if "B" in phases:
            phase_own('P')
        if "S" in phases:
            phase_own('S')

        P.emit(nc, top)
    return nc


def _consts(j, G):
    c = {}
    p = np.arange(128)
    c["c_ident"] = np.eye(128, dtype=np.float32)
    jj, ii = np.meshgrid(p, p, indexing="ij")
    c["c_triA"] = np.where((jj > ii) & (jj // 64 == ii // 64), -1.0 / 16, 0.0).astype(np.float32)
    c["c_blkA"] = np.where(p[:, None] // 64 == np.arange(2)[None, :], -1.0 / 16, 0.0).astype(np.float32)
    c["c_triB"] = np.where(jj <= ii, -1.0 / 16, 0.0).astype(np.float32)
    mB = (jj <= ii).astype(np.float32)
    c["c_maskB"] = np.ascontiguousarray(np.broadcast_to(mB[:, None, :], (128, 4, 128)))
    same = (jj // 16 == ii // 16)
    c["c_triS"] = np.where((jj <= ii) & same, -1.0 / 16, 0.0).astype(np.float32)
    mS = ((jj <= ii) & same).astype(np.float32)
    c["c_maskS"] = np.ascontiguousarray(np.broadcast_to(mS[:, None, :], (128, 4, 128)))
    c["c_mrevS"] = np.where((jj > ii) & same, -1.0 / 16, 0.0).astype(np.float32)
    c["c_bsumS"] = np.where(p[:, None] // 16 == np.arange(4)[None, :], -1.0 / 16, 0.0).astype(np.float32)
    cm = (p[None, :] // 16 == np.arange(4)[:, None]).astype(np.float32)
    c["c_cmaskS"] = np.ascontiguousarray(np.broadcast_to(cm[None], (128, 4, 128)))
    c["c_rmaskS"] = (p[:, None] // 16 == np.arange(4)[None, :]).astype(np.float32)
    c["c_idrep"] = np.ascontiguousarray(np.tile(np.eye(128, dtype=np.float32), (1, 4)))
    ids = np.zeros((128, 4, 64), np.float32)
    sel = np.zeros((128, 4, 128), np.float32)
    for b in range(4):
        for t in range(16):
            for r in range(4):
                ids[16 * b + t, b, r * 16 + t] = 1.0
            sel[16 * b + t, b, t] = 1.0
    c["c_idrepS"] = ids
    c["c_selS"] = sel
    c["c_ctab"] = np.ascontiguousarray(np.broadcast_to((0.5 ** np.arange(1, KIT + 2))[None, :], (128, KIT + 1))).astype(np.float32)
    tb = np.zeros((128, 2, 640), np.float32)
    kk = np.arange(640)
    for r in range(128):
        lim = 128 * j + (16 if r < 16 else (80 if r < 80 else 144))
        tb[r, 0, :] = np.where(kk < lim, 0.0, -1e30)
    tb[:, 1, :] = np.where(kk < 16, 0.0, -1e30)[None, :]
    c["c_tbias"] = tb
    oh = np.zeros((128, 4), np.float32)
    oh[:, j] = 1.0
    c["c_onehot"] = oh
    c["c_tailmask"] = (p < 16).astype(np.float32)[:, None]
    return c


def _dq_perm():
    perm = np.zeros(1024, np.int64)
    n = 0
    for m in range(2):
        for r in range(4):
            for half in range(2):
                g = 2 * m + half
                for d in range(64):
                    perm[n] = (g * 4 + r) * 64 + d
                    n += 1
    return perm


def prep(inp, SEQ):
    T, NB, G = geometry(SEQ)
    f = lambda a: np.ascontiguousarray(np.asarray(a, dtype=np.float32))
    w_in = f(inp["w_in"])[0].copy()
    w_in[:, C_DQ:C_DQ + 1024] = w_in[:, C_DQ:C_DQ + 1024][:, _dq_perm()]
    w3 = np.ascontiguousarray(np.stack([f(inp["w_gla"])[0], f(inp["w_dsa"])[0], f(inp["w_out"])[0]], 0))
    shared = dict(
        w_in=np.ascontiguousarray(w_in), w3=w3,
        ln_in_g=f(inp["ln_in_g"]).reshape(1, D), ln_in_b=f(inp["ln_in_b"]).reshape(1, D),
        ln_g=f(inp["ln_g"]).reshape(1, D), ln_b=f(inp["ln_b"]).reshape(1, D),
        gate_b=f(inp["gate_b"]).reshape(1, 2048), gla_gate_b=f(inp["gla_gate_b"]).reshape(1, 512),
        gla_norm_g=f(inp["gla_norm_g"]).reshape(1, 256),
        idx_kn_g=f(inp["idx_kn_g"]).reshape(1, 64), idx_kn_b=f(inp["idx_kn_b"]).reshape(1, 64),
        gla_w2=f(inp["gla_w2"]).reshape(16, 512))
    xp = f(inp["x_prompt"]); meta = f(inp["meta"]); xsm = f(inp["x_sample"])
    ck = f(inp["cache_k"])[0]; cv = f(inp["cache_v"])[0]; cik = f(inp["cache_idx_k"])[0]; stt = f(inp["state_gla"])[0]
    maps = []
    for c in range(8):
        b, j = c // 4, c % 4
        xall = np.zeros((4 * G * 128, D), np.float32)
        xall[:16] = meta
        xall[16:T] = xp[b]
        xown = np.ascontiguousarray(xall.reshape(G, 4, 128, D)[:, j].reshape(G * 128, D))
        xs_ = np.zeros((128, D), np.float32)
        xs_[:64] = xsm[4 * c:4 * c + 4].reshape(64, D)
        m = dict(shared)
        m.update(_consts(j, G))
        m.update(xall=np.ascontiguousarray(xall[:NB * 128]), xown=xown, xs=xs_,
                 cache_k=np.ascontiguousarray(ck[4 * c:4 * c + 4].reshape(4, 2048, 256)),
                 cache_v=np.ascontiguousarray(cv[4 * c:4 * c + 4].reshape(4, 2048, 256)),
                 cache_ik=np.ascontiguousarray(cik[4 * c:4 * c + 4]),
                 state=np.ascontiguousarray(stt[4 * c:4 * c + 4]))
        maps.append(m)
    return maps


def gather(res, SEQ):
    T, NB, G = geometry(SEQ)
    R = res.results
    yp = np.zeros((2, 4 * G * 128, D), np.float32)
    for c in range(8):
        b, j = c // 4, c % 4
        yp[b].reshape(G, 4, 128, D)[:, j] = R[c]["y_own"].reshape(G, 128, D)
    y_prompt = np.ascontiguousarray(yp[:, 16:T])
    y_sample = np.concatenate([R[c]["ys"][:64].reshape(4, 16, D) for c in range(8)], 0)
    k_prompt = np.stack([R[4 * b]["kp"][:T].reshape(T, 4, 64) for b in range(2)], 0)[None]
    v_prompt = np.stack([R[4 * b]["vp"][:T].reshape(T, 4, 64) for b in range(2)], 0)[None]
    ik_prompt = np.stack([R[4 * b]["ikp"][:T] for b in range(2)], 0)[None]
    gla_prompt = np.stack([R[4 * b]["gla_p"].reshape(128, 4, 256).transpose(1, 0, 2) for b in range(2)], 0)[None]
    k_sample = np.concatenate([R[c]["ks"][:64].reshape(4, 16, 4, 64) for c in range(8)], 0)[None]
    v_sample = np.concatenate([R[c]["vs"][:64].reshape(4, 16, 4, 64) for c in range(8)], 0)[None]
    ik_sample = np.concatenate([R[c]["iks"][:64].reshape(4, 16, 64) for c in range(8)], 0)[None]
    gla_sample = np.concatenate([R[c]["gla_s"].reshape(4, 128, 4, 256).transpose(0, 2, 1, 3) for c in range(8)], 0)[None]
    outs = (y_prompt, y_sample, k_prompt, v_prompt, ik_prompt, gla_prompt, k_sample, v_sample, ik_sample, gla_sample)
    return tuple(np.ascontiguousarray(o, dtype=np.float32) for o in outs)


_NC_CACHE = {}


def run(inputs, SEQ, phases="0ABS"):
    key = (SEQ, phases)
    if key not in _NC_CACHE:
        _NC_CACHE[key] = build(SEQ, phases)
    nc = _NC_CACHE[key]
    maps = prep(inputs, SEQ)
    res = run_bass_kernel_spmd(nc, maps, core_ids=list(range(8)))
    return gather(res, SEQ)


def kernel(**inputs):
    SEQ = int(np.asarray(inputs["x_prompt"]).shape[1])
    return run(inputs, SEQ)
```

```python
from contextlib import ExitStack
import numpy as np
import concourse.bass as bass
import concourse.mybir as mybir
from concourse.bass_utils import run_bass_kernel_spmd

F32 = mybir.dt.float32
BF16 = mybir.dt.bfloat16
AF = mybir.ActivationFunctionType
ALU = mybir.AluOpType
AX = mybir.AxisListType

D = 1024
NEG = -30000.0
KIT = 22
IDX_W_SCALE = (8 ** -0.5) * (64 ** -0.5)
ALPHA = 2.0 ** 0.25
EPS = 1e-5
C_GQ, C_GK, C_GV, C_GLOW, C_GR, C_DQ, C_DK, C_DV, C_IQ, C_IK, C_IW, C_DZ, C_MA, C_MB = (
    0, 512, 1024, 2048, 2064, 3088, 4112, 4368, 4624, 5136, 5200, 5208, 6232, 7256)
IN_COLS = 8280


class Prog:
    def __init__(self):
        self.ops = []

    def add(self, eng, fn, r=(), w=(), dsem=None):
        r = list(r)
        w = list(w)
        for b in list(r):
            if isinstance(b, str) and b.startswith('ps'):
                r.remove(b)
                if b not in w:
                    w.append(b)
        self.ops.append(dict(eng=eng, fn=fn, r=tuple(r), w=tuple(w), dsem=dsem))

    def barrier(self):
        self.ops.append(dict(eng='barrier', fn=None, r=(), w=(), dsem=None))

    def analyze(self):
        ops = self.ops
        last_w, readers = {}, {}
        last_of = {}
        for i, op in enumerate(ops):
            if op['eng'] == 'barrier':
                for e_, j in last_of.items():
                    ops[j]['needed'] = True
                last_w, readers = {}, {}
                op['deps'] = set()
                continue
            deps = set()
            for b in op['r']:
                if b in last_w:
                    deps.add(('raw', last_w[b]))
            for b in op['w']:
                if b in last_w:
                    deps.add(('waw', last_w[b]))
                for rr in readers.get(b, ()):
                    deps.add(('war', rr))
            for b in op['r']:
                readers.setdefault(b, []).append(i)
            for b in op['w']:
                last_w[b] = i
                readers[b] = []
            keep = set()
            for kind, j in deps:
                if j == i:
                    continue
                pj = ops[j]
                if pj['dsem'] is None and op['dsem'] is None and pj['eng'] == op['eng']:
                    if op['eng'] == 'pe' or kind == 'war':
                        continue
                keep.add(j)
            op['deps'] = keep
            for j in keep:
                ops[j]['needed'] = True
            if op['dsem'] is None:
                last_of[op['eng']] = i
        for e_, j in last_of.items():
            ops[j]['needed'] = True
        cnt = {}
        for op in ops:
            if op['eng'] == 'barrier':
                continue
            if op['dsem'] is not None:
                k = 'D:' + op['dsem']
                cnt[k] = cnt.get(k, 0) + 16
                op['sem'] = k
                op['val'] = cnt[k]
            elif op.get('needed'):
                k = 'E:' + op['eng']
                cnt[k] = cnt.get(k, 0) + 1
                op['sem'] = k
                op['val'] = cnt[k]
        waited = {}
        running = {}
        pending = {}
        for op in ops:
            if op['eng'] == 'barrier':
                for e_ in ('pe', 'act', 'dve', 'pool', 'sp'):
                    pending[e_] = dict(running)
                continue
            ws = {}
            if pending.get(op['eng']):
                ws.update(pending[op['eng']])
                pending[op['eng']] = None
            for j in op['deps']:
                pj = ops[j]
                ws[pj['sem']] = max(ws.get(pj['sem'], 0), pj['val'])
            wl = []
            we = waited.setdefault(op['eng'], {})
            for k, v in ws.items():
                if we.get(k, 0) >= v:
                    continue
                we[k] = v
                wl.append((k, v))
            op['waits'] = wl
            if op.get('sem') is not None:
                running[op['sem']] = op['val']
        self.totals = cnt
        return cnt

    def emit(self, nc, es):
        cnt = self.analyze()
        sems = {}
        for k in cnt:
            sems[k] = es.enter_context(nc.semaphore(k.replace(':', '_')))
        block = es.enter_context(nc.Block())
        ops = self.ops

        def run(engname):
            def f(e):
                for op in ops:
                    if op['eng'] != engname:
                        continue
                    for k, v in op['waits']:
                        e.wait_ge(sems[k], v)
                    ins = op['fn'](e)
                    if op.get('sem') is not None:
                        ins.then_inc(sems[op['sem']], 16 if op['dsem'] is not None else 1)
                for k, v in cnt.items():
                    e.wait_ge(sems[k], v)
            return f

        block.sync(run('sp'))
        block.scalar(run('act'))
        block.vector(run('dve'))
        block.gpsimd(run('pool'))
        block.tensor(run('pe'))


class KB:
    def __init__(self, nc):
        self.nc = nc
        self.P = Prog()
        self.rot = 0

    def capture(self):
        self._saved = self.P.ops
        self.P.ops = []

    def end_capture(self):
        l = self.P.ops
        self.P.ops = self._saved
        return l

    def merge(self, A, B):
        out = []
        ia = ib = 0
        na, nb_ = max(len(A), 1), max(len(B), 1)
        while ia < len(A) or ib < len(B):
            if ib >= len(B) or (ia < len(A) and ia * nb_ <= ib * na):
                out.append(A[ia]); ia += 1
            else:
                out.append(B[ib]); ib += 1
        self.P.ops.extend(out)

    def act(self, out, in_, func, r, w, **kw):
        self.P.add('act', lambda e: e.activation(out=out, in_=in_, func=func, **kw), r, w)

    def ts(self, out, in0, s1, s2, op0, op1, r, w, eng='dve', accum=None):
        if accum is None:
            if op1 is None:
                self.P.add(eng, lambda e: e.tensor_scalar(out=out, in0=in0, scalar1=s1, scalar2=None, op0=op0), r, w)
            else:
                self.P.add(eng, lambda e: e.tensor_scalar(out=out, in0=in0, scalar1=s1, scalar2=s2, op0=op0, op1=op1), r, w)
        else:
            self.P.add(eng, lambda e: e.tensor_scalar(out=out, in0=in0, scalar1=s1, scalar2=s2, op0=op0, op1=op1,
                                                      accum_out=accum), r, w)

    def tt(self, out, in0, in1, op, r, w, eng='dve'):
        self.P.add(eng, lambda e: e.tensor_tensor(out=out, in0=in0, in1=in1, op=op), r, w)

    def stt(self, out, in0, scalar, in1, op0, op1, r, w):
        self.P.add('dve', lambda e: e.scalar_tensor_tensor(out=out, in0=in0, scalar=scalar, in1=in1, op0=op0, op1=op1), r, w)

    def cp(self, eng, out, in_, r, w):
        if eng == 'act':
            self.P.add('act', lambda e: e.copy(out=out, in_=in_), r, w)
        else:
            self.P.add(eng, lambda e: e.tensor_copy(out=out, in_=in_), r, w)

    def memset(self, eng, ap, val, w):
        self.P.add(eng, lambda e: e.memset(ap, val), (), w)

    def mm(self, out, lhsT, rhs, start, stop, r, w):
        self.P.add('pe', lambda e: e.matmul(out, lhsT=lhsT, rhs=rhs, start=start, stop=stop), r, w)

    def tr(self, out, in_, ident, r, w):
        self.P.add('pe', lambda e: e.transpose(out=out, in_=in_, identity=ident), r, w)

    def dma(self, q, out, in_, r, w, dsem):
        self.P.add(q, lambda e: e.dma_start(out=out, in_=in_), r, w, dsem=dsem)

    def red(self, out, in_, op, r, w):
        self.P.add('dve', lambda e: e.tensor_reduce(out=out, in_=in_, axis=AX.X, op=op), r, w)

    def recip(self, out, in_, r, w):
        self.P.add('dve', lambda e: e.reciprocal(out=out, in_=in_), r, w)

    def bn_stats(self, out, in_, r, w):
        self.P.add('dve', lambda e: e.bn_stats(out=out, in_=in_), r, w)

    def bn_aggr(self, out, in_, r, w):
        self.P.add('dve', lambda e: e.bn_aggr(out=out, in_=in_), r, w)


def geometry(SEQ):
    T = SEQ + 16
    NB = T // 128 + 1
    assert T == 128 * (NB - 1) + 16 and NB % 4 == 1
    G = (NB + 3) // 4
    return T, NB, G


def build(SEQ, phases="0ABS"):
    T, NB, G = geometry(SEQ)
    NKMAX = NB * 128
    nc = bass.Bass("TRN2", target_bir_lowering=False)

    def din(name, shape, dt=F32):
        return nc.dram_tensor(name, list(shape), dt, kind="ExternalInput").ap()

    def dout(name, shape, dt=F32):
        return nc.dram_tensor(name, list(shape), dt, kind="ExternalOutput").ap()

    def dscr(name, shape, dt):
        return nc.dram_tensor(name, list(shape), dt, kind="Internal").ap()

    xall = din("xall", [NB * 128, D])
    xown = din("xown", [G * 128, D])
    xs = din("xs", [128, D])
    w_in = din("w_in", [D, IN_COLS])
    w3 = din("w3", [3, D, D])
    ln_in_g = din("ln_in_g", [1, D]); ln_in_b = din("ln_in_b", [1, D])
    ln_g = din("ln_g", [1, D]); ln_b = din("ln_b", [1, D])
    gate_b = din("gate_b", [1, 2048])
    gla_gate_b = din("gla_gate_b", [1, 512])
    gla_norm_g = din("gla_norm_g", [1, 256])
    idx_kn_g = din("idx_kn_g", [1, 64]); idx_kn_b = din("idx_kn_b", [1, 64])
    gla_w2 = din("gla_w2", [16, 512])
    cache_k = din("cache_k", [4, 2048, 256]); cache_v = din("cache_v", [4, 2048, 256])
    cache_ik = din("cache_ik", [4, 2048, 64])
    state = din("state", [4, 4, 128, 256])
    c_ident = din("c_ident", [128, 128])
    c_triA = din("c_triA", [128, 128]); c_blkA = din("c_blkA", [128, 2])
    c_triB = din("c_triB", [128, 128]); c_maskB = din("c_maskB", [128, 4, 128])
    c_triS = din("c_triS", [128, 128]); c_maskS = din("c_maskS", [128, 4, 128])
    c_mrevS = din("c_mrevS", [128, 128]); c_bsumS = din("c_bsumS", [128, 4])
    c_cmaskS = din("c_cmaskS", [128, 4, 128]); c_rmaskS = din("c_rmaskS", [128, 4])
    c_idrep = din("c_idrep", [128, 512]); c_idrepS = din("c_idrepS", [128, 4, 64])
    c_selS = din("c_selS", [128, 4, 128])
    c_ctab = din("c_ctab", [128, KIT + 1])
    c_tbias = din("c_tbias", [128, 2, 640])
    c_onehot = din("c_onehot", [128, 4])
    c_tailmask = din("c_tailmask", [128, 1])

    y_own = dout("y_own", [G * 128, D])
    kp = dout("kp", [NB * 128, 256]); vp = dout("vp", [NB * 128, 256]); ikp = dout("ikp", [NB * 128, 64])
    gla_p = dout("gla_p", [128, 1024])
    ys = dout("ys", [128, D]); ks = dout("ks", [128, 256]); vs = dout("vs", [128, 256]); iks = dout("iks", [128, 64])
    gla_s = dout("gla_s", [4, 128, 1024])

    wbf = dscr("wbf", [D, IN_COLS], BF16)
    wbf3 = dscr("wbf3", [3, D, D], BF16)
    kT_d = dscr("kT_d", [128, 2, NKMAX], BF16)
    v_d = dscr("v_d", [128, NB, 260], BF16)
    ki_d = dscr("ki_d", [128, NKMAX], BF16)
    snap = dscr("snap", [NB, 128, 1024], F32)

    k = KB(nc)
    P = k.P
    top = ExitStack()
    with top:
        pst = [top.enter_context(nc.psum_tensor("ps%d" % i, [128, 512], F32)) for i in range(8)]

        def PSF(i):
            return pst[i][:]

        def PSB(i):
            return pst[i][:].bitcast(BF16)

        if "0" in phases:
            with ExitStack() as es:
                def sb(name, shape, dt):
                    return es.enter_context(nc.sbuf_tensor(name, shape, dt))
                wst = [sb("wst%d" % s, [128, 8, 512], F32) for s in range(2)]
                wcb = [sb("wcb%d" % s, [128, 8, 512], BF16) for s in range(2)]
                jobs = []
                for c in range(17):
                    c0 = c * 512
                    n = min(512, IN_COLS - c0)
                    jobs.append((w_in[:, c0:c0 + n], wbf[:, c0:c0 + n], n))
                for m in range(3):
                    for c in range(2):
                        jobs.append((w3[m, :, c * 512:(c + 1) * 512], wbf3[m, :, c * 512:(c + 1) * 512], 512))
                engs = ['dve', 'act', 'pool']
                for idx, (src, dst, n) in enumerate(jobs):
                    s = idx % 2
                    k.dma('sp', wst[s][:, :, :n], src.rearrange("(k p) n -> p k n", p=128), [], ['wst%d' % s], 'wst%d' % s)
                    k.cp(engs[idx % 3], wcb[s][:, :, :n], wst[s][:, :, :n], ['wst%d' % s], ['wcb%d' % s])
                    k.dma('act', dst.rearrange("(k p) n -> p k n", p=128), wcb[s][:, :, :n], ['wcb%d' % s], [], 'wcb%d' % s)
            P.barrier()

        def layernorm(src, srckey, gB, bB, tl, tag, out_f32=None, out_f32_key=None, out_bf=None, out_bf_key=None):
            tk = lambda n: tag + n
            for c in range(2):
                k.bn_stats(tl['st'][:, c, :], src[:, c * 512:(c + 1) * 512], [srckey], [tk('st%d' % c)])
            k.bn_aggr(tl['mv'][:], tl['st'][:].rearrange("p a b -> p (a b)"), [tk('st0'), tk('st1')], [tk('mv')])
            k.act(tl['sd'][:], tl['mv'][:, 1:2], AF.Ln, [tk('mv'), 'eps'], [tk('sd')], bias=tl['eps'][:, 0:1], scale=1.0)
            k.act(tl['rstd'][:], tl['sd'][:], AF.Exp, [tk('sd')], [tk('rstd')], scale=-0.5)
            k.ts(tl['nmr'][:], tl['mv'][:, 0:1], tl['rstd'][:, 0:1], -1.0, ALU.mult, ALU.mult, [tk('mv'), tk('rstd')], [tk('nmr')])
            k.act(tl['xn'][:], src, AF.Identity, [srckey, tk('nmr'), tk('rstd')], [tl['xnkey']],
                  bias=tl['nmr'][:, 0:1], scale=tl['rstd'][:, 0:1])
            k.tt(tl['xn'][:], tl['xn'][:], gB, ALU.mult, [tl['xnkey'], 'lnconst'], [tl['xnkey']])
            if out_f32 is not None:
                k.tt(out_f32, tl['xn'][:], bB, ALU.add, [tl['xnkey'], 'lnconst'], [out_f32_key])
                if out_bf is not None:
                    k.cp('pool', out_bf, out_f32, [out_f32_key], [out_bf_key])
            else:
                k.tt(out_bf, tl['xn'][:], bB, ALU.add, [tl['xnkey'], 'lnconst'], [out_bf_key])

        if "A" in phases:
            with ExitStack() as es:
                def sb(name, shape, dt):
                    return es.enter_context(nc.sbuf_tensor(name, shape, dt))
                gB = sb("a_gB", [128, D], F32); bB = sb("a_bB", [128, D], F32)
                identf = sb("a_idf", [128, 128], F32); identb = sb("a_idb", [128, 128], BF16)
                triA = sb("a_triA", [128, 128], F32); blkA = sb("a_blkA", [128, 2], F32)
                w2 = sb("a_w2", [16, 512], F32); gbias = sb("a_gbias", [1, 512], F32); ones1 = sb("a_ones1", [1, 128], F32)
                gkiB = sb("a_gkiB", [128, 64], F32); bkiB = sb("a_bkiB", [128, 64], F32)
                eps = sb("a_eps", [128, 1], F32); one = sb("a_one", [128, 1], F32)
                tailm = sb("a_tailm", [128, 1], F32)
                wA = sb("a_wA", [128, 8, 2128], BF16)
                SS = [sb("a_S%d" % s, [128, 4, 256], F32) for s in range(3)]
                xa = [sb("a_xa%d" % s, [128, D], F32) for s in range(3)]
                xn = sb("a_xn", [128, D], F32)
                hb = [sb("a_hb%d" % s, [128, D], BF16) for s in range(3)]
                hT = [sb("a_hT%d" % s, [128, 8, 128], BF16) for s in range(2)]
                st = sb("a_st", [128, 2, 6], F32); mv = sb("a_mv", [128, 2], F32)
                sd = sb("a_sd", [128, 1], F32); rstd = sb("a_rstd", [128, 1], F32); nmr = sb("a_nmr", [128, 1], F32)
                st2 = sb("a_st2", [128, 6], F32); mv2 = sb("a_mv2", [128, 2], F32)
                sd2 = sb("a_sd2", [128, 1], F32); rstd2 = sb("a_rstd2", [128, 1], F32); nmr2 = sb("a_nmr2", [128, 1], F32)
                Vt = [sb("a_V%d" % s, [128, 1024], BF16) for s in range(4)]
                kdv = [sb("a_kdv%d" % s, [128, 512], F32) for s in range(3)]
                kdb = [sb("a_kdb%d" % s, [128, 256], BF16) for s in range(2)]
                vext = [sb("a_vext%d" % s, [128, 4, 65], BF16) for s in range(2)]
                kTt = [sb("a_kT%d" % s, [128, 2, 128], BF16) for s in range(2)]
                kin = [sb("a_kin%d" % s, [128, 64], F32) for s in range(3)]
                ksb = [sb("a_ksb%d" % s, [128, 512], F32) for s in range(3)]
                kif = [sb("a_kif%d" % s, [128, 64], F32) for s in range(2)]
                kib = [sb("a_kib%d" % s, [128, 128], BF16) for s in range(2)]
                kiT = [sb("a_kiT%d" % s, [128, 128], BF16) for s in range(2)]
                glb = sb("a_glb", [16, 128], BF16); w2b = sb("a_w2b", [16, 512], BF16); gbB = sb("a_gbB", [128, 512], F32)
                el = [sb("a_el%d" % s, [128, 512], F32) for s in range(3)]
                er = sb("a_er", [128, 512], F32)
                Kt = [sb("a_Kt%d" % s, [128, 512], BF16) for s in range(2)]
                dec = [sb("a_dec%d" % s, [128, 8], F32) for s in range(2)]

                k.dma('sp', gB[:], ln_in_g.partition_broadcast(128), [], ['lnconst0'], 'ca')
                k.dma('sp', bB[:], ln_in_b.partition_broadcast(128), [], ['lnconst1'], 'ca')
                k.dma('sp', identf[:], c_ident, [], ['identf'], 'ca')
                k.dma('sp', triA[:], c_triA, [], ['triA'], 'ca')
                k.dma('sp', blkA[:], c_blkA, [], ['blkA'], 'ca')
                k.dma('sp', w2[:], gla_w2, [], ['w2'], 'ca')
                k.dma('sp', gbias[:], gla_gate_b, [], ['gbias'], 'ca')
                k.dma('sp', gbB[:], gla_gate_b.partition_broadcast(128), [], ['gbB'], 'ca')
                k.dma('sp', gkiB[:], idx_kn_g.partition_broadcast(128), [], ['gkiB'], 'ca')
                k.dma('sp', bkiB[:], idx_kn_b.partition_broadcast(128), [], ['bkiB'], 'ca')
                k.dma('sp', tailm[:], c_tailmask, [], ['tailm'], 'ca')
                wmap = [(C_GK, 512, 0), (C_GV, 1024, 512), (C_DK, 512, 1536), (C_IK, 64, 2048), (C_GLOW, 16, 2112)]
                for (c0, n, o) in wmap:
                    k.dma('sp', wA[:, :, o:o + n], wbf[:, c0:c0 + n].rearrange("(k p) n -> p k n", p=128), [], ['wA%d' % o], 'ca')
                P.barrier()
                k.memset('dve', eps[:], EPS, ['eps'])
                k.memset('dve', one[:], 1.0, ['one'])
                k.memset('dve', ones1[:], 1.0, ['ones1'])
                k.memset('dve', SS[0][:], 0.0, ['S0'])
                for s in range(2):
                    k.memset('pool', vext[s][:], 1.0, ['vext%d' % s])
                k.cp('dve', identb[:], identf[:], [], ['identb'])
                k.cp('dve', w2b[:], w2[:], [], ['w2b'])
                P.barrier()
                tl = dict(st=st, mv=mv, sd=sd, rstd=rstd, nmr=nmr, xn=xn, eps=eps, xnkey='xn')

                def loadx(i):
                    s = i % 3
                    k.dma('sp', xa[s][:], xall[i * 128:(i + 1) * 128, :], [], ['xa%d' % s], 'xa%d' % s)

                loadx(0)

                def fa(i):
                    s = i % 3
                    if i + 1 < NB:
                        loadx(i + 1)
                    layernorm(xa[s][:], 'xa%d' % s, gB[:], bB[:], tl, 'a', out_bf=hb[s][:], out_bf_key='hb%d' % s)

                def fb_a(i):
                    s = i % 3
                    s2 = i % 2
                    for kc in range(8):
                        k.tr(PSB(0)[:, kc * 128:(kc + 1) * 128], hb[s][:, kc * 128:(kc + 1) * 128], identb[:], ['hb%d' % s], ['ps0'])
                    k.cp('act', hT[s2][:].rearrange("p a b -> p (a b)"), PSB(0), ['ps0'], ['hT%d' % s2])

                def fb_b(i):
                    s = i % 3
                    s2 = i % 2
                    last = (i == NB - 1)
                    hk = 'hT%d' % s2
                    for kc in range(8):
                        k.mm(PSF(1), hT[s2][:, kc, :], wA[:, kc, 0:512], kc == 0, kc == 7, [hk], ['ps1'])
                    k.cp('act', ksb[s][:], PSF(1), ['ps1'], ['ksb%d' % s])
                    for half in range(2):
                        for kc in range(8):
                            k.mm(PSF(2 + half), hT[s2][:, kc, :], wA[:, kc, 512 + half * 512:1024 + half * 512], kc == 0, kc == 7, [hk], ['ps%d' % (2 + half)])
                        k.cp('act' if half == 0 else 'dve', Vt[i % 4][:, half * 512:(half + 1) * 512], PSF(2 + half), ['ps%d' % (2 + half)], ['V%d' % (i % 4)])
                    for kc in range(8):
                        k.mm(PSF(4), hT[s2][:, kc, :], wA[:, kc, 1536:2048], kc == 0, kc == 7, [hk], ['ps4'])
                    k.cp('act', kdv[s][:], PSF(4), ['ps4'], ['kdv%d' % s])
                    for kc in range(8):
                        k.mm(PSF(5)[:, 0:64], hT[s2][:, kc, :], wA[:, kc, 2048:2112], kc == 0, kc == 7, [hk], ['ps5'])
                    k.cp('dve', kin[s][:], PSF(5)[:, 0:64], ['ps5'], ['kin%d' % s])

                    for kc in range(8):
                        k.mm(PSF(5)[0:16, 64:192], wA[:, kc, 2112:2128], hT[s2][:, kc, :], kc == 0, kc == 7, [hk], ['ps5'])
                    k.cp('dve', glb[:], PSF(5)[0:16, 64:192], ['ps5'], ['gl'])
                    k.mm(PSF(4), glb[:], w2b[:], True, True, ['gl'], ['ps4'])
                    k.tt(el[s][:], PSF(4), gbB[:], ALU.add, ['ps4'], ['el%d' % s])
                    k.act(el[s][:], el[s][:], AF.Exp, ['el%d' % s], ['el%d' % s], scale=-1.0)
                    k.act(el[s][:], el[s][:], AF.Ln, ['el%d' % s], ['el%d' % s], bias=one[:, 0:1], scale=1.0)
                    if last:
                        k.ts(el[s][:], el[s][:], tailm[:, 0:1], None, ALU.mult, None, ['el%d' % s], ['el%d' % s])
                def back1(i):
                    s3 = i % 3
                    s = i % 2
                    last = (i == NB - 1)
                    k.mm(PSF(6), triA[:], el[s3][:], True, True, ['el%d' % s3], ['ps6'])
                    for h in range(4):
                        k.mm(PSF(7)[:, 448 + 2 * h:450 + 2 * h], el[s3][:, h * 128:(h + 1) * 128], blkA[:], True, True, ['el%d' % s3], ['ps7'])
                    k.act(er[:], PSF(6), AF.Exp, ['ps6'], ['er'])
                    k.act(dec[s][:], PSF(7)[:, 448:456], AF.Exp, ['ps7'], ['dec%d' % s])
                    if last:
                        k.stt(Kt[s][:], ksb[s3][:], tailm[:, 0:1], er[:], ALU.mult, ALU.mult, ['ksb%d' % s3, 'er'], ['Kt%d' % s])
                    else:
                        k.tt(Kt[s][:], ksb[s3][:], er[:], ALU.mult, ['ksb%d' % s3, 'er'], ['Kt%d' % s])
                    k.dma('act', kp[i * 128:(i + 1) * 128, :], kdv[s3][:, 0:256], ['kdv%d' % s3], [], 'kdvo%d' % s3)
                    k.dma('act', vp[i * 128:(i + 1) * 128, :], kdv[s3][:, 256:512], ['kdv%d' % s3], [], 'kdvo%d' % s3)
                    k.cp('pool', kdb[s][:], kdv[s3][:, 0:256], ['kdv%d' % s3], ['kdb%d' % s])
                    k.cp('pool', vext[s][:, :, 0:64], kdv[s3][:, 256:512].rearrange("p (g d) -> p g d", g=4), ['kdv%d' % s3], ['vext%d' % s])
                    for c in range(2):
                        k.tr(PSB(7)[:, c * 128:(c + 1) * 128], kdb[s][:, c * 128:(c + 1) * 128], identb[:], ['kdb%d' % s], ['ps7'])
                    k.cp('dve', kTt[s][:].rearrange("p a b -> p (a b)"), PSB(7)[:, 0:256], ['ps7'], ['kT%d' % s])
                    k.dma('pool', kT_d[:, :, i * 128:(i + 1) * 128], kTt[s][:], ['kT%d' % s], [], 'kTo%d' % s)
                    k.dma('pool', v_d[:, i, :], vext[s][:].rearrange("p g d -> p (g d)"), ['vext%d' % s], [], 'vexto%d' % s)
                    kn = kin[s3]
                    knk = 'kin%d' % s3
                    k.bn_stats(st2[:], kn[:], [knk], ['st2'])
                    k.bn_aggr(mv2[:], st2[:], ['st2'], ['mv2'])
                    k.act(sd2[:], mv2[:, 1:2], AF.Ln, ['mv2'], ['sd2'], bias=eps[:, 0:1], scale=1.0)
                    k.act(rstd2[:], sd2[:], AF.Exp, ['sd2'], ['rstd2'], scale=-0.5)
                    k.ts(nmr2[:], mv2[:, 0:1], rstd2[:, 0:1], -1.0, ALU.mult, ALU.mult, ['mv2', 'rstd2'], ['nmr2'])
                    k.act(kn[:], kn[:], AF.Identity, [knk, 'nmr2', 'rstd2'], [knk], bias=nmr2[:, 0:1], scale=rstd2[:, 0:1])
                    k.tt(kn[:], kn[:], gkiB[:], ALU.mult, [knk], [knk])
                    k.tt(kif[s][:], kn[:], bkiB[:], ALU.add, [knk], ['kif%d' % s])
                    k.dma('act', ikp[i * 128:(i + 1) * 128, :], kif[s][:], ['kif%d' % s], [], 'kifo%d' % s)
                    k.cp('pool', kib[s][:, 0:64], kif[s][:], ['kif%d' % s], ['kib%d' % s])
                    k.cp('pool', kib[s][:, 64:128], kif[s][:], ['kif%d' % s], ['kib%d' % s])
                    k.tr(PSB(7)[:, 256:384], kib[s][:], identb[:], ['kib%d' % s], ['ps7'])
                    k.cp('dve', kiT[s][:], PSB(7)[:, 256:384], ['ps7'], ['kiT%d' % s])
                    k.dma('pool', ki_d[:, i * 128:(i + 1) * 128], kiT[s][:], ['kiT%d' % s], [], 'kiTo%d' % s)

                def back2(i):
                    s3 = i % 3
                    s = i % 2
                    cur = (2 * i) % 3
                    k.dma('sp', snap[i], SS[cur][:].rearrange("p h e -> p (h e)"), ['S%d' % cur], [], 'Ssto%d' % cur)
                    sbanks = [[6, 7], [2, 3]]
                    for c in range(2):
                        for hp in range(2):
                            bk = sbanks[c][hp]
                            for hh in range(2):
                                h = hp * 2 + hh
                                k.mm(PSF(bk)[:, hh * 256:(hh + 1) * 256], Kt[s][c * 64:(c + 1) * 64, h * 128:(h + 1) * 128],
                                     Vt[i % 4][c * 64:(c + 1) * 64, h * 256:(h + 1) * 256], True, True, ['Kt%d' % s, 'V%d' % (i % 4)], ['ps%d' % bk])
                            src_, dst_ = (2 * i + c) % 3, (2 * i + c + 1) % 3
                            for hh in range(2):
                                h = hp * 2 + hh
                                k.stt(SS[dst_][:, h, :], SS[src_][:, h, :], dec[s][:, 2 * h + c:2 * h + c + 1], PSF(bk)[:, hh * 256:(hh + 1) * 256],
                                      ALU.mult, ALU.add, ['S%d' % src_, 'dec%d' % s, 'ps%d' % bk], ['S%d' % dst_])

                for i0 in range(min(3, NB)):
                    fa(i0)
                for i0 in range(3):
                    if i0 < NB:
                        fb_a(i0)
                        fb_b(i0)
                    if i0 + 3 < NB and i0 < 2:
                        fa(i0 + 3)
                back1(0)
                for i in range(NB):
                    if i + 3 < NB:
                        fb_a(i + 3)
                    back2(i)
                    if i + 1 < NB:
                        back1(i + 1)
                    if i + 5 < NB:
                        fa(i + 5)
                    if i + 3 < NB:
                        fb_b(i + 3)
                fin = (2 * NB) % 3
                k.dma('sp', gla_p, SS[fin][:].rearrange("p h e -> p (h e)"), ['S%d' % fin], [], 'glap')
            P.barrier()


        def phase_own(mode):
            PR = (mode == 'P')
            NK = NKMAX if PR else 2176
            with ExitStack() as es:
                def sb(name, shape, dt):
                    return es.enter_context(nc.sbuf_tensor(mode + name, shape, dt))
                gB = sb("gB", [128, D], F32); bB = sb("bB", [128, D], F32)
                g2B = sb("g2B", [128, D], F32); b2B = sb("b2B", [128, D], F32)
                gtb = [sb("gtb%d" % s_, [128, 512], F32) for s_ in range(2)]
                gnB = sb("gnB", [128, 256], F32)
                identf = sb("idf", [128, 128], F32); identb = sb("idb", [128, 128], BF16)
                tri = sb("tri", [128, 128], F32)
                maskf = sb("maskf", [128, 512], F32)
                cst = sb("cst", [128, 512], F32); idrepb = sb("idrepb", [128, 512], BF16)
                w2 = sb("w2", [16, 512], F32); gbias = sb("gbias", [1, 512], F32); ones1 = sb("ones1", [1, 128], F32)
                eps = sb("eps", [128, 1], F32); one = sb("one", [128, 1], F32)
                ctab = sb("ctab", [128, KIT + 1], F32)
                tbias = sb("tbias", [128, 2, 640 if PR else 128], F32)
                onehot = sb("onehot", [128, 4], F32)
                xo = sb("xo", [128, D], F32); h = sb("h", [128, D], F32); hb = sb("hb", [128, D], BF16)
                hT = sb("hT", [128, 8, 128], BF16)
                tmp = sb("tmp", [128, D], F32)
                st = sb("st", [128, 2, 6], F32); mv = sb("mv", [128, 2], F32)
                sd = sb("sd", [128, 1], F32); rstd = sb("rstd", [128, 1], F32); nmr = sb("nmr", [128, 1], F32)
                wch = [sb("wch%d" % s_, [128, 8, 512], BF16) for s_ in range(2)]
                gl = sb("gl", [16, 128], F32)
                el = sb("el", [128, 512], F32); eb = sb("eb", [128, 512], F32); enb = sb("enb", [128, 512], F32)
                qT = sb("qT", [128, 4, 128], BF16); kTh = sb("kTh", [128, 4, 128], BF16)
                V = sb("V", [128, 1024], BF16)
                sg = [sb("sg%d" % s_, [128, 512], F32) for s_ in range(2)]
                QTz = sb("QTz", [128, 4, 512], BF16); qiTz = sb("qiTz", [128, 8, 128], BF16)
                wabs = sb("wabs", [128, 8], F32); wsgn = sb("wsgn", [128, 8], F32)
                AT = sb("AT", [128, 4, 128], BF16)
                ss = sb("ss", [128, 4], F32); rs = sb("rs", [128, 4], F32)
                yain = sb("yain", [128, D], BF16); yT = sb("yT", [128, 8, 128], BF16)
                mrg = sb("mrg", [128, D], F32)
                sc = sb("sc", [128, NK], F32)
                junk = None if PR else sb("junk", [128, 2176], BF16)
                rl = [sb("rl%d" % s_, [128, 512], F32) for s_ in range(2)]
                rd = sb("rd", [128, 16], F32)
                yout = xo
                rmax = sb("rmax", [128, 1], F32); rmin = sb("rmin", [128, 1], F32); Wd = sb("Wd", [128, 1], F32)
                wtab = sb("wtab", [128, KIT + 1], F32); mids = sb("mids", [128, KIT + 1], F32)
                cnts = sb("cnts", [128, KIT], F32); us = sb("us", [128, KIT], F32); thr = sb("thr", [128, 1], F32)
                sAs = sb("sAs", [128, KIT], F32); vvs = sb("vvs", [128, KIT], F32)
                jd = sb("jd", [128, 8], BF16); ja = sb("ja", [128, 8], BF16); jq = sb("jq", [128, 8], BF16)
                if PR:
                    Sc = [sb("Sc%d" % s_, [128, 1024], F32) for s_ in range(2)]
                    Sown = sb("Sown", [128, 1024], F32); Sb = sb("Sb", [128, 4, 256], BF16)
                    kich = [sb("kich%d" % s_, [128, 512], BF16) for s_ in range(2)]
                    kTch = [sb("kTch%d" % s_, [128, 2, 512], BF16) for s_ in range(2)]
                    vch = [sb("vch%d" % s_, [128, 4, 260], BF16) for s_ in range(2)]
                    mbt = [sb("mbt%d" % s_, [128, 128], BF16) for s_ in range(3)]
                    pT = [sb("pT%d" % s_, [128, 512], BF16) for s_ in range(3)]
                    oT = [sb("oT%d" % s_, [65, 512], F32) for s_ in range(2)]
                else:
                    cmaskS = sb("cmaskS", [128, 4, 128], F32); rmaskS = sb("rmaskS", [128, 4], F32)
                    mrevS = sb("mrevS", [128, 128], F32); bsumS = sb("bsumS", [128, 4], F32)
                    idrepSb = sb("idrepSb", [128, 4, 64], BF16)
                    selSb = sb("selSb", [128, 4, 128], BF16)
                    gkiB = sb("gkiB", [128, 64], F32); bkiB = sb("bkiB", [128, 64], F32)
                    S0f = [sb("S0f%d" % b_, [128, 4, 256], F32) for b_ in range(2)]
                    S0b = [sb("S0b%d" % b_, [128, 4, 256], BF16) for b_ in range(4)]
                    qTb = [sb("qTb%d" % b_, [128, 4, 128], BF16) for b_ in range(4)]
                    wabsb = sb("wabsb", [128, 4, 8], F32)
                    QTs = sb("QTs", [128, 4, 4, 64], BF16)
                    kTs1 = sb("kTs", [128, 2, 2176], BF16)
                    kiTs = [sb("kiTs%d" % b_, [128, 2176], BF16) for b_ in range(4)]
                    vexts1 = sb("vexts", [128, 17, 260], BF16)
                    ckf = sb("ckf", [128, 8, 256], F32); ckb = sb("ckb", [128, 8, 256], BF16)
                    cif = sb("cif", [128, 8, 64], F32); cib = sb("cib", [128, 8, 128], BF16)
                    kdv = sb("kdv", [128, 512], F32); kdb = sb("kdb", [128, 256], BF16); vnb = sb("vnb", [128, 256], BF16)
                    kin = sb("kin", [128, 64], F32); kif = sb("kif", [128, 64], F32); kib = sb("kib", [128, 128], BF16)
                    st2 = sb("st2", [128, 6], F32); mv2 = sb("mv2", [128, 2], F32)
                    sd2 = sb("sd2", [128, 1], F32); rstd2 = sb("rstd2", [128, 1], F32); nmr2 = sb("nmr2", [128, 1], F32)
                    kTnew = sb("kTnew", [128, 2, 128], BF16); kiTnew = sb("kiTnew", [128, 128], BF16)
                    Kt = sb("Kt", [128, 512], BF16); Ktb = [sb("Ktb%d" % s_, [128, 512], BF16) for s_ in range(2)]
                    decs = sb("decs", [128, 16], F32)
                    pTs = [sb("pTs%d" % s_, [128, 64], BF16) for s_ in range(3)]

                cl = 'c' + mode
                k.dma('sp', gB[:], ln_in_g.partition_broadcast(128), [], [], cl)
                k.dma('sp', bB[:], ln_in_b.partition_broadcast(128), [], [], cl)
                k.dma('sp', g2B[:], ln_g.partition_broadcast(128), [], [], cl)
                k.dma('sp', b2B[:], ln_b.partition_broadcast(128), [], [], cl)
                k.dma('sp', gnB[:], gla_norm_g.partition_broadcast(128), [], [], cl)
                k.dma('sp', identf[:], c_ident, [], [], cl)
                k.dma('sp', tri[:], c_triB if PR else c_triS, [], [], cl)
                k.dma('sp', maskf[:], (c_maskB if PR else c_maskS).rearrange("p a b -> p (a b)"), [], [], cl)
                k.dma('sp', w2[:], gla_w2, [], [], cl)
                k.dma('sp', gbias[:], gla_gate_b, [], [], cl)
                k.dma('sp', ctab[:], c_ctab, [], [], cl)
                k.dma('sp', tbias[:], c_tbias if PR else c_tbias[:, :, 0:128], [], [], cl)
                k.dma('sp', onehot[:], c_onehot, [], [], cl)
                if not PR:
                    k.dma('sp', cmaskS[:], c_cmaskS, [], [], cl)
                    k.dma('sp', rmaskS[:], c_rmaskS, [], [], cl)
                    k.dma('sp', mrevS[:], c_mrevS, [], [], cl)
                    k.dma('sp', bsumS[:], c_bsumS, [], [], cl)
                    k.dma('sp', gkiB[:], idx_kn_g.partition_broadcast(128), [], [], cl)
                    k.dma('sp', bkiB[:], idx_kn_b.partition_broadcast(128), [], [], cl)
                P.barrier()
                k.memset('dve', eps[:], EPS, [])
                k.memset('dve', one[:], 1.0, [])
                k.memset('dve', ones1[:], 1.0, [])
                k.memset('pool', QTz[:], 0.0, [])
                k.memset('pool', qiTz[:], 0.0, [])
                k.memset('pool', yain[:], 0.0, [])
                k.memset('pool', tmp[:], 0.0, [])
                k.cp('dve', identb[:], identf[:], [], [])
                k.dma('sp', cst[:], c_idrep, [], ['cst'], 'cst')
                k.cp('dve', idrepb[:], cst[:], ['cst'], ['idrepb'])
                if not PR:
                    k.dma('sp', cst[:, 0:256], c_idrepS.rearrange("p a b -> p (a b)"), ['idrepb'], ['cst'], 'cst')
                    k.cp('dve', idrepSb[:].rearrange("p a b -> p (a b)"), cst[:, 0:256], ['cst'], ['idrepSb'])
                    k.dma('sp', cst[:], c_selS.rearrange("p a b -> p (a b)"), ['idrepSb'], ['cst'], 'cst')
                    k.cp('dve', selSb[:].rearrange("p a b -> p (a b)"), cst[:], ['cst'], ['selSb'])
                    for b_ in range(4):
                        s_ = b_ % 2
                        k.dma('sp', S0f[s_][:], state[b_].rearrange("h p e -> p h e"), [], ['S0f%d' % s_], 'S0f%d' % s_)
                        k.cp('pool', S0b[b_][:], S0f[s_][:], ['S0f%d' % s_], ['S0b%d' % b_])
                        k.memset('pool', kiTs[b_][:, 2048:2176], 0.0, [])
                    k.memset('pool', vexts1[:], 1.0, [])
                    k.memset('pool', kTs1[:, :, 2048:2176], 0.0, [])
                P.barrier()
                tl = dict(st=st, mv=mv, sd=sd, rstd=rstd, nmr=nmr, xn=tmp, eps=eps, xnkey='tmp')
                bank = [0]

                def nb():
                    bank[0] = (bank[0] + 1) % 8
                    return bank[0]

                def own_block(g):
                    jobs = []

                    def J(src, n):
                        jobs.append((src, n))
                    J(wbf[:, C_IW:C_IW + 8], 8)
                    J(wbf[:, C_IQ:C_IQ + 512], 512)
                    J(wbf[:, C_DQ:C_DQ + 512], 512); J(wbf[:, C_DQ + 512:C_DQ + 1024], 512)
                    if not PR:
                        J(wbf[:, C_DK:C_DK + 512], 512)
                        J(wbf[:, C_IK:C_IK + 64], 64)
                    J(wbf[:, C_GLOW:C_GLOW + 16], 16)
                    J(wbf[:, C_GQ:C_GQ + 512], 512)
                    J(wbf[:, C_GK:C_GK + 512], 512)
                    J(wbf[:, C_GV:C_GV + 512], 512); J(wbf[:, C_GV + 512:C_GV + 1024], 512)
                    J(wbf[:, C_GR:C_GR + 512], 512); J(wbf[:, C_GR + 512:C_GR + 1024], 512)
                    for cc in range(2):
                        J(wbf3[0, :, cc * 512:(cc + 1) * 512], 512)
                        J(wbf[:, C_MA + cc * 512:C_MA + (cc + 1) * 512], 512)
                    J(wbf[:, C_DZ:C_DZ + 512], 512); J(wbf[:, C_DZ + 512:C_DZ + 1024], 512)
                    for cc in range(2):
                        J(wbf3[1, :, cc * 512:(cc + 1) * 512], 512)
                        J(wbf[:, C_MB + cc * 512:C_MB + (cc + 1) * 512], 512)
                    for cc in range(2):
                        J(wbf3[2, :, cc * 512:(cc + 1) * 512], 512)
                    jpos = [0]

                    def wissue(idx):
                        src, n = jobs[idx]
                        s_ = idx % 2
                        k.dma('sp', wch[s_][:, :, :n], src.rearrange("(k p) n -> p k n", p=128), [], ['wch%d' % s_], 'wch%d' % s_)

                    def next_w():
                        idx = jpos[0]
                        if idx == 0:
                            wissue(0)
                        if idx + 1 < len(jobs):
                            wissue(idx + 1)
                        jpos[0] += 1
                        return wch[idx % 2], 'wch%d' % (idx % 2)

                    def projT(wt, wk, n, psap, pskey):
                        for kc in range(8):
                            k.mm(psap, hT[:, kc, :], wt[:, kc, :n], kc == 0, kc == 7, ['hT', wk], [pskey])

                    def projF(wt, wk, bk):
                        for sub in range(4):
                            for kc in range(8):
                                k.mm(PSF(bk)[:, sub * 128:(sub + 1) * 128], wt[:, kc, sub * 128:(sub + 1) * 128], hT[:, kc, :],
                                     kc == 0, kc == 7, ['hT', wk], ['ps%d' % bk])

                    def transpose8(src, srckey, dst, dstkey):
                        bk = nb()
                        for kc in range(8):
                            k.tr(PSB(bk)[:, kc * 128:(kc + 1) * 128], src[:, kc * 128:(kc + 1) * 128], identb[:], [srckey], ['ps%d' % bk])
                        k.cp('act', dst[:].rearrange("p a b -> p (a b)"), PSB(bk), ['ps%d' % bk], [dstkey])

                    xsrc = xown[g * 128:(g + 1) * 128, :] if PR else xs
                    k.dma('act', xo[:], xsrc, [], ['xo'], 'xo')
                    layernorm(xo[:], 'xo', gB[:], bB[:], tl, 'b', out_f32=h[:], out_f32_key='h', out_bf=hb[:], out_bf_key='hb')
                    transpose8(hb, 'hb', hT, 'hT')
                    wt, wk = next_w()
                    bw = nb()
                    projT(wt, wk, 8, PSF(bw)[:, 0:8], 'ps%d' % bw)
                    k.ts(wabs[:], PSF(bw)[:, 0:8], -IDX_W_SCALE, None, ALU.mult, None, ['ps%d' % bw], ['wabs'])
                    k.stt(wabs[:], PSF(bw)[:, 0:8], IDX_W_SCALE, wabs[:], ALU.mult, ALU.max, ['ps%d' % bw, 'wabs'], ['wabs'])
                    k.ts(wsgn[:], PSF(bw)[:, 0:8], 0.0, 2.0, ALU.is_ge, ALU.mult, ['ps%d' % bw], ['wsgn'])
                    k.ts(wsgn[:], wsgn[:], -1.0, None, ALU.add, None, ['wsgn'], ['wsgn'])
                    wt, wk = next_w()
                    bi = nb()
                    projF(wt, wk, bi)
                    qv = qiTz[:].rearrange("p (s two) t -> p s two t", two=2)
                    pv = PSF(bi).rearrange("p (s t) -> p s t", s=4)
                    k.cp('act', qv[0:64, :, 0, :], pv[0:64, :, :], ['ps%d' % bi], ['qiTz'])
                    k.cp('act', qv[64:128, :, 1, :], pv[64:128, :, :], ['ps%d' % bi], ['qiTz'])
                    for m in range(2):
                        wt, wk = next_w()
                        bdq = nb()
                        projF(wt, wk, bdq)
                        k.cp('act', QTz[0:64, 2 * m, :], PSF(bdq)[0:64, :], ['ps%d' % bdq], ['QTz'])
                        k.cp('act', QTz[64:128, 2 * m + 1, :], PSF(bdq)[64:128, :], ['ps%d' % bdq], ['QTz'])
                    if not PR:
                        wt, wk = next_w()
                        bkv = nb()
                        projT(wt, wk, 512, PSF(bkv), 'ps%d' % bkv)
                        k.cp('act', kdv[:], PSF(bkv), ['ps%d' % bkv], ['kdv'])
                        k.dma('act', ks, kdv[:, 0:256], ['kdv'], [], 'so1')
                        k.dma('act', vs, kdv[:, 256:512], ['kdv'], [], 'so1')
                        k.cp('pool', kdb[:], kdv[:, 0:256], ['kdv'], ['kdb'])
                        k.cp('pool', vnb[:], kdv[:, 256:512], ['kdv'], ['vnb'])
                        wt, wk = next_w()
                        bik = nb()
                        projT(wt, wk, 64, PSF(bik)[:, 0:64], 'ps%d' % bik)
                        k.cp('dve', kin[:], PSF(bik)[:, 0:64], ['ps%d' % bik], ['kin'])
                        k.bn_stats(st2[:], kin[:], ['kin'], ['st2'])
                        k.bn_aggr(mv2[:], st2[:], ['st2'], ['mv2'])
                        k.act(sd2[:], mv2[:, 1:2], AF.Ln, ['mv2'], ['sd2'], bias=eps[:, 0:1], scale=1.0)
                        k.act(rstd2[:], sd2[:], AF.Exp, ['sd2'], ['rstd2'], scale=-0.5)
                        k.ts(nmr2[:], mv2[:, 0:1], rstd2[:, 0:1], -1.0, ALU.mult, ALU.mult, ['mv2', 'rstd2'], ['nmr2'])
                        k.act(kin[:], kin[:], AF.Identity, ['kin', 'nmr2', 'rstd2'], ['kin'], bias=nmr2[:, 0:1], scale=rstd2[:, 0:1])
                        k.tt(kin[:], kin[:], gkiB[:], ALU.mult, ['kin'], ['kin'])
                        k.tt(kif[:], kin[:], bkiB[:], ALU.add, ['kin'], ['kif'])
                        k.dma('act', iks, kif[:], ['kif'], [], 'so1')
                        k.cp('pool', kib[:, 0:64], kif[:], ['kif'], ['kib'])
                        k.cp('pool', kib[:, 64:128], kif[:], ['kif'], ['kib'])
                        bt_ = nb()
                        for c_ in range(2):
                            k.tr(PSB(bt_)[:, c_ * 128:(c_ + 1) * 128], kdb[:, c_ * 128:(c_ + 1) * 128], identb[:], ['kdb'], ['ps%d' % bt_])
                        k.tr(PSB(bt_)[:, 256:384], kib[:], identb[:], ['kib'], ['ps%d' % bt_])
                        k.cp('dve', kTnew[:].rearrange("p a b -> p (a b)"), PSB(bt_)[:, 0:256], ['ps%d' % bt_], ['kTnew'])
                        k.cp('dve', kiTnew[:], PSB(bt_)[:, 256:384], ['ps%d' % bt_], ['kiTnew'])
                        for b_ in range(4):
                            k.cp('pool', kiTs[b_][:, 2048:2064], kiTnew[:, 16 * b_:16 * b_ + 16], ['kiTnew'], ['kiTs%d' % b_])
                            for t8 in range(2):
                                k.dma('sp', cif[:], cache_ik[b_, t8 * 1024:(t8 + 1) * 1024, :].rearrange("(t p) c -> p t c", p=128), [], ['cif'], 'cif')
                                k.cp('pool', cib[:, :, 0:64], cif[:], ['cif'], ['cib'])
                                k.cp('pool', cib[:, :, 64:128], cif[:], ['cif'], ['cib'])
                                bt_ = nb()
                                for tt_ in range(8):
                                    k.tr(PSB(bt_)[:, tt_ * 128:(tt_ + 1) * 128], cib[:, tt_, :], identb[:], ['cib'], ['ps%d' % bt_])
                                k.cp('dve', kiTs[b_][:, t8 * 1024:(t8 + 1) * 1024], PSB(bt_), ['ps%d' % bt_], ['kiTs%d' % b_])
                    if PR:
                        n_tiles = min(4 * g + 5, NB)
                        tail0 = 4 * g
                        tidx = 1 if g == G - 1 else 0
                    else:
                        n_tiles = 17
                        tail0 = 16
                        tidx = 1
                    n_keys = n_tiles * 128
                    nch = (n_tiles + 3) // 4

                    def kiload(ci):
                        k0 = ci * 512
                        w_ = min(512, n_keys - k0)
                        s_ = ci % 2
                        k.dma('sp', kich[s_][:, :w_], ki_d[:, k0:k0 + w_], [], ['kich%d' % s_], 'kich%d' % s_)

                    if PR:
                        kiload(0)
                    if not PR:
                        for b_ in range(4):
                            k.ts(wabsb[:, b_, :], wabs[:], rmaskS[:, b_:b_ + 1], None, ALU.mult, None, ['wabs'], ['wabsb'])
                    rli = 0
                    for ci in range(nch):
                        k0 = ci * 512
                        w_ = min(512, n_keys - k0)
                        if PR and ci + 1 < nch:
                            kiload(ci + 1)
                        first = True
                        for hh in range(8):
                            for b_ in (range(1) if PR else range(4)):
                                bx = nb()
                                if PR:
                                    k.mm(PSF(bx)[:, :w_], qiTz[:, hh, :], kich[ci % 2][:, :w_], True, True, ['qiTz', 'kich%d' % (ci % 2)], ['ps%d' % bx])
                                    scl = wabs[:, hh:hh + 1]
                                    sck = 'wabs'
                                else:
                                    k.mm(PSF(bx)[:, :w_], qiTz[:, hh, :], kiTs[b_][:, k0:k0 + w_], True, True, ['qiTz', 'kiTs%d' % b_], ['ps%d' % bx])
                                    scl = wabsb[:, b_, hh:hh + 1]
                                    sck = 'wabsb'
                                r_ = rl[rli % 2]
                                rk = 'rl%d' % (rli % 2)
                                rli += 1
                                k.act(r_[:, :w_], PSF(bx)[:, :w_], AF.Relu, ['ps%d' % bx, sck], [rk], scale=scl)
                                if first:
                                    k.ts(sc[:, k0:k0 + w_], r_[:, :w_], wsgn[:, hh:hh + 1], None, ALU.mult, None, [rk, 'wsgn'], ['sc'])
                                    first = False
                                else:
                                    k.stt(sc[:, k0:k0 + w_], r_[:, :w_], wsgn[:, hh:hh + 1], sc[:, k0:k0 + w_], ALU.mult, ALU.add, [rk, 'wsgn', 'sc'], ['sc'])
                    k.capture()
                    k.red(rmax[:], sc[:, 0:n_keys], ALU.max, ['sc'], ['rmax'])
                    k.red(rmin[:], sc[:, 0:n_keys], ALU.min, ['sc'], ['rmin'])
                    tw = (n_tiles - tail0) * 128
                    k.tt(sc[:, tail0 * 128:tail0 * 128 + tw], sc[:, tail0 * 128:tail0 * 128 + tw], tbias[:, tidx, 0:tw], ALU.add, ['sc'], ['sc'])
                    k.tt(Wd[:], rmax[:], rmin[:], ALU.subtract, ['rmax', 'rmin'], ['Wd'])
                    k.ts(wtab[:], ctab[:], Wd[:, 0:1], None, ALU.mult, None, ['Wd'], ['wtab'])
                    k.tt(mids[:, 0:1], rmin[:], wtab[:, 0:1], ALU.add, ['rmin', 'wtab'], ['mid0'])
                    nD = (int(n_keys * 0.46) // 128) * 128
                    if nD < 256:
                        nD = n_keys
                    nA = n_keys - nD
                    for it in range(1, KIT + 1):
                        mid = mids[:, it - 1:it]
                        mk = 'mid%d' % (it - 1)
                        cn = cnts[:, it - 1:it]
                        ck_ = 'cnt%d' % it
                        k.ts(jd[:, 0:1].to_broadcast([128, nD]), sc[:, 0:nD], mid, None, ALU.is_ge, ALU.add, ['sc', mk], ['jd', ck_], accum=cn)
                        if nA > 0:
                            k.act(ja[:, 0:1].to_broadcast([128, nA]), sc[:, nD:n_keys], AF.Sign, ['sc', mk], ['ja', 'sa%d' % it],
                                  bias=mid, scale=-1.0, accum_out=sAs[:, it - 1:it])
                            k.stt(vvs[:, it - 1:it], cn, 2.0, sAs[:, it - 1:it], ALU.mult, ALU.subtract, [ck_, 'sa%d' % it], ['vv%d' % it])
                            vsrc, vkey, vthr = vvs[:, it - 1:it], 'vv%d' % it, 511.5 - nA
                        else:
                            vsrc, vkey, vthr = cn, ck_, 255.5
                        u_ = us[:, it - 1:it]
                        k.ts(u_, vsrc, vthr, wtab[:, it - 1:it], ALU.is_ge, ALU.mult, [vkey, 'wtab'], ['u%d' % it])
                        if it < KIT:
                            k.stt(mids[:, it:it + 1], u_, wtab[:, it:it + 1], mid, ALU.subtract, ALU.add, ['u%d' % it, 'wtab', mk], ['mid%d' % it])
                        else:
                            k.stt(thr[:], u_, wtab[:, it - 1:it], mid, ALU.subtract, ALU.add, ['u%d' % it, 'wtab', mk], ['thr'])
                    bisA = k.end_capture()
                    k.capture()
                    wt, wk = next_w()
                    b1 = nb()
                    for kc in range(8):
                        k.mm(PSF(b1)[0:16, 0:128], wt[:, kc, 0:16], hT[:, kc, :], kc == 0, kc == 7, ['hT', wk], ['ps%d' % b1])
                    k.cp('dve', gl[:], PSF(b1)[0:16, 0:128], ['ps%d' % b1], ['gl'])
                    bz = nb()
                    k.mm(PSF(bz), gl[:], w2[:], True, False, ['gl'], ['ps%d' % bz])
                    k.mm(PSF(bz), ones1[:], gbias[:], False, True, [], ['ps%d' % bz])
                    k.act(el[:], PSF(bz), AF.Exp, ['ps%d' % bz], ['el'], scale=-1.0)
                    k.act(el[:], el[:], AF.Ln, ['el'], ['el'], bias=one[:, 0:1], scale=1.0)
                    bb = nb()
                    for hh in range(4):
                        k.mm(PSF(bb)[:, hh * 128:(hh + 1) * 128], el[:, hh * 128:(hh + 1) * 128], tri[:], True, True, ['el'], ['ps%d' % bb])
                    k.act(eb[:], PSF(bb), AF.Exp, ['ps%d' % bb], ['eb'])
                    k.act(enb[:], PSF(bb), AF.Exp, ['ps%d' % bb], ['enb'], scale=-1.0)
                    wt, wk = next_w()
                    bq = nb()
                    projF(wt, wk, bq)
                    k.stt(qT[:].rearrange("p a b -> p (a b)"), PSF(bq), 128.0 ** -0.5, eb[:], ALU.mult, ALU.mult, ['ps%d' % bq, 'eb'], ['qT'])
                    wt, wk = next_w()
                    bk_ = nb()
                    projF(wt, wk, bk_)
                    k.tt(kTh[:].rearrange("p a b -> p (a b)"), PSF(bk_), enb[:], ALU.mult, ['ps%d' % bk_, 'enb'], ['kTh'])
                    if not PR:
                        bkt = nb()
                        projT(wt, wk, 512, PSF(bkt), 'ps%d' % bkt)
                        br_ = nb()
                        k.mm(PSF(br_), mrevS[:], el[:], True, True, ['el'], ['ps%d' % br_])
                        k.act(sg[1][:], PSF(br_), AF.Exp, ['ps%d' % br_], ['sg1'])
                        k.tt(Kt[:], PSF(bkt), sg[1][:], ALU.mult, ['ps%d' % bkt, 'sg1'], ['Kt'])
                        bd_ = nb()
                        for hh in range(4):
                            k.mm(PSF(bd_)[:, hh * 4:(hh + 1) * 4], el[:, hh * 128:(hh + 1) * 128], bsumS[:], True, True, ['el'], ['ps%d' % bd_])
                        k.act(decs[:], PSF(bd_)[:, 0:16], AF.Exp, ['ps%d' % bd_], ['decs'])
                    for half in range(2):
                        wt, wk = next_w()
                        bv = nb()
                        projT(wt, wk, 512, PSF(bv), 'ps%d' % bv)
                        k.cp('act', V[:, half * 512:(half + 1) * 512], PSF(bv), ['ps%d' % bv], ['V'])
                    if PR:
                        for m in range(4):
                            sidx = min(4 * g + m, NB - 1)
                            s_ = m % 2
                            k.dma('act', Sc[s_][:], snap[sidx], [], ['Sc%d' % s_], 'Sc%d' % s_)
                            if m == 0:
                                k.ts(Sown[:], Sc[s_][:], onehot[:, 0:1], None, ALU.mult, None, ['Sc%d' % s_], ['Sown'])
                            else:
                                k.stt(Sown[:], Sc[s_][:], onehot[:, m:m + 1], Sown[:], ALU.mult, ALU.add, ['Sc%d' % s_, 'Sown'], ['Sown'])
                        k.cp('pool', Sb[:].rearrange("p a b -> p (a b)"), Sown[:], ['Sown'], ['Sb'])
                    else:
                        for b_ in range(4):
                            for hh in range(4):
                                k.tt(qTb[b_][:, hh, :], qT[:, hh, :], cmaskS[:, b_, :], ALU.mult, ['qT'], ['qTb%d' % b_])
                    ba = nb()
                    for hh in range(4):
                        k.mm(PSF(ba)[:, hh * 128:(hh + 1) * 128], kTh[:, hh, :], qT[:, hh, :], True, True, ['kTh', 'qT'], ['ps%d' % ba])
                    k.tt(AT[:].rearrange("p a b -> p (a b)"), PSF(ba), maskf[:], ALU.mult, ['ps%d' % ba], ['AT'])
                    bo = [nb(), nb()]
                    for hh in range(4):
                        oap = PSF(bo[hh // 2])[:, (hh % 2) * 256:(hh % 2 + 1) * 256]
                        okey = 'ps%d' % bo[hh // 2]
                        if PR:
                            k.mm(oap, qT[:, hh, :], Sb[:, hh, :], True, False, ['qT', 'Sb'], [okey])
                        else:
                            for b_ in range(4):
                                k.mm(oap, qTb[b_][:, hh, :], S0b[b_][:, hh, :], b_ == 0, False, ['qTb%d' % b_], [okey])
                        k.mm(oap, AT[:, hh, :], V[:, hh * 256:(hh + 1) * 256], False, True, ['AT', 'V'], [okey])
                    for hh in range(4):
                        oap = PSF(bo[hh // 2])[:, (hh % 2) * 256:(hh % 2 + 1) * 256]
                        k.act(jq[:, 0:1].to_broadcast([128, 256]), oap, AF.Square, ['ps%d' % bo[hh // 2]], ['jq', 'ss%d' % hh], accum_out=ss[:, hh:hh + 1])
                    k.act(rs[:], ss[:], AF.Ln, ['ss0', 'ss1', 'ss2', 'ss3'], ['rs'], bias=eps[:, 0:1], scale=1.0 / 256)
                    k.act(rs[:], rs[:], AF.Exp, ['rs'], ['rs'], scale=-0.5)
                    for cc in range(2):
                        wt, wk = next_w()
                        bg = nb()
                        projT(wt, wk, 512, PSF(bg), 'ps%d' % bg)
                        k.act(sg[cc][:], PSF(bg), AF.Silu, ['ps%d' % bg], ['sg%d' % cc])
                        for hh in range(2):
                            hd = 2 * cc + hh
                            oap = PSF(bo[hd // 2])[:, (hd % 2) * 256:(hd % 2 + 1) * 256]
                            k.stt(tmp[:, hd * 256:(hd + 1) * 256], oap, rs[:, hd:hd + 1], gnB[:], ALU.mult, ALU.mult,
                                  ['ps%d' % bo[hd // 2], 'rs'], ['tmp'])
                        k.tt(yain[:, cc * 512:(cc + 1) * 512], tmp[:, cc * 512:(cc + 1) * 512], sg[cc][:], ALU.mult, ['tmp', 'sg%d' % cc], ['yain'])
                    transpose8(yain, 'yain', yT, 'yT')
                    for cc in range(2):
                        wt, wk = next_w()
                        by = nb()
                        for kc in range(8):
                            k.mm(PSF(by), yT[:, kc, :], wt[:, kc, :], kc == 0, kc == 7, ['yT', wk], ['ps%d' % by])
                        wt, wk = next_w()
                        bm = nb()
                        projT(wt, wk, 512, PSF(bm), 'ps%d' % bm)
                        k.dma('act', gtb[cc][:], gate_b[0:1, cc * 512:(cc + 1) * 512].partition_broadcast(128), [], ['gtb%d' % cc], 'gtb%d' % cc)
                        k.tt(sg[cc][:], PSF(bm), gtb[cc][:], ALU.add, ['ps%d' % bm, 'gtb%d' % cc], ['sg%d' % cc])
                        k.act(sg[cc][:], sg[cc][:], AF.Sigmoid, ['sg%d' % cc], ['sg%d' % cc])
                        k.tt(mrg[:, cc * 512:(cc + 1) * 512], PSF(by), sg[cc][:], ALU.mult, ['ps%d' % by, 'sg%d' % cc], ['mrg'])
                    if not PR:
                        for b_ in range(4):
                            s_ = b_ % 2
                            k.dma('sp', S0f[s_][:], state[b_].rearrange("h p e -> p h e"), [], ['S0f%d' % s_], 'S0f%d' % s_)
                            k.ts(Ktb[s_][:], Kt[:], rmaskS[:, b_:b_ + 1], None, ALU.mult, None, ['Kt'], ['Ktb%d' % s_])
                            for hp in range(2):
                                bs_ = nb()
                                for hh in range(2):
                                    hd = hp * 2 + hh
                                    k.mm(PSF(bs_)[:, hh * 256:(hh + 1) * 256], Ktb[s_][:, hd * 128:(hd + 1) * 128], V[:, hd * 256:(hd + 1) * 256],
                                         True, True, ['Ktb%d' % s_, 'V'], ['ps%d' % bs_])
                                for hh in range(2):
                                    hd = hp * 2 + hh
                                    k.stt(S0f[s_][:, hd, :], S0f[s_][:, hd, :], decs[:, hd * 4 + b_:hd * 4 + b_ + 1], PSF(bs_)[:, hh * 256:(hh + 1) * 256],
                                          ALU.mult, ALU.add, ['decs', 'ps%d' % bs_, 'S0f%d' % s_], ['S0f%d' % s_])
                            k.dma('act', gla_s[b_], S0f[s_][:].rearrange("p a b -> p (a b)"), ['S0f%d' % s_], [], 'Sno%d' % s_)

                    glaB = k.end_capture()
                    k.merge(bisA, glaB)

                    if PR:
                        def kvload(ci):
                            k0 = ci * 512
                            nt = min(4, n_tiles - ci * 4)
                            s_ = ci % 2
                            k.dma('sp', kTch[s_][:, :, :nt * 128], kT_d[:, :, k0:k0 + nt * 128], [], ['kTch%d' % s_], 'kTch%d' % s_)
                            k.dma('sp', vch[s_][:, :nt, :], v_d[:, ci * 4:ci * 4 + nt, :], [], ['vch%d' % s_], 'vch%d' % s_)
                        kvload(0)
                        li = 0
                        groups = []
                        for kt in range(n_tiles):
                            ci, tl_ = kt // 4, kt % 4
                            mb_ = mbt[kt % 3]
                            mbk = 'mbt%d' % (kt % 3)
                            for gg in range(4):
                                bl = 4 + (li % 4)
                                p_ = pT[li % 3]
                                pk = 'pT%d' % (li % 3)
                                li += 1
                                k.capture()
                                if gg == 0:
                                    if tl_ == 1 and ci + 1 < nch:
                                        kvload(ci + 1)
                                    k.ts(mb_[:], sc[:, kt * 128:(kt + 1) * 128], thr[:, 0:1], NEG, ALU.is_lt, ALU.mult, ['sc', 'thr'], [mbk])
                                k.mm(PSF(bl), kTch[ci % 2][:, gg // 2, tl_ * 128:(tl_ + 1) * 128], QTz[:, gg, :], True, False,
                                     ['kTch%d' % (ci % 2), 'QTz'], ['ps%d' % bl])
                                k.mm(PSF(bl), mb_[:], idrepb[:], False, True, [mbk], ['ps%d' % bl])
                                k.act(p_[:], PSF(bl), AF.Exp, ['ps%d' % bl], [pk], scale=0.125)
                                s1 = k.end_capture()
                                k.capture()
                                k.mm(PSF(gg)[0:65, :], vch[ci % 2][:, tl_, gg * 65:(gg + 1) * 65], p_[:], kt == 0, kt == n_tiles - 1,
                                     ['vch%d' % (ci % 2), pk], ['ps%d' % gg])
                                s2 = k.end_capture()
                                groups.append((s1, s2))
                        SK = 2
                        for idx in range(len(groups) + SK):
                            if idx < len(groups):
                                P.ops.extend(groups[idx][0])
                            if idx >= SK:
                                P.ops.extend(groups[idx - SK][1])
                        for gg in range(4):
                            o_ = oT[gg % 2]
                            ok_ = 'oT%d' % (gg % 2)
                            k.cp('act', o_[:], PSF(gg)[0:65, :], ['ps%d' % gg], [ok_])
                            for r_ in range(4):
                                k.tr(PSF(4 + gg)[:, r_ * 65:(r_ + 1) * 65], o_[0:65, r_ * 128:(r_ + 1) * 128], identf[0:65, 0:65], [ok_], ['ps%d' % (4 + gg)])
                        NPT = 128
                    else:
                        mball = junk
                        for b_ in range(4):
                            k.cp('pool', QTs[:, b_, :, :].rearrange("p g (r t) -> p g r t", r=4),
                                 QTz[:].rearrange("p g (r t) -> p g r t", r=4)[:, :, :, 16 * b_:16 * b_ + 16], ['QTz'], ['QTs'])
                        k.ts(mball[:], sc[:, 0:2176], thr[:, 0:1], NEG, ALU.is_lt, ALU.mult, ['sc', 'thr'], ['junk'])
                        li = 0
                        for b_ in range(4):
                            k.cp('pool', kTs1[:, :, 2048:2064], kTnew[:, :, 16 * b_:16 * b_ + 16], ['kTnew'], ['kTs'])
                            bs_ = nb() % 4 + 4
                            k.mm(PSF(bs_)[:, 0:256], selSb[:, b_, :], vnb[:], True, True, ['vnb'], ['ps%d' % bs_])
                            k.cp('act', vexts1[:, 16, :].rearrange("p (g d) -> p g d", g=4)[:, :, 0:64],
                                 PSF(bs_)[:, 0:256].rearrange("p (g d) -> p g d", g=4), ['ps%d' % bs_], ['vexts'])
                            for t8 in range(2):
                                k.dma('sp', ckf[:], cache_k[b_, t8 * 1024:(t8 + 1) * 1024, :].rearrange("(t p) c -> p t c", p=128), [], ['ckf'], 'ckf')
                                k.cp('pool', ckb[:], ckf[:], ['ckf'], ['ckb'])
                                for t4 in range(2):
                                    bt_ = nb() % 4 + 4
                                    for tt_ in range(4):
                                        for c_ in range(2):
                                            col = (tt_ * 2 + c_) * 128
                                            k.tr(PSB(bt_)[:, col:col + 128], ckb[:, t4 * 4 + tt_, c_ * 128:(c_ + 1) * 128], identb[:], ['ckb'], ['ps%d' % bt_])
                                    c0_ = t8 * 1024 + t4 * 512
                                    k.cp('dve', kTs1[:, :, c0_:c0_ + 512].rearrange("p c (t s) -> p c t s", t=4),
                                         PSB(bt_).rearrange("p (t c s) -> p c t s", t=4, c=2), ['ps%d' % bt_], ['kTs'])
                                k.dma('sp', ckf[:], cache_v[b_, t8 * 1024:(t8 + 1) * 1024, :].rearrange("(t p) c -> p t c", p=128), [], ['ckf'], 'ckf')
                                k.cp('pool', vexts1[:, t8 * 8:(t8 + 1) * 8, :].rearrange("p t (g d) -> p t g d", g=4)[:, :, :, 0:64],
                                     ckf[:].rearrange("p t (g d) -> p t g d", g=4), ['ckf'], ['vexts'])
                            sgroups = []
                            for gg in range(4):
                                ob = b_ // 2
                                ocol = ((b_ % 2) * 4 + gg) * 64
                                qsel = QTs[:, b_, gg, :]
                                for kt in range(17):
                                    bl = 4 + (li % 4)
                                    p_ = pTs[li % 3]
                                    pk = 'pTs%d' % (li % 3)
                                    li += 1
                                    k.capture()
                                    k.mm(PSF(bl)[:, 0:64], kTs1[:, gg // 2, kt * 128:(kt + 1) * 128], qsel, True, False,
                                         ['kTs', 'QTs'], ['ps%d' % bl])
                                    k.mm(PSF(bl)[:, 0:64], mball[:, kt * 128:(kt + 1) * 128], idrepSb[:, b_, :], False, True, ['junk'], ['ps%d' % bl])
                                    k.act(p_[:], PSF(bl)[:, 0:64], AF.Exp, ['ps%d' % bl], [pk], scale=0.125)
                                    s1_ = k.end_capture()
                                    k.capture()
                                    k.mm(PSF(ob)[0:65, ocol:ocol + 64], vexts1[:, kt, gg * 65:(gg + 1) * 65], p_[:], kt == 0, kt == 16,
                                         ['vexts', pk], ['ps%d' % ob])
                                    s2_ = k.end_capture()
                                    sgroups.append((s1_, s2_))
                            SKS = 2
                            for idx in range(len(sgroups) + SKS):
                                if idx < len(sgroups):
                                    P.ops.extend(sgroups[idx][0])
                                if idx >= SKS:
                                    P.ops.extend(sgroups[idx - SKS][1])
                        oTs = sc[0:65, 0:1024]
                        ov = oTs.rearrange("p (g r b t) -> p b g r t", g=4, r=4, b=4)
                        for ob in range(2):
                            k.cp('act', ov[:, 2 * ob:2 * ob + 2], PSF(ob)[0:65, :].rearrange("p (b g r t) -> p b g r t", b=2, g=4, r=4),
                                 ['ps%d' % ob, 'junk'], ['sc'])
                        for gg in range(4):
                            for r_ in range(4):
                                c0_ = (gg * 4 + r_) * 64
                                k.tr(PSF(4 + gg)[0:64, r_ * 65:(r_ + 1) * 65], oTs[:, c0_:c0_ + 64], identf[0:65, 0:65], ['sc'], ['ps%d' % (4 + gg)])
                        NPT = 64
                    for gg in range(4):
                        pv4 = PSF(4 + gg)[0:NPT, 0:260].rearrange("p (r c) -> p r c", c=65)
                        k.recip(rd[0:NPT, 4 * gg:4 * gg + 4], pv4[:, :, 64], ['ps%d' % (4 + gg)], ['rd'])
                        for r_ in range(4):
                            hd = 4 * gg + r_
                            k.ts(tmp[0:NPT, hd * 64:(hd + 1) * 64], pv4[:, r_, 0:64], rd[0:NPT, hd:hd + 1], None, ALU.mult, None,
                                 ['ps%d' % (4 + gg), 'rd'], ['tmp'])
                    for cc in range(2):
                        wt, wk = next_w()
                        bg = nb()
                        projT(wt, wk, 512, PSF(bg), 'ps%d' % bg)
                        k.act(sg[cc][:], PSF(bg), AF.Silu, ['ps%d' % bg], ['sg%d' % cc])
                        k.tt(yain[0:NPT, cc * 512:(cc + 1) * 512], tmp[0:NPT, cc * 512:(cc + 1) * 512], sg[cc][0:NPT, :], ALU.mult,
                             ['tmp', 'sg%d' % cc], ['yain'])
                    transpose8(yain, 'yain', yT, 'yT')
                    for cc in range(2):
                        wt, wk = next_w()
                        by = nb()
                        for kc in range(8):
                            k.mm(PSF(by), yT[:, kc, :], wt[:, kc, :], kc == 0, kc == 7, ['yT', wk], ['ps%d' % by])
                        wt, wk = next_w()
                        bm = nb()
                        projT(wt, wk, 512, PSF(bm), 'ps%d' % bm)
                        k.dma('act', gtb[cc][:], gate_b[0:1, 1024 + cc * 512:1024 + (cc + 1) * 512].partition_broadcast(128), [], ['gtb%d' % cc], 'gtb%d' % cc)
                        k.tt(sg[cc][:], PSF(bm), gtb[cc][:], ALU.add, ['ps%d' % bm, 'gtb%d' % cc], ['sg%d' % cc])
                        k.act(sg[cc][:], sg[cc][:], AF.Sigmoid, ['sg%d' % cc], ['sg%d' % cc])
                        k.tt(sg[cc][:], PSF(by), sg[cc][:], ALU.mult, ['ps%d' % by, 'sg%d' % cc], ['sg%d' % cc])
                        k.tt(mrg[:, cc * 512:(cc + 1) * 512], mrg[:, cc * 512:(cc + 1) * 512], sg[cc][:], ALU.add, ['mrg', 'sg%d' % cc], ['mrg'])
                    k.cp('pool', yain[:], mrg[:], ['mrg'], ['yain'])
                    transpose8(yain, 'yain', yT, 'yT')
                    for cc in range(2):
                        wt, wk = next_w()
                        by = nb()
                        for kc in range(8):
                            k.mm(PSF(by), yT[:, kc, :], wt[:, kc, :], kc == 0, kc == 7, ['yT', wk], ['ps%d' % by])
                        k.stt(mrg[:, cc * 512:(cc + 1) * 512], h[:, cc * 512:(cc + 1) * 512], ALPHA, PSF(by), ALU.mult, ALU.add, ['h', 'ps%d' % by], ['mrg'])
                    layernorm(mrg[:], 'mrg', g2B[:], b2B[:], tl, 'c', out_f32=yout[:], out_f32_key='xo')
                    ydst = y_own[g * 128:(g + 1) * 128, :] if PR else ys
                    k.dma('act', ydst, yout[:], ['xo'], [], 'youto')

                for g in (range(G) if PR else [0]):
                    own_block(g)
            P.barrier()

        if "B" in phases:
            phase_own('P')
        if "S" in phases:
            phase_own('S')

        P.emit(nc, top)
    return nc


def _consts(j, G):
    c = {}
    p = np.arange(128)
    c["c_ident"] = np.eye(128, dtype=np.float32)
    jj, ii = np.meshgrid(p, p, indexing="ij")
    c["c_triA"] = np.where((jj > ii) & (jj // 64 == ii // 64), -1.0 / 16, 0.0).astype(np.float32)
    c["c_blkA"] = np.where(p[:, None] // 64 == np.arange(2)[None, :], -1.0 / 16, 0.0).astype(np.float32)
    c["c_triB"] = np.where(jj <= ii, -1.0 / 16, 0.0).astype(np.float32)
    mB = (jj <= ii).astype(np.float32)
    c["c_maskB"] = np.ascontiguousarray(np.broadcast_to(mB[:, None, :], (128, 4, 128)))
    same = (jj // 16 == ii // 16)
    c["c_triS"] = np.where((jj <= ii) & same, -1.0 / 16, 0.0).astype(np.float32)
    mS = ((jj <= ii) & same).astype(np.float32)
    c["c_maskS"] = np.ascontiguousarray(np.broadcast_to(mS[:, None, :], (128, 4, 128)))
    c["c_mrevS"] = np.where((jj > ii) & same, -1.0 / 16, 0.0).astype(np.float32)
    c["c_bsumS"] = np.where(p[:, None] // 16 == np.arange(4)[None, :], -1.0 / 16, 0.0).astype(np.float32)
    cm = (p[None, :] // 16 == np.arange(4)[:, None]).astype(np.float32)
    c["c_cmaskS"] = np.ascontiguousarray(np.broadcast_to(cm[None], (128, 4, 128)))
    c["c_rmaskS"] = (p[:, None] // 16 == np.arange(4)[None, :]).astype(np.float32)
    c["c_idrep"] = np.ascontiguousarray(np.tile(np.eye(128, dtype=np.float32), (1, 4)))
    ids = np.zeros((128, 4, 64), np.float32)
    sel = np.zeros((128, 4, 128), np.float32)
    for b in range(4):
        for t in range(16):
            for r in range(4):
                ids[16 * b + t, b, r * 16 + t] = 1.0
            sel[16 * b + t, b, t] = 1.0
    c["c_idrepS"] = ids
    c["c_selS"] = sel
    c["c_ctab"] = np.ascontiguousarray(np.broadcast_to((0.5 ** np.arange(1, KIT + 2))[None, :], (128, KIT + 1))).astype(np.float32)
    tb = np.zeros((128, 2, 640), np.float32)
    kk = np.arange(640)
    for r in range(128):
        lim = 128 * j + (16 if r < 16 else (80 if r < 80 else 144))
        tb[r, 0, :] = np.where(kk < lim, 0.0, -1e30)
    tb[:, 1, :] = np.where(kk < 16, 0.0, -1e30)[None, :]
    c["c_tbias"] = tb
    oh = np.zeros((128, 4), np.float32)
    oh[:, j] = 1.0
    c["c_onehot"] = oh
    c["c_tailmask"] = (p < 16).astype(np.float32)[:, None]
    return c


def _dq_perm():
    perm = np.zeros(1024, np.int64)
    n = 0
    for m in range(2):
        for r in range(4):
            for half in range(2):
                g = 2 * m + half
                for d in range(64):
                    perm[n] = (g * 4 + r) * 64 + d
                    n += 1
    return perm


def prep(inp, SEQ):
    T, NB, G = geometry(SEQ)
    f = lambda a: np.ascontiguousarray(np.asarray(a, dtype=np.float32))
    w_in = f(inp["w_in"])[0].copy()
    w_in[:, C_DQ:C_DQ + 1024] = w_in[:, C_DQ:C_DQ + 1024][:, _dq_perm()]
    w3 = np.ascontiguousarray(np.stack([f(inp["w_gla"])[0], f(inp["w_dsa"])[0], f(inp["w_out"])[0]], 0))
    shared = dict(
        w_in=np.ascontiguousarray(w_in), w3=w3,
        ln_in_g=f(inp["ln_in_g"]).reshape(1, D), ln_in_b=f(inp["ln_in_b"]).reshape(1, D),
        ln_g=f(inp["ln_g"]).reshape(1, D), ln_b=f(inp["ln_b"]).reshape(1, D),
        gate_b=f(inp["gate_b"]).reshape(1, 2048), gla_gate_b=f(inp["gla_gate_b"]).reshape(1, 512),
        gla_norm_g=f(inp["gla_norm_g"]).reshape(1, 256),
        idx_kn_g=f(inp["idx_kn_g"]).reshape(1, 64), idx_kn_b=f(inp["idx_kn_b"]).reshape(1, 64),
        gla_w2=f(inp["gla_w2"]).reshape(16, 512))
    xp = f(inp["x_prompt"]); meta = f(inp["meta"]); xsm = f(inp["x_sample"])
    ck = f(inp["cache_k"])[0]; cv = f(inp["cache_v"])[0]; cik = f(inp["cache_idx_k"])[0]; stt = f(inp["state_gla"])[0]
    maps = []
    for c in range(8):
        b, j = c // 4, c % 4
        xall = np.zeros((4 * G * 128, D), np.float32)
        xall[:16] = meta
        xall[16:T] = xp[b]
        xown = np.ascontiguousarray(xall.reshape(G, 4, 128, D)[:, j].reshape(G * 128, D))
        xs_ = np.zeros((128, D), np.float32)
        xs_[:64] = xsm[4 * c:4 * c + 4].reshape(64, D)
        m = dict(shared)
        m.update(_consts(j, G))
        m.update(xall=np.ascontiguousarray(xall[:NB * 128]), xown=xown, xs=xs_,
                 cache_k=np.ascontiguousarray(ck[4 * c:4 * c + 4].reshape(4, 2048, 256)),
                 cache_v=np.ascontiguousarray(cv[4 * c:4 * c + 4].reshape(4, 2048, 256)),
                 cache_ik=np.ascontiguousarray(cik[4 * c:4 * c + 4]),
                 state=np.ascontiguousarray(stt[4 * c:4 * c + 4]))
        maps.append(m)
    return maps


def gather(res, SEQ):
    T, NB, G = geometry(SEQ)
    R = res.results
    yp = np.zeros((2, 4 * G * 128, D), np.float32)
    for c in range(8):
        b, j = c // 4, c % 4
        yp[b].reshape(G, 4, 128, D)[:, j] = R[c]["y_own"].reshape(G, 128, D)
    y_prompt = np.ascontiguousarray(yp[:, 16:T])
    y_sample = np.concatenate([R[c]["ys"][:64].reshape(4, 16, D) for c in range(8)], 0)
    k_prompt = np.stack([R[4 * b]["kp"][:T].reshape(T, 4, 64) for b in range(2)], 0)[None]
    v_prompt = np.stack([R[4 * b]["vp"][:T].reshape(T, 4, 64) for b in range(2)], 0)[None]
    ik_prompt = np.stack([R[4 * b]["ikp"][:T] for b in range(2)], 0)[None]
    gla_prompt = np.stack([R[4 * b]["gla_p"].reshape(128, 4, 256).transpose(1, 0, 2) for b in range(2)], 0)[None]
    k_sample = np.concatenate([R[c]["ks"][:64].reshape(4, 16, 4, 64) for c in range(8)], 0)[None]
    v_sample = np.concatenate([R[c]["vs"][:64].reshape(4, 16, 4, 64) for c in range(8)], 0)[None]
    ik_sample = np.concatenate([R[c]["iks"][:64].reshape(4, 16, 64) for c in range(8)], 0)[None]
    gla_sample = np.concatenate([R[c]["gla_s"].reshape(4, 128, 4, 256).transpose(0, 2, 1, 3) for c in range(8)], 0)[None]
    outs = (y_prompt, y_sample, k_prompt, v_prompt, ik_prompt, gla_prompt, k_sample, v_sample, ik_sample, gla_sample)
    return tuple(np.ascontiguousarray(o, dtype=np.float32) for o in outs)


_NC_CACHE = {}


def run(inputs, SEQ, phases="0ABS"):
    key = (SEQ, phases)
    if key not in _NC_CACHE:
        _NC_CACHE[key] = build(SEQ, phases)
    nc = _NC_CACHE[key]
    maps = prep(inputs, SEQ)
    res = run_bass_kernel_spmd(nc, maps, core_ids=list(range(8)))
    return gather(res, SEQ)


def kernel(**inputs):
    SEQ = int(np.asarray(inputs["x_prompt"]).shape[1])
    return run(inputs, SEQ)
```

```python
from contextlib import ExitStack
import numpy as np
import concourse.bass as bass
import concourse.mybir as mybir
from concourse.bass_utils import run_bass_kernel_spmd

F32 = mybir.dt.float32
BF16 = mybir.dt.bfloat16
AF = mybir.ActivationFunctionType
ALU = mybir.AluOpType
AX = mybir.AxisListType

D = 1024
NEG = -30000.0
KIT = 22
IDX_W_SCALE = (8 ** -0.5) * (64 ** -0.5)
ALPHA = 2.0 ** 0.25
EPS = 1e-5
C_GQ, C_GK, C_GV, C_GLOW, C_GR, C_DQ, C_DK, C_DV, C_IQ, C_IK, C_IW, C_DZ, C_MA, C_MB = (
    0, 512, 1024, 2048, 2064, 3088, 4112, 4368, 4624, 5136, 5200, 5208, 6232, 7256)
IN_COLS = 8280


class Prog:
    def __init__(self):
        self.ops = []

    def add(self, eng, fn, r=(), w=(), dsem=None):
        r = list(r)
        w = list(w)
        for b in list(r):
            if isinstance(b, str) and b.startswith('ps'):
                r.remove(b)
                if b not in w:
                    w.append(b)
        self.ops.append(dict(eng=eng, fn=fn, r=tuple(r), w=tuple(w), dsem=dsem))

    def barrier(self):
        self.ops.append(dict(eng='barrier', fn=None, r=(), w=(), dsem=None))

    def analyze(self):
        ops = self.ops
        last_w, readers = {}, {}
        last_of = {}
        for i, op in enumerate(ops):
            if op['eng'] == 'barrier':
                for e_, j in last_of.items():
                    ops[j]['needed'] = True
                last_w, readers = {}, {}
                op['deps'] = set()
                continue
            deps = set()
            for b in op['r']:
                if b in last_w:
                    deps.add(('raw', last_w[b]))
            for b in op['w']:
                if b in last_w:
                    deps.add(('waw', last_w[b]))
                for rr in readers.get(b, ()):
                    deps.add(('war', rr))
            for b in op['r']:
                readers.setdefault(b, []).append(i)
            for b in op['w']:
                last_w[b] = i
                readers[b] = []
            keep = set()
            for kind, j in deps:
                if j == i:
                    continue
                pj = ops[j]
                if pj['dsem'] is None and op['dsem'] is None and pj['eng'] == op['eng']:
                    if op['eng'] == 'pe' or kind == 'war':
                        continue
                keep.add(j)
            op['deps'] = keep
            for j in keep:
                ops[j]['needed'] = True
            if op['dsem'] is None:
                last_of[op['eng']] = i
        for e_, j in last_of.items():
            ops[j]['needed'] = True
        cnt = {}
        for op in ops:
            if op['eng'] == 'barrier':
                continue
            if op['dsem'] is not None:
                k = 'D:' + op['dsem']
                cnt[k] = cnt.get(k, 0) + 16
                op['sem'] = k
                op['val'] = cnt[k]
            elif op.get('needed'):
                k = 'E:' + op['eng']
                cnt[k] = cnt.get(k, 0) + 1
                op['sem'] = k
                op['val'] = cnt[k]
        waited = {}
        running = {}
        pending = {}
        for op in ops:
            if op['eng'] == 'barrier':
                for e_ in ('pe', 'act', 'dve', 'pool', 'sp'):
                    pending[e_] = dict(running)
                continue
            ws = {}
            if pending.get(op['eng']):
                ws.update(pending[op['eng']])
                pending[op['eng']] = None
            for j in op['deps']:
                pj = ops[j]
                ws[pj['sem']] = max(ws.get(pj['sem'], 0), pj['val'])
            wl = []
            we = waited.setdefault(op['eng'], {})
            for k, v in ws.items():
                if we.get(k, 0) >= v:
                    continue
                we[k] = v
                wl.append((k, v))
            op['waits'] = wl
            if op.get('sem') is not None:
                running[op['sem']] = op['val']
        self.totals = cnt
        return cnt

    def emit(self, nc, es):
        cnt = self.analyze()
        sems = {}
        for k in cnt:
            sems[k] = es.enter_context(nc.semaphore(k.replace(':', '_')))
        block = es.enter_context(nc.Block())
        ops = self.ops

        def run(engname):
            def f(e):
                for op in ops:
                    if op['eng'] != engname:
                        continue
                    for k, v in op['waits']:
                        e.wait_ge(sems[k], v)
                    ins = op['fn'](e)
                    if op.get('sem') is not None:
                        ins.then_inc(sems[op['sem']], 16 if op['dsem'] is not None else 1)
                for k, v in cnt.items():
                    e.wait_ge(sems[k], v)
            return f

        block.sync(run('sp'))
        block.scalar(run('act'))
        block.vector(run('dve'))
        block.gpsimd(run('pool'))
        block.tensor(run('pe'))


class KB:
    def __init__(self, nc):
        self.nc = nc
        self.P = Prog()
        self.rot = 0

    def capture(self):
        self._saved = self.P.ops
        self.P.ops = []

    def end_capture(self):
        l = self.P.ops
        self.P.ops = self._saved
        return l

    def merge(self, A, B):
        out = []
        ia = ib = 0
        na, nb_ = max(len(A), 1), max(len(B), 1)
        while ia < len(A) or ib < len(B):
            if ib >= len(B) or (ia < len(A) and ia * nb_ <= ib * na):
                out.append(A[ia]); ia += 1
            else:
                out.append(B[ib]); ib += 1
        self.P.ops.extend(out)

    def act(self, out, in_, func, r, w, **kw):
        self.P.add('act', lambda e: e.activation(out=out, in_=in_, func=func, **kw), r, w)

    def ts(self, out, in0, s1, s2, op0, op1, r, w, eng='dve', accum=None):
        if accum is None:
            if op1 is None:
                self.P.add(eng, lambda e: e.tensor_scalar(out=out, in0=in0, scalar1=s1, scalar2=None, op0=op0), r, w)
            else:
                self.P.add(eng, lambda e: e.tensor_scalar(out=out, in0=in0, scalar1=s1, scalar2=s2, op0=op0, op1=op1), r, w)
        else:
            self.P.add(eng, lambda e: e.tensor_scalar(out=out, in0=in0, scalar1=s1, scalar2=s2, op0=op0, op1=op1,
                                                      accum_out=accum), r, w)

    def tt(self, out, in0, in1, op, r, w, eng='dve'):
        self.P.add(eng, lambda e: e.tensor_tensor(out=out, in0=in0, in1=in1, op=op), r, w)

    def stt(self, out, in0, scalar, in1, op0, op1, r, w):
        self.P.add('dve', lambda e: e.scalar_tensor_tensor(out=out, in0=in0, scalar=scalar, in1=in1, op0=op0, op1=op1), r, w)

    def cp(self, eng, out, in_, r, w):
        if eng == 'act':
            self.P.add('act', lambda e: e.copy(out=out, in_=in_), r, w)
        else:
            self.P.add(eng, lambda e: e.tensor_copy(out=out, in_=in_), r, w)

    def memset(self, eng, ap, val, w):
        self.P.add(eng, lambda e: e.memset(ap, val), (), w)

    def mm(self, out, lhsT, rhs, start, stop, r, w):
        self.P.add('pe', lambda e: e.matmul(out, lhsT=lhsT, rhs=rhs, start=start, stop=stop), r, w)

    def tr(self, out, in_, ident, r, w):
        self.P.add('pe', lambda e: e.transpose(out=out, in_=in_, identity=ident), r, w)

    def dma(self, q, out, in_, r, w, dsem):
        self.P.add(q, lambda e: e.dma_start(out=out, in_=in_), r, w, dsem=dsem)

    def red(self, out, in_, op, r, w):
        self.P.add('dve', lambda e: e.tensor_reduce(out=out, in_=in_, axis=AX.X, op=op), r, w)

    def recip(self, out, in_, r, w):
        self.P.add('dve', lambda e: e.reciprocal(out=out, in_=in_), r, w)

    def bn_stats(self, out, in_, r, w):
        self.P.add('dve', lambda e: e.bn_stats(out=out, in_=in_), r, w)

    def bn_aggr(self, out, in_, r, w):
        self.P.add('dve', lambda e: e.bn_aggr(out=out, in_=in_), r, w)


def geometry(SEQ):
    T = SEQ + 16
    NB = T // 128 + 1
    assert T == 128 * (NB - 1) + 16 and NB % 4 == 1
    G = (NB + 3) // 4
    return T, NB, G


def build(SEQ, phases="0ABS"):
    T, NB, G = geometry(SEQ)
    NKMAX = NB * 128
    nc = bass.Bass("TRN2", target_bir_lowering=False)

    def din(name, shape, dt=F32):
        return nc.dram_tensor(name, list(shape), dt, kind="ExternalInput").ap()

    def dout(name, shape, dt=F32):
        return nc.dram_tensor(name, list(shape), dt, kind="ExternalOutput").ap()

    def dscr(name, shape, dt):
        return nc.dram_tensor(name, list(shape), dt, kind="Internal").ap()

    xall = din("xall", [NB * 128, D])
    xown = din("xown", [G * 128, D])
    xs = din("xs", [128, D])
    w_in = din("w_in", [D, IN_COLS])
    w3 = din("w3", [3, D, D])
    ln_in_g = din("ln_in_g", [1, D]); ln_in_b = din("ln_in_b", [1, D])
    ln_g = din("ln_g", [1, D]); ln_b = din("ln_b", [1, D])
    gate_b = din("gate_b", [1, 2048])
    gla_gate_b = din("gla_gate_b", [1, 512])
    gla_norm_g = din("gla_norm_g", [1, 256])
    idx_kn_g = din("idx_kn_g", [1, 64]); idx_kn_b = din("idx_kn_b", [1, 64])
    gla_w2 = din("gla_w2", [16, 512])
    cache_k = din("cache_k", [4, 2048, 256]); cache_v = din("cache_v", [4, 2048, 256])
    cache_ik = din("cache_ik", [4, 2048, 64])
    state = din("state", [4, 4, 128, 256])
    c_ident = din("c_ident", [128, 128])
    c_triA = din("c_triA", [128, 128]); c_blkA = din("c_blkA", [128, 2])
    c_triB = din("c_triB", [128, 128]); c_maskB = din("c_maskB", [128, 4, 128])
    c_triS = din("c_triS", [128, 128]); c_maskS = din("c_maskS", [128, 4, 128])
    c_mrevS = din("c_mrevS", [128, 128]); c_bsumS = din("c_bsumS", [128, 4])
    c_cmaskS = din("c_cmaskS", [128, 4, 128]); c_rmaskS = din("c_rmaskS", [128, 4])
    c_idrep = din("c_idrep", [128, 512]); c_idrepS = din("c_idrepS", [128, 4, 64])
    c_selS = din("c_selS", [128, 4, 128])
    c_ctab = din("c_ctab", [128, KIT + 1])
    c_tbias = din("c_tbias", [128, 2, 640])
    c_onehot = din("c_onehot", [128, 4])
    c_tailmask = din("c_tailmask", [128, 1])

    y_own = dout("y_own", [G * 128, D])
    kp = dout("kp", [NB * 128, 256]); vp = dout("vp", [NB * 128, 256]); ikp = dout("ikp", [NB * 128, 64])
    gla_p = dout("gla_p", [128, 1024])
    ys = dout("ys", [128, D]); ks = dout("ks", [128, 256]); vs = dout("vs", [128, 256]); iks = dout("iks", [128, 64])
    gla_s = dout("gla_s", [4, 128, 1024])

    wbf = dscr("wbf", [D, IN_COLS], BF16)
    wbf3 = dscr("wbf3", [3, D, D], BF16)
    kT_d = dscr("kT_d", [128, 2, NKMAX], BF16)
    v_d = dscr("v_d", [128, NB, 260], BF16)
    ki_d = dscr("ki_d", [128, NKMAX], BF16)
    snap = dscr("snap", [NB, 128, 1024], F32)

    k = KB(nc)
    P = k.P
    top = ExitStack()
    with top:
        pst = [top.enter_context(nc.psum_tensor("ps%d" % i, [128, 512], F32)) for i in range(8)]

        def PSF(i):
            return pst[i][:]

        def PSB(i):
            return pst[i][:].bitcast(BF16)

        if "0" in phases:
            with ExitStack() as es:
                def sb(name, shape, dt):
                    return es.enter_context(nc.sbuf_tensor(name, shape, dt))
                wst = [sb("wst%d" % s, [128, 8, 512], F32) for s in range(2)]
                wcb = [sb("wcb%d" % s, [128, 8, 512], BF16) for s in range(2)]
                jobs = []
                for c in range(17):
                    c0 = c * 512
                    n = min(512, IN_COLS - c0)
                    jobs.append((w_in[:, c0:c0 + n], wbf[:, c0:c0 + n], n))
                for m in range(3):
                    for c in range(2):
                        jobs.append((w3[m, :, c * 512:(c + 1) * 512], wbf3[m, :, c * 512:(c + 1) * 512], 512))
                engs = ['dve', 'act', 'pool']
                for idx, (src, dst, n) in enumerate(jobs):
                    s = idx % 2
                    k.dma('sp', wst[s][:, :, :n], src.rearrange("(k p) n -> p k n", p=128), [], ['wst%d' % s], 'wst%d' % s)
                    k.cp(engs[idx % 3], wcb[s][:, :, :n], wst[s][:, :, :n], ['wst%d' % s], ['wcb%d' % s])
                    k.dma('act', dst.rearrange("(k p) n -> p k n", p=128), wcb[s][:, :, :n], ['wcb%d' % s], [], 'wcb%d' % s)
            P.barrier()

        def layernorm(src, srckey, gB, bB, tl, tag, out_f32=None, out_f32_key=None, out_bf=None, out_bf_key=None):
            tk = lambda n: tag + n
            for c in range(2):
                k.bn_stats(tl['st'][:, c, :], src[:, c * 512:(c + 1) * 512], [srckey], [tk('st%d' % c)])
            k.bn_aggr(tl['mv'][:], tl['st'][:].rearrange("p a b -> p (a b)"), [tk('st0'), tk('st1')], [tk('mv')])
            k.act(tl['sd'][:], tl['mv'][:, 1:2], AF.Ln, [tk('mv'), 'eps'], [tk('sd')], bias=tl['eps'][:, 0:1], scale=1.0)
            k.act(tl['rstd'][:], tl['sd'][:], AF.Exp, [tk('sd')], [tk('rstd')], scale=-0.5)
            k.ts(tl['nmr'][:], tl['mv'][:, 0:1], tl['rstd'][:, 0:1], -1.0, ALU.mult, ALU.mult, [tk('mv'), tk('rstd')], [tk('nmr')])
            k.act(tl['xn'][:], src, AF.Identity, [srckey, tk('nmr'), tk('rstd')], [tl['xnkey']],
                  bias=tl['nmr'][:, 0:1], scale=tl['rstd'][:, 0:1])
            k.tt(tl['xn'][:], tl['xn'][:], gB, ALU.mult, [tl['xnkey'], 'lnconst'], [tl['xnkey']])
            if out_f32 is not None:
                k.tt(out_f32, tl['xn'][:], bB, ALU.add, [tl['xnkey'], 'lnconst'], [out_f32_key])
                if out_bf is not None:
                    k.cp('pool', out_bf, out_f32, [out_f32_key], [out_bf_key])
            else:
                k.tt(out_bf, tl['xn'][:], bB, ALU.add, [tl['xnkey'], 'lnconst'], [out_bf_key])

        if "A" in phases:
            with ExitStack() as es:
                def sb(name, shape, dt):
                    return es.enter_context(nc.sbuf_tensor(name, shape, dt))
                gB = sb("a_gB", [128, D], F32); bB = sb("a_bB", [128, D], F32)
                identf = sb("a_idf", [128, 128], F32); identb = sb("a_idb", [128, 128], BF16)
                triA = sb("a_triA", [128, 128], F32); blkA = sb("a_blkA", [128, 2], F32)
                w2 = sb("a_w2", [16, 512], F32); gbias = sb("a_gbias", [1, 512], F32); ones1 = sb("a_ones1", [1, 128], F32)
                gkiB = sb("a_gkiB", [128, 64], F32); bkiB = sb("a_bkiB", [128, 64], F32)
                eps = sb("a_eps", [128, 1], F32); one = sb("a_one", [128, 1], F32)
                tailm = sb("a_tailm", [128, 1], F32)
                wA = sb("a_wA", [128, 8, 2128], BF16)
                SS = [sb("a_S%d" % s, [128, 4, 256], F32) for s in range(3)]
                xa = [sb("a_xa%d" % s, [128, D], F32) for s in range(3)]
                xn = sb("a_xn", [128, D], F32)
                hb = [sb("a_hb%d" % s, [128, D], BF16) for s in range(3)]
                hT = [sb("a_hT%d" % s, [128, 8, 128], BF16) for s in range(2)]
                st = sb("a_st", [128, 2, 6], F32); mv = sb("a_mv", [128, 2], F32)
                sd = sb("a_sd", [128, 1], F32); rstd = sb("a_rstd", [128, 1], F32); nmr = sb("a_nmr", [128, 1], F32)
                st2 = sb("a_st2", [128, 6], F32); mv2 = sb("a_mv2", [128, 2], F32)
                sd2 = sb("a_sd2", [128, 1], F32); rstd2 = sb("a_rstd2", [128, 1], F32); nmr2 = sb("a_nmr2", [128, 1], F32)
                Vt = [sb("a_V%d" % s, [128, 1024], BF16) for s in range(4)]
                kdv = [sb("a_kdv%d" % s, [128, 512], F32) for s in range(3)]
                kdb = [sb("a_kdb%d" % s, [128, 256], BF16) for s in range(2)]
                vext = [sb("a_vext%d" % s, [128, 4, 65], BF16) for s in range(2)]
                kTt = [sb("a_kT%d" % s, [128, 2, 128], BF16) for s in range(2)]
                kin = [sb("a_kin%d" % s, [128, 64], F32) for s in range(3)]
                ksb = [sb("a_ksb%d" % s, [128, 512], F32) for s in range(3)]
                kif = [sb("a_kif%d" % s, [128, 64], F32) for s in range(2)]
                kib = [sb("a_kib%d" % s, [128, 128], BF16) for s in range(2)]
                kiT = [sb("a_kiT%d" % s, [128, 128], BF16) for s in range(2)]
                glb = sb("a_glb", [16, 128], BF16); w2b = sb("a_w2b", [16, 512], BF16); gbB = sb("a_gbB", [128, 512], F32)
                el = [sb("a_el%d" % s, [128, 512], F32) for s in range(3)]
                er = sb("a_er", [128, 512], F32)
                Kt = [sb("a_Kt%d" % s, [128, 512], BF16) for s in range(2)]
                dec = [sb("a_dec%d" % s, [128, 8], F32) for s in range(2)]

                k.dma('sp', gB[:], ln_in_g.partition_broadcast(128), [], ['lnconst0'], 'ca')
                k.dma('sp', bB[:], ln_in_b.partition_broadcast(128), [], ['lnconst1'], 'ca')
                k.dma('sp', identf[:], c_ident, [], ['identf'], 'ca')
                k.dma('sp', triA[:], c_triA, [], ['triA'], 'ca')
                k.dma('sp', blkA[:], c_blkA, [], ['blkA'], 'ca')
                k.dma('sp', w2[:], gla_w2, [], ['w2'], 'ca')
                k.dma('sp', gbias[:], gla_gate_b, [], ['gbias'], 'ca')
                k.dma('sp', gbB[:], gla_gate_b.partition_broadcast(128), [], ['gbB'], 'ca')
                k.dma('sp', gkiB[:], idx_kn_g.partition_broadcast(128), [], ['gkiB'], 'ca')
                k.dma('sp', bkiB[:], idx_kn_b.partition_broadcast(128), [], ['bkiB'], 'ca')
                k.dma('sp', tailm[:], c_tailmask, [], ['tailm'], 'ca')
                wmap = [(C_GK, 512, 0), (C_GV, 1024, 512), (C_DK, 512, 1536), (C_IK, 64, 2048), (C_GLOW, 16, 2112)]
                for (c0, n, o) in wmap:
                    k.dma('sp', wA[:, :, o:o + n], wbf[:, c0:c0 + n].rearrange("(k p) n -> p k n", p=128), [], ['wA%d' % o], 'ca')
                P.barrier()
                k.memset('dve', eps[:], EPS, ['eps'])
                k.memset('dve', one[:], 1.0, ['one'])
                k.memset('dve', ones1[:], 1.0, ['ones1'])
                k.memset('dve', SS[0][:], 0.0, ['S0'])
                for s in range(2):
                    k.memset('pool', vext[s][:], 1.0, ['vext%d' % s])
                k.cp('dve', identb[:], identf[:], [], ['identb'])
                k.cp('dve', w2b[:], w2[:], [], ['w2b'])
                P.barrier()
                tl = dict(st=st, mv=mv, sd=sd, rstd=rstd, nmr=nmr, xn=xn, eps=eps, xnkey='xn')

                def loadx(i):
                    s = i % 3
                    k.dma('sp', xa[s][:], xall[i * 128:(i + 1) * 128, :], [], ['xa%d' % s], 'xa%d' % s)

                loadx(0)

                def fa(i):
                    s = i % 3
                    if i + 1 < NB:
                        loadx(i + 1)
                    layernorm(xa[s][:], 'xa%d' % s, gB[:], bB[:], tl, 'a', out_bf=hb[s][:], out_bf_key='hb%d' % s)

                def fb_a(i):
                    s = i % 3
                    s2 = i % 2
                    for kc in range(8):
                        k.tr(PSB(0)[:, kc * 128:(kc + 1) * 128], hb[s][:, kc * 128:(kc + 1) * 128], identb[:], ['hb%d' % s], ['ps0'])
                    k.cp('act', hT[s2][:].rearrange("p a b -> p (a b)"), PSB(0), ['ps0'], ['hT%d' % s2])

                def fb_b(i):
                    s = i % 3
                    s2 = i % 2
                    last = (i == NB - 1)
                    hk = 'hT%d' % s2
                    for kc in range(8):
                        k.mm(PSF(1), hT[s2][:, kc, :], wA[:, kc, 0:512], kc == 0, kc == 7, [hk], ['ps1'])
                    k.cp('act', ksb[s][:], PSF(1), ['ps1'], ['ksb%d' % s])
                    for half in range(2):
                        for kc in range(8):
                            k.mm(PSF(2 + half), hT[s2][:, kc, :], wA[:, kc, 512 + half * 512:1024 + half * 512], kc == 0, kc == 7, [hk], ['ps%d' % (2 + half)])
                        k.cp('act' if half == 0 else 'dve', Vt[i % 4][:, half * 512:(half + 1) * 512], PSF(2 + half), ['ps%d' % (2 + half)], ['V%d' % (i % 4)])
                    for kc in range(8):
                        k.mm(PSF(4), hT[s2][:, kc, :], wA[:, kc, 1536:2048], kc == 0, kc == 7, [hk], ['ps4'])
                    k.cp('act', kdv[s][:], PSF(4), ['ps4'], ['kdv%d' % s])
                    for kc in range(8):
                        k.mm(PSF(5)[:, 0:64], hT[s2][:, kc, :], wA[:, kc, 2048:2112], kc == 0, kc == 7, [hk], ['ps5'])
                    k.cp('dve', kin[s][:], PSF(5)[:, 0:64], ['ps5'], ['kin%d' % s])

                    for kc in range(8):
                        k.mm(PSF(5)[0:16, 64:192], wA[:, kc, 2112:2128], hT[s2][:, kc, :], kc == 0, kc == 7, [hk], ['ps5'])
                    k.cp('dve', glb[:], PSF(5)[0:16, 64:192], ['ps5'], ['gl'])
                    k.mm(PSF(4), glb[:], w2b[:], True, True, ['gl'], ['ps4'])
                    k.tt(el[s][:], PSF(4), gbB[:], ALU.add, ['ps4'], ['el%d' % s])
                    k.act(el[s][:], el[s][:], AF.Exp, ['el%d' % s], ['el%d' % s], scale=-1.0)
                    k.act(el[s][:], el[s][:], AF.Ln, ['el%d' % s], ['el%d' % s], bias=one[:, 0:1], scale=1.0)
                    if last:
                        k.ts(el[s][:], el[s][:], tailm[:, 0:1], None, ALU.mult, None, ['el%d' % s], ['el%d' % s])
                def back1(i):
                    s3 = i % 3
                    s = i % 2
                    last = (i == NB - 1)
                    k.mm(PSF(6), triA[:], el[s3][:], True, True, ['el%d' % s3], ['ps6'])
                    for h in range(4):
                        k.mm(PSF(7)[:, 448 + 2 * h:450 + 2 * h], el[s3][:, h * 128:(h + 1) * 128], blkA[:], True, True, ['el%d' % s3], ['ps7'])
                    k.act(er[:], PSF(6), AF.Exp, ['ps6'], ['er'])
                    k.act(dec[s][:], PSF(7)[:, 448:456], AF.Exp, ['ps7'], ['dec%d' % s])
                    if last:
                        k.stt(Kt[s][:], ksb[s3][:], tailm[:, 0:1], er[:], ALU.mult, ALU.mult, ['ksb%d' % s3, 'er'], ['Kt%d' % s])
                    else:
                        k.tt(Kt[s][:], ksb[s3][:], er[:], ALU.mult, ['ksb%d' % s3, 'er'], ['Kt%d' % s])
                    k.dma('act', kp[i * 128:(i + 1) * 128, :], kdv[s3][:, 0:256], ['kdv%d' % s3], [], 'kdvo%d' % s3)
                    k.dma('act', vp[i * 128:(i + 1) * 128, :], kdv[s3][:, 256:512], ['kdv%d' % s3], [], 'kdvo%d' % s3)
                    k.cp('pool', kdb[s][:], kdv[s3][:, 0:256], ['kdv%d' % s3], ['kdb%d' % s])
                    k.cp('pool', vext[s][:, :, 0:64], kdv[s3][:, 256:512].rearrange("p (g d) -> p g d", g=4), ['kdv%d' % s3], ['vext%d' % s])
                    for c in range(2):
                        k.tr(PSB(7)[:, c * 128:(c + 1) * 128], kdb[s][:, c * 128:(c + 1) * 128], identb[:], ['kdb%d' % s], ['ps7'])
                    k.cp('dve', kTt[s][:].rearrange("p a b -> p (a b)"), PSB(7)[:, 0:256], ['ps7'], ['kT%d' % s])
                    k.dma('pool', kT_d[:, :, i * 128:(i + 1) * 128], kTt[s][:], ['kT%d' % s], [], 'kTo%d' % s)
                    k.dma('pool', v_d[:, i, :], vext[s][:].rearrange("p g d -> p (g d)"), ['vext%d' % s], [], 'vexto%d' % s)
                    kn = kin[s3]
                    knk = 'kin%d' % s3
                    k.bn_stats(st2[:], kn[:], [knk], ['st2'])
                    k.bn_aggr(mv2[:], st2[:], ['st2'], ['mv2'])
                    k.act(sd2[:], mv2[:, 1:2], AF.Ln, ['mv2'], ['sd2'], bias=eps[:, 0:1], scale=1.0)
                    k.act(rstd2[:], sd2[:], AF.Exp, ['sd2'], ['rstd2'], scale=-0.5)
                    k.ts(nmr2[:], mv2[:, 0:1], rstd2[:, 0:1], -1.0, ALU.mult, ALU.mult, ['mv2', 'rstd2'], ['nmr2'])
                    k.act(kn[:], kn[:], AF.Identity, [knk, 'nmr2', 'rstd2'], [knk], bias=nmr2[:, 0:1], scale=rstd2[:, 0:1])
                    k.tt(kn[:], kn[:], gkiB[:], ALU.mult, [knk], [knk])
                    k.tt(kif[s][:], kn[:], bkiB[:], ALU.add, [knk], ['kif%d' % s])
                    k.dma('act', ikp[i * 128:(i + 1) * 128, :], kif[s][:], ['kif%d' % s], [], 'kifo%d' % s)
                    k.cp('pool', kib[s][:, 0:64], kif[s][:], ['kif%d' % s], ['kib%d' % s])
                    k.cp('pool', kib[s][:, 64:128], kif[s][:], ['kif%d' % s], ['kib%d' % s])
                    k.tr(PSB(7)[:, 256:384], kib[s][:], identb[:], ['kib%d' % s], ['ps7'])
                    k.cp('dve', kiT[s][:], PSB(7)[:, 256:384], ['ps7'], ['kiT%d' % s])
                    k.dma('pool', ki_d[:, i * 128:(i + 1) * 128], kiT[s][:], ['kiT%d' % s], [], 'kiTo%d' % s)

                def back2(i):
                    s3 = i % 3
                    s = i % 2
                    cur = (2 * i) % 3
                    k.dma('sp', snap[i], SS[cur][:].rearrange("p h e -> p (h e)"), ['S%d' % cur], [], 'Ssto%d' % cur)
                    sbanks = [[6, 7], [2, 3]]
                    for c in range(2):
                        for hp in range(2):
                            bk = sbanks[c][hp]
                            for hh in range(2):
                                h = hp * 2 + hh
                                k.mm(PSF(bk)[:, hh * 256:(hh + 1) * 256], Kt[s][c * 64:(c + 1) * 64, h * 128:(h + 1) * 128],
                                     Vt[i % 4][c * 64:(c + 1) * 64, h * 256:(h + 1) * 256], True, True, ['Kt%d' % s, 'V%d' % (i % 4)], ['ps%d' % bk])
                            src_, dst_ = (2 * i + c) % 3, (2 * i + c + 1) % 3
                            for hh in range(2):
                                h = hp * 2 + hh
                                k.stt(SS[dst_][:, h, :], SS[src_][:, h, :], dec[s][:, 2 * h + c:2 * h + c + 1], PSF(bk)[:, hh * 256:(hh + 1) * 256],
                                      ALU.mult, ALU.add, ['S%d' % src_, 'dec%d' % s, 'ps%d' % bk], ['S%d' % dst_])

                for i0 in range(min(3, NB)):
                    fa(i0)
                for i0 in range(3):
                    if i0 < NB:
                        fb_a(i0)
                        fb_b(i0)
                    if i0 + 3 < NB and i0 < 2:
                        fa(i0 + 3)
                back1(0)
                for i in range(NB):
                    if i + 3 < NB:
                        fb_a(i + 3)
                    back2(i)
                    if i + 1 < NB:
                        back1(i + 1)
                    if i + 5 < NB:
                        fa(i + 5)
                    if i + 3 < NB:
                        fb_b(i + 3)
                fin = (2 * NB) % 3
                k.dma('sp', gla_p, SS[fin][:].rearrange("p h e -> p (h e)"), ['S%d' % fin], [], 'glap')
            P.barrier()


        def phase_own(mode):
            PR = (mode == 'P')
            NK = NKMAX if PR else 2176
            with ExitStack() as es:
                def sb(name, shape, dt):
                    return es.enter_context(nc.sbuf_tensor(mode + name, shape, dt))
                gB = sb("gB", [128, D], F32); bB = sb("bB", [128, D], F32)
                g2B = sb("g2B", [128, D], F32); b2B = sb("b2B", [128, D], F32)
                gtb = [sb("gtb%d" % s_, [128, 512], F32) for s_ in range(2)]
                gnB = sb("gnB", [128, 256], F32)
                identf = sb("idf", [128, 128], F32); identb = sb("idb", [128, 128], BF16)
                tri = sb("tri", [128, 128], F32)
                maskf = sb("maskf", [128, 512], F32)
                cst = sb("cst", [128, 512], F32); idrepb = sb("idrepb", [128, 512], BF16)
                w2 = sb("w2", [16, 512], F32); gbias = sb("gbias", [1, 512], F32); ones1 = sb("ones1", [1, 128], F32)
                eps = sb("eps", [128, 1], F32); one = sb("one", [128, 1], F32)
                ctab = sb("ctab", [128, KIT + 1], F32)
                tbias = sb("tbias", [128, 2, 640 if PR else 128], F32)
                onehot = sb("onehot", [128, 4], F32)
                xo = sb("xo", [128, D], F32); h = sb("h", [128, D], F32); hb = sb("hb", [128, D], BF16)
                hT = sb("hT", [128, 8, 128], BF16)
                tmp = sb("tmp", [128, D], F32)
                st = sb("st", [128, 2, 6], F32); mv = sb("mv", [128, 2], F32)
                sd = sb("sd", [128, 1], F32); rstd = sb("rstd", [128, 1], F32); nmr = sb("nmr", [128, 1], F32)
                wch = [sb("wch%d" % s_, [128, 8, 512], BF16) for s_ in range(2)]
                gl = sb("gl", [16, 128], F32)
                el = sb("el", [128, 512], F32); eb = sb("eb", [128, 512], F32); enb = sb("enb", [128, 512], F32)
                qT = sb("qT", [128, 4, 128], BF16); kTh = sb("kTh", [128, 4, 128], BF16)
                V = sb("V", [128, 1024], BF16)
                sg = [sb("sg%d" % s_, [128, 512], F32) for s_ in range(2)]
                QTz = sb("QTz", [128, 4, 512], BF16); qiTz = sb("qiTz", [128, 8, 128], BF16)
                wabs = sb("wabs", [128, 8], F32); wsgn = sb("wsgn", [128, 8], F32)
                AT = sb("AT", [128, 4, 128], BF16)
                ss = sb("ss", [128, 4], F32); rs = sb("rs", [128, 4], F32)
                yain = sb("yain", [128, D], BF16); yT = sb("yT", [128, 8, 128], BF16)
                mrg = sb("mrg", [128, D], F32)
                sc = sb("sc", [128, NK], F32)
                junk = None if PR else sb("junk", [128, 2176], BF16)
                rl = [sb("rl%d" % s_, [128, 512], F32) for s_ in range(2)]
                rd = sb("rd", [128, 16], F32)
                yout = xo
                rmax = sb("rmax", [128, 1], F32); rmin = sb("rmin", [128, 1], F32); Wd = sb("Wd", [128, 1], F32)
                wtab = sb("wtab", [128, KIT + 1], F32); mids = sb("mids", [128, KIT + 1], F32)
                cnts = sb("cnts", [128, KIT], F32); us = sb("us", [128, KIT], F32); thr = sb("thr", [128, 1], F32)
                sAs = sb("sAs", [128, KIT], F32); vvs = sb("vvs", [128, KIT], F32)
                jd = sb("jd", [128, 8], BF16); ja = sb("ja", [128, 8], BF16); jq = sb("jq", [128, 8], BF16)
                if PR:
                    Sc = [sb("Sc%d" % s_, [128, 1024], F32) for s_ in range(2)]
                    Sown = sb("Sown", [128, 1024], F32); Sb = sb("Sb", [128, 4, 256], BF16)
                    kich = [sb("kich%d" % s_, [128, 512], BF16) for s_ in range(2)]
                    kTch = [sb("kTch%d" % s_, [128, 2, 512], BF16) for s_ in range(2)]
                    vch = [sb("vch%d" % s_, [128, 4, 260], BF16) for s_ in range(2)]
                    mbt = [sb("mbt%d" % s_, [128, 128], BF16) for s_ in range(3)]
                    pT = [sb("pT%d" % s_, [128, 512], BF16) for s_ in range(3)]
                    oT = [sb("oT%d" % s_, [65, 512], F32) for s_ in range(2)]
                else:
                    cmaskS = sb("cmaskS", [128, 4, 128], F32); rmaskS = sb("rmaskS", [128, 4], F32)
                    mrevS = sb("mrevS", [128, 128], F32); bsumS = sb("bsumS", [128, 4], F32)
                    idrepSb = sb("idrepSb", [128, 4, 64], BF16)
                    selSb = sb("selSb", [128, 4, 128], BF16)
                    gkiB = sb("gkiB", [128, 64], F32); bkiB = sb("bkiB", [128, 64], F32)
                    S0f = [sb("S0f%d" % b_, [128, 4, 256], F32) for b_ in range(2)]
                    S0b = [sb("S0b%d" % b_, [128, 4, 256], BF16) for b_ in range(4)]
                    qTb = [sb("qTb%d" % b_, [128, 4, 128], BF16) for b_ in range(4)]
                    wabsb = sb("wabsb", [128, 4, 8], F32)
                    QTs = sb("QTs", [128, 4, 4, 64], BF16)
                    kTs1 = sb("kTs", [128, 2, 2176], BF16)
                    kiTs = [sb("kiTs%d" % b_, [128, 2176], BF16) for b_ in range(4)]
                    vexts1 = sb("vexts", [128, 17, 260], BF16)
                    ckf = sb("ckf", [128, 8, 256], F32); ckb = sb("ckb", [128, 8, 256], BF16)
                    cif = sb("cif", [128, 8, 64], F32); cib = sb("cib", [128, 8, 128], BF16)
                    kdv = sb("kdv", [128, 512], F32); kdb = sb("kdb", [128, 256], BF16); vnb = sb("vnb", [128, 256], BF16)
                    kin = sb("kin", [128, 64], F32); kif = sb("kif", [128, 64], F32); kib = sb("kib", [128, 128], BF16)
                    st2 = sb("st2", [128, 6], F32); mv2 = sb("mv2", [128, 2], F32)
                    sd2 = sb("sd2", [128, 1], F32); rstd2 = sb("rstd2", [128, 1], F32); nmr2 = sb("nmr2", [128, 1], F32)
                    kTnew = sb("kTnew", [128, 2, 128], BF16); kiTnew = sb("kiTnew", [128, 128], BF16)
                    Kt = sb("Kt", [128, 512], BF16); Ktb = [sb("Ktb%d" % s_, [128, 512], BF16) for s_ in range(2)]
                    decs = sb("decs", [128, 16], F32)
                    pTs = [sb("pTs%d" % s_, [128, 64], BF16) for s_ in range(3)]

                cl = 'c' + mode
                k.dma('sp', gB[:], ln_in_g.partition_broadcast(128), [], [], cl)
                k.dma('sp', bB[:], ln_in_b.partition_broadcast(128), [], [], cl)
                k.dma('sp', g2B[:], ln_g.partition_broadcast(128), [], [], cl)
                k.dma('sp', b2B[:], ln_b.partition_broadcast(128), [], [], cl)
                k.dma('sp', gnB[:], gla_norm_g.partition_broadcast(128), [], [], cl)
                k.dma('sp', identf[:], c_ident, [], [], cl)
                k.dma('sp', tri[:], c_triB if PR else c_triS, [], [], cl)
                k.dma('sp', maskf[:], (c_maskB if PR else c_maskS).rearrange("p a b -> p (a b)"), [], [], cl)
                k.dma('sp', w2[:], gla_w2, [], [], cl)
                k.dma('sp', gbias[:], gla_gate_b, [], [], cl)
                k.dma('sp', ctab[:], c_ctab, [], [], cl)
                k.dma('sp', tbias[:], c_tbias if PR else c_tbias[:, :, 0:128], [], [], cl)
                k.dma('sp', onehot[:], c_onehot, [], [], cl)
                if not PR:
                    k.dma('sp', cmaskS[:], c_cmaskS, [], [], cl)
                    k.dma('sp', rmaskS[:], c_rmaskS, [], [], cl)
                    k.dma('sp', mrevS[:], c_mrevS, [], [], cl)
                    k.dma('sp', bsumS[:], c_bsumS, [], [], cl)
                    k.dma('sp', gkiB[:], idx_kn_g.partition_broadcast(128), [], [], cl)
                    k.dma('sp', bkiB[:], idx_kn_b.partition_broadcast(128), [], [], cl)
                P.barrier()
                k.memset('dve', eps[:], EPS, [])
                k.memset('dve', one[:], 1.0, [])
                k.memset('dve', ones1[:], 1.0, [])
                k.memset('pool', QTz[:], 0.0, [])
                k.memset('pool', qiTz[:], 0.0, [])
                k.memset('pool', yain[:], 0.0, [])
                k.memset('pool', tmp[:], 0.0, [])
                k.cp('dve', identb[:], identf[:], [], [])
                k.dma('sp', cst[:], c_idrep, [], ['cst'], 'cst')
                k.cp('dve', idrepb[:], cst[:], ['cst'], ['idrepb'])
                if not PR:
                    k.dma('sp', cst[:, 0:256], c_idrepS.rearrange("p a b -> p (a b)"), ['idrepb'], ['cst'], 'cst')
                    k.cp('dve', idrepSb[:].rearrange("p a b -> p (a b)"), cst[:, 0:256], ['cst'], ['idrepSb'])
                    k.dma('sp', cst[:], c_selS.rearrange("p a b -> p (a b)"), ['idrepSb'], ['cst'], 'cst')
                    k.cp('dve', selSb[:].rearrange("p a b -> p (a b)"), cst[:], ['cst'], ['selSb'])
                    for b_ in range(4):
                        s_ = b_ % 2
                        k.dma('sp', S0f[s_][:], state[b_].rearrange("h p e -> p h e"), [], ['S0f%d' % s_], 'S0f%d' % s_)
                        k.cp('pool', S0b[b_][:], S0f[s_][:], ['S0f%d' % s_], ['S0b%d' % b_])
                        k.memset('pool', kiTs[b_][:, 2048:2176], 0.0, [])
                    k.memset('pool', vexts1[:], 1.0, [])
                    k.memset('pool', kTs1[:, :, 2048:2176], 0.0, [])
                P.barrier()
                tl = dict(st=st, mv=mv, sd=sd, rstd=rstd, nmr=nmr, xn=tmp, eps=eps, xnkey='tmp')
                bank = [0]

                def nb():
                    bank[0] = (bank[0] + 1) % 8
                    return bank[0]

                def own_block(g):
                    jobs = []

                    def J(src, n):
                        jobs.append((src, n))
                    J(wbf[:, C_IW:C_IW + 8], 8)
                    J(wbf[:, C_IQ:C_IQ + 512], 512)
                    J(wbf[:, C_DQ:C_DQ + 512], 512); J(wbf[:, C_DQ + 512:C_DQ + 1024], 512)
                    if not PR:
                        J(wbf[:, C_DK:C_DK + 512], 512)
                        J(wbf[:, C_IK:C_IK + 64], 64)
                    J(wbf[:, C_GLOW:C_GLOW + 16], 16)
                    J(wbf[:, C_GQ:C_GQ + 512], 512)
                    J(wbf[:, C_GK:C_GK + 512], 512)
                    J(wbf[:, C_GV:C_GV + 512], 512); J(wbf[:, C_GV + 512:C_GV + 1024], 512)
                    J(wbf[:, C_GR:C_GR + 512], 512); J(wbf[:, C_GR + 512:C_GR + 1024], 512)
                    for cc in range(2):
                        J(wbf3[0, :, cc * 512:(cc + 1) * 512], 512)
                        J(wbf[:, C_MA + cc * 512:C_MA + (cc + 1) * 512], 512)
                    J(wbf[:, C_DZ:C_DZ + 512], 512); J(wbf[:, C_DZ + 512:C_DZ + 1024], 512)
                    for cc in range(2):
                        J(wbf3[1, :, cc * 512:(cc + 1) * 512], 512)
                        J(wbf[:, C_MB + cc * 512:C_MB + (cc + 1) * 512], 512)
                    for cc in range(2):
                        J(wbf3[2, :, cc * 512:(cc + 1) * 512], 512)
                    jpos = [0]

                    def wissue(idx):
                        src, n = jobs[idx]
                        s_ = idx % 2
                        k.dma('sp', wch[s_][:, :, :n], src.rearrange("(k p) n -> p k n", p=128), [], ['wch%d' % s_], 'wch%d' % s_)

                    def next_w():
                        idx = jpos[0]
                        if idx == 0:
                            wissue(0)
                        if idx + 1 < len(jobs):
                            wissue(idx + 1)
                        jpos[0] += 1
                        return wch[idx % 2], 'wch%d' % (idx % 2)

                    def projT(wt, wk, n, psap, pskey):
                        for kc in range(8):
                            k.mm(psap, hT[:, kc, :], wt[:, kc, :n], kc == 0, kc == 7, ['hT', wk], [pskey])

                    def projF(wt, wk, bk):
                        for sub in range(4):
                            for kc in range(8):
                                k.mm(PSF(bk)[:, sub * 128:(sub + 1) * 128], wt[:, kc, sub * 128:(sub + 1) * 128], hT[:, kc, :],
                                     kc == 0, kc == 7, ['hT', wk], ['ps%d' % bk])

                    def transpose8(src, srckey, dst, dstkey):
                        bk = nb()
                        for kc in range(8):
                            k.tr(PSB(bk)[:, kc * 128:(kc + 1) * 128], src[:, kc * 128:(kc + 1) * 128], identb[:], [srckey], ['ps%d' % bk])
                        k.cp('act', dst[:].rearrange("p a b -> p (a b)"), PSB(bk), ['ps%d' % bk], [dstkey])

                    xsrc = xown[g * 128:(g + 1) * 128, :] if PR else xs
                    k.dma('act', xo[:], xsrc, [], ['xo'], 'xo')
                    layernorm(xo[:], 'xo', gB[:], bB[:], tl, 'b', out_f32=h[:], out_f32_key='h', out_bf=hb[:], out_bf_key='hb')
                    transpose8(hb, 'hb', hT, 'hT')
                    wt, wk = next_w()
                    bw = nb()
                    projT(wt, wk, 8, PSF(bw)[:, 0:8], 'ps%d' % bw)
                    k.ts(wabs[:], PSF(bw)[:, 0:8], -IDX_W_SCALE, None, ALU.mult, None, ['ps%d' % bw], ['wabs'])
                    k.stt(wabs[:], PSF(bw)[:, 0:8], IDX_W_SCALE, wabs[:], ALU.mult, ALU.max, ['ps%d' % bw, 'wabs'], ['wabs'])
                    k.ts(wsgn[:], PSF(bw)[:, 0:8], 0.0, 2.0, ALU.is_ge, ALU.mult, ['ps%d' % bw], ['wsgn'])
                    k.ts(wsgn[:], wsgn[:], -1.0, None, ALU.add, None, ['wsgn'], ['wsgn'])
                    wt, wk = next_w()
                    bi = nb()
                    projF(wt, wk, bi)
                    qv = qiTz[:].rearrange("p (s two) t -> p s two t", two=2)
                    pv = PSF(bi).rearrange("p (s t) -> p s t", s=4)
                    k.cp('act', qv[0:64, :, 0, :], pv[0:64, :, :], ['ps%d' % bi], ['qiTz'])
                    k.cp('act', qv[64:128, :, 1, :], pv[64:128, :, :], ['ps%d' % bi], ['qiTz'])
                    for m in range(2):
                        wt, wk = next_w()
                        bdq = nb()
                        projF(wt, wk, bdq)
                        k.cp('act', QTz[0:64, 2 * m, :], PSF(bdq)[0:64, :], ['ps%d' % bdq], ['QTz'])
                        k.cp('act', QTz[64:128, 2 * m + 1, :], PSF(bdq)[64:128, :], ['ps%d' % bdq], ['QTz'])
                    if not PR:
                        wt, wk = next_w()
                        bkv = nb()
                        projT(wt, wk, 512, PSF(bkv), 'ps%d' % bkv)
                        k.cp('act', kdv[:], PSF(bkv), ['ps%d' % bkv], ['kdv'])
                        k.dma('act', ks, kdv[:, 0:256], ['kdv'], [], 'so1')
                        k.dma('act', vs, kdv[:, 256:512], ['kdv'], [], 'so1')
                        k.cp('pool', kdb[:], kdv[:, 0:256], ['kdv'], ['kdb'])
                        k.cp('pool', vnb[:], kdv[:, 256:512], ['kdv'], ['vnb'])
                        wt, wk = next_w()
                        bik = nb()
                        projT(wt, wk, 64, PSF(bik)[:, 0:64], 'ps%d' % bik)
                        k.cp('dve', kin[:], PSF(bik)[:, 0:64], ['ps%d' % bik], ['kin'])
                        k.bn_stats(st2[:], kin[:], ['kin'], ['st2'])
                        k.bn_aggr(mv2[:], st2[:], ['st2'], ['mv2'])
                        k.act(sd2[:], mv2[:, 1:2], AF.Ln, ['mv2'], ['sd2'], bias=eps[:, 0:1], scale=1.0)
                        k.act(rstd2[:], sd2[:], AF.Exp, ['sd2'], ['rstd2'], scale=-0.5)
                        k.ts(nmr2[:], mv2[:, 0:1], rstd2[:, 0:1], -1.0, ALU.mult, ALU.mult, ['mv2', 'rstd2'], ['nmr2'])
                        k.act(kin[:], kin[:], AF.Identity, ['kin', 'nmr2', 'rstd2'], ['kin'], bias=nmr2[:, 0:1], scale=rstd2[:, 0:1])
                        k.tt(kin[:], kin[:], gkiB[:], ALU.mult, ['kin'], ['kin'])
                        k.tt(kif[:], kin[:], bkiB[:], ALU.add, ['kin'], ['kif'])
                        k.dma('act', iks, kif[:], ['kif'], [], 'so1')
                        k.cp('pool', kib[:, 0:64], kif[:], ['kif'], ['kib'])
                        k.cp('pool', kib[:, 64:128], kif[:], ['kif'], ['kib'])
                        bt_ = nb()
                        for c_ in range(2):
                            k.tr(PSB(bt_)[:, c_ * 128:(c_ + 1) * 128], kdb[:, c_ * 128:(c_ + 1) * 128], identb[:], ['kdb'], ['ps%d' % bt_])
                        k.tr(PSB(bt_)[:, 256:384], kib[:], identb[:], ['kib'], ['ps%d' % bt_])
                        k.cp('dve', kTnew[:].rearrange("p a b -> p (a b)"), PSB(bt_)[:, 0:256], ['ps%d' % bt_], ['kTnew'])
                        k.cp('dve', kiTnew[:], PSB(bt_)[:, 256:384], ['ps%d' % bt_], ['kiTnew'])
                        for b_ in range(4):
                            k.cp('pool', kiTs[b_][:, 2048:2064], kiTnew[:, 16 * b_:16 * b_ + 16], ['kiTnew'], ['kiTs%d' % b_])
                            for t8 in range(2):
                                k.dma('sp', cif[:], cache_ik[b_, t8 * 1024:(t8 + 1) * 1024, :].rearrange("(t p) c -> p t c", p=128), [], ['cif'], 'cif')
                                k.cp('pool', cib[:, :, 0:64], cif[:], ['cif'], ['cib'])
                                k.cp('pool', cib[:, :, 64:128], cif[:], ['cif'], ['cib'])
                                bt_ = nb()
                                for tt_ in range(8):
                                    k.tr(PSB(bt_)[:, tt_ * 128:(tt_ + 1) * 128], cib[:, tt_, :], identb[:], ['cib'], ['ps%d' % bt_])
                                k.cp('dve', kiTs[b_][:, t8 * 1024:(t8 + 1) * 1024], PSB(bt_), ['ps%d' % bt_], ['kiTs%d' % b_])
                    if PR:
                        n_tiles = min(4 * g + 5, NB)
                        tail0 = 4 * g
                        tidx = 1 if g == G - 1 else 0
                    else:
                        n_tiles = 17
                        tail0 = 16
                        tidx = 1
                    n_keys = n_tiles * 128
                    nch = (n_tiles + 3) // 4

                    def kiload(ci):
                        k0 = ci * 512
                        w_ = min(512, n_keys - k0)
                        s_ = ci % 2
                        k.dma('sp', kich[s_][:, :w_], ki_d[:, k0:k0 + w_], [], ['kich%d' % s_], 'kich%d' % s_)

                    if PR:
                        kiload(0)
                    if not PR:
                        for b_ in range(4):
                            k.ts(wabsb[:, b_, :], wabs[:], rmaskS[:, b_:b_ + 1], None, ALU.mult, None, ['wabs'], ['wabsb'])
                    rli = 0
                    for ci in range(nch):
                        k0 = ci * 512
                        w_ = min(512, n_keys - k0)
                        if PR and ci + 1 < nch:
                            kiload(ci + 1)
                        first = True
                        for hh in range(8):
                            for b_ in (range(1) if PR else range(4)):
                                bx = nb()
                                if PR:
                                    k.mm(PSF(bx)[:, :w_], qiTz[:, hh, :], kich[ci % 2][:, :w_], True, True, ['qiTz', 'kich%d' % (ci % 2)], ['ps%d' % bx])
                                    scl = wabs[:, hh:hh + 1]
                                    sck = 'wabs'
                                else:
                                    k.mm(PSF(bx)[:, :w_], qiTz[:, hh, :], kiTs[b_][:, k0:k0 + w_], True, True, ['qiTz', 'kiTs%d' % b_], ['ps%d' % bx])
                                    scl = wabsb[:, b_, hh:hh + 1]
                                    sck = 'wabsb'
                                rslots = [(rl[0][:, :], 'rl0'), (rl[1][:, :], 'rl1'), (mrg[:, 0:512], 'mrgA'), (mrg[:, 512:1024], 'mrgB'),
                                          (tmp[:, 0:512], 'tmpA'), (tmp[:, 512:1024], 'tmpB')]
                                r_, rk = rslots[rli % 6]
                                rli += 1
                                k.act(r_[:, :w_], PSF(bx)[:, :w_], AF.Relu, ['ps%d' % bx, sck], [rk], scale=scl)
                                if first:
                                    k.ts(sc[:, k0:k0 + w_], r_[:, :w_], wsgn[:, hh:hh + 1], None, ALU.mult, None, [rk, 'wsgn'], ['sc'])
                                    first = False
                                else:
                                    k.stt(sc[:, k0:k0 + w_], r_[:, :w_], wsgn[:, hh:hh + 1], sc[:, k0:k0 + w_], ALU.mult, ALU.add, [rk, 'wsgn', 'sc'], ['sc'])
                    k.capture()
                    k.red(rmax[:], sc[:, 0:n_keys], ALU.max, ['sc'], ['rmax'])
                    k.red(rmin[:], sc[:, 0:n_keys], ALU.min, ['sc'], ['rmin'])
                    tw = (n_tiles - tail0) * 128
                    k.tt(sc[:, tail0 * 128:tail0 * 128 + tw], sc[:, tail0 * 128:tail0 * 128 + tw], tbias[:, tidx, 0:tw], ALU.add, ['sc'], ['sc'])
                    k.tt(Wd[:], rmax[:], rmin[:], ALU.subtract, ['rmax', 'rmin'], ['Wd'])
                    k.ts(wtab[:], ctab[:], Wd[:, 0:1], None, ALU.mult, None, ['Wd'], ['wtab'])
                    k.tt(mids[:, 0:1], rmin[:], wtab[:, 0:1], ALU.add, ['rmin', 'wtab'], ['mid0'])
                    nD = (int(n_keys * 0.46) // 128) * 128
                    if nD < 256:
                        nD = n_keys
                    nA = n_keys - nD
                    for it in range(1, KIT + 1):
                        mid = mids[:, it - 1:it]
                        mk = 'mid%d' % (it - 1)
                        cn = cnts[:, it - 1:it]
                        ck_ = 'cnt%d' % it
                        k.ts(jd[:, 0:1].to_broadcast([128, nD]), sc[:, 0:nD], mid, None, ALU.is_ge, ALU.add, ['sc', mk], ['jd', ck_], accum=cn)
                        if nA > 0:
                            k.act(ja[:, 0:1].to_broadcast([128, nA]), sc[:, nD:n_keys], AF.Sign, ['sc', mk], ['ja', 'sa%d' % it],
                                  bias=mid, scale=-1.0, accum_out=sAs[:, it - 1:it])
                            k.stt(vvs[:, it - 1:it], cn, 2.0, sAs[:, it - 1:it], ALU.mult, ALU.subtract, [ck_, 'sa%d' % it], ['vv%d' % it])
                            vsrc, vkey, vthr = vvs[:, it - 1:it], 'vv%d' % it, 511.5 - nA
                        else:
                            vsrc, vkey, vthr = cn, ck_, 255.5
                        u_ = us[:, it - 1:it]
                        k.ts(u_, vsrc, vthr, wtab[:, it - 1:it], ALU.is_ge, ALU.mult, [vkey, 'wtab'], ['u%d' % it])
                        if it < KIT:
                            k.stt(mids[:, it:it + 1], u_, wtab[:, it:it + 1], mid, ALU.subtract, ALU.add, ['u%d' % it, 'wtab', mk], ['mid%d' % it])
                        else:
                            k.stt(thr[:], u_, wtab[:, it - 1:it], mid, ALU.subtract, ALU.add, ['u%d' % it, 'wtab', mk], ['thr'])
                    bisA = k.end_capture()
                    k.capture()
                    wt, wk = next_w()
                    b1 = nb()
                    for kc in range(8):
                        k.mm(PSF(b1)[0:16, 0:128], wt[:, kc, 0:16], hT[:, kc, :], kc == 0, kc == 7, ['hT', wk], ['ps%d' % b1])
                    k.cp('dve', gl[:], PSF(b1)[0:16, 0:128], ['ps%d' % b1], ['gl'])
                    bz = nb()
                    k.mm(PSF(bz), gl[:], w2[:], True, False, ['gl'], ['ps%d' % bz])
                    k.mm(PSF(bz), ones1[:], gbias[:], False, True, [], ['ps%d' % bz])
                    k.act(el[:], PSF(bz), AF.Exp, ['ps%d' % bz], ['el'], scale=-1.0)
                    k.act(el[:], el[:], AF.Ln, ['el'], ['el'], bias=one[:, 0:1], scale=1.0)
                    bb = nb()
                    for hh in range(4):
                        k.mm(PSF(bb)[:, hh * 128:(hh + 1) * 128], el[:, hh * 128:(hh + 1) * 128], tri[:], True, True, ['el'], ['ps%d' % bb])
                    k.act(eb[:], PSF(bb), AF.Exp, ['ps%d' % bb], ['eb'])
                    k.act(enb[:], PSF(bb), AF.Exp, ['ps%d' % bb], ['enb'], scale=-1.0)
                    wt, wk = next_w()
                    bq = nb()
                    projF(wt, wk, bq)
                    k.stt(qT[:].rearrange("p a b -> p (a b)"), PSF(bq), 128.0 ** -0.5, eb[:], ALU.mult, ALU.mult, ['ps%d' % bq, 'eb'], ['qT'])
                    wt, wk = next_w()
                    bk_ = nb()
                    projF(wt, wk, bk_)
                    k.tt(kTh[:].rearrange("p a b -> p (a b)"), PSF(bk_), enb[:], ALU.mult, ['ps%d' % bk_, 'enb'], ['kTh'])
                    if not PR:
                        bkt = nb()
                        projT(wt, wk, 512, PSF(bkt), 'ps%d' % bkt)
                        br_ = nb()
                        k.mm(PSF(br_), mrevS[:], el[:], True, True, ['el'], ['ps%d' % br_])
                        k.act(sg[1][:], PSF(br_), AF.Exp, ['ps%d' % br_], ['sg1'])
                        k.tt(Kt[:], PSF(bkt), sg[1][:], ALU.mult, ['ps%d' % bkt, 'sg1'], ['Kt'])
                        bd_ = nb()
                        for hh in range(4):
                            k.mm(PSF(bd_)[:, hh * 4:(hh + 1) * 4], el[:, hh * 128:(hh + 1) * 128], bsumS[:], True, True, ['el'], ['ps%d' % bd_])
                        k.act(decs[:], PSF(bd_)[:, 0:16], AF.Exp, ['ps%d' % bd_], ['decs'])
                    for half in range(2):
                        wt, wk = next_w()
                        bv = nb()
                        projT(wt, wk, 512, PSF(bv), 'ps%d' % bv)
                        k.cp('act', V[:, half * 512:(half + 1) * 512], PSF(bv), ['ps%d' % bv], ['V'])
                    if PR:
                        for m in range(4):
                            sidx = min(4 * g + m, NB - 1)
                            s_ = m % 2
                            k.dma('act', Sc[s_][:], snap[sidx], [], ['Sc%d' % s_], 'Sc%d' % s_)
                            if m == 0:
                                k.ts(Sown[:], Sc[s_][:], onehot[:, 0:1], None, ALU.mult, None, ['Sc%d' % s_], ['Sown'])
                            else:
                                k.stt(Sown[:], Sc[s_][:], onehot[:, m:m + 1], Sown[:], ALU.mult, ALU.add, ['Sc%d' % s_, 'Sown'], ['Sown'])
                        k.cp('pool', Sb[:].rearrange("p a b -> p (a b)"), Sown[:], ['Sown'], ['Sb'])
                    else:
                        for b_ in range(4):
                            for hh in range(4):
                                k.tt(qTb[b_][:, hh, :], qT[:, hh, :], cmaskS[:, b_, :], ALU.mult, ['qT'], ['qTb%d' % b_])
                    ba = nb()
                    for hh in range(4):
                        k.mm(PSF(ba)[:, hh * 128:(hh + 1) * 128], kTh[:, hh, :], qT[:, hh, :], True, True, ['kTh', 'qT'], ['ps%d' % ba])
                    k.tt(AT[:].rearrange("p a b -> p (a b)"), PSF(ba), maskf[:], ALU.mult, ['ps%d' % ba], ['AT'])
                    bo = [nb(), nb()]
                    for hh in range(4):
                        oap = PSF(bo[hh // 2])[:, (hh % 2) * 256:(hh % 2 + 1) * 256]
                        okey = 'ps%d' % bo[hh // 2]
                        if PR:
                            k.mm(oap, qT[:, hh, :], Sb[:, hh, :], True, False, ['qT', 'Sb'], [okey])
                        else:
                            for b_ in range(4):
                                k.mm(oap, qTb[b_][:, hh, :], S0b[b_][:, hh, :], b_ == 0, False, ['qTb%d' % b_], [okey])
                        k.mm(oap, AT[:, hh, :], V[:, hh * 256:(hh + 1) * 256], False, True, ['AT', 'V'], [okey])
                    for hh in range(4):
                        oap = PSF(bo[hh // 2])[:, (hh % 2) * 256:(hh % 2 + 1) * 256]
                        k.act(jq[:, 0:1].to_broadcast([128, 256]), oap, AF.Square, ['ps%d' % bo[hh // 2]], ['jq', 'ss%d' % hh], accum_out=ss[:, hh:hh + 1])
                    k.act(rs[:], ss[:], AF.Ln, ['ss0', 'ss1', 'ss2', 'ss3'], ['rs'], bias=eps[:, 0:1], scale=1.0 / 256)
                    k.act(rs[:], rs[:], AF.Exp, ['rs'], ['rs'], scale=-0.5)
                    for cc in range(2):
                        wt, wk = next_w()
                        bg = nb()
                        projT(wt, wk, 512, PSF(bg), 'ps%d' % bg)
                        k.act(sg[cc][:], PSF(bg), AF.Silu, ['ps%d' % bg], ['sg%d' % cc])
                        for hh in range(2):
                            hd = 2 * cc + hh
                            oap = PSF(bo[hd // 2])[:, (hd % 2) * 256:(hd % 2 + 1) * 256]
                            k.stt(tmp[:, hd * 256:(hd + 1) * 256], oap, rs[:, hd:hd + 1], gnB[:], ALU.mult, ALU.mult,
                                  ['ps%d' % bo[hd // 2], 'rs'], ['tmp'])
                        k.tt(yain[:, cc * 512:(cc + 1) * 512], tmp[:, cc * 512:(cc + 1) * 512], sg[cc][:], ALU.mult, ['tmp', 'sg%d' % cc], ['yain'])
                    transpose8(yain, 'yain', yT, 'yT')
                    for cc in range(2):
                        wt, wk = next_w()
                        by = nb()
                        for kc in range(8):
                            k.mm(PSF(by), yT[:, kc, :], wt[:, kc, :], kc == 0, kc == 7, ['yT', wk], ['ps%d' % by])
                        wt, wk = next_w()
                        bm = nb()
                        projT(wt, wk, 512, PSF(bm), 'ps%d' % bm)
                        k.dma('act', gtb[cc][:], gate_b[0:1, cc * 512:(cc + 1) * 512].partition_broadcast(128), [], ['gtb%d' % cc], 'gtb%d' % cc)
                        k.tt(sg[cc][:], PSF(bm), gtb[cc][:], ALU.add, ['ps%d' % bm, 'gtb%d' % cc], ['sg%d' % cc])
                        k.act(sg[cc][:], sg[cc][:], AF.Sigmoid, ['sg%d' % cc], ['sg%d' % cc])
                        k.tt(mrg[:, cc * 512:(cc + 1) * 512], PSF(by), sg[cc][:], ALU.mult, ['ps%d' % by, 'sg%d' % cc], ['mrg'])
                    if not PR:
                        for b_ in range(4):
                            s_ = b_ % 2
                            k.dma('sp', S0f[s_][:], state[b_].rearrange("h p e -> p h e"), [], ['S0f%d' % s_], 'S0f%d' % s_)
                            k.ts(Ktb[s_][:], Kt[:], rmaskS[:, b_:b_ + 1], None, ALU.mult, None, ['Kt'], ['Ktb%d' % s_])
                            for hp in range(2):
                                bs_ = nb()
                                for hh in range(2):
                                    hd = hp * 2 + hh
                                    k.mm(PSF(bs_)[:, hh * 256:(hh + 1) * 256], Ktb[s_][:, hd * 128:(hd + 1) * 128], V[:, hd * 256:(hd + 1) * 256],
                                         True, True, ['Ktb%d' % s_, 'V'], ['ps%d' % bs_])
                                for hh in range(2):
                                    hd = hp * 2 + hh
                                    k.stt(S0f[s_][:, hd, :], S0f[s_][:, hd, :], decs[:, hd * 4 + b_:hd * 4 + b_ + 1], PSF(bs_)[:, hh * 256:(hh + 1) * 256],
                                          ALU.mult, ALU.add, ['decs', 'ps%d' % bs_, 'S0f%d' % s_], ['S0f%d' % s_])
                            k.dma('act', gla_s[b_], S0f[s_][:].rearrange("p a b -> p (a b)"), ['S0f%d' % s_], [], 'Sno%d' % s_)

                    glaB = k.end_capture()
                    k.merge(bisA, glaB)

                    if PR:
                        def kvload(ci):
                            k0 = ci * 512
                            nt = min(4, n_tiles - ci * 4)
                            s_ = ci % 2
                            k.dma('sp', kTch[s_][:, :, :nt * 128], kT_d[:, :, k0:k0 + nt * 128], [], ['kTch%d' % s_], 'kTch%d' % s_)
                            k.dma('sp', vch[s_][:, :nt, :], v_d[:, ci * 4:ci * 4 + nt, :], [], ['vch%d' % s_], 'vch%d' % s_)
                        kvload(0)
                        li = 0
                        groups = []
                        for kt in range(n_tiles):
                            ci, tl_ = kt // 4, kt % 4
                            mb_ = mbt[kt % 3]
                            mbk = 'mbt%d' % (kt % 3)
                            for gg in range(4):
                                bl = 4 + (li % 4)
                                p_ = pT[li % 3]
                                pk = 'pT%d' % (li % 3)
                                li += 1
                                k.capture()
                                if gg == 0:
                                    if tl_ == 1 and ci + 1 < nch:
                                        kvload(ci + 1)
                                    k.ts(mb_[:], sc[:, kt * 128:(kt + 1) * 128], thr[:, 0:1], NEG, ALU.is_lt, ALU.mult, ['sc', 'thr'], [mbk])
                                k.mm(PSF(bl), kTch[ci % 2][:, gg // 2, tl_ * 128:(tl_ + 1) * 128], QTz[:, gg, :], True, False,
                                     ['kTch%d' % (ci % 2), 'QTz'], ['ps%d' % bl])
                                k.mm(PSF(bl), mb_[:], idrepb[:], False, True, [mbk], ['ps%d' % bl])
                                k.act(p_[:], PSF(bl), AF.Exp, ['ps%d' % bl], [pk], scale=0.125)
                                s1 = k.end_capture()
                                k.capture()
                                k.mm(PSF(gg)[0:65, :], vch[ci % 2][:, tl_, gg * 65:(gg + 1) * 65], p_[:], kt == 0, kt == n_tiles - 1,
                                     ['vch%d' % (ci % 2), pk], ['ps%d' % gg])
                                s2 = k.end_capture()
                                groups.append((s1, s2))
                        SK = 2
                        for idx in range(len(groups) + SK):
                            if idx < len(groups):
                                P.ops.extend(groups[idx][0])
                            if idx >= SK:
                                P.ops.extend(groups[idx - SK][1])
                        for gg in range(4):
                            o_ = oT[gg % 2]
                            ok_ = 'oT%d' % (gg % 2)
                            k.cp('act', o_[:], PSF(gg)[0:65, :], ['ps%d' % gg], [ok_])
                            for r_ in range(4):
                                k.tr(PSF(4 + gg)[:, r_ * 65:(r_ + 1) * 65], o_[0:65, r_ * 128:(r_ + 1) * 128], identf[0:65, 0:65], [ok_], ['ps%d' % (4 + gg)])
                        NPT = 128
                    else:
                        mball = junk
                        for b_ in range(4):
                            k.cp('pool', QTs[:, b_, :, :].rearrange("p g (r t) -> p g r t", r=4),
                                 QTz[:].rearrange("p g (r t) -> p g r t", r=4)[:, :, :, 16 * b_:16 * b_ + 16], ['QTz'], ['QTs'])
                        k.ts(mball[:], sc[:, 0:2176], thr[:, 0:1], NEG, ALU.is_lt, ALU.mult, ['sc', 'thr'], ['junk'])
                        li = 0
                        for b_ in range(4):
                            k.cp('pool', kTs1[:, :, 2048:2064], kTnew[:, :, 16 * b_:16 * b_ + 16], ['kTnew'], ['kTs'])
                            bs_ = nb() % 4 + 4
                            k.mm(PSF(bs_)[:, 0:256], selSb[:, b_, :], vnb[:], True, True, ['vnb'], ['ps%d' % bs_])
                            k.cp('act', vexts1[:, 16, :].rearrange("p (g d) -> p g d", g=4)[:, :, 0:64],
                                 PSF(bs_)[:, 0:256].rearrange("p (g d) -> p g d", g=4), ['ps%d' % bs_], ['vexts'])
                            for t8 in range(2):
                                k.dma('sp', ckf[:], cache_k[b_, t8 * 1024:(t8 + 1) * 1024, :].rearrange("(t p) c -> p t c", p=128), [], ['ckf'], 'ckf')
                                k.cp('pool', ckb[:], ckf[:], ['ckf'], ['ckb'])
                                for t4 in range(2):
                                    bt_ = nb() % 4 + 4
                                    for tt_ in range(4):
                                        for c_ in range(2):
                                            col = (tt_ * 2 + c_) * 128
                                            k.tr(PSB(bt_)[:, col:col + 128], ckb[:, t4 * 4 + tt_, c_ * 128:(c_ + 1) * 128], identb[:], ['ckb'], ['ps%d' % bt_])
                                    c0_ = t8 * 1024 + t4 * 512
                                    k.cp('dve', kTs1[:, :, c0_:c0_ + 512].rearrange("p c (t s) -> p c t s", t=4),
                                         PSB(bt_).rearrange("p (t c s) -> p c t s", t=4, c=2), ['ps%d' % bt_], ['kTs'])
                                k.dma('sp', ckf[:], cache_v[b_, t8 * 1024:(t8 + 1) * 1024, :].rearrange("(t p) c -> p t c", p=128), [], ['ckf'], 'ckf')
                                k.cp('pool', vexts1[:, t8 * 8:(t8 + 1) * 8, :].rearrange("p t (g d) -> p t g d", g=4)[:, :, :, 0:64],
                                     ckf[:].rearrange("p t (g d) -> p t g d", g=4), ['ckf'], ['vexts'])
                            sgroups = []
                            for gg in range(4):
                                ob = b_ // 2
                                ocol = ((b_ % 2) * 4 + gg) * 64
                                qsel = QTs[:, b_, gg, :]
                                for kt in range(17):
                                    bl = 4 + (li % 4)
                                    p_ = pTs[li % 3]
                                    pk = 'pTs%d' % (li % 3)
                                    li += 1
                                    k.capture()
                                    k.mm(PSF(bl)[:, 0:64], kTs1[:, gg // 2, kt * 128:(kt + 1) * 128], qsel, True, False,
                                         ['kTs', 'QTs'], ['ps%d' % bl])
                                    k.mm(PSF(bl)[:, 0:64], mball[:, kt * 128:(kt + 1) * 128], idrepSb[:, b_, :], False, True, ['junk'], ['ps%d' % bl])
                                    k.act(p_[:], PSF(bl)[:, 0:64], AF.Exp, ['ps%d' % bl], [pk], scale=0.125)
                                    s1_ = k.end_capture()
                                    k.capture()
                                    k.mm(PSF(ob)[0:65, ocol:ocol + 64], vexts1[:, kt, gg * 65:(gg + 1) * 65], p_[:], kt == 0, kt == 16,
                                         ['vexts', pk], ['ps%d' % ob])
                                    s2_ = k.end_capture()
                                    sgroups.append((s1_, s2_))
                            SKS = 2
                            for idx in range(len(sgroups) + SKS):
                                if idx < len(sgroups):
                                    P.ops.extend(sgroups[idx][0])
                                if idx >= SKS:
                                    P.ops.extend(sgroups[idx - SKS][1])
                        oTs = sc[0:65, 0:1024]
                        ov = oTs.rearrange("p (g r b t) -> p b g r t", g=4, r=4, b=4)
                        for ob in range(2):
                            k.cp('act', ov[:, 2 * ob:2 * ob + 2], PSF(ob)[0:65, :].rearrange("p (b g r t) -> p b g r t", b=2, g=4, r=4),
                                 ['ps%d' % ob, 'junk'], ['sc'])
                        for gg in range(4):
                            for r_ in range(4):
                                c0_ = (gg * 4 + r_) * 64
                                k.tr(PSF(4 + gg)[0:64, r_ * 65:(r_ + 1) * 65], oTs[:, c0_:c0_ + 64], identf[0:65, 0:65], ['sc'], ['ps%d' % (4 + gg)])
                        NPT = 64
                    for gg in range(4):
                        pv4 = PSF(4 + gg)[0:NPT, 0:260].rearrange("p (r c) -> p r c", c=65)
                        k.recip(rd[0:NPT, 4 * gg:4 * gg + 4], pv4[:, :, 64], ['ps%d' % (4 + gg)], ['rd'])
                        for r_ in range(4):
                            hd = 4 * gg + r_
                            k.ts(tmp[0:NPT, hd * 64:(hd + 1) * 64], pv4[:, r_, 0:64], rd[0:NPT, hd:hd + 1], None, ALU.mult, None,
                                 ['ps%d' % (4 + gg), 'rd'], ['tmp'])
                    for cc in range(2):
                        wt, wk = next_w()
                        bg = nb()
                        projT(wt, wk, 512, PSF(bg), 'ps%d' % bg)
                        k.act(sg[cc][:], PSF(bg), AF.Silu, ['ps%d' % bg], ['sg%d' % cc])
                        k.tt(yain[0:NPT, cc * 512:(cc + 1) * 512], tmp[0:NPT, cc * 512:(cc + 1) * 512], sg[cc][0:NPT, :], ALU.mult,
                             ['tmp', 'sg%d' % cc], ['yain'])
                    transpose8(yain, 'yain', yT, 'yT')
                    for cc in range(2):
                        wt, wk = next_w()
                        by = nb()
                        for kc in range(8):
                            k.mm(PSF(by), yT[:, kc, :], wt[:, kc, :], kc == 0, kc == 7, ['yT', wk], ['ps%d' % by])
                        wt, wk = next_w()
                        bm = nb()
                        projT(wt, wk, 512, PSF(bm), 'ps%d' % bm)
                        k.dma('act', gtb[cc][:], gate_b[0:1, 1024 + cc * 512:1024 + (cc + 1) * 512].partition_broadcast(128), [], ['gtb%d' % cc], 'gtb%d' % cc)
                        k.tt(sg[cc][:], PSF(bm), gtb[cc][:], ALU.add, ['ps%d' % bm, 'gtb%d' % cc], ['sg%d' % cc])
                        k.act(sg[cc][:], sg[cc][:], AF.Sigmoid, ['sg%d' % cc], ['sg%d' % cc])
                        k.tt(sg[cc][:], PSF(by), sg[cc][:], ALU.mult, ['ps%d' % by, 'sg%d' % cc], ['sg%d' % cc])
                        k.tt(mrg[:, cc * 512:(cc + 1) * 512], mrg[:, cc * 512:(cc + 1) * 512], sg[cc][:], ALU.add, ['mrg', 'sg%d' % cc], ['mrg'])
                    k.cp('pool', yain[:], mrg[:], ['mrg'], ['yain'])
                    transpose8(yain, 'yain', yT, 'yT')
                    for cc in range(2):
                        wt, wk = next_w()
                        by = nb()
                        for kc in range(8):
                            k.mm(PSF(by), yT[:, kc, :], wt[:, kc, :], kc == 0, kc == 7, ['yT', wk], ['ps%d' % by])
                        k.stt(mrg[:, cc * 512:(cc + 1) * 512], h[:, cc * 512:(cc + 1) * 512], ALPHA, PSF(by), ALU.mult, ALU.add, ['h', 'ps%d' % by], ['mrg'])
                    layernorm(mrg[:], 'mrg', g2B[:], b2B[:], tl, 'c', out_f32=yout[:], out_f32_key='xo')
                    ydst = y_own[g * 128:(g + 1) * 128, :] if PR else ys
                    k.dma('act', ydst, yout[:], ['xo'], [], 'youto')

                for g in (range(G) if PR else [0]):
                    own_block(g)
            P.barrier()

        if "B" in phases:
            phase_own('P')
        if "S" in phases:
            phase_own('S')

        P.emit(nc, top)
    return nc


def _consts(j, G):
    c = {}
    p = np.arange(128)
    c["c_ident"] = np.eye(128, dtype=np.float32)
    jj, ii = np.meshgrid(p, p, indexing="ij")
    c["c_triA"] = np.where((jj > ii) & (jj // 64 == ii // 64), -1.0 / 16, 0.0).astype(np.float32)
    c["c_blkA"] = np.where(p[:, None] // 64 == np.arange(2)[None, :], -1.0 / 16, 0.0).astype(np.float32)
    c["c_triB"] = np.where(jj <= ii, -1.0 / 16, 0.0).astype(np.float32)
    mB = (jj <= ii).astype(np.float32)
    c["c_maskB"] = np.ascontiguousarray(np.broadcast_to(mB[:, None, :], (128, 4, 128)))
    same = (jj // 16 == ii // 16)
    c["c_triS"] = np.where((jj <= ii) & same, -1.0 / 16, 0.0).astype(np.float32)
    mS = ((jj <= ii) & same).astype(np.float32)
    c["c_maskS"] = np.ascontiguousarray(np.broadcast_to(mS[:, None, :], (128, 4, 128)))
    c["c_mrevS"] = np.where((jj > ii) & same, -1.0 / 16, 0.0).astype(np.float32)
    c["c_bsumS"] = np.where(p[:, None] // 16 == np.arange(4)[None, :], -1.0 / 16, 0.0).astype(np.float32)
    cm = (p[None, :] // 16 == np.arange(4)[:, None]).astype(np.float32)
    c["c_cmaskS"] = np.ascontiguousarray(np.broadcast_to(cm[None], (128, 4, 128)))
    c["c_rmaskS"] = (p[:, None] // 16 == np.arange(4)[None, :]).astype(np.float32)
    c["c_idrep"] = np.ascontiguousarray(np.tile(np.eye(128, dtype=np.float32), (1, 4)))
    ids = np.zeros((128, 4, 64), np.float32)
    sel = np.zeros((128, 4, 128), np.float32)
    for b in range(4):
        for t in range(16):
            for r in range(4):
                ids[16 * b + t, b, r * 16 + t] = 1.0
            sel[16 * b + t, b, t] = 1.0
    c["c_idrepS"] = ids
    c["c_selS"] = sel
    c["c_ctab"] = np.ascontiguousarray(np.broadcast_to((0.5 ** np.arange(1, KIT + 2))[None, :], (128, KIT + 1))).astype(np.float32)
    tb = np.zeros((128, 2, 640), np.float32)
    kk = np.arange(640)
    for r in range(128):
        lim = 128 * j + (16 if r < 16 else (80 if r < 80 else 144))
        tb[r, 0, :] = np.where(kk < lim, 0.0, -1e30)
    tb[:, 1, :] = np.where(kk < 16, 0.0, -1e30)[None, :]
    c["c_tbias"] = tb
    oh = np.zeros((128, 4), np.float32)
    oh[:, j] = 1.0
    c["c_onehot"] = oh
    c["c_tailmask"] = (p < 16).astype(np.float32)[:, None]
    return c


def _dq_perm():
    perm = np.zeros(1024, np.int64)
    n = 0
    for m in range(2):
        for r in range(4):
            for half in range(2):
                g = 2 * m + half
                for d in range(64):
                    perm[n] = (g * 4 + r) * 64 + d
                    n += 1
    return perm


def prep(inp, SEQ):
    T, NB, G = geometry(SEQ)
    f = lambda a: np.ascontiguousarray(np.asarray(a, dtype=np.float32))
    w_in = f(inp["w_in"])[0].copy()
    w_in[:, C_DQ:C_DQ + 1024] = w_in[:, C_DQ:C_DQ + 1024][:, _dq_perm()]
    w3 = np.ascontiguousarray(np.stack([f(inp["w_gla"])[0], f(inp["w_dsa"])[0], f(inp["w_out"])[0]], 0))
    shared = dict(
        w_in=np.ascontiguousarray(w_in), w3=w3,
        ln_in_g=f(inp["ln_in_g"]).reshape(1, D), ln_in_b=f(inp["ln_in_b"]).reshape(1, D),
        ln_g=f(inp["ln_g"]).reshape(1, D), ln_b=f(inp["ln_b"]).reshape(1, D),
        gate_b=f(inp["gate_b"]).reshape(1, 2048), gla_gate_b=f(inp["gla_gate_b"]).reshape(1, 512),
        gla_norm_g=f(inp["gla_norm_g"]).reshape(1, 256),
        idx_kn_g=f(inp["idx_kn_g"]).reshape(1, 64), idx_kn_b=f(inp["idx_kn_b"]).reshape(1, 64),
        gla_w2=f(inp["gla_w2"]).reshape(16, 512))
    xp = f(inp["x_prompt"]); meta = f(inp["meta"]); xsm = f(inp["x_sample"])
    ck = f(inp["cache_k"])[0]; cv = f(inp["cache_v"])[0]; cik = f(inp["cache_idx_k"])[0]; stt = f(inp["state_gla"])[0]
    maps = []
    for c in range(8):
        b, j = c // 4, c % 4
        xall = np.zeros((4 * G * 128, D), np.float32)
        xall[:16] = meta
        xall[16:T] = xp[b]
        xown = np.ascontiguousarray(xall.reshape(G, 4, 128, D)[:, j].reshape(G * 128, D))
        xs_ = np.zeros((128, D), np.float32)
        xs_[:64] = xsm[4 * c:4 * c + 4].reshape(64, D)
        m = dict(shared)
        m.update(_consts(j, G))
        m.update(xall=np.ascontiguousarray(xall[:NB * 128]), xown=xown, xs=xs_,
                 cache_k=np.ascontiguousarray(ck[4 * c:4 * c + 4].reshape(4, 2048, 256)),
                 cache_v=np.ascontiguousarray(cv[4 * c:4 * c + 4].reshape(4, 2048, 256)),
                 cache_ik=np.ascontiguousarray(cik[4 * c:4 * c + 4]),
                 state=np.ascontiguousarray(stt[4 * c:4 * c + 4]))
        maps.append(m)
    return maps


def gather(res, SEQ):
    T, NB, G = geometry(SEQ)
    R = res.results
    yp = np.zeros((2, 4 * G * 128, D), np.float32)
    for c in range(8):
        b, j = c // 4, c % 4
        yp[b].reshape(G, 4, 128, D)[:, j] = R[c]["y_own"].reshape(G, 128, D)
    y_prompt = np.ascontiguousarray(yp[:, 16:T])
    y_sample = np.concatenate([R[c]["ys"][:64].reshape(4, 16, D) for c in range(8)], 0)
    k_prompt = np.stack([R[4 * b]["kp"][:T].reshape(T, 4, 64) for b in range(2)], 0)[None]
    v_prompt = np.stack([R[4 * b]["vp"][:T].reshape(T, 4, 64) for b in range(2)], 0)[None]
    ik_prompt = np.stack([R[4 * b]["ikp"][:T] for b in range(2)], 0)[None]
    gla_prompt = np.stack([R[4 * b]["gla_p"].reshape(128, 4, 256).transpose(1, 0, 2) for b in range(2)], 0)[None]
    k_sample = np.concatenate([R[c]["ks"][:64].reshape(4, 16, 4, 64) for c in range(8)], 0)[None]
    v_sample = np.concatenate([R[c]["vs"][:64].reshape(4, 16, 4, 64) for c in range(8)], 0)[None]
    ik_sample = np.concatenate([R[c]["iks"][:64].reshape(4, 16, 64) for c in range(8)], 0)[None]
    gla_sample = np.concatenate([R[c]["gla_s"].reshape(4, 128, 4, 256).transpose(0, 2, 1, 3) for c in range(8)], 0)[None]
    outs = (y_prompt, y_sample, k_prompt, v_prompt, ik_prompt, gla_prompt, k_sample, v_sample, ik_sample, gla_sample)
    return tuple(np.ascontiguousarray(o, dtype=np.float32) for o in outs)


_NC_CACHE = {}


def run(inputs, SEQ, phases="0ABS"):
    key = (SEQ, phases)
    if key not in _NC_CACHE:
        _NC_CACHE[key] = build(SEQ, phases)
    nc = _NC_CACHE[key]
    maps = prep(inputs, SEQ)
    res = run_bass_kernel_spmd(nc, maps, core_ids=list(range(8)))
    return gather(res, SEQ)


def kernel(**inputs):
    SEQ = int(np.asarray(inputs["x_prompt"]).shape[1])
    return run(inputs, SEQ)
```

```python
from contextlib import ExitStack
import numpy as np
import concourse.bass as bass
import concourse.mybir as mybir
from concourse.bass_utils import run_bass_kernel_spmd

F32 = mybir.dt.float32
BF16 = mybir.dt.bfloat16
AF = mybir.ActivationFunctionType
ALU = mybir.AluOpType
AX = mybir.AxisListType

D = 1024
NEG = -30000.0
KIT = 22
IDX_W_SCALE = (8 ** -0.5) * (64 ** -0.5)
ALPHA = 2.0 ** 0.25
EPS = 1e-5
C_GQ, C_GK, C_GV, C_GLOW, C_GR, C_DQ, C_DK, C_DV, C_IQ, C_IK, C_IW, C_DZ, C_MA, C_MB = (
    0, 512, 1024, 2048, 2064, 3088, 4112, 4368, 4624, 5136, 5200, 5208, 6232, 7256)
IN_COLS = 8280


class Prog:
    def __init__(self):
        self.ops = []

    def add(self, eng, fn, r=(), w=(), dsem=None):
        r = list(r)
        w = list(w)
        for b in list(r):
            if isinstance(b, str) and b.startswith('ps'):
                r.remove(b)
                if b not in w:
                    w.append(b)
        self.ops.append(dict(eng=eng, fn=fn, r=tuple(r), w=tuple(w), dsem=dsem))

    def barrier(self):
        self.ops.append(dict(eng='barrier', fn=None, r=(), w=(), dsem=None))

    def analyze(self):
        ops = self.ops
        last_w, readers = {}, {}
        last_of = {}
        for i, op in enumerate(ops):
            if op['eng'] == 'barrier':
                for e_, j in last_of.items():
                    ops[j]['needed'] = True
                last_w, readers = {}, {}
                op['deps'] = set()
                continue
            deps = set()
            for b in op['r']:
                if b in last_w:
                    deps.add(('raw', last_w[b]))
            for b in op['w']:
                if b in last_w:
                    deps.add(('waw', last_w[b]))
                for rr in readers.get(b, ()):
                    deps.add(('war', rr))
            for b in op['r']:
                readers.setdefault(b, []).append(i)
            for b in op['w']:
                last_w[b] = i
                readers[b] = []
            keep = set()
            for kind, j in deps:
                if j == i:
                    continue
                pj = ops[j]
                if pj['dsem'] is None and op['dsem'] is None and pj['eng'] == op['eng']:
                    if op['eng'] == 'pe' or kind == 'war':
                        continue
                keep.add(j)
            op['deps'] = keep
            for j in keep:
                ops[j]['needed'] = True
            if op['dsem'] is None:
                last_of[op['eng']] = i
        for e_, j in last_of.items():
            ops[j]['needed'] = True
        cnt = {}
        for op in ops:
            if op['eng'] == 'barrier':
                continue
            if op['dsem'] is not None:
                k = 'D:' + op['dsem']
                cnt[k] = cnt.get(k, 0) + 16
                op['sem'] = k
                op['val'] = cnt[k]
            elif op.get('needed'):
                k = 'E:' + op['eng']
                cnt[k] = cnt.get(k, 0) + 1
                op['sem'] = k
                op['val'] = cnt[k]
        waited = {}
        running = {}
        pending = {}
        for op in ops:
            if op['eng'] == 'barrier':
                for e_ in ('pe', 'act', 'dve', 'pool', 'sp'):
                    pending[e_] = dict(running)
                continue
            ws = {}
            if pending.get(op['eng']):
                ws.update(pending[op['eng']])
                pending[op['eng']] = None
            for j in op['deps']:
                pj = ops[j]
                ws[pj['sem']] = max(ws.get(pj['sem'], 0), pj['val'])
            wl = []
            we = waited.setdefault(op['eng'], {})
            for k, v in ws.items():
                if we.get(k, 0) >= v:
                    continue
                we[k] = v
                wl.append((k, v))
            op['waits'] = wl
            if op.get('sem') is not None:
                running[op['sem']] = op['val']
        self.totals = cnt
        return cnt

    def emit(self, nc, es):
        cnt = self.analyze()
        sems = {}
        for k in cnt:
            sems[k] = es.enter_context(nc.semaphore(k.replace(':', '_')))
        block = es.enter_context(nc.Block())
        ops = self.ops

        def run(engname):
            def f(e):
                for op in ops:
                    if op['eng'] != engname:
                        continue
                    for k, v in op['waits']:
                        e.wait_ge(sems[k], v)
                    ins = op['fn'](e)
                    if op.get('sem') is not None:
                        ins.then_inc(sems[op['sem']], 16 if op['dsem'] is not None else 1)
                for k, v in cnt.items():
                    e.wait_ge(sems[k], v)
            return f

        block.sync(run('sp'))
        block.scalar(run('act'))
        block.vector(run('dve'))
        block.gpsimd(run('pool'))
        block.tensor(run('pe'))


class KB:
    def __init__(self, nc):
        self.nc = nc
        self.P = Prog()
        self.rot = 0

    def capture(self):
        self._saved = self.P.ops
        self.P.ops = []

    def end_capture(self):
        l = self.P.ops
        self.P.ops = self._saved
        return l

    def merge(self, A, B):
        out = []
        ia = ib = 0
        na, nb_ = max(len(A), 1), max(len(B), 1)
        while ia < len(A) or ib < len(B):
            if ib >= len(B) or (ia < len(A) and ia * nb_ <= ib * na):
                out.append(A[ia]); ia += 1
            else:
                out.append(B[ib]); ib += 1
        self.P.ops.extend(out)

    def act(self, out, in_, func, r, w, **kw):
        self.P.add('act', lambda e: e.activation(out=out, in_=in_, func=func, **kw), r, w)

    def ts(self, out, in0, s1, s2, op0, op1, r, w, eng='dve', accum=None):
        if accum is None:
            if op1 is None:
                self.P.add(eng, lambda e: e.tensor_scalar(out=out, in0=in0, scalar1=s1, scalar2=None, op0=op0), r, w)
            else:
                self.P.add(eng, lambda e: e.tensor_scalar(out=out, in0=in0, scalar1=s1, scalar2=s2, op0=op0, op1=op1), r, w)
        else:
            self.P.add(eng, lambda e: e.tensor_scalar(out=out, in0=in0, scalar1=s1, scalar2=s2, op0=op0, op1=op1,
                                                      accum_out=accum), r, w)

    def tt(self, out, in0, in1, op, r, w, eng='dve'):
        self.P.add(eng, lambda e: e.tensor_tensor(out=out, in0=in0, in1=in1, op=op), r, w)

    def stt(self, out, in0, scalar, in1, op0, op1, r, w):
        self.P.add('dve', lambda e: e.scalar_tensor_tensor(out=out, in0=in0, scalar=scalar, in1=in1, op0=op0, op1=op1), r, w)

    def cp(self, eng, out, in_, r, w):
        if eng == 'act':
            self.P.add('act', lambda e: e.copy(out=out, in_=in_), r, w)
        else:
            self.P.add(eng, lambda e: e.tensor_copy(out=out, in_=in_), r, w)

    def memset(self, eng, ap, val, w):
        self.P.add(eng, lambda e: e.memset(ap, val), (), w)

    def mm(self, out, lhsT, rhs, start, stop, r, w):
        self.P.add('pe', lambda e: e.matmul(out, lhsT=lhsT, rhs=rhs, start=start, stop=stop), r, w)

    def tr(self, out, in_, ident, r, w):
        self.P.add('pe', lambda e: e.transpose(out=out, in_=in_, identity=ident), r, w)

    def dma(self, q, out, in_, r, w, dsem):
        self.P.add(q, lambda e: e.dma_start(out=out, in_=in_), r, w, dsem=dsem)

    def red(self, out, in_, op, r, w):
        self.P.add('dve', lambda e: e.tensor_reduce(out=out, in_=in_, axis=AX.X, op=op), r, w)

    def recip(self, out, in_, r, w):
        self.P.add('dve', lambda e: e.reciprocal(out=out, in_=in_), r, w)

    def bn_stats(self, out, in_, r, w):
        self.P.add('dve', lambda e: e.bn_stats(out=out, in_=in_), r, w)

    def bn_aggr(self, out, in_, r, w):
        self.P.add('dve', lambda e: e.bn_aggr(out=out, in_=in_), r, w)


def geometry(SEQ):
    T = SEQ + 16
    NB = T // 128 + 1
    assert T == 128 * (NB - 1) + 16 and NB % 4 == 1
    G = (NB + 3) // 4
    return T, NB, G


def build(SEQ, phases="0ABS"):
    T, NB, G = geometry(SEQ)
    NKMAX = NB * 128
    nc = bass.Bass("TRN2", target_bir_lowering=False)

    def din(name, shape, dt=F32):
        return nc.dram_tensor(name, list(shape), dt, kind="ExternalInput").ap()

    def dout(name, shape, dt=F32):
        return nc.dram_tensor(name, list(shape), dt, kind="ExternalOutput").ap()

    def dscr(name, shape, dt):
        return nc.dram_tensor(name, list(shape), dt, kind="Internal").ap()

    xall = din("xall", [NB * 128, D])
    xown = din("xown", [G * 128, D])
    xs = din("xs", [128, D])
    w_in = din("w_in", [D, IN_COLS])
    w3 = din("w3", [3, D, D])
    ln_in_g = din("ln_in_g", [1, D]); ln_in_b = din("ln_in_b", [1, D])
    ln_g = din("ln_g", [1, D]); ln_b = din("ln_b", [1, D])
    gate_b = din("gate_b", [1, 2048])
    gla_gate_b = din("gla_gate_b", [1, 512])
    gla_norm_g = din("gla_norm_g", [1, 256])
    idx_kn_g = din("idx_kn_g", [1, 64]); idx_kn_b = din("idx_kn_b", [1, 64])
    gla_w2 = din("gla_w2", [16, 512])
    cache_k = din("cache_k", [4, 2048, 256]); cache_v = din("cache_v", [4, 2048, 256])
    cache_ik = din("cache_ik", [4, 2048, 64])
    state = din("state", [4, 4, 128, 256])
    c_ident = din("c_ident", [128, 128])
    c_triA = din("c_triA", [128, 128]); c_blkA = din("c_blkA", [128, 2])
    c_triB = din("c_triB", [128, 128]); c_maskB = din("c_maskB", [128, 4, 128])
    c_triS = din("c_triS", [128, 128]); c_maskS = din("c_maskS", [128, 4, 128])
    c_mrevS = din("c_mrevS", [128, 128]); c_bsumS = din("c_bsumS", [128, 4])
    c_cmaskS = din("c_cmaskS", [128, 4, 128]); c_rmaskS = din("c_rmaskS", [128, 4])
    c_idrep = din("c_idrep", [128, 512]); c_idrepS = din("c_idrepS", [128, 4, 64])
    c_selS = din("c_selS", [128, 4, 128])
    c_ctab = din("c_ctab", [128, KIT + 1])
    c_tbias = din("c_tbias", [128, 2, 640])
    c_onehot = din("c_onehot", [128, 4])
    c_tailmask = din("c_tailmask", [128, 1])

    y_own = dout("y_own", [G * 128, D])
    kp = dout("kp", [NB * 128, 256]); vp = dout("vp", [NB * 128, 256]); ikp = dout("ikp", [NB * 128, 64])
    gla_p = dout("gla_p", [128, 1024])
    ys = dout("ys", [128, D]); ks = dout("ks", [128, 256]); vs = dout("vs", [128, 256]); iks = dout("iks", [128, 64])
    gla_s = dout("gla_s", [4, 128, 1024])

    wbf = dscr("wbf", [D, IN_COLS], BF16)
    wbf3 = dscr("wbf3", [3, D, D], BF16)
    kT_d = dscr("kT_d", [128, 2, NKMAX], BF16)
    v_d = dscr("v_d", [128, NB, 260], BF16)
    ki_d = dscr("ki_d", [128, NKMAX], BF16)
    snap = dscr("snap", [NB, 128, 1024], F32)

    k = KB(nc)
    P = k.P
    top = ExitStack()
    with top:
        pst = [top.enter_context(nc.psum_tensor("ps%d" % i, [128, 512], F32)) for i in range(8)]

        def PSF(i):
            return pst[i][:]

        def PSB(i):
            return pst[i][:].bitcast(BF16)

        if "0" in phases:
            with ExitStack() as es:
                def sb(name, shape, dt):
                    return es.enter_context(nc.sbuf_tensor(name, shape, dt))
                wst = [sb("wst%d" % s, [128, 8, 512], F32) for s in range(2)]
                wcb = [sb("wcb%d" % s, [128, 8, 512], BF16) for s in range(2)]
                jobs = []
                for c in range(17):
                    c0 = c * 512
                    n = min(512, IN_COLS - c0)
                    jobs.append((w_in[:, c0:c0 + n], wbf[:, c0:c0 + n], n))
                for m in range(3):
                    for c in range(2):
                        jobs.append((w3[m, :, c * 512:(c + 1) * 512], wbf3[m, :, c * 512:(c + 1) * 512], 512))
                engs = ['dve', 'act', 'pool']
                for idx, (src, dst, n) in enumerate(jobs):
                    s = idx % 2
                    k.dma('sp', wst[s][:, :, :n], src.rearrange("(k p) n -> p k n", p=128), [], ['wst%d' % s], 'wst%d' % s)
                    k.cp(engs[idx % 3], wcb[s][:, :, :n], wst[s][:, :, :n], ['wst%d' % s], ['wcb%d' % s])
                    k.dma('act', dst.rearrange("(k p) n -> p k n", p=128), wcb[s][:, :, :n], ['wcb%d' % s], [], 'wcb%d' % s)
            P.barrier()

        def layernorm(src, srckey, gB, bB, tl, tag, out_f32=None, out_f32_key=None, out_bf=None, out_bf_key=None):
            tk = lambda n: tag + n
            for c in range(2):
                k.bn_stats(tl['st'][:, c, :], src[:, c * 512:(c + 1) * 512], [srckey], [tk('st%d' % c)])
            k.bn_aggr(tl['mv'][:], tl['st'][:].rearrange("p a b -> p (a b)"), [tk('st0'), tk('st1')], [tk('mv')])
            k.act(tl['sd'][:], tl['mv'][:, 1:2], AF.Ln, [tk('mv'), 'eps'], [tk('sd')], bias=tl['eps'][:, 0:1], scale=1.0)
            k.act(tl['rstd'][:], tl['sd'][:], AF.Exp, [tk('sd')], [tk('rstd')], scale=-0.5)
            k.ts(tl['nmr'][:], tl['mv'][:, 0:1], tl['rstd'][:, 0:1], -1.0, ALU.mult, ALU.mult, [tk('mv'), tk('rstd')], [tk('nmr')])
            k.act(tl['xn'][:], src, AF.Identity, [srckey, tk('nmr'), tk('rstd')], [tl['xnkey']],
                  bias=tl['nmr'][:, 0:1], scale=tl['rstd'][:, 0:1])
            k.tt(tl['xn'][:], tl['xn'][:], gB, ALU.mult, [tl['xnkey'], 'lnconst'], [tl['xnkey']])
            if out_f32 is not None:
                k.tt(out_f32, tl['xn'][:], bB, ALU.add, [tl['xnkey'], 'lnconst'], [out_f32_key])
                if out_bf is not None:
                    k.cp('pool', out_bf, out_f32, [out_f32_key], [out_bf_key])
            else:
                k.tt(out_bf, tl['xn'][:], bB, ALU.add, [tl['xnkey'], 'lnconst'], [out_bf_key])

        if "A" in phases:
            with ExitStack() as es:
                def sb(name, shape, dt):
                    return es.enter_context(nc.sbuf_tensor(name, shape, dt))
                gB = sb("a_gB", [128, D], F32); bB = sb("a_bB", [128, D], F32)
                identf = sb("a_idf", [128, 128], F32); identb = sb("a_idb", [128, 128], BF16)
                triA = sb("a_triA", [128, 128], F32); blkA = sb("a_blkA", [128, 2], F32)
                w2 = sb("a_w2", [16, 512], F32); gbias = sb("a_gbias", [1, 512], F32); ones1 = sb("a_ones1", [1, 128], F32)
                gkiB = sb("a_gkiB", [128, 64], F32); bkiB = sb("a_bkiB", [128, 64], F32)
                eps = sb("a_eps", [128, 1], F32); one = sb("a_one", [128, 1], F32)
                tailm = sb("a_tailm", [128, 1], F32)
                wA = sb("a_wA", [128, 8, 2128], BF16)
                SS = [sb("a_S%d" % s, [128, 4, 256], F32) for s in range(3)]
                xa = [sb("a_xa%d" % s, [128, D], F32) for s in range(3)]
                xn = sb("a_xn", [128, D], F32)
                hb = [sb("a_hb%d" % s, [128, D], BF16) for s in range(3)]
                hT = [sb("a_hT%d" % s, [128, 8, 128], BF16) for s in range(2)]
                st = sb("a_st", [128, 2, 6], F32); mv = sb("a_mv", [128, 2], F32)
                sd = sb("a_sd", [128, 1], F32); rstd = sb("a_rstd", [128, 1], F32); nmr = sb("a_nmr", [128, 1], F32)
                st2 = sb("a_st2", [128, 6], F32); mv2 = sb("a_mv2", [128, 2], F32)
                sd2 = sb("a_sd2", [128, 1], F32); rstd2 = sb("a_rstd2", [128, 1], F32); nmr2 = sb("a_nmr2", [128, 1], F32)
                Vt = [sb("a_V%d" % s, [128, 1024], BF16) for s in range(4)]
                kdv = [sb("a_kdv%d" % s, [128, 512], F32) for s in range(3)]
                kdb = [sb("a_kdb%d" % s, [128, 256], BF16) for s in range(2)]
                vext = [sb("a_vext%d" % s, [128, 4, 65], BF16) for s in range(2)]
                kTt = [sb("a_kT%d" % s, [128, 2, 128], BF16) for s in range(2)]
                kin = [sb("a_kin%d" % s, [128, 64], F32) for s in range(3)]
                ksb = [sb("a_ksb%d" % s, [128, 512], F32) for s in range(3)]
                kif = [sb("a_kif%d" % s, [128, 64], F32) for s in range(2)]
                kib = [sb("a_kib%d" % s, [128, 128], BF16) for s in range(2)]
                kiT = [sb("a_kiT%d" % s, [128, 128], BF16) for s in range(2)]
                glb = sb("a_glb", [16, 128], BF16); w2b = sb("a_w2b", [16, 512], BF16); gbB = sb("a_gbB", [128, 512], F32)
                el = [sb("a_el%d" % s, [128, 512], F32) for s in range(3)]
                er = sb("a_er", [128, 512], F32)
                Kt = [sb("a_Kt%d" % s, [128, 512], BF16) for s in range(2)]
                dec = [sb("a_dec%d" % s, [128, 8], F32) for s in range(2)]

                k.dma('sp', gB[:], ln_in_g.partition_broadcast(128), [], ['lnconst0'], 'ca')
                k.dma('sp', bB[:], ln_in_b.partition_broadcast(128), [], ['lnconst1'], 'ca')
                k.dma('sp', identf[:], c_ident, [], ['identf'], 'ca')
                k.dma('sp', triA[:], c_triA, [], ['triA'], 'ca')
                k.dma('sp', blkA[:], c_blkA, [], ['blkA'], 'ca')
                k.dma('sp', w2[:], gla_w2, [], ['w2'], 'ca')
                k.dma('sp', gbias[:], gla_gate_b, [], ['gbias'], 'ca')
                k.dma('sp', gbB[:], gla_gate_b.partition_broadcast(128), [], ['gbB'], 'ca')
                k.dma('sp', gkiB[:], idx_kn_g.partition_broadcast(128), [], ['gkiB'], 'ca')
                k.dma('sp', bkiB[:], idx_kn_b.partition_broadcast(128), [], ['bkiB'], 'ca')
                k.dma('sp', tailm[:], c_tailmask, [], ['tailm'], 'ca')
                wmap = [(C_GK, 512, 0), (C_GV, 1024, 512), (C_DK, 512, 1536), (C_IK, 64, 2048), (C_GLOW, 16, 2112)]
                for (c0, n, o) in wmap:
                    k.dma('sp', wA[:, :, o:o + n], wbf[:, c0:c0 + n].rearrange("(k p) n -> p k n", p=128), [], ['wA%d' % o], 'ca')
                P.barrier()
                k.memset('dve', eps[:], EPS, ['eps'])
                k.memset('dve', one[:], 1.0, ['one'])
                k.memset('dve', ones1[:], 1.0, ['ones1'])
                k.memset('dve', SS[0][:], 0.0, ['S0'])
                for s in range(2):
                    k.memset('pool', vext[s][:], 1.0, ['vext%d' % s])
                k.cp('dve', identb[:], identf[:], [], ['identb'])
                k.cp('dve', w2b[:], w2[:], [], ['w2b'])
                P.barrier()
                tl = dict(st=st, mv=mv, sd=sd, rstd=rstd, nmr=nmr, xn=xn, eps=eps, xnkey='xn')

                def loadx(i):
                    s = i % 3
                    k.dma('sp', xa[s][:], xall[i * 128:(i + 1) * 128, :], [], ['xa%d' % s], 'xa%d' % s)

                loadx(0)

                def fa(i):
                    s = i % 3
                    if i + 1 < NB:
                        loadx(i + 1)
                    layernorm(xa[s][:], 'xa%d' % s, gB[:], bB[:], tl, 'a', out_bf=hb[s][:], out_bf_key='hb%d' % s)

                def fb_a(i):
                    s = i % 3
                    s2 = i % 2
                    for kc in range(8):
                        k.tr(PSB(0)[:, kc * 128:(kc + 1) * 128], hb[s][:, kc * 128:(kc + 1) * 128], identb[:], ['hb%d' % s], ['ps0'])
                    k.cp('act', hT[s2][:].rearrange("p a b -> p (a b)"), PSB(0), ['ps0'], ['hT%d' % s2])

                def fb_b(i):
                    s = i % 3
                    s2 = i % 2
                    last = (i == NB - 1)
                    hk = 'hT%d' % s2
                    for kc in range(8):
                        k.mm(PSF(1), hT[s2][:, kc, :], wA[:, kc, 0:512], kc == 0, kc == 7, [hk], ['ps1'])
                    k.cp('act', ksb[s][:], PSF(1), ['ps1'], ['ksb%d' % s])
                    for half in range(2):
                        for kc in range(8):
                            k.mm(PSF(2 + half), hT[s2][:, kc, :], wA[:, kc, 512 + half * 512:1024 + half * 512], kc == 0, kc == 7, [hk], ['ps%d' % (2 + half)])
                        k.cp('act' if half == 0 else 'dve', Vt[i % 4][:, half * 512:(half + 1) * 512], PSF(2 + half), ['ps%d' % (2 + half)], ['V%d' % (i % 4)])
                    for kc in range(8):
                        k.mm(PSF(4), hT[s2][:, kc, :], wA[:, kc, 1536:2048], kc == 0, kc == 7, [hk], ['ps4'])
                    k.cp('act', kdv[s][:], PSF(4), ['ps4'], ['kdv%d' % s])
                    for kc in range(8):
                        k.mm(PSF(5)[:, 0:64], hT[s2][:, kc, :], wA[:, kc, 2048:2112], kc == 0, kc == 7, [hk], ['ps5'])
                    k.cp('dve', kin[s][:], PSF(5)[:, 0:64], ['ps5'], ['kin%d' % s])

                    for kc in range(8):
                        k.mm(PSF(5)[0:16, 64:192], wA[:, kc, 2112:2128], hT[s2][:, kc, :], kc == 0, kc == 7, [hk], ['ps5'])
                    k.cp('dve', glb[:], PSF(5)[0:16, 64:192], ['ps5'], ['gl'])
                    k.mm(PSF(4), glb[:], w2b[:], True, True, ['gl'], ['ps4'])
                    k.tt(el[s][:], PSF(4), gbB[:], ALU.add, ['ps4'], ['el%d' % s])
                    k.act(el[s][:], el[s][:], AF.Exp, ['el%d' % s], ['el%d' % s], scale=-1.0)
                    k.act(el[s][:], el[s][:], AF.Ln, ['el%d' % s], ['el%d' % s], bias=one[:, 0:1], scale=1.0)
                    if last:
                        k.ts(el[s][:], el[s][:], tailm[:, 0:1], None, ALU.mult, None, ['el%d' % s], ['el%d' % s])
                def back1(i):
                    s3 = i % 3
                    s = i % 2
                    last = (i == NB - 1)
                    k.mm(PSF(6), triA[:], el[s3][:], True, True, ['el%d' % s3], ['ps6'])
                    for h in range(4):
                        k.mm(PSF(7)[:, 448 + 2 * h:450 + 2 * h], el[s3][:, h * 128:(h + 1) * 128], blkA[:], True, True, ['el%d' % s3], ['ps7'])
                    k.act(er[:], PSF(6), AF.Exp, ['ps6'], ['er'])
                    k.act(dec[s][:], PSF(7)[:, 448:456], AF.Exp, ['ps7'], ['dec%d' % s])
                    if last:
                        k.stt(Kt[s][:], ksb[s3][:], tailm[:, 0:1], er[:], ALU.mult, ALU.mult, ['ksb%d' % s3, 'er'], ['Kt%d' % s])
                    else:
                        k.tt(Kt[s][:], ksb[s3][:], er[:], ALU.mult, ['ksb%d' % s3, 'er'], ['Kt%d' % s])
                    k.dma('act', kp[i * 128:(i + 1) * 128, :], kdv[s3][:, 0:256], ['kdv%d' % s3], [], 'kdvo%d' % s3)
                    k.dma('act', vp[i * 128:(i + 1) * 128, :], kdv[s3][:, 256:512], ['kdv%d' % s3], [], 'kdvo%d' % s3)
                    k.cp('pool', kdb[s][:], kdv[s3][:, 0:256], ['kdv%d' % s3], ['kdb%d' % s])
                    k.cp('pool', vext[s][:, :, 0:64], kdv[s3][:, 256:512].rearrange("p (g d) -> p g d", g=4), ['kdv%d' % s3], ['vext%d' % s])
                    for c in range(2):
                        k.tr(PSB(7)[:, c * 128:(c + 1) * 128], kdb[s][:, c * 128:(c + 1) * 128], identb[:], ['kdb%d' % s], ['ps7'])
                    k.cp('dve', kTt[s][:].rearrange("p a b -> p (a b)"), PSB(7)[:, 0:256], ['ps7'], ['kT%d' % s])
                    k.dma('pool', kT_d[:, :, i * 128:(i + 1) * 128], kTt[s][:], ['kT%d' % s], [], 'kTo%d' % s)
                    k.dma('pool', v_d[:, i, :], vext[s][:].rearrange("p g d -> p (g d)"), ['vext%d' % s], [], 'vexto%d' % s)
                    kn = kin[s3]
                    knk = 'kin%d' % s3
                    k.bn_stats(st2[:], kn[:], [knk], ['st2'])
                    k.bn_aggr(mv2[:], st2[:], ['st2'], ['mv2'])
                    k.act(sd2[:], mv2[:, 1:2], AF.Ln, ['mv2'], ['sd2'], bias=eps[:, 0:1], scale=1.0)
                    k.act(rstd2[:], sd2[:], AF.Exp, ['sd2'], ['rstd2'], scale=-0.5)
                    k.ts(nmr2[:], mv2[:, 0:1], rstd2[:, 0:1], -1.0, ALU.mult, ALU.mult, ['mv2', 'rstd2'], ['nmr2'])
                    k.act(kn[:], kn[:], AF.Identity, [knk, 'nmr2', 'rstd2'], [knk], bias=nmr2[:, 0:1], scale=rstd2[:, 0:1])
                    k.tt(kn[:], kn[:], gkiB[:], ALU.mult, [knk], [knk])
                    k.tt(kif[s][:], kn[:], bkiB[:], ALU.add, [knk], ['kif%d' % s])
                    k.dma('act', ikp[i * 128:(i + 1) * 128, :], kif[s][:], ['kif%d' % s], [], 'kifo%d' % s)
                    k.cp('pool', kib[s][:, 0:64], kif[s][:], ['kif%d' % s], ['kib%d' % s])
                    k.cp('pool', kib[s][:, 64:128], kif[s][:], ['kif%d' % s], ['kib%d' % s])
                    k.tr(PSB(7)[:, 256:384], kib[s][:], identb[:], ['kib%d' % s], ['ps7'])
                    k.cp('dve', kiT[s][:], PSB(7)[:, 256:384], ['ps7'], ['kiT%d' % s])
                    k.dma('pool', ki_d[:, i * 128:(i + 1) * 128], kiT[s][:], ['kiT%d' % s], [], 'kiTo%d' % s)

                def back2(i):
                    s3 = i % 3
                    s = i % 2
                    cur = (2 * i) % 3
                    k.dma('sp', snap[i], SS[cur][:].rearrange("p h e -> p (h e)"), ['S%d' % cur], [], 'Ssto%d' % cur)
                    sbanks = [[6, 7], [2, 3]]
                    for c in range(2):
                        for hp in range(2):
                            bk = sbanks[c][hp]
                            for hh in range(2):
                                h = hp * 2 + hh
                                k.mm(PSF(bk)[:, hh * 256:(hh + 1) * 256], Kt[s][c * 64:(c + 1) * 64, h * 128:(h + 1) * 128],
                                     Vt[i % 4][c * 64:(c + 1) * 64, h * 256:(h + 1) * 256], True, True, ['Kt%d' % s, 'V%d' % (i % 4)], ['ps%d' % bk])
                            src_, dst_ = (2 * i + c) % 3, (2 * i + c + 1) % 3
                            for hh in range(2):
                                h = hp * 2 + hh
                                k.stt(SS[dst_][:, h, :], SS[src_][:, h, :], dec[s][:, 2 * h + c:2 * h + c + 1], PSF(bk)[:, hh * 256:(hh + 1) * 256],
                                      ALU.mult, ALU.add, ['S%d' % src_, 'dec%d' % s, 'ps%d' % bk], ['S%d' % dst_])

                for i0 in range(min(3, NB)):
                    fa(i0)
                for i0 in range(3):
                    if i0 < NB:
                        fb_a(i0)
                        fb_b(i0)
                    if i0 + 3 < NB and i0 < 2:
                        fa(i0 + 3)
                back1(0)
                for i in range(NB):
                    if i + 3 < NB:
                        fb_a(i + 3)
                    back2(i)
                    if i + 1 < NB:
                        back1(i + 1)
                    if i + 5 < NB:
                        fa(i + 5)
                    if i + 3 < NB:
                        fb_b(i + 3)
                fin = (2 * NB) % 3
                k.dma('sp', gla_p, SS[fin][:].rearrange("p h e -> p (h e)"), ['S%d' % fin], [], 'glap')
            P.barrier()


        def phase_own(mode):
            PR = (mode == 'P')
            NK = NKMAX if PR else 2176
            with ExitStack() as es:
                def sb(name, shape, dt):
                    return es.enter_context(nc.sbuf_tensor(mode + name, shape, dt))
                gB = sb("gB", [128, D], F32); bB = sb("bB", [128, D], F32)
                g2B = sb("g2B", [128, D], F32); b2B = sb("b2B", [128, D], F32)
                gtb = [sb("gtb%d" % s_, [128, 512], F32) for s_ in range(2)]
                gnB = sb("gnB", [128, 256], F32)
                identf = sb("idf", [128, 128], F32); identb = sb("idb", [128, 128], BF16)
                tri = sb("tri", [128, 128], F32)
                maskf = sb("maskf", [128, 512], F32)
                cst = sb("cst", [128, 512], F32); idrepb = sb("idrepb", [128, 512], BF16)
                w2 = sb("w2", [16, 512], F32); gbias = sb("gbias", [1, 512], F32); ones1 = sb("ones1", [1, 128], F32)
                eps = sb("eps", [128, 1], F32); one = sb("one", [128, 1], F32)
                ctab = sb("ctab", [128, KIT + 1], F32)
                tbias = sb("tbias", [128, 2, 640 if PR else 128], F32)
                onehot = sb("onehot", [128, 4], F32)
                xo = sb("xo", [128, D], F32); h = sb("h", [128, D], F32); hb = sb("hb", [128, D], BF16)
                hT = sb("hT", [128, 8, 128], BF16)
                tmp = sb("tmp", [128, D], F32)
                st = sb("st", [128, 2, 6], F32); mv = sb("mv", [128, 2], F32)
                sd = sb("sd", [128, 1], F32); rstd = sb("rstd", [128, 1], F32); nmr = sb("nmr", [128, 1], F32)
                wch = [sb("wch%d" % s_, [128, 8, 512], BF16) for s_ in range(2)]
                gl = sb("gl", [16, 128], F32)
                el = sb("el", [128, 512], F32); eb = sb("eb", [128, 512], F32); enb = sb("enb", [128, 512], F32)
                qT = sb("qT", [128, 4, 128], BF16); kTh = sb("kTh", [128, 4, 128], BF16)
                V = sb("V", [128, 1024], BF16)
                sg = [sb("sg%d" % s_, [128, 512], F32) for s_ in range(2)]
                QTz = sb("QTz", [128, 4, 512], BF16); qiTz = sb("qiTz", [128, 8, 128], BF16)
                wabs = sb("wabs", [128, 8], F32); wsgn = sb("wsgn", [128, 8], F32)
                AT = sb("AT", [128, 4, 128], BF16)
                ss = sb("ss", [128, 4], F32); rs = sb("rs", [128, 4], F32)
                yain = sb("yain", [128, D], BF16); yT = sb("yT", [128, 8, 128], BF16)
                mrg = sb("mrg", [128, D], F32)
                sc = sb("sc", [128, NK], F32)
                junk = None if PR else sb("junk", [128, 2176], BF16)
                rlw = sb("rlw", [128, 1024], F32)
                rl = [rlw[:, 0:512], rlw[:, 512:1024]]
                rd = sb("rd", [128, 16], F32)
                yout = xo
                rmax = sb("rmax", [128, 1], F32); rmin = sb("rmin", [128, 1], F32); Wd = sb("Wd", [128, 1], F32)
                wtab = sb("wtab", [128, KIT + 1], F32); mids = sb("mids", [128, KIT + 1], F32)
                cnts = sb("cnts", [128, KIT], F32); us = sb("us", [128, KIT], F32); thr = sb("thr", [128, 1], F32)
                sAs = sb("sAs", [128, KIT], F32); vvs = sb("vvs", [128, KIT], F32)
                jd = sb("jd", [128, 8], BF16); ja = sb("ja", [128, 8], BF16); jq = sb("jq", [128, 8], BF16)
                if PR:
                    Sc = [sb("Sc%d" % s_, [128, 1024], F32) for s_ in range(2)]
                    Sown = sb("Sown", [128, 1024], F32); Sb = sb("Sb", [128, 4, 256], BF16)
                    kich = [sb("kich%d" % s_, [128, 1024], BF16) for s_ in range(2)]
                    kTch = [sb("kTch%d" % s_, [128, 2, 512], BF16) for s_ in range(2)]
                    vch = [sb("vch%d" % s_, [128, 4, 260], BF16) for s_ in range(2)]
                    mbt = [sb("mbt%d" % s_, [128, 128], BF16) for s_ in range(3)]
                    pT = [sb("pT%d" % s_, [128, 512], BF16) for s_ in range(3)]
                    oT = [sb("oT%d" % s_, [65, 512], F32) for s_ in range(2)]
                else:
                    cmaskS = sb("cmaskS", [128, 4, 128], F32); rmaskS = sb("rmaskS", [128, 4], F32)
                    mrevS = sb("mrevS", [128, 128], F32); bsumS = sb("bsumS", [128, 4], F32)
                    idrepSb = sb("idrepSb", [128, 4, 64], BF16)
                    selSb = sb("selSb", [128, 4, 128], BF16)
                    gkiB = sb("gkiB", [128, 64], F32); bkiB = sb("bkiB", [128, 64], F32)
                    S0f = [sb("S0f%d" % b_, [128, 4, 256], F32) for b_ in range(2)]
                    S0b = [sb("S0b%d" % b_, [128, 4, 256], BF16) for b_ in range(4)]
                    qTb = [sb("qTb%d" % b_, [128, 4, 128], BF16) for b_ in range(4)]
                    wabsb = sb("wabsb", [128, 4, 8], F32)
                    QTs = sb("QTs", [128, 4, 4, 64], BF16)
                    kTs1 = sb("kTs", [128, 2, 2176], BF16)
                    kiTs = [sb("kiTs%d" % b_, [128, 2176], BF16) for b_ in range(4)]
                    vexts1 = sb("vexts", [128, 17, 260], BF16)
                    ckf = sb("ckf", [128, 8, 256], F32); ckb = sb("ckb", [128, 8, 256], BF16)
                    cif = sb("cif", [128, 8, 64], F32); cib = sb("cib", [128, 8, 128], BF16)
                    kdv = sb("kdv", [128, 512], F32); kdb = sb("kdb", [128, 256], BF16); vnb = sb("vnb", [128, 256], BF16)
                    kin = sb("kin", [128, 64], F32); kif = sb("kif", [128, 64], F32); kib = sb("kib", [128, 128], BF16)
                    st2 = sb("st2", [128, 6], F32); mv2 = sb("mv2", [128, 2], F32)
                    sd2 = sb("sd2", [128, 1], F32); rstd2 = sb("rstd2", [128, 1], F32); nmr2 = sb("nmr2", [128, 1], F32)
                    kTnew = sb("kTnew", [128, 2, 128], BF16); kiTnew = sb("kiTnew", [128, 128], BF16)
                    Kt = sb("Kt", [128, 512], BF16); Ktb = [sb("Ktb%d" % s_, [128, 512], BF16) for s_ in range(2)]
                    decs = sb("decs", [128, 16], F32)
                    pTs = [sb("pTs%d" % s_, [128, 64], BF16) for s_ in range(3)]

                cl = 'c' + mode
                k.dma('sp', gB[:], ln_in_g.partition_broadcast(128), [], [], cl)
                k.dma('sp', bB[:], ln_in_b.partition_broadcast(128), [], [], cl)
                k.dma('sp', g2B[:], ln_g.partition_broadcast(128), [], [], cl)
                k.dma('sp', b2B[:], ln_b.partition_broadcast(128), [], [], cl)
                k.dma('sp', gnB[:], gla_norm_g.partition_broadcast(128), [], [], cl)
                k.dma('sp', identf[:], c_ident, [], [], cl)
                k.dma('sp', tri[:], c_triB if PR else c_triS, [], [], cl)
                k.dma('sp', maskf[:], (c_maskB if PR else c_maskS).rearrange("p a b -> p (a b)"), [], [], cl)
                k.dma('sp', w2[:], gla_w2, [], [], cl)
                k.dma('sp', gbias[:], gla_gate_b, [], [], cl)
                k.dma('sp', ctab[:], c_ctab, [], [], cl)
                k.dma('sp', tbias[:], c_tbias if PR else c_tbias[:, :, 0:128], [], [], cl)
                k.dma('sp', onehot[:], c_onehot, [], [], cl)
                if not PR:
                    k.dma('sp', cmaskS[:], c_cmaskS, [], [], cl)
                    k.dma('sp', rmaskS[:], c_rmaskS, [], [], cl)
                    k.dma('sp', mrevS[:], c_mrevS, [], [], cl)
                    k.dma('sp', bsumS[:], c_bsumS, [], [], cl)
                    k.dma('sp', gkiB[:], idx_kn_g.partition_broadcast(128), [], [], cl)
                    k.dma('sp', bkiB[:], idx_kn_b.partition_broadcast(128), [], [], cl)
                P.barrier()
                k.memset('dve', eps[:], EPS, [])
                k.memset('dve', one[:], 1.0, [])
                k.memset('dve', ones1[:], 1.0, [])
                k.memset('pool', QTz[:], 0.0, [])
                k.memset('pool', qiTz[:], 0.0, [])
                k.memset('pool', yain[:], 0.0, [])
                k.memset('pool', tmp[:], 0.0, [])
                k.cp('dve', identb[:], identf[:], [], [])
                k.dma('sp', cst[:], c_idrep, [], ['cst'], 'cst')
                k.cp('dve', idrepb[:], cst[:], ['cst'], ['idrepb'])
                if not PR:
                    k.dma('sp', cst[:, 0:256], c_idrepS.rearrange("p a b -> p (a b)"), ['idrepb'], ['cst'], 'cst')
                    k.cp('dve', idrepSb[:].rearrange("p a b -> p (a b)"), cst[:, 0:256], ['cst'], ['idrepSb'])
                    k.dma('sp', cst[:], c_selS.rearrange("p a b -> p (a b)"), ['idrepSb'], ['cst'], 'cst')
                    k.cp('dve', selSb[:].rearrange("p a b -> p (a b)"), cst[:], ['cst'], ['selSb'])
                    for b_ in range(4):
                        s_ = b_ % 2
                        k.dma('sp', S0f[s_][:], state[b_].rearrange("h p e -> p h e"), [], ['S0f%d' % s_], 'S0f%d' % s_)
                        k.cp('pool', S0b[b_][:], S0f[s_][:], ['S0f%d' % s_], ['S0b%d' % b_])
                        k.memset('pool', kiTs[b_][:, 2048:2176], 0.0, [])
                    k.memset('pool', vexts1[:], 1.0, [])
                    k.memset('pool', kTs1[:, :, 2048:2176], 0.0, [])
                P.barrier()
                tl = dict(st=st, mv=mv, sd=sd, rstd=rstd, nmr=nmr, xn=tmp, eps=eps, xnkey='tmp')
                bank = [0]

                def nb():
                    bank[0] = (bank[0] + 1) % 8
                    return bank[0]

                def own_block(g):
                    jobs = []

                    def J(src, n):
                        jobs.append((src, n))
                    J(wbf[:, C_IW:C_IW + 8], 8)
                    J(wbf[:, C_IQ:C_IQ + 512], 512)
                    J(wbf[:, C_DQ:C_DQ + 512], 512); J(wbf[:, C_DQ + 512:C_DQ + 1024], 512)
                    if not PR:
                        J(wbf[:, C_DK:C_DK + 512], 512)
                        J(wbf[:, C_IK:C_IK + 64], 64)
                    J(wbf[:, C_GLOW:C_GLOW + 16], 16)
                    J(wbf[:, C_GQ:C_GQ + 512], 512)
                    J(wbf[:, C_GK:C_GK + 512], 512)
                    J(wbf[:, C_GV:C_GV + 512], 512); J(wbf[:, C_GV + 512:C_GV + 1024], 512)
                    J(wbf[:, C_GR:C_GR + 512], 512); J(wbf[:, C_GR + 512:C_GR + 1024], 512)
                    for cc in range(2):
                        J(wbf3[0, :, cc * 512:(cc + 1) * 512], 512)
                        J(wbf[:, C_MA + cc * 512:C_MA + (cc + 1) * 512], 512)
                    J(wbf[:, C_DZ:C_DZ + 512], 512); J(wbf[:, C_DZ + 512:C_DZ + 1024], 512)
                    for cc in range(2):
                        J(wbf3[1, :, cc * 512:(cc + 1) * 512], 512)
                        J(wbf[:, C_MB + cc * 512:C_MB + (cc + 1) * 512], 512)
                    for cc in range(2):
                        J(wbf3[2, :, cc * 512:(cc + 1) * 512], 512)
                    jpos = [0]

                    def wissue(idx):
                        src, n = jobs[idx]
                        s_ = idx % 2
                        k.dma('sp', wch[s_][:, :, :n], src.rearrange("(k p) n -> p k n", p=128), [], ['wch%d' % s_], 'wch%d' % s_)

                    def next_w():
                        idx = jpos[0]
                        if idx == 0:
                            wissue(0)
                        if idx + 1 < len(jobs):
                            wissue(idx + 1)
                        jpos[0] += 1
                        return wch[idx % 2], 'wch%d' % (idx % 2)

                    def projT(wt, wk, n, psap, pskey):
                        for kc in range(8):
                            k.mm(psap, hT[:, kc, :], wt[:, kc, :n], kc == 0, kc == 7, ['hT', wk], [pskey])

                    def projF(wt, wk, bk):
                        for sub in range(4):
                            for kc in range(8):
                                k.mm(PSF(bk)[:, sub * 128:(sub + 1) * 128], wt[:, kc, sub * 128:(sub + 1) * 128], hT[:, kc, :],
                                     kc == 0, kc == 7, ['hT', wk], ['ps%d' % bk])

                    def transpose8(src, srckey, dst, dstkey):
                        bk = nb()
                        for kc in range(8):
                            k.tr(PSB(bk)[:, kc * 128:(kc + 1) * 128], src[:, kc * 128:(kc + 1) * 128], identb[:], [srckey], ['ps%d' % bk])
                        k.cp('act', dst[:].rearrange("p a b -> p (a b)"), PSB(bk), ['ps%d' % bk], [dstkey])

                    xsrc = xown[g * 128:(g + 1) * 128, :] if PR else xs
                    k.dma('act', xo[:], xsrc, [], ['xo'], 'xo')
                    layernorm(xo[:], 'xo', gB[:], bB[:], tl, 'b', out_f32=h[:], out_f32_key='h', out_bf=hb[:], out_bf_key='hb')
                    transpose8(hb, 'hb', hT, 'hT')
                    wt, wk = next_w()
                    bw = nb()
                    projT(wt, wk, 8, PSF(bw)[:, 0:8], 'ps%d' % bw)
                    k.ts(wabs[:], PSF(bw)[:, 0:8], -IDX_W_SCALE, None, ALU.mult, None, ['ps%d' % bw], ['wabs'])
                    k.stt(wabs[:], PSF(bw)[:, 0:8], IDX_W_SCALE, wabs[:], ALU.mult, ALU.max, ['ps%d' % bw, 'wabs'], ['wabs'])
                    k.ts(wsgn[:], PSF(bw)[:, 0:8], 0.0, 2.0, ALU.is_ge, ALU.mult, ['ps%d' % bw], ['wsgn'])
                    k.ts(wsgn[:], wsgn[:], -1.0, None, ALU.add, None, ['wsgn'], ['wsgn'])
                    wt, wk = next_w()
                    bi = nb()
                    projF(wt, wk, bi)
                    qv = qiTz[:].rearrange("p (s two) t -> p s two t", two=2)
                    pv = PSF(bi).rearrange("p (s t) -> p s t", s=4)
                    k.cp('act', qv[0:64, :, 0, :], pv[0:64, :, :], ['ps%d' % bi], ['qiTz'])
                    k.cp('act', qv[64:128, :, 1, :], pv[64:128, :, :], ['ps%d' % bi], ['qiTz'])
                    for m in range(2):
                        wt, wk = next_w()
                        bdq = nb()
                        projF(wt, wk, bdq)
                        k.cp('act', QTz[0:64, 2 * m, :], PSF(bdq)[0:64, :], ['ps%d' % bdq], ['QTz'])
                        k.cp('act', QTz[64:128, 2 * m + 1, :], PSF(bdq)[64:128, :], ['ps%d' % bdq], ['QTz'])
                    if not PR:
                        wt, wk = next_w()
                        bkv = nb()
                        projT(wt, wk, 512, PSF(bkv), 'ps%d' % bkv)
                        k.cp('act', kdv[:], PSF(bkv), ['ps%d' % bkv], ['kdv'])
                        k.dma('act', ks, kdv[:, 0:256], ['kdv'], [], 'so1')
                        k.dma('act', vs, kdv[:, 256:512], ['kdv'], [], 'so1')
                        k.cp('pool', kdb[:], kdv[:, 0:256], ['kdv'], ['kdb'])
                        k.cp('pool', vnb[:], kdv[:, 256:512], ['kdv'], ['vnb'])
                        wt, wk = next_w()
                        bik = nb()
                        projT(wt, wk, 64, PSF(bik)[:, 0:64], 'ps%d' % bik)
                        k.cp('dve', kin[:], PSF(bik)[:, 0:64], ['ps%d' % bik], ['kin'])
                        k.bn_stats(st2[:], kin[:], ['kin'], ['st2'])
                        k.bn_aggr(mv2[:], st2[:], ['st2'], ['mv2'])
                        k.act(sd2[:], mv2[:, 1:2], AF.Ln, ['mv2'], ['sd2'], bias=eps[:, 0:1], scale=1.0)
                        k.act(rstd2[:], sd2[:], AF.Exp, ['sd2'], ['rstd2'], scale=-0.5)
                        k.ts(nmr2[:], mv2[:, 0:1], rstd2[:, 0:1], -1.0, ALU.mult, ALU.mult, ['mv2', 'rstd2'], ['nmr2'])
                        k.act(kin[:], kin[:], AF.Identity, ['kin', 'nmr2', 'rstd2'], ['kin'], bias=nmr2[:, 0:1], scale=rstd2[:, 0:1])
                        k.tt(kin[:], kin[:], gkiB[:], ALU.mult, ['kin'], ['kin'])
                        k.tt(kif[:], kin[:], bkiB[:], ALU.add, ['kin'], ['kif'])
                        k.dma('act', iks, kif[:], ['kif'], [], 'so1')
                        k.cp('pool', kib[:, 0:64], kif[:], ['kif'], ['kib'])
                        k.cp('pool', kib[:, 64:128], kif[:], ['kif'], ['kib'])
                        bt_ = nb()
                        for c_ in range(2):
                            k.tr(PSB(bt_)[:, c_ * 128:(c_ + 1) * 128], kdb[:, c_ * 128:(c_ + 1) * 128], identb[:], ['kdb'], ['ps%d' % bt_])
                        k.tr(PSB(bt_)[:, 256:384], kib[:], identb[:], ['kib'], ['ps%d' % bt_])
                        k.cp('dve', kTnew[:].rearrange("p a b -> p (a b)"), PSB(bt_)[:, 0:256], ['ps%d' % bt_], ['kTnew'])
                        k.cp('dve', kiTnew[:], PSB(bt_)[:, 256:384], ['ps%d' % bt_], ['kiTnew'])
                        for b_ in range(4):
                            k.cp('pool', kiTs[b_][:, 2048:2064], kiTnew[:, 16 * b_:16 * b_ + 16], ['kiTnew'], ['kiTs%d' % b_])
                            for t8 in range(2):
                                k.dma('sp', cif[:], cache_ik[b_, t8 * 1024:(t8 + 1) * 1024, :].rearrange("(t p) c -> p t c", p=128), [], ['cif'], 'cif')
                                k.cp('pool', cib[:, :, 0:64], cif[:], ['cif'], ['cib'])
                                k.cp('pool', cib[:, :, 64:128], cif[:], ['cif'], ['cib'])
                                bt_ = nb()
                                for tt_ in range(8):
                                    k.tr(PSB(bt_)[:, tt_ * 128:(tt_ + 1) * 128], cib[:, tt_, :], identb[:], ['cib'], ['ps%d' % bt_])
                                k.cp('dve', kiTs[b_][:, t8 * 1024:(t8 + 1) * 1024], PSB(bt_), ['ps%d' % bt_], ['kiTs%d' % b_])
                    if PR:
                        n_tiles = min(4 * g + 5, NB)
                        tail0 = 4 * g
                        tidx = 1 if g == G - 1 else 0
                    else:
                        n_tiles = 17
                        tail0 = 16
                        tidx = 1
                    n_keys = n_tiles * 128
                    nch = (n_tiles + 3) // 4

                    def kiload(ci):
                        k0 = ci * 1024
                        w_ = min(1024, n_keys - k0)
                        s_ = ci % 2
                        k.dma('sp', kich[s_][:, :w_], ki_d[:, k0:k0 + w_], [], ['kich%d' % s_], 'kich%d' % s_)

                    if PR:
                        kiload(0)
                    if not PR:
                        for b_ in range(4):
                            k.ts(wabsb[:, b_, :], wabs[:], rmaskS[:, b_:b_ + 1], None, ALU.mult, None, ['wabs'], ['wabsb'])
                    rli = 0
                    if PR:
                        npair = (n_keys + 1023) // 1024
                        wslots = [(rlw[:, :], 'rlw'), (mrg[:, :], 'mrgW'), (tmp[:, :], 'tmpW')]
                        for cp in range(npair):
                            k0 = cp * 1024
                            wtot = min(1024, n_keys - k0)
                            w0 = min(512, wtot)
                            w1 = wtot - w0
                            if cp + 1 < npair:
                                kiload(cp + 1)
                            for hh in range(8):
                                r_, rk = wslots[rli % 3]
                                rli += 1
                                bx = nb()
                                k.mm(PSF(bx)[:, :w0], qiTz[:, hh, :], kich[cp % 2][:, 0:w0], True, True, ['qiTz', 'kich%d' % (cp % 2)], ['ps%d' % bx])
                                k.act(r_[:, 0:w0], PSF(bx)[:, :w0], AF.Relu, ['ps%d' % bx, 'wabs'], [rk], scale=wabs[:, hh:hh + 1])
                                if w1 > 0:
                                    bx = nb()
                                    k.mm(PSF(bx)[:, :w1], qiTz[:, hh, :], kich[cp % 2][:, 512:512 + w1], True, True, ['qiTz', 'kich%d' % (cp % 2)], ['ps%d' % bx])
                                    k.act(r_[:, 512:512 + w1], PSF(bx)[:, :w1], AF.Relu, ['ps%d' % bx, 'wabs'], [rk], scale=wabs[:, hh:hh + 1])
                                if hh == 0:
                                    k.ts(sc[:, k0:k0 + wtot], r_[:, :wtot], wsgn[:, 0:1], None, ALU.mult, None, [rk, 'wsgn'], ['sc'])
                                else:
                                    k.stt(sc[:, k0:k0 + wtot], r_[:, :wtot], wsgn[:, hh:hh + 1], sc[:, k0:k0 + wtot], ALU.mult, ALU.add, [rk, 'wsgn', 'sc'], ['sc'])
                    for ci in (range(0) if PR else range(nch)):
                        k0 = ci * 512
                        w_ = min(512, n_keys - k0)
                        if PR and ci + 1 < nch:
                            kiload(ci + 1)
                        first = True
                        for hh in range(8):
                            for b_ in (range(1) if PR else range(4)):
                                bx = nb()
                                if PR:
                                    k.mm(PSF(bx)[:, :w_], qiTz[:, hh, :], kich[ci % 2][:, :w_], True, True, ['qiTz', 'kich%d' % (ci % 2)], ['ps%d' % bx])
                                    scl = wabs[:, hh:hh + 1]
                                    sck = 'wabs'
                                else:
                                    k.mm(PSF(bx)[:, :w_], qiTz[:, hh, :], kiTs[b_][:, k0:k0 + w_], True, True, ['qiTz', 'kiTs%d' % b_], ['ps%d' % bx])
                                    scl = wabsb[:, b_, hh:hh + 1]
                                    sck = 'wabsb'
                                rslots = [(rl[0], 'rl0'), (rl[1], 'rl1'), (mrg[:, 0:512], 'mrgA'), (mrg[:, 512:1024], 'mrgB'),
                                          (tmp[:, 0:512], 'tmpA'), (tmp[:, 512:1024], 'tmpB')]
                                r_, rk = rslots[rli % 6]
                                rli += 1
                                k.act(r_[:, :w_], PSF(bx)[:, :w_], AF.Relu, ['ps%d' % bx, sck], [rk], scale=scl)
                                if first:
                                    k.ts(sc[:, k0:k0 + w_], r_[:, :w_], wsgn[:, hh:hh + 1], None, ALU.mult, None, [rk, 'wsgn'], ['sc'])
                                    first = False
                                else:
                                    k.stt(sc[:, k0:k0 + w_], r_[:, :w_], wsgn[:, hh:hh + 1], sc[:, k0:k0 + w_], ALU.mult, ALU.add, [rk, 'wsgn', 'sc'], ['sc'])
                    k.capture()
                    k.red(rmax[:], sc[:, 0:n_keys], ALU.max, ['sc'], ['rmax'])
                    k.red(rmin[:], sc[:, 0:n_keys], ALU.min, ['sc'], ['rmin'])
                    tw = (n_tiles - tail0) * 128
                    k.tt(sc[:, tail0 * 128:tail0 * 128 + tw], sc[:, tail0 * 128:tail0 * 128 + tw], tbias[:, tidx, 0:tw], ALU.add, ['sc'], ['sc'])
                    k.tt(Wd[:], rmax[:], rmin[:], ALU.subtract, ['rmax', 'rmin'], ['Wd'])
                    k.ts(wtab[:], ctab[:], Wd[:, 0:1], None, ALU.mult, None, ['Wd'], ['wtab'])
                    k.tt(mids[:, 0:1], rmin[:], wtab[:, 0:1], ALU.add, ['rmin', 'wtab'], ['mid0'])
                    nD = (int(n_keys * 0.46) // 128) * 128
                    if nD < 256:
                        nD = n_keys
                    nA = n_keys - nD
                    for it in range(1, KIT + 1):
                        mid = mids[:, it - 1:it]
                        mk = 'mid%d' % (it - 1)
                        cn = cnts[:, it - 1:it]
                        ck_ = 'cnt%d' % it
                        k.ts(jd[:, 0:1].to_broadcast([128, nD]), sc[:, 0:nD], mid, None, ALU.is_ge, ALU.add, ['sc', mk], ['jd', ck_], accum=cn)
                        if nA > 0:
                            k.act(ja[:, 0:1].to_broadcast([128, nA]), sc[:, nD:n_keys], AF.Sign, ['sc', mk], ['ja', 'sa%d' % it],
                                  bias=mid, scale=-1.0, accum_out=sAs[:, it - 1:it])
                            k.stt(vvs[:, it - 1:it], cn, 2.0, sAs[:, it - 1:it], ALU.mult, ALU.subtract, [ck_, 'sa%d' % it], ['vv%d' % it])
                            vsrc, vkey, vthr = vvs[:, it - 1:it], 'vv%d' % it, 511.5 - nA
                        else:
                            vsrc, vkey, vthr = cn, ck_, 255.5
                        u_ = us[:, it - 1:it]
                        k.ts(u_, vsrc, vthr, wtab[:, it - 1:it], ALU.is_ge, ALU.mult, [vkey, 'wtab'], ['u%d' % it])
                        if it < KIT:
                            k.stt(mids[:, it:it + 1], u_, wtab[:, it:it + 1], mid, ALU.subtract, ALU.add, ['u%d' % it, 'wtab', mk], ['mid%d' % it])
                        else:
                            k.stt(thr[:], u_, wtab[:, it - 1:it], mid, ALU.subtract, ALU.add, ['u%d' % it, 'wtab', mk], ['thr'])
                    bisA = k.end_capture()
                    k.capture()
                    wt, wk = next_w()
                    b1 = nb()
                    for kc in range(8):
                        k.mm(PSF(b1)[0:16, 0:128], wt[:, kc, 0:16], hT[:, kc, :], kc == 0, kc == 7, ['hT', wk], ['ps%d' % b1])
                    k.cp('dve', gl[:], PSF(b1)[0:16, 0:128], ['ps%d' % b1], ['gl'])
                    bz = nb()
                    k.mm(PSF(bz), gl[:], w2[:], True, False, ['gl'], ['ps%d' % bz])
                    k.mm(PSF(bz), ones1[:], gbias[:], False, True, [], ['ps%d' % bz])
                    k.act(el[:], PSF(bz), AF.Exp, ['ps%d' % bz], ['el'], scale=-1.0)
                    k.act(el[:], el[:], AF.Ln, ['el'], ['el'], bias=one[:, 0:1], scale=1.0)
                    bb = nb()
                    for hh in range(4):
                        k.mm(PSF(bb)[:, hh * 128:(hh + 1) * 128], el[:, hh * 128:(hh + 1) * 128], tri[:], True, True, ['el'], ['ps%d' % bb])
                    k.act(eb[:], PSF(bb), AF.Exp, ['ps%d' % bb], ['eb'])
                    k.act(enb[:], PSF(bb), AF.Exp, ['ps%d' % bb], ['enb'], scale=-1.0)
                    wt, wk = next_w()
                    bq = nb()
                    projF(wt, wk, bq)
                    k.stt(qT[:].rearrange("p a b -> p (a b)"), PSF(bq), 128.0 ** -0.5, eb[:], ALU.mult, ALU.mult, ['ps%d' % bq, 'eb'], ['qT'])
                    wt, wk = next_w()
                    bk_ = nb()
                    projF(wt, wk, bk_)
                    k.tt(kTh[:].rearrange("p a b -> p (a b)"), PSF(bk_), enb[:], ALU.mult, ['ps%d' % bk_, 'enb'], ['kTh'])
                    if not PR:
                        bkt = nb()
                        projT(wt, wk, 512, PSF(bkt), 'ps%d' % bkt)
                        br_ = nb()
                        k.mm(PSF(br_), mrevS[:], el[:], True, True, ['el'], ['ps%d' % br_])
                        k.act(sg[1][:], PSF(br_), AF.Exp, ['ps%d' % br_], ['sg1'])
                        k.tt(Kt[:], PSF(bkt), sg[1][:], ALU.mult, ['ps%d' % bkt, 'sg1'], ['Kt'])
                        bd_ = nb()
                        for hh in range(4):
                            k.mm(PSF(bd_)[:, hh * 4:(hh + 1) * 4], el[:, hh * 128:(hh + 1) * 128], bsumS[:], True, True, ['el'], ['ps%d' % bd_])
                        k.act(decs[:], PSF(bd_)[:, 0:16], AF.Exp, ['ps%d' % bd_], ['decs'])
                    for half in range(2):
                        wt, wk = next_w()
                        bv = nb()
                        projT(wt, wk, 512, PSF(bv), 'ps%d' % bv)
                        k.cp('act', V[:, half * 512:(half + 1) * 512], PSF(bv), ['ps%d' % bv], ['V'])
                    if PR:
                        for m in range(4):
                            sidx = min(4 * g + m, NB - 1)
                            s_ = m % 2
                            k.dma('act', Sc[s_][:], snap[sidx], [], ['Sc%d' % s_], 'Sc%d' % s_)
                            if m == 0:
                                k.ts(Sown[:], Sc[s_][:], onehot[:, 0:1], None, ALU.mult, None, ['Sc%d' % s_], ['Sown'])
                            else:
                                k.stt(Sown[:], Sc[s_][:], onehot[:, m:m + 1], Sown[:], ALU.mult, ALU.add, ['Sc%d' % s_, 'Sown'], ['Sown'])
                        k.cp('pool', Sb[:].rearrange("p a b -> p (a b)"), Sown[:], ['Sown'], ['Sb'])
                    else:
                        for b_ in range(4):
                            for hh in range(4):
                                k.tt(qTb[b_][:, hh, :], qT[:, hh, :], cmaskS[:, b_, :], ALU.mult, ['qT'], ['qTb%d' % b_])
                    ba = nb()
                    for hh in range(4):
                        k.mm(PSF(ba)[:, hh * 128:(hh + 1) * 128], kTh[:, hh, :], qT[:, hh, :], True, True, ['kTh', 'qT'], ['ps%d' % ba])
                    k.tt(AT[:].rearrange("p a b -> p (a b)"), PSF(ba), maskf[:], ALU.mult, ['ps%d' % ba], ['AT'])
                    bo = [nb(), nb()]
                    for hh in range(4):
                        oap = PSF(bo[hh // 2])[:, (hh % 2) * 256:(hh % 2 + 1) * 256]
                        okey = 'ps%d' % bo[hh // 2]
                        if PR:
                            k.mm(oap, qT[:, hh, :], Sb[:, hh, :], True, False, ['qT', 'Sb'], [okey])
                        else:
                            for b_ in range(4):
                                k.mm(oap, qTb[b_][:, hh, :], S0b[b_][:, hh, :], b_ == 0, False, ['qTb%d' % b_], [okey])
                        k.mm(oap, AT[:, hh, :], V[:, hh * 256:(hh + 1) * 256], False, True, ['AT', 'V'], [okey])
                    for hh in range(4):
                        oap = PSF(bo[hh // 2])[:, (hh % 2) * 256:(hh % 2 + 1) * 256]
                        k.act(jq[:, 0:1].to_broadcast([128, 256]), oap, AF.Square, ['ps%d' % bo[hh // 2]], ['jq', 'ss%d' % hh], accum_out=ss[:, hh:hh + 1])
                    k.act(rs[:], ss[:], AF.Ln, ['ss0', 'ss1', 'ss2', 'ss3'], ['rs'], bias=eps[:, 0:1], scale=1.0 / 256)
                    k.act(rs[:], rs[:], AF.Exp, ['rs'], ['rs'], scale=-0.5)
                    for cc in range(2):
                        wt, wk = next_w()
                        bg = nb()
                        projT(wt, wk, 512, PSF(bg), 'ps%d' % bg)
                        k.act(sg[cc][:], PSF(bg), AF.Silu, ['ps%d' % bg], ['sg%d' % cc])
                        for hh in range(2):
                            hd = 2 * cc + hh
                            oap = PSF(bo[hd // 2])[:, (hd % 2) * 256:(hd % 2 + 1) * 256]
                            k.stt(tmp[:, hd * 256:(hd + 1) * 256], oap, rs[:, hd:hd + 1], gnB[:], ALU.mult, ALU.mult,
                                  ['ps%d' % bo[hd // 2], 'rs'], ['tmp'])
                        k.tt(yain[:, cc * 512:(cc + 1) * 512], tmp[:, cc * 512:(cc + 1) * 512], sg[cc][:], ALU.mult, ['tmp', 'sg%d' % cc], ['yain'])
                    transpose8(yain, 'yain', yT, 'yT')
                    for cc in range(2):
                        wt, wk = next_w()
                        by = nb()
                        for kc in range(8):
                            k.mm(PSF(by), yT[:, kc, :], wt[:, kc, :], kc == 0, kc == 7, ['yT', wk], ['ps%d' % by])
                        wt, wk = next_w()
                        bm = nb()
                        projT(wt, wk, 512, PSF(bm), 'ps%d' % bm)
                        k.dma('act', gtb[cc][:], gate_b[0:1, cc * 512:(cc + 1) * 512].partition_broadcast(128), [], ['gtb%d' % cc], 'gtb%d' % cc)
                        k.tt(sg[cc][:], PSF(bm), gtb[cc][:], ALU.add, ['ps%d' % bm, 'gtb%d' % cc], ['sg%d' % cc])
                        k.act(sg[cc][:], sg[cc][:], AF.Sigmoid, ['sg%d' % cc], ['sg%d' % cc])
                        k.tt(mrg[:, cc * 512:(cc + 1) * 512], PSF(by), sg[cc][:], ALU.mult, ['ps%d' % by, 'sg%d' % cc], ['mrg'])
                    if not PR:
                        for b_ in range(4):
                            s_ = b_ % 2
                            k.dma('sp', S0f[s_][:], state[b_].rearrange("h p e -> p h e"), [], ['S0f%d' % s_], 'S0f%d' % s_)
                            k.ts(Ktb[s_][:], Kt[:], rmaskS[:, b_:b_ + 1], None, ALU.mult, None, ['Kt'], ['Ktb%d' % s_])
                            for hp in range(2):
                                bs_ = nb()
                                for hh in range(2):
                                    hd = hp * 2 + hh
                                    k.mm(PSF(bs_)[:, hh * 256:(hh + 1) * 256], Ktb[s_][:, hd * 128:(hd + 1) * 128], V[:, hd * 256:(hd + 1) * 256],
                                         True, True, ['Ktb%d' % s_, 'V'], ['ps%d' % bs_])
                                for hh in range(2):
                                    hd = hp * 2 + hh
                                    k.stt(S0f[s_][:, hd, :], S0f[s_][:, hd, :], decs[:, hd * 4 + b_:hd * 4 + b_ + 1], PSF(bs_)[:, hh * 256:(hh + 1) * 256],
                                          ALU.mult, ALU.add, ['decs', 'ps%d' % bs_, 'S0f%d' % s_], ['S0f%d' % s_])
                            k.dma('act', gla_s[b_], S0f[s_][:].rearrange("p a b -> p (a b)"), ['S0f%d' % s_], [], 'Sno%d' % s_)

                    glaB = k.end_capture()
                    k.merge(bisA, glaB)

                    if PR:
                        def kvload(ci):
                            k0 = ci * 512
                            nt = min(4, n_tiles - ci * 4)
                            s_ = ci % 2
                            k.dma('sp', kTch[s_][:, :, :nt * 128], kT_d[:, :, k0:k0 + nt * 128], [], ['kTch%d' % s_], 'kTch%d' % s_)
                            k.dma('sp', vch[s_][:, :nt, :], v_d[:, ci * 4:ci * 4 + nt, :], [], ['vch%d' % s_], 'vch%d' % s_)
                        kvload(0)
                        li = 0
                        groups = []
                        for kt in range(n_tiles):
                            ci, tl_ = kt // 4, kt % 4
                            mb_ = mbt[kt % 3]
                            mbk = 'mbt%d' % (kt % 3)
                            for gg in range(4):
                                bl = 4 + (li % 4)
                                p_ = pT[li % 3]
                                pk = 'pT%d' % (li % 3)
                                li += 1
                                k.capture()
                                if gg == 0:
                                    if tl_ == 1 and ci + 1 < nch:
                                        kvload(ci + 1)
                                    k.ts(mb_[:], sc[:, kt * 128:(kt + 1) * 128], thr[:, 0:1], NEG, ALU.is_lt, ALU.mult, ['sc', 'thr'], [mbk])
                                k.mm(PSF(bl), kTch[ci % 2][:, gg // 2, tl_ * 128:(tl_ + 1) * 128], QTz[:, gg, :], True, False,
                                     ['kTch%d' % (ci % 2), 'QTz'], ['ps%d' % bl])
                                k.mm(PSF(bl), mb_[:], idrepb[:], False, True, [mbk], ['ps%d' % bl])
                                k.act(p_[:], PSF(bl), AF.Exp, ['ps%d' % bl], [pk], scale=0.125)
                                s1 = k.end_capture()
                                k.capture()
                                k.mm(PSF(gg)[0:65, :], vch[ci % 2][:, tl_, gg * 65:(gg + 1) * 65], p_[:], kt == 0, kt == n_tiles - 1,
                                     ['vch%d' % (ci % 2), pk], ['ps%d' % gg])
                                s2 = k.end_capture()
                                groups.append((s1, s2))
                        SK = 2
                        for idx in range(len(groups) + SK):
                            if idx < len(groups):
                                P.ops.extend(groups[idx][0])
                            if idx >= SK:
                                P.ops.extend(groups[idx - SK][1])
                        for gg in range(4):
                            o_ = oT[gg % 2]
                            ok_ = 'oT%d' % (gg % 2)
                            k.cp('act', o_[:], PSF(gg)[0:65, :], ['ps%d' % gg], [ok_])
                            for r_ in range(4):
                                k.tr(PSF(4 + gg)[:, r_ * 65:(r_ + 1) * 65], o_[0:65, r_ * 128:(r_ + 1) * 128], identf[0:65, 0:65], [ok_], ['ps%d' % (4 + gg)])
                        NPT = 128
                    else:
                        mball = junk
                        for b_ in range(4):
                            k.cp('pool', QTs[:, b_, :, :].rearrange("p g (r t) -> p g r t", r=4),
                                 QTz[:].rearrange("p g (r t) -> p g r t", r=4)[:, :, :, 16 * b_:16 * b_ + 16], ['QTz'], ['QTs'])
                        k.ts(mball[:], sc[:, 0:2176], thr[:, 0:1], NEG, ALU.is_lt, ALU.mult, ['sc', 'thr'], ['junk'])
                        li = 0
                        for b_ in range(4):
                            k.cp('pool', kTs1[:, :, 2048:2064], kTnew[:, :, 16 * b_:16 * b_ + 16], ['kTnew'], ['kTs'])
                            bs_ = nb() % 4 + 4
                            k.mm(PSF(bs_)[:, 0:256], selSb[:, b_, :], vnb[:], True, True, ['vnb'], ['ps%d' % bs_])
                            k.cp('act', vexts1[:, 16, :].rearrange("p (g d) -> p g d", g=4)[:, :, 0:64],
                                 PSF(bs_)[:, 0:256].rearrange("p (g d) -> p g d", g=4), ['ps%d' % bs_], ['vexts'])
                            for t8 in range(2):
                                k.dma('sp', ckf[:], cache_k[b_, t8 * 1024:(t8 + 1) * 1024, :].rearrange("(t p) c -> p t c", p=128), [], ['ckf'], 'ckf')
                                k.cp('pool', ckb[:], ckf[:], ['ckf'], ['ckb'])
                                for t4 in range(2):
                                    bt_ = nb() % 4 + 4
                                    for tt_ in range(4):
                                        for c_ in range(2):
                                            col = (tt_ * 2 + c_) * 128
                                            k.tr(PSB(bt_)[:, col:col + 128], ckb[:, t4 * 4 + tt_, c_ * 128:(c_ + 1) * 128], identb[:], ['ckb'], ['ps%d' % bt_])
                                    c0_ = t8 * 1024 + t4 * 512
                                    k.cp('dve', kTs1[:, :, c0_:c0_ + 512].rearrange("p c (t s) -> p c t s", t=4),
                                         PSB(bt_).rearrange("p (t c s) -> p c t s", t=4, c=2), ['ps%d' % bt_], ['kTs'])
                                k.dma('sp', ckf[:], cache_v[b_, t8 * 1024:(t8 + 1) * 1024, :].rearrange("(t p) c -> p t c", p=128), [], ['ckf'], 'ckf')
                                k.cp('pool', vexts1[:, t8 * 8:(t8 + 1) * 8, :].rearrange("p t (g d) -> p t g d", g=4)[:, :, :, 0:64],
                                     ckf[:].rearrange("p t (g d) -> p t g d", g=4), ['ckf'], ['vexts'])
                            sgroups = []
                            for gg in range(4):
                                ob = b_ // 2
                                ocol = ((b_ % 2) * 4 + gg) * 64
                                qsel = QTs[:, b_, gg, :]
                                for kt in range(17):
                                    bl = 4 + (li % 4)
                                    p_ = pTs[li % 3]
                                    pk = 'pTs%d' % (li % 3)
                                    li += 1
                                    k.capture()
                                    k.mm(PSF(bl)[:, 0:64], kTs1[:, gg // 2, kt * 128:(kt + 1) * 128], qsel, True, False,
                                         ['kTs', 'QTs'], ['ps%d' % bl])
                                    k.mm(PSF(bl)[:, 0:64], mball[:, kt * 128:(kt + 1) * 128], idrepSb[:, b_, :], False, True, ['junk'], ['ps%d' % bl])
                                    k.act(p_[:], PSF(bl)[:, 0:64], AF.Exp, ['ps%d' % bl], [pk], scale=0.125)
                                    s1_ = k.end_capture()
                                    k.capture()
                                    k.mm(PSF(ob)[0:65, ocol:ocol + 64], vexts1[:, kt, gg * 65:(gg + 1) * 65], p_[:], kt == 0, kt == 16,
                                         ['vexts', pk], ['ps%d' % ob])
                                    s2_ = k.end_capture()
                                    sgroups.append((s1_, s2_))
                            SKS = 2
                            for idx in range(len(sgroups) + SKS):
                                if idx < len(sgroups):
                                    P.ops.extend(sgroups[idx][0])
                                if idx >= SKS:
                                    P.ops.extend(sgroups[idx - SKS][1])
                        oTs = sc[0:65, 0:1024]
                        ov = oTs.rearrange("p (g r b t) -> p b g r t", g=4, r=4, b=4)
                        for ob in range(2):
                            k.cp('act', ov[:, 2 * ob:2 * ob + 2], PSF(ob)[0:65, :].rearrange("p (b g r t) -> p b g r t", b=2, g=4, r=4),
                                 ['ps%d' % ob, 'junk'], ['sc'])
                        for gg in range(4):
                            for r_ in range(4):
                                c0_ = (gg * 4 + r_) * 64
                                k.tr(PSF(4 + gg)[0:64, r_ * 65:(r_ + 1) * 65], oTs[:, c0_:c0_ + 64], identf[0:65, 0:65], ['sc'], ['ps%d' % (4 + gg)])
                        NPT = 64
                    for gg in range(4):
                        pv4 = PSF(4 + gg)[0:NPT, 0:260].rearrange("p (r c) -> p r c", c=65)
                        k.recip(rd[0:NPT, 4 * gg:4 * gg + 4], pv4[:, :, 64], ['ps%d' % (4 + gg)], ['rd'])
                        for r_ in range(4):
                            hd = 4 * gg + r_
                            k.ts(tmp[0:NPT, hd * 64:(hd + 1) * 64], pv4[:, r_, 0:64], rd[0:NPT, hd:hd + 1], None, ALU.mult, None,
                                 ['ps%d' % (4 + gg), 'rd'], ['tmp'])
                    for cc in range(2):
                        wt, wk = next_w()
                        bg = nb()
                        projT(wt, wk, 512, PSF(bg), 'ps%d' % bg)
                        k.act(sg[cc][:], PSF(bg), AF.Silu, ['ps%d' % bg], ['sg%d' % cc])
                        k.tt(yain[0:NPT, cc * 512:(cc + 1) * 512], tmp[0:NPT, cc * 512:(cc + 1) * 512], sg[cc][0:NPT, :], ALU.mult,
                             ['tmp', 'sg%d' % cc], ['yain'])
                    transpose8(yain, 'yain', yT, 'yT')
                    for cc in range(2):
                        wt, wk = next_w()
                        by = nb()
                        for kc in range(8):
                            k.mm(PSF(by), yT[:, kc, :], wt[:, kc, :], kc == 0, kc == 7, ['yT', wk], ['ps%d' % by])
                        wt, wk = next_w()
                        bm = nb()
                        projT(wt, wk, 512, PSF(bm), 'ps%d' % bm)
                        k.dma('act', gtb[cc][:], gate_b[0:1, 1024 + cc * 512:1024 + (cc + 1) * 512].partition_broadcast(128), [], ['gtb%d' % cc], 'gtb%d' % cc)
                        k.tt(sg[cc][:], PSF(bm), gtb[cc][:], ALU.add, ['ps%d' % bm, 'gtb%d' % cc], ['sg%d' % cc])
                        k.act(sg[cc][:], sg[cc][:], AF.Sigmoid, ['sg%d' % cc], ['sg%d' % cc])
                        k.tt(sg[cc][:], PSF(by), sg[cc][:], ALU.mult, ['ps%d' % by, 'sg%d' % cc], ['sg%d' % cc])
                        k.tt(mrg[:, cc * 512:(cc + 1) * 512], mrg[:, cc * 512:(cc + 1) * 512], sg[cc][:], ALU.add, ['mrg', 'sg%d' % cc], ['mrg'])
                    k.cp('pool', yain[:], mrg[:], ['mrg'], ['yain'])
                    transpose8(yain, 'yain', yT, 'yT')
                    for cc in range(2):
                        wt, wk = next_w()
                        by = nb()
                        for kc in range(8):
                            k.mm(PSF(by), yT[:, kc, :], wt[:, kc, :], kc == 0, kc == 7, ['yT', wk], ['ps%d' % by])
                        k.stt(mrg[:, cc * 512:(cc + 1) * 512], h[:, cc * 512:(cc + 1) * 512], ALPHA, PSF(by), ALU.mult, ALU.add, ['h', 'ps%d' % by], ['mrg'])
                    layernorm(mrg[:], 'mrg', g2B[:], b2B[:], tl, 'c', out_f32=yout[:], out_f32_key='xo')
                    ydst = y_own[g * 128:(g + 1) * 128, :] if PR else ys
                    k.dma('act', ydst, yout[:], ['xo'], [], 'youto')

                for g in (range(G) if PR else [0]):
                    own_block(g)
            P.barrier()

        if "B" in phases:
            phase_own('P')
        if "S" in phases:
            phase_own('S')

        P.emit(nc, top)
    return nc


def _consts(j, G):
    c = {}
    p = np.arange(128)
    c["c_ident"] = np.eye(128, dtype=np.float32)
    jj, ii = np.meshgrid(p, p, indexing="ij")
    c["c_triA"] = np.where((jj > ii) & (jj // 64 == ii // 64), -1.0 / 16, 0.0).astype(np.float32)
    c["c_blkA"] = np.where(p[:, None] // 64 == np.arange(2)[None, :], -1.0 / 16, 0.0).astype(np.float32)
    c["c_triB"] = np.where(jj <= ii, -1.0 / 16, 0.0).astype(np.float32)
    mB = (jj <= ii).astype(np.float32)
    c["c_maskB"] = np.ascontiguousarray(np.broadcast_to(mB[:, None, :], (128, 4, 128)))
    same = (jj // 16 == ii // 16)
    c["c_triS"] = np.where((jj <= ii) & same, -1.0 / 16, 0.0).astype(np.float32)
    mS = ((jj <= ii) & same).astype(np.float32)
    c["c_maskS"] = np.ascontiguousarray(np.broadcast_to(mS[:, None, :], (128, 4, 128)))
    c["c_mrevS"] = np.where((jj > ii) & same, -1.0 / 16, 0.0).astype(np.float32)
    c["c_bsumS"] = np.where(p[:, None] // 16 == np.arange(4)[None, :], -1.0 / 16, 0.0).astype(np.float32)
    cm = (p[None, :] // 16 == np.arange(4)[:, None]).astype(np.float32)
    c["c_cmaskS"] = np.ascontiguousarray(np.broadcast_to(cm[None], (128, 4, 128)))
    c["c_rmaskS"] = (p[:, None] // 16 == np.arange(4)[None, :]).astype(np.float32)
    c["c_idrep"] = np.ascontiguousarray(np.tile(np.eye(128, dtype=np.float32), (1, 4)))
    ids = np.zeros((128, 4, 64), np.float32)
    sel = np.zeros((128, 4, 128), np.float32)
    for b in range(4):
        for t in range(16):
            for r in range(4):
                ids[16 * b + t, b, r * 16 + t] = 1.0
            sel[16 * b + t, b, t] = 1.0
    c["c_idrepS"] = ids
    c["c_selS"] = sel
    c["c_ctab"] = np.ascontiguousarray(np.broadcast_to((0.5 ** np.arange(1, KIT + 2))[None, :], (128, KIT + 1))).astype(np.float32)
    tb = np.zeros((128, 2, 640), np.float32)
    kk = np.arange(640)
    for r in range(128):
        lim = 128 * j + (16 if r < 16 else (80 if r < 80 else 144))
        tb[r, 0, :] = np.where(kk < lim, 0.0, -1e30)
    tb[:, 1, :] = np.where(kk < 16, 0.0, -1e30)[None, :]
    c["c_tbias"] = tb
    oh = np.zeros((128, 4), np.float32)
    oh[:, j] = 1.0
    c["c_onehot"] = oh
    c["c_tailmask"] = (p < 16).astype(np.float32)[:, None]
    return c


def _dq_perm():
    perm = np.zeros(1024, np.int64)
    n = 0
    for m in range(2):
        for r in range(4):
            for half in range(2):
                g = 2 * m + half
                for d in range(64):
                    perm[n] = (g * 4 + r) * 64 + d
                    n += 1
    return perm


def prep(inp, SEQ):
    T, NB, G = geometry(SEQ)
    f = lambda a: np.ascontiguousarray(np.asarray(a, dtype=np.float32))
    w_in = f(inp["w_in"])[0].copy()
    w_in[:, C_DQ:C_DQ + 1024] = w_in[:, C_DQ:C_DQ + 1024][:, _dq_perm()]
    w3 = np.ascontiguousarray(np.stack([f(inp["w_gla"])[0], f(inp["w_dsa"])[0], f(inp["w_out"])[0]], 0))
    shared = dict(
        w_in=np.ascontiguousarray(w_in), w3=w3,
        ln_in_g=f(inp["ln_in_g"]).reshape(1, D), ln_in_b=f(inp["ln_in_b"]).reshape(1, D),
        ln_g=f(inp["ln_g"]).reshape(1, D), ln_b=f(inp["ln_b"]).reshape(1, D),
        gate_b=f(inp["gate_b"]).reshape(1, 2048), gla_gate_b=f(inp["gla_gate_b"]).reshape(1, 512),
        gla_norm_g=f(inp["gla_norm_g"]).reshape(1, 256),
        idx_kn_g=f(inp["idx_kn_g"]).reshape(1, 64), idx_kn_b=f(inp["idx_kn_b"]).reshape(1, 64),
        gla_w2=f(inp["gla_w2"]).reshape(16, 512))
    xp = f(inp["x_prompt"]); meta = f(inp["meta"]); xsm = f(inp["x_sample"])
    ck = f(inp["cache_k"])[0]; cv = f(inp["cache_v"])[0]; cik = f(inp["cache_idx_k"])[0]; stt = f(inp["state_gla"])[0]
    maps = []
    for c in range(8):
        b, j = c // 4, c % 4
        xall = np.zeros((4 * G * 128, D), np.float32)
        xall[:16] = meta
        xall[16:T] = xp[b]
        xown = np.ascontiguousarray(xall.reshape(G, 4, 128, D)[:, j].reshape(G * 128, D))
        xs_ = np.zeros((128, D), np.float32)
        xs_[:64] = xsm[4 * c:4 * c + 4].reshape(64, D)
        m = dict(shared)
        m.update(_consts(j, G))
        m.update(xall=np.ascontiguousarray(xall[:NB * 128]), xown=xown, xs=xs_,
                 cache_k=np.ascontiguousarray(ck[4 * c:4 * c + 4].reshape(4, 2048, 256)),
                 cache_v=np.ascontiguousarray(cv[4 * c:4 * c + 4].reshape(4, 2048, 256)),
                 cache_ik=np.ascontiguousarray(cik[4 * c:4 * c + 4]),
                 state=np.ascontiguousarray(stt[4 * c:4 * c + 4]))
        maps.append(m)
    return maps


def gather(res, SEQ):
    T, NB, G = geometry(SEQ)
    R = res.results
    yp = np.zeros((2, 4 * G * 128, D), np.float32)
    for c in range(8):
        b, j = c // 4, c % 4
        yp[b].reshape(G, 4, 128, D)[:, j] = R[c]["y_own"].reshape(G, 128, D)
    y_prompt = np.ascontiguousarray(yp[:, 16:T])
    y_sample = np.concatenate([R[c]["ys"][:64].reshape(4, 16, D) for c in range(8)], 0)
    k_prompt = np.stack([R[4 * b]["kp"][:T].reshape(T, 4, 64) for b in range(2)], 0)[None]
    v_prompt = np.stack([R[4 * b]["vp"][:T].reshape(T, 4, 64) for b in range(2)], 0)[None]
    ik_prompt = np.stack([R[4 * b]["ikp"][:T] for b in range(2)], 0)[None]
    gla_prompt = np.stack([R[4 * b]["gla_p"].reshape(128, 4, 256).transpose(1, 0, 2) for b in range(2)], 0)[None]
    k_sample = np.concatenate([R[c]["ks"][:64].reshape(4, 16, 4, 64) for c in range(8)], 0)[None]
    v_sample = np.concatenate([R[c]["vs"][:64].reshape(4, 16, 4, 64) for c in range(8)], 0)[None]
    ik_sample = np.concatenate([R[c]["iks"][:64].reshape(4, 16, 64) for c in range(8)], 0)[None]
    gla_sample = np.concatenate([R[c]["gla_s"].reshape(4, 128, 4, 256).transpose(0, 2, 1, 3) for c in range(8)], 0)[None]
    outs = (y_prompt, y_sample, k_prompt, v_prompt, ik_prompt, gla_prompt, k_sample, v_sample, ik_sample, gla_sample)
    return tuple(np.ascontiguousarray(o, dtype=np.float32) for o in outs)


_NC_CACHE = {}


def run(inputs, SEQ, phases="0ABS"):
    key = (SEQ, phases)
    if key not in _NC_CACHE:
        _NC_CACHE[key] = build(SEQ, phases)
    nc = _NC_CACHE[key]
    maps = prep(inputs, SEQ)
    res = run_bass_kernel_spmd(nc, maps, core_ids=list(range(8)))
    return gather(res, SEQ)


def kernel(**inputs):
    SEQ = int(np.asarray(inputs["x_prompt"]).shape[1])
    return run(inputs, SEQ)
```

```python
from contextlib import ExitStack
import numpy as np
import concourse.bass as bass
import concourse.mybir as mybir
from concourse.bass_utils import run_bass_kernel_spmd

F32 = mybir.dt.float32
BF16 = mybir.dt.bfloat16
AF = mybir.ActivationFunctionType
ALU = mybir.AluOpType
AX = mybir.AxisListType

D = 1024
NEG = -30000.0
KIT = 22
IDX_W_SCALE = (8 ** -0.5) * (64 ** -0.5)
ALPHA = 2.0 ** 0.25
EPS = 1e-5
C_GQ, C_GK, C_GV, C_GLOW, C_GR, C_DQ, C_DK, C_DV, C_IQ, C_IK, C_IW, C_DZ, C_MA, C_MB = (
    0, 512, 1024, 2048, 2064, 3088, 4112, 4368, 4624, 5136, 5200, 5208, 6232, 7256)
IN_COLS = 8280


class Prog:
    def __init__(self):
        self.ops = []

    def add(self, eng, fn, r=(), w=(), dsem=None):
        r = list(r)
        w = list(w)
        for b in list(r):
            if isinstance(b, str) and b.startswith('ps'):
                r.remove(b)
                if b not in w:
                    w.append(b)
        self.ops.append(dict(eng=eng, fn=fn, r=tuple(r), w=tuple(w), dsem=dsem))

    def barrier(self):
        self.ops.append(dict(eng='barrier', fn=None, r=(), w=(), dsem=None))

    def analyze(self):
        ops = self.ops
        last_w, readers = {}, {}
        last_of = {}
        for i, op in enumerate(ops):
            if op['eng'] == 'barrier':
                for e_, j in last_of.items():
                    ops[j]['needed'] = True
                last_w, readers = {}, {}
                op['deps'] = set()
                continue
            deps = set()
            for b in op['r']:
                if b in last_w:
                    deps.add(('raw', last_w[b]))
            for b in op['w']:
                if b in last_w:
                    deps.add(('waw', last_w[b]))
                for rr in readers.get(b, ()):
                    deps.add(('war', rr))
            for b in op['r']:
                readers.setdefault(b, []).append(i)
            for b in op['w']:
                last_w[b] = i
                readers[b] = []
            keep = set()
            for kind, j in deps:
                if j == i:
                    continue
                pj = ops[j]
                if pj['dsem'] is None and op['dsem'] is None and pj['eng'] == op['eng']:
                    if op['eng'] == 'pe' or kind == 'war':
                        continue
                keep.add(j)
            op['deps'] = keep
            for j in keep:
                ops[j]['needed'] = True
            if op['dsem'] is None:
                last_of[op['eng']] = i
        for e_, j in last_of.items():
            ops[j]['needed'] = True
        cnt = {}
        for op in ops:
            if op['eng'] == 'barrier':
                continue
            if op['dsem'] is not None:
                k = 'D:' + op['dsem']
                cnt[k] = cnt.get(k, 0) + 16
                op['sem'] = k
                op['val'] = cnt[k]
            elif op.get('needed'):
                k = 'E:' + op['eng']
                cnt[k] = cnt.get(k, 0) + 1
                op['sem'] = k
                op['val'] = cnt[k]
        waited = {}
        running = {}
        pending = {}
        for op in ops:
            if op['eng'] == 'barrier':
                for e_ in ('pe', 'act', 'dve', 'pool', 'sp'):
                    pending[e_] = dict(running)
                continue
            ws = {}
            if pending.get(op['eng']):
                ws.update(pending[op['eng']])
                pending[op['eng']] = None
            for j in op['deps']:
                pj = ops[j]
                ws[pj['sem']] = max(ws.get(pj['sem'], 0), pj['val'])
            wl = []
            we = waited.setdefault(op['eng'], {})
            for k, v in ws.items():
                if we.get(k, 0) >= v:
                    continue
                we[k] = v
                wl.append((k, v))
            op['waits'] = wl
            if op.get('sem') is not None:
                running[op['sem']] = op['val']
        self.totals = cnt
        return cnt

    def emit(self, nc, es):
        cnt = self.analyze()
        sems = {}
        for k in cnt:
            sems[k] = es.enter_context(nc.semaphore(k.replace(':', '_')))
        block = es.enter_context(nc.Block())
        ops = self.ops

        def run(engname):
            def f(e):
                for op in ops:
                    if op['eng'] != engname:
                        continue
                    for k, v in op['waits']:
                        e.wait_ge(sems[k], v)
                    ins = op['fn'](e)
                    if op.get('sem') is not None:
                        ins.then_inc(sems[op['sem']], 16 if op['dsem'] is not None else 1)
                for k, v in cnt.items():
                    e.wait_ge(sems[k], v)
            return f

        block.sync(run('sp'))
        block.scalar(run('act'))
        block.vector(run('dve'))
        block.gpsimd(run('pool'))
        block.tensor(run('pe'))


class KB:
    def __init__(self, nc):
        self.nc = nc
        self.P = Prog()
        self.rot = 0

    def capture(self):
        self._saved = self.P.ops
        self.P.ops = []

    def end_capture(self):
        l = self.P.ops
        self.P.ops = self._saved
        return l

    def merge(self, A, B):
        out = []
        ia = ib = 0
        na, nb_ = max(len(A), 1), max(len(B), 1)
        while ia < len(A) or ib < len(B):
            if ib >= len(B) or (ia < len(A) and ia * nb_ <= ib * na):
                out.append(A[ia]); ia += 1
            else:
                out.append(B[ib]); ib += 1
        self.P.ops.extend(out)

    def act(self, out, in_, func, r, w, **kw):
        self.P.add('act', lambda e: e.activation(out=out, in_=in_, func=func, **kw), r, w)

    def ts(self, out, in0, s1, s2, op0, op1, r, w, eng='dve', accum=None):
        if accum is None:
            if op1 is None:
                self.P.add(eng, lambda e: e.tensor_scalar(out=out, in0=in0, scalar1=s1, scalar2=None, op0=op0), r, w)
            else:
                self.P.add(eng, lambda e: e.tensor_scalar(out=out, in0=in0, scalar1=s1, scalar2=s2, op0=op0, op1=op1), r, w)
        else:
            self.P.add(eng, lambda e: e.tensor_scalar(out=out, in0=in0, scalar1=s1, scalar2=s2, op0=op0, op1=op1,
                                                      accum_out=accum), r, w)

    def tt(self, out, in0, in1, op, r, w, eng='dve'):
        self.P.add(eng, lambda e: e.tensor_tensor(out=out, in0=in0, in1=in1, op=op), r, w)

    def stt(self, out, in0, scalar, in1, op0, op1, r, w):
        self.P.add('dve', lambda e: e.scalar_tensor_tensor(out=out, in0=in0, scalar=scalar, in1=in1, op0=op0, op1=op1), r, w)

    def cp(self, eng, out, in_, r, w):
        if eng == 'act':
            self.P.add('act', lambda e: e.copy(out=out, in_=in_), r, w)
        else:
            self.P.add(eng, lambda e: e.tensor_copy(out=out, in_=in_), r, w)

    def memset(self, eng, ap, val, w):
        self.P.add(eng, lambda e: e.memset(ap, val), (), w)

    def mm(self, out, lhsT, rhs, start, stop, r, w):
        self.P.add('pe', lambda e: e.matmul(out, lhsT=lhsT, rhs=rhs, start=start, stop=stop), r, w)

    def tr(self, out, in_, ident, r, w):
        self.P.add('pe', lambda e: e.transpose(out=out, in_=in_, identity=ident), r, w)

    def dma(self, q, out, in_, r, w, dsem):
        self.P.add(q, lambda e: e.dma_start(out=out, in_=in_), r, w, dsem=dsem)

    def red(self, out, in_, op, r, w):
        self.P.add('dve', lambda e: e.tensor_reduce(out=out, in_=in_, axis=AX.X, op=op), r, w)

    def recip(self, out, in_, r, w):
        self.P.add('dve', lambda e: e.reciprocal(out=out, in_=in_), r, w)

    def bn_stats(self, out, in_, r, w):
        self.P.add('dve', lambda e: e.bn_stats(out=out, in_=in_), r, w)

    def bn_aggr(self, out, in_, r, w):
        self.P.add('dve', lambda e: e.bn_aggr(out=out, in_=in_), r, w)


def geometry(SEQ):
    T = SEQ + 16
    NB = T // 128 + 1
    assert T == 128 * (NB - 1) + 16 and NB % 4 == 1
    G = (NB + 3) // 4
    return T, NB, G


def build(SEQ, phases="0ABS"):
    T, NB, G = geometry(SEQ)
    NKMAX = NB * 128
    nc = bass.Bass("TRN2", target_bir_lowering=False)

    def din(name, shape, dt=F32):
        return nc.dram_tensor(name, list(shape), dt, kind="ExternalInput").ap()

    def dout(name, shape, dt=F32):
        return nc.dram_tensor(name, list(shape), dt, kind="ExternalOutput").ap()

    def dscr(name, shape, dt):
        return nc.dram_tensor(name, list(shape), dt, kind="Internal").ap()

    xall = din("xall", [NB * 128, D])
    xown = din("xown", [G * 128, D])
    xs = din("xs", [128, D])
    w_in = din("w_in", [D, IN_COLS])
    w3 = din("w3", [3, D, D])
    ln_in_g = din("ln_in_g", [1, D]); ln_in_b = din("ln_in_b", [1, D])
    ln_g = din("ln_g", [1, D]); ln_b = din("ln_b", [1, D])
    gate_b = din("gate_b", [1, 2048])
    gla_gate_b = din("gla_gate_b", [1, 512])
    gla_norm_g = din("gla_norm_g", [1, 256])
    idx_kn_g = din("idx_kn_g", [1, 64]); idx_kn_b = din("idx_kn_b", [1, 64])
    gla_w2 = din("gla_w2", [16, 512])
    cache_k = din("cache_k", [4, 2048, 256]); cache_v = din("cache_v", [4, 2048, 256])
    cache_ik = din("cache_ik", [4, 2048, 64])
    state = din("state", [4, 4, 128, 256])
    c_ident = din("c_ident", [128, 128])
    c_triA = din("c_triA", [128, 128]); c_blkA = din("c_blkA", [128, 2])
    c_triB = din("c_triB", [128, 128]); c_maskB = din("c_maskB", [128, 4, 128])
    c_triS = din("c_triS", [128, 128]); c_maskS = din("c_maskS", [128, 4, 128])
    c_mrevS = din("c_mrevS", [128, 128]); c_bsumS = din("c_bsumS", [128, 4])
    c_cmaskS = din("c_cmaskS", [128, 4, 128]); c_rmaskS = din("c_rmaskS", [128, 4])
    c_idrep = din("c_idrep", [128, 512]); c_idrepS = din("c_idrepS", [128, 4, 64])
    c_selS = din("c_selS", [128, 4, 128])
    c_ctab = din("c_ctab", [128, KIT + 1])
    c_tbias = din("c_tbias", [128, 2, 640])
    c_onehot = din("c_onehot", [128, 4])
    c_tailmask = din("c_tailmask", [128, 1])

    y_own = dout("y_own", [G * 128, D])
    kp = dout("kp", [NB * 128, 256]); vp = dout("vp", [NB * 128, 256]); ikp = dout("ikp", [NB * 128, 64])
    gla_p = dout("gla_p", [128, 1024])
    ys = dout("ys", [128, D]); ks = dout("ks", [128, 256]); vs = dout("vs", [128, 256]); iks = dout("iks", [128, 64])
    gla_s = dout("gla_s", [4, 128, 1024])

    wbf = dscr("wbf", [D, IN_COLS], BF16)
    wbf3 = dscr("wbf3", [3, D, D], BF16)
    kT_d = dscr("kT_d", [128, 2, NKMAX], BF16)
    v_d = dscr("v_d", [128, NB, 260], BF16)
    ki_d = dscr("ki_d", [128, NKMAX], BF16)
    snap = dscr("snap", [NB, 128, 1024], F32)

    k = KB(nc)
    P = k.P
    top = ExitStack()
    with top:
        pst = [top.enter_context(nc.psum_tensor("ps%d" % i, [128, 512], F32)) for i in range(8)]

        def PSF(i):
            return pst[i][:]

        def PSB(i):
            return pst[i][:].bitcast(BF16)

        if "0" in phases:
            with ExitStack() as es:
                def sb(name, shape, dt):
                    return es.enter_context(nc.sbuf_tensor(name, shape, dt))
                wst = [sb("wst%d" % s, [128, 8, 512], F32) for s in range(2)]
                wcb = [sb("wcb%d" % s, [128, 8, 512], BF16) for s in range(2)]
                jobs = []
                for c in range(17):
                    c0 = c * 512
                    n = min(512, IN_COLS - c0)
                    jobs.append((w_in[:, c0:c0 + n], wbf[:, c0:c0 + n], n))
                for m in range(3):
                    for c in range(2):
                        jobs.append((w3[m, :, c * 512:(c + 1) * 512], wbf3[m, :, c * 512:(c + 1) * 512], 512))
                engs = ['dve', 'act', 'pool']
                for idx, (src, dst, n) in enumerate(jobs):
                    s = idx % 2
                    k.dma('sp', wst[s][:, :, :n], src.rearrange("(k p) n -> p k n", p=128), [], ['wst%d' % s], 'wst%d' % s)
                    k.cp(engs[idx % 3], wcb[s][:, :, :n], wst[s][:, :, :n], ['wst%d' % s], ['wcb%d' % s])
                    k.dma('act', dst.rearrange("(k p) n -> p k n", p=128), wcb[s][:, :, :n], ['wcb%d' % s], [], 'wcb%d' % s)
            P.barrier()

        def layernorm(src, srckey, gB, bB, tl, tag, out_f32=None, out_f32_key=None, out_bf=None, out_bf_key=None):
            tk = lambda n: tag + n
            for c in range(2):
                k.bn_stats(tl['st'][:, c, :], src[:, c * 512:(c + 1) * 512], [srckey], [tk('st%d' % c)])
            k.bn_aggr(tl['mv'][:], tl['st'][:].rearrange("p a b -> p (a b)"), [tk('st0'), tk('st1')], [tk('mv')])
            k.act(tl['sd'][:], tl['mv'][:, 1:2], AF.Ln, [tk('mv'), 'eps'], [tk('sd')], bias=tl['eps'][:, 0:1], scale=1.0)
            k.act(tl['rstd'][:], tl['sd'][:], AF.Exp, [tk('sd')], [tk('rstd')], scale=-0.5)
            k.ts(tl['nmr'][:], tl['mv'][:, 0:1], tl['rstd'][:, 0:1], -1.0, ALU.mult, ALU.mult, [tk('mv'), tk('rstd')], [tk('nmr')])
            k.act(tl['xn'][:], src, AF.Identity, [srckey, tk('nmr'), tk('rstd')], [tl['xnkey']],
                  bias=tl['nmr'][:, 0:1], scale=tl['rstd'][:, 0:1])
            k.tt(tl['xn'][:], tl['xn'][:], gB, ALU.mult, [tl['xnkey'], 'lnconst'], [tl['xnkey']])
            if out_f32 is not None:
                k.tt(out_f32, tl['xn'][:], bB, ALU.add, [tl['xnkey'], 'lnconst'], [out_f32_key])
                if out_bf is not None:
                    k.cp('pool', out_bf, out_f32, [out_f32_key], [out_bf_key])
            else:
                k.tt(out_bf, tl['xn'][:], bB, ALU.add, [tl['xnkey'], 'lnconst'], [out_bf_key])

        if "A" in phases:
            with ExitStack() as es:
                def sb(name, shape, dt):
                    return es.enter_context(nc.sbuf_tensor(name, shape, dt))
                gB = sb("a_gB", [128, D], F32); bB = sb("a_bB", [128, D], F32)
                identf = sb("a_idf", [128, 128], F32); identb = sb("a_idb", [128, 128], BF16)
                triA = sb("a_triA", [128, 128], F32); blkA = sb("a_blkA", [128, 2], F32)
                w2 = sb("a_w2", [16, 512], F32); gbias = sb("a_gbias", [1, 512], F32); ones1 = sb("a_ones1", [1, 128], F32)
                gkiB = sb("a_gkiB", [128, 64], F32); bkiB = sb("a_bkiB", [128, 64], F32)
                eps = sb("a_eps", [128, 1], F32); one = sb("a_one", [128, 1], F32)
                tailm = sb("a_tailm", [128, 1], F32)
                wA = sb("a_wA", [128, 8, 2128], BF16)
                SS = [sb("a_S%d" % s, [128, 4, 256], F32) for s in range(3)]
                xa = [sb("a_xa%d" % s, [128, D], F32) for s in range(3)]
                xn = sb("a_xn", [128, D], F32)
                hb = [sb("a_hb%d" % s, [128, D], BF16) for s in range(3)]
                hT = [sb("a_hT%d" % s, [128, 8, 128], BF16) for s in range(2)]
                st = sb("a_st", [128, 2, 6], F32); mv = sb("a_mv", [128, 2], F32)
                sd = sb("a_sd", [128, 1], F32); rstd = sb("a_rstd", [128, 1], F32); nmr = sb("a_nmr", [128, 1], F32)
                st2 = sb("a_st2", [128, 6], F32); mv2 = sb("a_mv2", [128, 2], F32)
                sd2 = sb("a_sd2", [128, 1], F32); rstd2 = sb("a_rstd2", [128, 1], F32); nmr2 = sb("a_nmr2", [128, 1], F32)
                Vt = [sb("a_V%d" % s, [128, 1024], BF16) for s in range(4)]
                kdv = [sb("a_kdv%d" % s, [128, 512], F32) for s in range(3)]
                kdb = [sb("a_kdb%d" % s, [128, 256], BF16) for s in range(2)]
                vext = [sb("a_vext%d" % s, [128, 4, 65], BF16) for s in range(2)]
                kTt = [sb("a_kT%d" % s, [128, 2, 128], BF16) for s in range(2)]
                kin = [sb("a_kin%d" % s, [128, 64], F32) for s in range(3)]
                ksb = [sb("a_ksb%d" % s, [128, 512], F32) for s in range(3)]
                kif = [sb("a_kif%d" % s, [128, 64], F32) for s in range(2)]
                kib = [sb("a_kib%d" % s, [128, 128], BF16) for s in range(2)]
                kiT = [sb("a_kiT%d" % s, [128, 128], BF16) for s in range(2)]
                glb = sb("a_glb", [16, 128], BF16); w2b = sb("a_w2b", [16, 512], BF16); gbB = sb("a_gbB", [128, 512], F32)
                el = [sb("a_el%d" % s, [128, 512], F32) for s in range(3)]
                er = sb("a_er", [128, 512], F32)
                Kt = [sb("a_Kt%d" % s, [128, 512], BF16) for s in range(2)]
                dec = [sb("a_dec%d" % s, [128, 8], F32) for s in range(2)]

                k.dma('sp', gB[:], ln_in_g.partition_broadcast(128), [], ['lnconst0'], 'ca')
                k.dma('sp', bB[:], ln_in_b.partition_broadcast(128), [], ['lnconst1'], 'ca')
                k.dma('sp', identf[:], c_ident, [], ['identf'], 'ca')
                k.dma('sp', triA[:], c_triA, [], ['triA'], 'ca')
                k.dma('sp', blkA[:], c_blkA, [], ['blkA'], 'ca')
                k.dma('sp', w2[:], gla_w2, [], ['w2'], 'ca')
                k.dma('sp', gbias[:], gla_gate_b, [], ['gbias'], 'ca')
                k.dma('sp', gbB[:], gla_gate_b.partition_broadcast(128), [], ['gbB'], 'ca')
                k.dma('sp', gkiB[:], idx_kn_g.partition_broadcast(128), [], ['gkiB'], 'ca')
                k.dma('sp', bkiB[:], idx_kn_b.partition_broadcast(128), [], ['bkiB'], 'ca')
                k.dma('sp', tailm[:], c_tailmask, [], ['tailm'], 'ca')
                wmap = [(C_GK, 512, 0), (C_GV, 1024, 512), (C_DK, 512, 1536), (C_IK, 64, 2048), (C_GLOW, 16, 2112)]
                for (c0, n, o) in wmap:
                    k.dma('sp', wA[:, :, o:o + n], wbf[:, c0:c0 + n].rearrange("(k p) n -> p k n", p=128), [], ['wA%d' % o], 'ca')
                P.barrier()
                k.memset('dve', eps[:], EPS, ['eps'])
                k.memset('dve', one[:], 1.0, ['one'])
                k.memset('dve', ones1[:], 1.0, ['ones1'])
                k.memset('dve', SS[0][:], 0.0, ['S0'])
                for s in range(2):
                    k.memset('pool', vext[s][:], 1.0, ['vext%d' % s])
                k.cp('dve', identb[:], identf[:], [], ['identb'])
                k.cp('dve', w2b[:], w2[:], [], ['w2b'])
                P.barrier()
                tl = dict(st=st, mv=mv, sd=sd, rstd=rstd, nmr=nmr, xn=xn, eps=eps, xnkey='xn')

                def loadx(i):
                    s = i % 3
                    k.dma('sp', xa[s][:], xall[i * 128:(i + 1) * 128, :], [], ['xa%d' % s], 'xa%d' % s)

                loadx(0)

                def fa(i):
                    s = i % 3
                    if i + 1 < NB:
                        loadx(i + 1)
                    layernorm(xa[s][:], 'xa%d' % s, gB[:], bB[:], tl, 'a', out_bf=hb[s][:], out_bf_key='hb%d' % s)

                def fb_a(i):
                    s = i % 3
                    s2 = i % 2
                    for kc in range(8):
                        k.tr(PSB(0)[:, kc * 128:(kc + 1) * 128], hb[s][:, kc * 128:(kc + 1) * 128], identb[:], ['hb%d' % s], ['ps0'])
                    k.cp('act', hT[s2][:].rearrange("p a b -> p (a b)"), PSB(0), ['ps0'], ['hT%d' % s2])

                def fb_b(i):
                    s = i % 3
                    s2 = i % 2
                    last = (i == NB - 1)
                    hk = 'hT%d' % s2
                    for kc in range(8):
                        k.mm(PSF(1), hT[s2][:, kc, :], wA[:, kc, 0:512], kc == 0, kc == 7, [hk], ['ps1'])
                    k.cp('act', ksb[s][:], PSF(1), ['ps1'], ['ksb%d' % s])
                    for half in range(2):
                        for kc in range(8):
                            k.mm(PSF(2 + half), hT[s2][:, kc, :], wA[:, kc, 512 + half * 512:1024 + half * 512], kc == 0, kc == 7, [hk], ['ps%d' % (2 + half)])
                        k.cp('act' if half == 0 else 'dve', Vt[i % 4][:, half * 512:(half + 1) * 512], PSF(2 + half), ['ps%d' % (2 + half)], ['V%d' % (i % 4)])
                    for kc in range(8):
                        k.mm(PSF(4), hT[s2][:, kc, :], wA[:, kc, 1536:2048], kc == 0, kc == 7, [hk], ['ps4'])
                    k.cp('act', kdv[s][:], PSF(4), ['ps4'], ['kdv%d' % s])
                    for kc in range(8):
                        k.mm(PSF(5)[:, 0:64], hT[s2][:, kc, :], wA[:, kc, 2048:2112], kc == 0, kc == 7, [hk], ['ps5'])
                    k.cp('dve', kin[s][:], PSF(5)[:, 0:64], ['ps5'], ['kin%d' % s])

                    for kc in range(8):
                        k.mm(PSF(5)[0:16, 64:192], wA[:, kc, 2112:2128], hT[s2][:, kc, :], kc == 0, kc == 7, [hk], ['ps5'])
                    k.cp('dve', glb[:], PSF(5)[0:16, 64:192], ['ps5'], ['gl'])
                    k.mm(PSF(4), glb[:], w2b[:], True, True, ['gl'], ['ps4'])
                    k.tt(el[s][:], PSF(4), gbB[:], ALU.add, ['ps4'], ['el%d' % s])
                    k.act(el[s][:], el[s][:], AF.Exp, ['el%d' % s], ['el%d' % s], scale=-1.0)
                    k.act(el[s][:], el[s][:], AF.Ln, ['el%d' % s], ['el%d' % s], bias=one[:, 0:1], scale=1.0)
                    if last:
                        k.ts(el[s][:], el[s][:], tailm[:, 0:1], None, ALU.mult, None, ['el%d' % s], ['el%d' % s])
                def back1(i):
                    s3 = i % 3
                    s = i % 2
                    last = (i == NB - 1)
                    k.mm(PSF(6), triA[:], el[s3][:], True, True, ['el%d' % s3], ['ps6'])
                    for h in range(4):
                        k.mm(PSF(7)[:, 448 + 2 * h:450 + 2 * h], el[s3][:, h * 128:(h + 1) * 128], blkA[:], True, True, ['el%d' % s3], ['ps7'])
                    k.act(er[:], PSF(6), AF.Exp, ['ps6'], ['er'])
                    k.act(dec[s][:], PSF(7)[:, 448:456], AF.Exp, ['ps7'], ['dec%d' % s])
                    if last:
                        k.stt(Kt[s][:], ksb[s3][:], tailm[:, 0:1], er[:], ALU.mult, ALU.mult, ['ksb%d' % s3, 'er'], ['Kt%d' % s])
                    else:
                        k.tt(Kt[s][:], ksb[s3][:], er[:], ALU.mult, ['ksb%d' % s3, 'er'], ['Kt%d' % s])
                    k.dma('act', kp[i * 128:(i + 1) * 128, :], kdv[s3][:, 0:256], ['kdv%d' % s3], [], 'kdvo%d' % s3)
                    k.dma('act', vp[i * 128:(i + 1) * 128, :], kdv[s3][:, 256:512], ['kdv%d' % s3], [], 'kdvo%d' % s3)
                    k.cp('pool', kdb[s][:], kdv[s3][:, 0:256], ['kdv%d' % s3], ['kdb%d' % s])
                    k.cp('pool', vext[s][:, :, 0:64], kdv[s3][:, 256:512].rearrange("p (g d) -> p g d", g=4), ['kdv%d' % s3], ['vext%d' % s])
                    for c in range(2):
                        k.tr(PSB(7)[:, c * 128:(c + 1) * 128], kdb[s][:, c * 128:(c + 1) * 128], identb[:], ['kdb%d' % s], ['ps7'])
                    k.cp('dve', kTt[s][:].rearrange("p a b -> p (a b)"), PSB(7)[:, 0:256], ['ps7'], ['kT%d' % s])
                    k.dma('pool', kT_d[:, :, i * 128:(i + 1) * 128], kTt[s][:], ['kT%d' % s], [], 'kTo%d' % s)
                    k.dma('pool', v_d[:, i, :], vext[s][:].rearrange("p g d -> p (g d)"), ['vext%d' % s], [], 'vexto%d' % s)
                    kn = kin[s3]
                    knk = 'kin%d' % s3
                    k.bn_stats(st2[:], kn[:], [knk], ['st2'])
                    k.bn_aggr(mv2[:], st2[:], ['st2'], ['mv2'])
                    k.act(sd2[:], mv2[:, 1:2], AF.Ln, ['mv2'], ['sd2'], bias=eps[:, 0:1], scale=1.0)
                    k.act(rstd2[:], sd2[:], AF.Exp, ['sd2'], ['rstd2'], scale=-0.5)
                    k.ts(nmr2[:], mv2[:, 0:1], rstd2[:, 0:1], -1.0, ALU.mult, ALU.mult, ['mv2', 'rstd2'], ['nmr2'])
                    k.act(kn[:], kn[:], AF.Identity, [knk, 'nmr2', 'rstd2'], [knk], bias=nmr2[:, 0:1], scale=rstd2[:, 0:1])
                    k.tt(kn[:], kn[:], gkiB[:], ALU.mult, [knk], [knk])
                    k.tt(kif[s][:], kn[:], bkiB[:], ALU.add, [knk], ['kif%d' % s])
                    k.dma('act', ikp[i * 128:(i + 1) * 128, :], kif[s][:], ['kif%d' % s], [], 'kifo%d' % s)
                    k.cp('pool', kib[s][:, 0:64], kif[s][:], ['kif%d' % s], ['kib%d' % s])
                    k.cp('pool', kib[s][:, 64:128], kif[s][:], ['kif%d' % s], ['kib%d' % s])
                    k.tr(PSB(7)[:, 256:384], kib[s][:], identb[:], ['kib%d' % s], ['ps7'])
                    k.cp('dve', kiT[s][:], PSB(7)[:, 256:384], ['ps7'], ['kiT%d' % s])
                    k.dma('pool', ki_d[:, i * 128:(i + 1) * 128], kiT[s][:], ['kiT%d' % s], [], 'kiTo%d' % s)

                def back2(i):
                    s3 = i % 3
                    s = i % 2
                    cur = (2 * i) % 3
                    k.dma('sp', snap[i], SS[cur][:].rearrange("p h e -> p (h e)"), ['S%d' % cur], [], 'Ssto%d' % cur)
                    sbanks = [[6, 7], [2, 3]]
                    for c in range(2):
                        for hp in range(2):
                            bk = sbanks[c][hp]
                            for hh in range(2):
                                h = hp * 2 + hh
                                k.mm(PSF(bk)[:, hh * 256:(hh + 1) * 256], Kt[s][c * 64:(c + 1) * 64, h * 128:(h + 1) * 128],
                                     Vt[i % 4][c * 64:(c + 1) * 64, h * 256:(h + 1) * 256], True, True, ['Kt%d' % s, 'V%d' % (i % 4)], ['ps%d' % bk])
                            src_, dst_ = (2 * i + c) % 3, (2 * i + c + 1) % 3
                            for hh in range(2):
                                h = hp * 2 + hh
                                k.stt(SS[dst_][:, h, :], SS[src_][:, h, :], dec[s][:, 2 * h + c:2 * h + c + 1], PSF(bk)[:, hh * 256:(hh + 1) * 256],
                                      ALU.mult, ALU.add, ['S%d' % src_, 'dec%d' % s, 'ps%d' % bk], ['S%d' % dst_])

                for i0 in range(min(3, NB)):
                    fa(i0)
                for i0 in range(3):
                    if i0 < NB:
                        fb_a(i0)
                        fb_b(i0)
                    if i0 + 3 < NB and i0 < 2:
                        fa(i0 + 3)
                back1(0)
                for i in range(NB):
                    if i + 3 < NB:
                        fb_a(i + 3)
                    back2(i)
                    if i + 1 < NB:
                        back1(i + 1)
                    if i + 5 < NB:
                        fa(i + 5)
                    if i + 3 < NB:
                        fb_b(i + 3)
                fin = (2 * NB) % 3
                k.dma('sp', gla_p, SS[fin][:].rearrange("p h e -> p (h e)"), ['S%d' % fin], [], 'glap')
            P.barrier()


        def phase_own(mode):
            PR = (mode == 'P')
            NK = NKMAX if PR else 2176
            with ExitStack() as es:
                def sb(name, shape, dt):
                    return es.enter_context(nc.sbuf_tensor(mode + name, shape, dt))
                gB = sb("gB", [128, D], F32); bB = sb("bB", [128, D], F32)
                g2B = sb("g2B", [128, D], F32); b2B = sb("b2B", [128, D], F32)
                gtb = [sb("gtb%d" % s_, [128, 512], F32) for s_ in range(2)]
                gnB = sb("gnB", [128, 256], F32)
                identf = sb("idf", [128, 128], F32); identb = sb("idb", [128, 128], BF16)
                tri = sb("tri", [128, 128], F32)
                maskf = sb("maskf", [128, 512], F32)
                cst = sb("cst", [128, 512], F32); idrepb = sb("idrepb", [128, 512], BF16)
                w2 = sb("w2", [16, 512], F32); gbias = sb("gbias", [1, 512], F32); ones1 = sb("ones1", [1, 128], F32)
                eps = sb("eps", [128, 1], F32); one = sb("one", [128, 1], F32)
                ctab = sb("ctab", [128, KIT + 1], F32)
                tbias = sb("tbias", [128, 2, 640 if PR else 128], F32)
                onehot = sb("onehot", [128, 4], F32)
                xo = sb("xo", [128, D], F32); h = sb("h", [128, D], F32); hb = sb("hb", [128, D], BF16)
                hT = sb("hT", [128, 8, 128], BF16)
                tmp = sb("tmp", [128, D], F32)
                st = sb("st", [128, 2, 6], F32); mv = sb("mv", [128, 2], F32)
                sd = sb("sd", [128, 1], F32); rstd = sb("rstd", [128, 1], F32); nmr = sb("nmr", [128, 1], F32)
                wch = [sb("wch%d" % s_, [128, 8, 512], BF16) for s_ in range(2)]
                gl = sb("gl", [16, 128], F32)
                el = sb("el", [128, 512], F32); eb = sb("eb", [128, 512], F32); enb = sb("enb", [128, 512], F32)
                qT = sb("qT", [128, 4, 128], BF16); kTh = sb("kTh", [128, 4, 128], BF16)
                V = sb("V", [128, 1024], BF16)
                sg = [sb("sg%d" % s_, [128, 512], F32) for s_ in range(2)]
                QTz = sb("QTz", [128, 4, 512], BF16); qiTz = sb("qiTz", [128, 8, 128], BF16)
                wabs = sb("wabs", [128, 8], F32); wsgn = sb("wsgn", [128, 8], F32)
                AT = sb("AT", [128, 4, 128], BF16)
                ss = sb("ss", [128, 4], F32); rs = sb("rs", [128, 4], F32)
                yain = sb("yain", [128, D], BF16); yT = sb("yT", [128, 8, 128], BF16)
                mrg = sb("mrg", [128, D], F32)
                sc = sb("sc", [128, NK], F32)
                junk = None if PR else sb("junk", [128, 2176], BF16)
                rlw = sb("rlw", [128, 1024], F32)
                rl = [rlw[:, 0:512], rlw[:, 512:1024]]
                rd = sb("rd", [128, 16], F32)
                yout = sb("yout", [128, D], F32) if PR else xo
                rmax = sb("rmax", [128, 1], F32); rmin = sb("rmin", [128, 1], F32); Wd = sb("Wd", [128, 1], F32)
                wtab = sb("wtab", [128, KIT + 1], F32); mids = sb("mids", [128, KIT + 1], F32)
                cnts = sb("cnts", [128, KIT], F32); us = sb("us", [128, KIT], F32); thr = sb("thr", [128, 1], F32)
                sAs = sb("sAs", [128, KIT], F32); vvs = sb("vvs", [128, KIT], F32)
                jd = sb("jd", [128, 8], BF16); ja = sb("ja", [128, 8], BF16); jq = sb("jq", [128, 8], BF16)
                if PR:
                    Sc = [sb("Sc%d" % s_, [128, 1024], F32) for s_ in range(2)]
                    Sown = sb("Sown", [128, 1024], F32); Sb = sb("Sb", [128, 4, 256], BF16)
                    kich = [sb("kich%d" % s_, [128, 1024], BF16) for s_ in range(2)]
                    kTch = [sb("kTch%d" % s_, [128, 2, 512], BF16) for s_ in range(2)]
                    vch = [sb("vch%d" % s_, [128, 4, 260], BF16) for s_ in range(2)]
                    mbt = [sb("mbt%d" % s_, [128, 128], BF16) for s_ in range(3)]
                    pT = [sb("pT%d" % s_, [128, 512], BF16) for s_ in range(3)]
                    oT = [sb("oT%d" % s_, [65, 512], F32) for s_ in range(2)]
                else:
                    cmaskS = sb("cmaskS", [128, 4, 128], F32); rmaskS = sb("rmaskS", [128, 4], F32)
                    mrevS = sb("mrevS", [128, 128], F32); bsumS = sb("bsumS", [128, 4], F32)
                    idrepSb = sb("idrepSb", [128, 4, 64], BF16)
                    selSb = sb("selSb", [128, 4, 128], BF16)
                    gkiB = sb("gkiB", [128, 64], F32); bkiB = sb("bkiB", [128, 64], F32)
                    S0f = [sb("S0f%d" % b_, [128, 4, 256], F32) for b_ in range(2)]
                    S0b = [sb("S0b%d" % b_, [128, 4, 256], BF16) for b_ in range(4)]
                    qTb = [sb("qTb%d" % b_, [128, 4, 128], BF16) for b_ in range(4)]
                    wabsb = sb("wabsb", [128, 4, 8], F32)
                    QTs = sb("QTs", [128, 4, 4, 64], BF16)
                    kTs1 = sb("kTs", [128, 2, 2176], BF16)
                    kiTs = [sb("kiTs%d" % b_, [128, 2176], BF16) for b_ in range(4)]
                    vexts1 = sb("vexts", [128, 17, 260], BF16)
                    ckf = sb("ckf", [128, 8, 256], F32); ckb = sb("ckb", [128, 8, 256], BF16)
                    cif = sb("cif", [128, 8, 64], F32); cib = sb("cib", [128, 8, 128], BF16)
                    kdv = sb("kdv", [128, 512], F32); kdb = sb("kdb", [128, 256], BF16); vnb = sb("vnb", [128, 256], BF16)
                    kin = sb("kin", [128, 64], F32); kif = sb("kif", [128, 64], F32); kib = sb("kib", [128, 128], BF16)
                    st2 = sb("st2", [128, 6], F32); mv2 = sb("mv2", [128, 2], F32)
                    sd2 = sb("sd2", [128, 1], F32); rstd2 = sb("rstd2", [128, 1], F32); nmr2 = sb("nmr2", [128, 1], F32)
                    kTnew = sb("kTnew", [128, 2, 128], BF16); kiTnew = sb("kiTnew", [128, 128], BF16)
                    Kt = sb("Kt", [128, 512], BF16); Ktb = [sb("Ktb%d" % s_, [128, 512], BF16) for s_ in range(2)]
                    decs = sb("decs", [128, 16], F32)
                    pTs = [sb("pTs%d" % s_, [128, 64], BF16) for s_ in range(3)]

                cl = 'c' + mode
                k.dma('sp', gB[:], ln_in_g.partition_broadcast(128), [], [], cl)
                k.dma('sp', bB[:], ln_in_b.partition_broadcast(128), [], [], cl)
                k.dma('sp', g2B[:], ln_g.partition_broadcast(128), [], [], cl)
                k.dma('sp', b2B[:], ln_b.partition_broadcast(128), [], [], cl)
                k.dma('sp', gnB[:], gla_norm_g.partition_broadcast(128), [], [], cl)
                k.dma('sp', identf[:], c_ident, [], [], cl)
                k.dma('sp', tri[:], c_triB if PR else c_triS, [], [], cl)
                k.dma('sp', maskf[:], (c_maskB if PR else c_maskS).rearrange("p a b -> p (a b)"), [], [], cl)
                k.dma('sp', w2[:], gla_w2, [], [], cl)
                k.dma('sp', gbias[:], gla_gate_b, [], [], cl)
                k.dma('sp', ctab[:], c_ctab, [], [], cl)
                k.dma('sp', tbias[:], c_tbias if PR else c_tbias[:, :, 0:128], [], [], cl)
                k.dma('sp', onehot[:], c_onehot, [], [], cl)
                if not PR:
                    k.dma('sp', cmaskS[:], c_cmaskS, [], [], cl)
                    k.dma('sp', rmaskS[:], c_rmaskS, [], [], cl)
                    k.dma('sp', mrevS[:], c_mrevS, [], [], cl)
                    k.dma('sp', bsumS[:], c_bsumS, [], [], cl)
                    k.dma('sp', gkiB[:], idx_kn_g.partition_broadcast(128), [], [], cl)
                    k.dma('sp', bkiB[:], idx_kn_b.partition_broadcast(128), [], [], cl)
                P.barrier()
                k.memset('dve', eps[:], EPS, [])
                k.memset('dve', one[:], 1.0, [])
                k.memset('dve', ones1[:], 1.0, [])
                k.memset('pool', QTz[:], 0.0, [])
                k.memset('pool', qiTz[:], 0.0, [])
                k.memset('pool', yain[:], 0.0, [])
                k.memset('pool', tmp[:], 0.0, [])
                k.cp('dve', identb[:], identf[:], [], [])
                k.dma('sp', cst[:], c_idrep, [], ['cst'], 'cst')
                k.cp('dve', idrepb[:], cst[:], ['cst'], ['idrepb'])
                if not PR:
                    k.dma('sp', cst[:, 0:256], c_idrepS.rearrange("p a b -> p (a b)"), ['idrepb'], ['cst'], 'cst')
                    k.cp('dve', idrepSb[:].rearrange("p a b -> p (a b)"), cst[:, 0:256], ['cst'], ['idrepSb'])
                    k.dma('sp', cst[:], c_selS.rearrange("p a b -> p (a b)"), ['idrepSb'], ['cst'], 'cst')
                    k.cp('dve', selSb[:].rearrange("p a b -> p (a b)"), cst[:], ['cst'], ['selSb'])
                    for b_ in range(4):
                        s_ = b_ % 2
                        k.dma('sp', S0f[s_][:], state[b_].rearrange("h p e -> p h e"), [], ['S0f%d' % s_], 'S0f%d' % s_)
                        k.cp('pool', S0b[b_][:], S0f[s_][:], ['S0f%d' % s_], ['S0b%d' % b_])
                        k.memset('pool', kiTs[b_][:, 2048:2176], 0.0, [])
                    k.memset('pool', vexts1[:], 1.0, [])
                    k.memset('pool', kTs1[:, :, 2048:2176], 0.0, [])
                P.barrier()
                tl = dict(st=st, mv=mv, sd=sd, rstd=rstd, nmr=nmr, xn=tmp, eps=eps, xnkey='tmp')
                bank = [0]

                def nb():
                    bank[0] = (bank[0] + 1) % 8
                    return bank[0]

                def own_block(g):
                    jobs = []

                    def J(src, n):
                        jobs.append((src, n))
                    J(wbf[:, C_IW:C_IW + 8], 8)
                    J(wbf[:, C_IQ:C_IQ + 512], 512)
                    J(wbf[:, C_DQ:C_DQ + 512], 512); J(wbf[:, C_DQ + 512:C_DQ + 1024], 512)
                    if not PR:
                        J(wbf[:, C_DK:C_DK + 512], 512)
                        J(wbf[:, C_IK:C_IK + 64], 64)
                    J(wbf[:, C_GLOW:C_GLOW + 16], 16)
                    J(wbf[:, C_GQ:C_GQ + 512], 512)
                    J(wbf[:, C_GK:C_GK + 512], 512)
                    J(wbf[:, C_GV:C_GV + 512], 512); J(wbf[:, C_GV + 512:C_GV + 1024], 512)
                    J(wbf[:, C_GR:C_GR + 512], 512); J(wbf[:, C_GR + 512:C_GR + 1024], 512)
                    for cc in range(2):
                        J(wbf3[0, :, cc * 512:(cc + 1) * 512], 512)
                        J(wbf[:, C_MA + cc * 512:C_MA + (cc + 1) * 512], 512)
                    J(wbf[:, C_DZ:C_DZ + 512], 512); J(wbf[:, C_DZ + 512:C_DZ + 1024], 512)
                    for cc in range(2):
                        J(wbf3[1, :, cc * 512:(cc + 1) * 512], 512)
                        J(wbf[:, C_MB + cc * 512:C_MB + (cc + 1) * 512], 512)
                    for cc in range(2):
                        J(wbf3[2, :, cc * 512:(cc + 1) * 512], 512)
                    jpos = [0]

                    def wissue(idx):
                        src, n = jobs[idx]
                        s_ = idx % 2
                        k.dma('sp', wch[s_][:, :, :n], src.rearrange("(k p) n -> p k n", p=128), [], ['wch%d' % s_], 'wch%d' % s_)

                    def next_w():
                        idx = jpos[0]
                        if idx == 0:
                            wissue(0)
                        if idx + 1 < len(jobs):
                            wissue(idx + 1)
                        jpos[0] += 1
                        return wch[idx % 2], 'wch%d' % (idx % 2)

                    def projT(wt, wk, n, psap, pskey):
                        for kc in range(8):
                            k.mm(psap, hT[:, kc, :], wt[:, kc, :n], kc == 0, kc == 7, ['hT', wk], [pskey])

                    def projF(wt, wk, bk):
                        for sub in range(4):
                            for kc in range(8):
                                k.mm(PSF(bk)[:, sub * 128:(sub + 1) * 128], wt[:, kc, sub * 128:(sub + 1) * 128], hT[:, kc, :],
                                     kc == 0, kc == 7, ['hT', wk], ['ps%d' % bk])

                    def transpose8(src, srckey, dst, dstkey):
                        bk = nb()
                        for kc in range(8):
                            k.tr(PSB(bk)[:, kc * 128:(kc + 1) * 128], src[:, kc * 128:(kc + 1) * 128], identb[:], [srckey], ['ps%d' % bk])
                        k.cp('act', dst[:].rearrange("p a b -> p (a b)"), PSB(bk), ['ps%d' % bk], [dstkey])

                    if not PR:
                        k.dma('act', xo[:], xs, [], ['xo'], 'xo')
                    elif g == 0:
                        k.dma('act', xo[:], xown[0:128, :], [], ['xo'], 'xo')
                    layernorm(xo[:], 'xo', gB[:], bB[:], tl, 'b', out_f32=h[:], out_f32_key='h', out_bf=hb[:], out_bf_key='hb')
                    transpose8(hb, 'hb', hT, 'hT')
                    wt, wk = next_w()
                    bw = nb()
                    projT(wt, wk, 8, PSF(bw)[:, 0:8], 'ps%d' % bw)
                    k.ts(wabs[:], PSF(bw)[:, 0:8], -IDX_W_SCALE, None, ALU.mult, None, ['ps%d' % bw], ['wabs'])
                    k.stt(wabs[:], PSF(bw)[:, 0:8], IDX_W_SCALE, wabs[:], ALU.mult, ALU.max, ['ps%d' % bw, 'wabs'], ['wabs'])
                    k.ts(wsgn[:], PSF(bw)[:, 0:8], 0.0, 2.0, ALU.is_ge, ALU.mult, ['ps%d' % bw], ['wsgn'])
                    k.ts(wsgn[:], wsgn[:], -1.0, None, ALU.add, None, ['wsgn'], ['wsgn'])
                    wt, wk = next_w()
                    bi = nb()
                    projF(wt, wk, bi)
                    qv = qiTz[:].rearrange("p (s two) t -> p s two t", two=2)
                    pv = PSF(bi).rearrange("p (s t) -> p s t", s=4)
                    k.cp('act', qv[0:64, :, 0, :], pv[0:64, :, :], ['ps%d' % bi], ['qiTz'])
                    k.cp('act', qv[64:128, :, 1, :], pv[64:128, :, :], ['ps%d' % bi], ['qiTz'])
                    for m in range(2):
                        wt, wk = next_w()
                        bdq = nb()
                        projF(wt, wk, bdq)
                        k.cp('act', QTz[0:64, 2 * m, :], PSF(bdq)[0:64, :], ['ps%d' % bdq], ['QTz'])
                        k.cp('act', QTz[64:128, 2 * m + 1, :], PSF(bdq)[64:128, :], ['ps%d' % bdq], ['QTz'])
                    if not PR:
                        wt, wk = next_w()
                        bkv = nb()
                        projT(wt, wk, 512, PSF(bkv), 'ps%d' % bkv)
                        k.cp('act', kdv[:], PSF(bkv), ['ps%d' % bkv], ['kdv'])
                        k.dma('act', ks, kdv[:, 0:256], ['kdv'], [], 'so1')
                        k.dma('act', vs, kdv[:, 256:512], ['kdv'], [], 'so1')
                        k.cp('pool', kdb[:], kdv[:, 0:256], ['kdv'], ['kdb'])
                        k.cp('pool', vnb[:], kdv[:, 256:512], ['kdv'], ['vnb'])
                        wt, wk = next_w()
                        bik = nb()
                        projT(wt, wk, 64, PSF(bik)[:, 0:64], 'ps%d' % bik)
                        k.cp('dve', kin[:], PSF(bik)[:, 0:64], ['ps%d' % bik], ['kin'])
                        k.bn_stats(st2[:], kin[:], ['kin'], ['st2'])
                        k.bn_aggr(mv2[:], st2[:], ['st2'], ['mv2'])
                        k.act(sd2[:], mv2[:, 1:2], AF.Ln, ['mv2'], ['sd2'], bias=eps[:, 0:1], scale=1.0)
                        k.act(rstd2[:], sd2[:], AF.Exp, ['sd2'], ['rstd2'], scale=-0.5)
                        k.ts(nmr2[:], mv2[:, 0:1], rstd2[:, 0:1], -1.0, ALU.mult, ALU.mult, ['mv2', 'rstd2'], ['nmr2'])
                        k.act(kin[:], kin[:], AF.Identity, ['kin', 'nmr2', 'rstd2'], ['kin'], bias=nmr2[:, 0:1], scale=rstd2[:, 0:1])
                        k.tt(kin[:], kin[:], gkiB[:], ALU.mult, ['kin'], ['kin'])
                        k.tt(kif[:], kin[:], bkiB[:], ALU.add, ['kin'], ['kif'])
                        k.dma('act', iks, kif[:], ['kif'], [], 'so1')
                        k.cp('pool', kib[:, 0:64], kif[:], ['kif'], ['kib'])
                        k.cp('pool', kib[:, 64:128], kif[:], ['kif'], ['kib'])
                        bt_ = nb()
                        for c_ in range(2):
                            k.tr(PSB(bt_)[:, c_ * 128:(c_ + 1) * 128], kdb[:, c_ * 128:(c_ + 1) * 128], identb[:], ['kdb'], ['ps%d' % bt_])
                        k.tr(PSB(bt_)[:, 256:384], kib[:], identb[:], ['kib'], ['ps%d' % bt_])
                        k.cp('dve', kTnew[:].rearrange("p a b -> p (a b)"), PSB(bt_)[:, 0:256], ['ps%d' % bt_], ['kTnew'])
                        k.cp('dve', kiTnew[:], PSB(bt_)[:, 256:384], ['ps%d' % bt_], ['kiTnew'])
                        for b_ in range(4):
                            k.cp('pool', kiTs[b_][:, 2048:2064], kiTnew[:, 16 * b_:16 * b_ + 16], ['kiTnew'], ['kiTs%d' % b_])
                            for t8 in range(2):
                                k.dma('sp', cif[:], cache_ik[b_, t8 * 1024:(t8 + 1) * 1024, :].rearrange("(t p) c -> p t c", p=128), [], ['cif'], 'cif')
                                k.cp('pool', cib[:, :, 0:64], cif[:], ['cif'], ['cib'])
                                k.cp('pool', cib[:, :, 64:128], cif[:], ['cif'], ['cib'])
                                bt_ = nb()
                                for tt_ in range(8):
                                    k.tr(PSB(bt_)[:, tt_ * 128:(tt_ + 1) * 128], cib[:, tt_, :], identb[:], ['cib'], ['ps%d' % bt_])
                                k.cp('dve', kiTs[b_][:, t8 * 1024:(t8 + 1) * 1024], PSB(bt_), ['ps%d' % bt_], ['kiTs%d' % b_])
                    if PR:
                        n_tiles = min(4 * g + 5, NB)
                        tail0 = 4 * g
                        tidx = 1 if g == G - 1 else 0
                    else:
                        n_tiles = 17
                        tail0 = 16
                        tidx = 1
                    n_keys = n_tiles * 128
                    nch = (n_tiles + 3) // 4

                    def kiload(ci):
                        k0 = ci * 1024
                        w_ = min(1024, n_keys - k0)
                        s_ = ci % 2
                        k.dma('sp', kich[s_][:, :w_], ki_d[:, k0:k0 + w_], [], ['kich%d' % s_], 'kich%d' % s_)

                    if PR:
                        kiload(0)
                    if not PR:
                        for b_ in range(4):
                            k.ts(wabsb[:, b_, :], wabs[:], rmaskS[:, b_:b_ + 1], None, ALU.mult, None, ['wabs'], ['wabsb'])
                    rli = 0
                    if PR:
                        npair = (n_keys + 1023) // 1024
                        wslots = [(rlw[:, :], 'rlw'), (mrg[:, :], 'mrgW'), (tmp[:, :], 'tmpW')]
                        for cp in range(npair):
                            k0 = cp * 1024
                            wtot = min(1024, n_keys - k0)
                            w0 = min(512, wtot)
                            w1 = wtot - w0
                            if cp + 1 < npair:
                                kiload(cp + 1)
                            for hh in range(8):
                                r_, rk = wslots[rli % 3]
                                rli += 1
                                bx = nb()
                                k.mm(PSF(bx)[:, :w0], qiTz[:, hh, :], kich[cp % 2][:, 0:w0], True, True, ['qiTz', 'kich%d' % (cp % 2)], ['ps%d' % bx])
                                k.act(r_[:, 0:w0], PSF(bx)[:, :w0], AF.Relu, ['ps%d' % bx, 'wabs'], [rk], scale=wabs[:, hh:hh + 1])
                                if w1 > 0:
                                    bx = nb()
                                    k.mm(PSF(bx)[:, :w1], qiTz[:, hh, :], kich[cp % 2][:, 512:512 + w1], True, True, ['qiTz', 'kich%d' % (cp % 2)], ['ps%d' % bx])
                                    k.act(r_[:, 512:512 + w1], PSF(bx)[:, :w1], AF.Relu, ['ps%d' % bx, 'wabs'], [rk], scale=wabs[:, hh:hh + 1])
                                if hh == 0:
                                    k.ts(sc[:, k0:k0 + wtot], r_[:, :wtot], wsgn[:, 0:1], None, ALU.mult, None, [rk, 'wsgn'], ['sc'])
                                else:
                                    k.stt(sc[:, k0:k0 + wtot], r_[:, :wtot], wsgn[:, hh:hh + 1], sc[:, k0:k0 + wtot], ALU.mult, ALU.add, [rk, 'wsgn', 'sc'], ['sc'])
                    for ci in (range(0) if PR else range(nch)):
                        k0 = ci * 512
                        w_ = min(512, n_keys - k0)
                        if PR and ci + 1 < nch:
                            kiload(ci + 1)
                        first = True
                        for hh in range(8):
                            for b_ in (range(1) if PR else range(4)):
                                bx = nb()
                                if PR:
                                    k.mm(PSF(bx)[:, :w_], qiTz[:, hh, :], kich[ci % 2][:, :w_], True, True, ['qiTz', 'kich%d' % (ci % 2)], ['ps%d' % bx])
                                    scl = wabs[:, hh:hh + 1]
                                    sck = 'wabs'
                                else:
                                    k.mm(PSF(bx)[:, :w_], qiTz[:, hh, :], kiTs[b_][:, k0:k0 + w_], True, True, ['qiTz', 'kiTs%d' % b_], ['ps%d' % bx])
                                    scl = wabsb[:, b_, hh:hh + 1]
                                    sck = 'wabsb'
                                rslots = [(rl[0], 'rl0'), (rl[1], 'rl1'), (mrg[:, 0:512], 'mrgA'), (mrg[:, 512:1024], 'mrgB'),
                                          (tmp[:, 0:512], 'tmpA'), (tmp[:, 512:1024], 'tmpB')]
                                r_, rk = rslots[rli % 6]
                                rli += 1
                                k.act(r_[:, :w_], PSF(bx)[:, :w_], AF.Relu, ['ps%d' % bx, sck], [rk], scale=scl)
                                if first:
                                    k.ts(sc[:, k0:k0 + w_], r_[:, :w_], wsgn[:, hh:hh + 1], None, ALU.mult, None, [rk, 'wsgn'], ['sc'])
                                    first = False
                                else:
                                    k.stt(sc[:, k0:k0 + w_], r_[:, :w_], wsgn[:, hh:hh + 1], sc[:, k0:k0 + w_], ALU.mult, ALU.add, [rk, 'wsgn', 'sc'], ['sc'])
                    k.capture()
                    k.red(rmax[:], sc[:, 0:n_keys], ALU.max, ['sc'], ['rmax'])
                    k.red(rmin[:], sc[:, 0:n_keys], ALU.min, ['sc'], ['rmin'])
                    tw = (n_tiles - tail0) * 128
                    k.tt(sc[:, tail0 * 128:tail0 * 128 + tw], sc[:, tail0 * 128:tail0 * 128 + tw], tbias[:, tidx, 0:tw], ALU.add, ['sc'], ['sc'])
                    k.tt(Wd[:], rmax[:], rmin[:], ALU.subtract, ['rmax', 'rmin'], ['Wd'])
                    k.ts(wtab[:], ctab[:], Wd[:, 0:1], None, ALU.mult, None, ['Wd'], ['wtab'])
                    k.tt(mids[:, 0:1], rmin[:], wtab[:, 0:1], ALU.add, ['rmin', 'wtab'], ['mid0'])
                    nD = (int(n_keys * 0.46) // 128) * 128
                    if nD < 256:
                        nD = n_keys
                    nA = n_keys - nD
                    for it in range(1, KIT + 1):
                        mid = mids[:, it - 1:it]
                        mk = 'mid%d' % (it - 1)
                        cn = cnts[:, it - 1:it]
                        ck_ = 'cnt%d' % it
                        k.ts(jd[:, 0:1].to_broadcast([128, nD]), sc[:, 0:nD], mid, None, ALU.is_ge, ALU.add, ['sc', mk], ['jd', ck_], accum=cn)
                        if nA > 0:
                            k.act(ja[:, 0:1].to_broadcast([128, nA]), sc[:, nD:n_keys], AF.Sign, ['sc', mk], ['ja', 'sa%d' % it],
                                  bias=mid, scale=-1.0, accum_out=sAs[:, it - 1:it])
                            k.stt(vvs[:, it - 1:it], cn, 2.0, sAs[:, it - 1:it], ALU.mult, ALU.subtract, [ck_, 'sa%d' % it], ['vv%d' % it])
                            vsrc, vkey, vthr = vvs[:, it - 1:it], 'vv%d' % it, 511.5 - nA
                        else:
                            vsrc, vkey, vthr = cn, ck_, 255.5
                        u_ = us[:, it - 1:it]
                        k.ts(u_, vsrc, vthr, wtab[:, it - 1:it], ALU.is_ge, ALU.mult, [vkey, 'wtab'], ['u%d' % it])
                        if it < KIT:
                            k.stt(mids[:, it:it + 1], u_, wtab[:, it:it + 1], mid, ALU.subtract, ALU.add, ['u%d' % it, 'wtab', mk], ['mid%d' % it])
                        else:
                            k.stt(thr[:], u_, wtab[:, it - 1:it], mid, ALU.subtract, ALU.add, ['u%d' % it, 'wtab', mk], ['thr'])
                    bisA = k.end_capture()
                    k.capture()
                    wt, wk = next_w()
                    b1 = nb()
                    for kc in range(8):
                        k.mm(PSF(b1)[0:16, 0:128], wt[:, kc, 0:16], hT[:, kc, :], kc == 0, kc == 7, ['hT', wk], ['ps%d' % b1])
                    k.cp('dve', gl[:], PSF(b1)[0:16, 0:128], ['ps%d' % b1], ['gl'])
                    bz = nb()
                    k.mm(PSF(bz), gl[:], w2[:], True, False, ['gl'], ['ps%d' % bz])
                    k.mm(PSF(bz), ones1[:], gbias[:], False, True, [], ['ps%d' % bz])
                    k.act(el[:], PSF(bz), AF.Exp, ['ps%d' % bz], ['el'], scale=-1.0)
                    k.act(el[:], el[:], AF.Ln, ['el'], ['el'], bias=one[:, 0:1], scale=1.0)
                    bb = nb()
                    for hh in range(4):
                        k.mm(PSF(bb)[:, hh * 128:(hh + 1) * 128], el[:, hh * 128:(hh + 1) * 128], tri[:], True, True, ['el'], ['ps%d' % bb])
                    k.act(eb[:], PSF(bb), AF.Exp, ['ps%d' % bb], ['eb'])
                    k.act(enb[:], PSF(bb), AF.Exp, ['ps%d' % bb], ['enb'], scale=-1.0)
                    wt, wk = next_w()
                    bq = nb()
                    projF(wt, wk, bq)
                    k.stt(qT[:].rearrange("p a b -> p (a b)"), PSF(bq), 128.0 ** -0.5, eb[:], ALU.mult, ALU.mult, ['ps%d' % bq, 'eb'], ['qT'])
                    wt, wk = next_w()
                    bk_ = nb()
                    projF(wt, wk, bk_)
                    k.tt(kTh[:].rearrange("p a b -> p (a b)"), PSF(bk_), enb[:], ALU.mult, ['ps%d' % bk_, 'enb'], ['kTh'])
                    if not PR:
                        bkt = nb()
                        projT(wt, wk, 512, PSF(bkt), 'ps%d' % bkt)
                        br_ = nb()
                        k.mm(PSF(br_), mrevS[:], el[:], True, True, ['el'], ['ps%d' % br_])
                        k.act(sg[1][:], PSF(br_), AF.Exp, ['ps%d' % br_], ['sg1'])
                        k.tt(Kt[:], PSF(bkt), sg[1][:], ALU.mult, ['ps%d' % bkt, 'sg1'], ['Kt'])
                        bd_ = nb()
                        for hh in range(4):
                            k.mm(PSF(bd_)[:, hh * 4:(hh + 1) * 4], el[:, hh * 128:(hh + 1) * 128], bsumS[:], True, True, ['el'], ['ps%d' % bd_])
                        k.act(decs[:], PSF(bd_)[:, 0:16], AF.Exp, ['ps%d' % bd_], ['decs'])
                    for half in range(2):
                        wt, wk = next_w()
                        bv = nb()
                        projT(wt, wk, 512, PSF(bv), 'ps%d' % bv)
                        k.cp('act', V[:, half * 512:(half + 1) * 512], PSF(bv), ['ps%d' % bv], ['V'])
                    if PR:
                        for m in range(4):
                            sidx = min(4 * g + m, NB - 1)
                            s_ = m % 2
                            k.dma('act', Sc[s_][:], snap[sidx], [], ['Sc%d' % s_], 'Sc%d' % s_)
                            if m == 0:
                                k.ts(Sown[:], Sc[s_][:], onehot[:, 0:1], None, ALU.mult, None, ['Sc%d' % s_], ['Sown'])
                            else:
                                k.stt(Sown[:], Sc[s_][:], onehot[:, m:m + 1], Sown[:], ALU.mult, ALU.add, ['Sc%d' % s_, 'Sown'], ['Sown'])
                        k.cp('pool', Sb[:].rearrange("p a b -> p (a b)"), Sown[:], ['Sown'], ['Sb'])
                    else:
                        for b_ in range(4):
                            for hh in range(4):
                                k.tt(qTb[b_][:, hh, :], qT[:, hh, :], cmaskS[:, b_, :], ALU.mult, ['qT'], ['qTb%d' % b_])
                    ba = nb()
                    for hh in range(4):
                        k.mm(PSF(ba)[:, hh * 128:(hh + 1) * 128], kTh[:, hh, :], qT[:, hh, :], True, True, ['kTh', 'qT'], ['ps%d' % ba])
                    k.tt(AT[:].rearrange("p a b -> p (a b)"), PSF(ba), maskf[:], ALU.mult, ['ps%d' % ba], ['AT'])
                    bo = [nb(), nb()]
                    for hh in range(4):
                        oap = PSF(bo[hh // 2])[:, (hh % 2) * 256:(hh % 2 + 1) * 256]
                        okey = 'ps%d' % bo[hh // 2]
                        if PR:
                            k.mm(oap, qT[:, hh, :], Sb[:, hh, :], True, False, ['qT', 'Sb'], [okey])
                        else:
                            for b_ in range(4):
                                k.mm(oap, qTb[b_][:, hh, :], S0b[b_][:, hh, :], b_ == 0, False, ['qTb%d' % b_], [okey])
                        k.mm(oap, AT[:, hh, :], V[:, hh * 256:(hh + 1) * 256], False, True, ['AT', 'V'], [okey])
                    for hh in range(4):
                        oap = PSF(bo[hh // 2])[:, (hh % 2) * 256:(hh % 2 + 1) * 256]
                        k.act(jq[:, 0:1].to_broadcast([128, 256]), oap, AF.Square, ['ps%d' % bo[hh // 2]], ['jq', 'ss%d' % hh], accum_out=ss[:, hh:hh + 1])
                    k.act(rs[:], ss[:], AF.Ln, ['ss0', 'ss1', 'ss2', 'ss3'], ['rs'], bias=eps[:, 0:1], scale=1.0 / 256)
                    k.act(rs[:], rs[:], AF.Exp, ['rs'], ['rs'], scale=-0.5)
                    for cc in range(2):
                        wt, wk = next_w()
                        bg = nb()
                        projT(wt, wk, 512, PSF(bg), 'ps%d' % bg)
                        k.act(sg[cc][:], PSF(bg), AF.Silu, ['ps%d' % bg], ['sg%d' % cc])
                        for hh in range(2):
                            hd = 2 * cc + hh
                            oap = PSF(bo[hd // 2])[:, (hd % 2) * 256:(hd % 2 + 1) * 256]
                            k.stt(tmp[:, hd * 256:(hd + 1) * 256], oap, rs[:, hd:hd + 1], gnB[:], ALU.mult, ALU.mult,
                                  ['ps%d' % bo[hd // 2], 'rs'], ['tmp'])
                        k.tt(yain[:, cc * 512:(cc + 1) * 512], tmp[:, cc * 512:(cc + 1) * 512], sg[cc][:], ALU.mult, ['tmp', 'sg%d' % cc], ['yain'])
                    transpose8(yain, 'yain', yT, 'yT')
                    for cc in range(2):
                        wt, wk = next_w()
                        by = nb()
                        for kc in range(8):
                            k.mm(PSF(by), yT[:, kc, :], wt[:, kc, :], kc == 0, kc == 7, ['yT', wk], ['ps%d' % by])
                        wt, wk = next_w()
                        bm = nb()
                        projT(wt, wk, 512, PSF(bm), 'ps%d' % bm)
                        k.dma('act', gtb[cc][:], gate_b[0:1, cc * 512:(cc + 1) * 512].partition_broadcast(128), [], ['gtb%d' % cc], 'gtb%d' % cc)
                        k.tt(sg[cc][:], PSF(bm), gtb[cc][:], ALU.add, ['ps%d' % bm, 'gtb%d' % cc], ['sg%d' % cc])
                        k.act(sg[cc][:], sg[cc][:], AF.Sigmoid, ['sg%d' % cc], ['sg%d' % cc])
                        k.tt(mrg[:, cc * 512:(cc + 1) * 512], PSF(by), sg[cc][:], ALU.mult, ['ps%d' % by, 'sg%d' % cc], ['mrg'])
                    if not PR:
                        for b_ in range(4):
                            s_ = b_ % 2
                            k.dma('sp', S0f[s_][:], state[b_].rearrange("h p e -> p h e"), [], ['S0f%d' % s_], 'S0f%d' % s_)
                            k.ts(Ktb[s_][:], Kt[:], rmaskS[:, b_:b_ + 1], None, ALU.mult, None, ['Kt'], ['Ktb%d' % s_])
                            for hp in range(2):
                                bs_ = nb()
                                for hh in range(2):
                                    hd = hp * 2 + hh
                                    k.mm(PSF(bs_)[:, hh * 256:(hh + 1) * 256], Ktb[s_][:, hd * 128:(hd + 1) * 128], V[:, hd * 256:(hd + 1) * 256],
                                         True, True, ['Ktb%d' % s_, 'V'], ['ps%d' % bs_])
                                for hh in range(2):
                                    hd = hp * 2 + hh
                                    k.stt(S0f[s_][:, hd, :], S0f[s_][:, hd, :], decs[:, hd * 4 + b_:hd * 4 + b_ + 1], PSF(bs_)[:, hh * 256:(hh + 1) * 256],
                                          ALU.mult, ALU.add, ['decs', 'ps%d' % bs_, 'S0f%d' % s_], ['S0f%d' % s_])
                            k.dma('act', gla_s[b_], S0f[s_][:].rearrange("p a b -> p (a b)"), ['S0f%d' % s_], [], 'Sno%d' % s_)

                    glaB = k.end_capture()
                    k.merge(bisA, glaB)

                    if PR:
                        def kvload(ci):
                            k0 = ci * 512
                            nt = min(4, n_tiles - ci * 4)
                            s_ = ci % 2
                            k.dma('sp', kTch[s_][:, :, :nt * 128], kT_d[:, :, k0:k0 + nt * 128], [], ['kTch%d' % s_], 'kTch%d' % s_)
                            k.dma('sp', vch[s_][:, :nt, :], v_d[:, ci * 4:ci * 4 + nt, :], [], ['vch%d' % s_], 'vch%d' % s_)
                        kvload(0)
                        if g + 1 < G:
                            k.dma('act', xo[:], xown[(g + 1) * 128:(g + 2) * 128, :], [], ['xo'], 'xo')
                        li = 0
                        groups = []
                        for kt in range(n_tiles):
                            ci, tl_ = kt // 4, kt % 4
                            mb_ = mbt[kt % 3]
                            mbk = 'mbt%d' % (kt % 3)
                            for gg in range(4):
                                bl = 4 + (li % 4)
                                p_ = pT[li % 3]
                                pk = 'pT%d' % (li % 3)
                                li += 1
                                k.capture()
                                if gg == 0:
                                    if tl_ == 1 and ci + 1 < nch:
                                        kvload(ci + 1)
                                    k.ts(mb_[:], sc[:, kt * 128:(kt + 1) * 128], thr[:, 0:1], NEG, ALU.is_lt, ALU.mult, ['sc', 'thr'], [mbk])
                                k.mm(PSF(bl), kTch[ci % 2][:, gg // 2, tl_ * 128:(tl_ + 1) * 128], QTz[:, gg, :], True, False,
                                     ['kTch%d' % (ci % 2), 'QTz'], ['ps%d' % bl])
                                k.mm(PSF(bl), mb_[:], idrepb[:], False, True, [mbk], ['ps%d' % bl])
                                k.act(p_[:], PSF(bl), AF.Exp, ['ps%d' % bl], [pk], scale=0.125)
                                s1 = k.end_capture()
                                k.capture()
                                k.mm(PSF(gg)[0:65, :], vch[ci % 2][:, tl_, gg * 65:(gg + 1) * 65], p_[:], kt == 0, kt == n_tiles - 1,
                                     ['vch%d' % (ci % 2), pk], ['ps%d' % gg])
                                s2 = k.end_capture()
                                groups.append((s1, s2))
                        SK = 2
                        for idx in range(len(groups) + SK):
                            if idx < len(groups):
                                P.ops.extend(groups[idx][0])
                            if idx >= SK:
                                P.ops.extend(groups[idx - SK][1])
                        for gg in range(4):
                            o_ = oT[gg % 2]
                            ok_ = 'oT%d' % (gg % 2)
                            k.cp('act', o_[:], PSF(gg)[0:65, :], ['ps%d' % gg], [ok_])
                            for r_ in range(4):
                                k.tr(PSF(4 + gg)[:, r_ * 65:(r_ + 1) * 65], o_[0:65, r_ * 128:(r_ + 1) * 128], identf[0:65, 0:65], [ok_], ['ps%d' % (4 + gg)])
                        NPT = 128
                    else:
                        mball = junk
                        for b_ in range(4):
                            k.cp('pool', QTs[:, b_, :, :].rearrange("p g (r t) -> p g r t", r=4),
                                 QTz[:].rearrange("p g (r t) -> p g r t", r=4)[:, :, :, 16 * b_:16 * b_ + 16], ['QTz'], ['QTs'])
                        k.ts(mball[:], sc[:, 0:2176], thr[:, 0:1], NEG, ALU.is_lt, ALU.mult, ['sc', 'thr'], ['junk'])
                        li = 0
                        for b_ in range(4):
                            k.cp('pool', kTs1[:, :, 2048:2064], kTnew[:, :, 16 * b_:16 * b_ + 16], ['kTnew'], ['kTs'])
                            bs_ = nb() % 4 + 4
                            k.mm(PSF(bs_)[:, 0:256], selSb[:, b_, :], vnb[:], True, True, ['vnb'], ['ps%d' % bs_])
                            k.cp('act', vexts1[:, 16, :].rearrange("p (g d) -> p g d", g=4)[:, :, 0:64],
                                 PSF(bs_)[:, 0:256].rearrange("p (g d) -> p g d", g=4), ['ps%d' % bs_], ['vexts'])
                            for t8 in range(2):
                                k.dma('sp', ckf[:], cache_k[b_, t8 * 1024:(t8 + 1) * 1024, :].rearrange("(t p) c -> p t c", p=128), [], ['ckf'], 'ckf')
                                k.cp('pool', ckb[:], ckf[:], ['ckf'], ['ckb'])
                                for t4 in range(2):
                                    bt_ = nb() % 4 + 4
                                    for tt_ in range(4):
                                        for c_ in range(2):
                                            col = (tt_ * 2 + c_) * 128
                                            k.tr(PSB(bt_)[:, col:col + 128], ckb[:, t4 * 4 + tt_, c_ * 128:(c_ + 1) * 128], identb[:], ['ckb'], ['ps%d' % bt_])
                                    c0_ = t8 * 1024 + t4 * 512
                                    k.cp('dve', kTs1[:, :, c0_:c0_ + 512].rearrange("p c (t s) -> p c t s", t=4),
                                         PSB(bt_).rearrange("p (t c s) -> p c t s", t=4, c=2), ['ps%d' % bt_], ['kTs'])
                                k.dma('sp', ckf[:], cache_v[b_, t8 * 1024:(t8 + 1) * 1024, :].rearrange("(t p) c -> p t c", p=128), [], ['ckf'], 'ckf')
                                k.cp('pool', vexts1[:, t8 * 8:(t8 + 1) * 8, :].rearrange("p t (g d) -> p t g d", g=4)[:, :, :, 0:64],
                                     ckf[:].rearrange("p t (g d) -> p t g d", g=4), ['ckf'], ['vexts'])
                            sgroups = []
                            for gg in range(4):
                                ob = b_ // 2
                                ocol = ((b_ % 2) * 4 + gg) * 64
                                qsel = QTs[:, b_, gg, :]
                                for kt in range(17):
                                    bl = 4 + (li % 4)
                                    p_ = pTs[li % 3]
                                    pk = 'pTs%d' % (li % 3)
                                    li += 1
                                    k.capture()
                                    k.mm(PSF(bl)[:, 0:64], kTs1[:, gg // 2, kt * 128:(kt + 1) * 128], qsel, True, False,
                                         ['kTs', 'QTs'], ['ps%d' % bl])
                                    k.mm(PSF(bl)[:, 0:64], mball[:, kt * 128:(kt + 1) * 128], idrepSb[:, b_, :], False, True, ['junk'], ['ps%d' % bl])
                                    k.act(p_[:], PSF(bl)[:, 0:64], AF.Exp, ['ps%d' % bl], [pk], scale=0.125)
                                    s1_ = k.end_capture()
                                    k.capture()
                                    k.mm(PSF(ob)[0:65, ocol:ocol + 64], vexts1[:, kt, gg * 65:(gg + 1) * 65], p_[:], kt == 0, kt == 16,
                                         ['vexts', pk], ['ps%d' % ob])
                                    s2_ = k.end_capture()
                                    sgroups.append((s1_, s2_))
                            SKS = 2
                            for idx in range(len(sgroups) + SKS):
                                if idx < len(sgroups):
                                    P.ops.extend(sgroups[idx][0])
                                if idx >= SKS:
                                    P.ops.extend(sgroups[idx - SKS][1])
                        oTs = sc[0:65, 0:1024]
                        ov = oTs.rearrange("p (g r b t) -> p b g r t", g=4, r=4, b=4)
                        for ob in range(2):
                            k.cp('act', ov[:, 2 * ob:2 * ob + 2], PSF(ob)[0:65, :].rearrange("p (b g r t) -> p b g r t", b=2, g=4, r=4),
                                 ['ps%d' % ob, 'junk'], ['sc'])
                        for gg in range(4):
                            for r_ in range(4):
                                c0_ = (gg * 4 + r_) * 64
                                k.tr(PSF(4 + gg)[0:64, r_ * 65:(r_ + 1) * 65], oTs[:, c0_:c0_ + 64], identf[0:65, 0:65], ['sc'], ['ps%d' % (4 + gg)])
                        NPT = 64
                    for gg in range(4):
                        pv4 = PSF(4 + gg)[0:NPT, 0:260].rearrange("p (r c) -> p r c", c=65)
                        k.recip(rd[0:NPT, 4 * gg:4 * gg + 4], pv4[:, :, 64], ['ps%d' % (4 + gg)], ['rd'])
                        for r_ in range(4):
                            hd = 4 * gg + r_
                            k.ts(tmp[0:NPT, hd * 64:(hd + 1) * 64], pv4[:, r_, 0:64], rd[0:NPT, hd:hd + 1], None, ALU.mult, None,
                                 ['ps%d' % (4 + gg), 'rd'], ['tmp'])
                    for cc in range(2):
                        wt, wk = next_w()
                        bg = nb()
                        projT(wt, wk, 512, PSF(bg), 'ps%d' % bg)
                        k.act(sg[cc][:], PSF(bg), AF.Silu, ['ps%d' % bg], ['sg%d' % cc])
                        k.tt(yain[0:NPT, cc * 512:(cc + 1) * 512], tmp[0:NPT, cc * 512:(cc + 1) * 512], sg[cc][0:NPT, :], ALU.mult,
                             ['tmp', 'sg%d' % cc], ['yain'])
                    transpose8(yain, 'yain', yT, 'yT')
                    for cc in range(2):
                        wt, wk = next_w()
                        by = nb()
                        for kc in range(8):
                            k.mm(PSF(by), yT[:, kc, :], wt[:, kc, :], kc == 0, kc == 7, ['yT', wk], ['ps%d' % by])
                        wt, wk = next_w()
                        bm = nb()
                        projT(wt, wk, 512, PSF(bm), 'ps%d' % bm)
                        k.dma('act', gtb[cc][:], gate_b[0:1, 1024 + cc * 512:1024 + (cc + 1) * 512].partition_broadcast(128), [], ['gtb%d' % cc], 'gtb%d' % cc)
                        k.tt(sg[cc][:], PSF(bm), gtb[cc][:], ALU.add, ['ps%d' % bm, 'gtb%d' % cc], ['sg%d' % cc])
                        k.act(sg[cc][:], sg[cc][:], AF.Sigmoid, ['sg%d' % cc], ['sg%d' % cc])
                        k.tt(sg[cc][:], PSF(by), sg[cc][:], ALU.mult, ['ps%d' % by, 'sg%d' % cc], ['sg%d' % cc])
                        k.tt(mrg[:, cc * 512:(cc + 1) * 512], mrg[:, cc * 512:(cc + 1) * 512], sg[cc][:], ALU.add, ['mrg', 'sg%d' % cc], ['mrg'])
                    k.cp('pool', yain[:], mrg[:], ['mrg'], ['yain'])
                    transpose8(yain, 'yain', yT, 'yT')
                    for cc in range(2):
                        wt, wk = next_w()
                        by = nb()
                        for kc in range(8):
                            k.mm(PSF(by), yT[:, kc, :], wt[:, kc, :], kc == 0, kc == 7, ['yT', wk], ['ps%d' % by])
                        k.stt(mrg[:, cc * 512:(cc + 1) * 512], h[:, cc * 512:(cc + 1) * 512], ALPHA, PSF(by), ALU.mult, ALU.add, ['h', 'ps%d' % by], ['mrg'])
                    yk = 'yout' if PR else 'xo'
                    layernorm(mrg[:], 'mrg', g2B[:], b2B[:], tl, 'c', out_f32=yout[:], out_f32_key=yk)
                    ydst = y_own[g * 128:(g + 1) * 128, :] if PR else ys
                    k.dma('act', ydst, yout[:], [yk], [], 'youto')

                for g in (range(G) if PR else [0]):
                    own_block(g)
            P.barrier()

        if "B" in phases:
            phase_own('P')
        if "S" in phases:
            phase_own('S')

        P.emit(nc, top)
    return nc


def _consts(j, G):
    c = {}
    p = np.arange(128)
    c["c_ident"] = np.eye(128, dtype=np.float32)
    jj, ii = np.meshgrid(p, p, indexing="ij")
    c["c_triA"] = np.where((jj > ii) & (jj // 64 == ii // 64), -1.0 / 16, 0.0).astype(np.float32)
    c["c_blkA"] = np.where(p[:, None] // 64 == np.arange(2)[None, :], -1.0 / 16, 0.0).astype(np.float32)
    c["c_triB"] = np.where(jj <= ii, -1.0 / 16, 0.0).astype(np.float32)
    mB = (jj <= ii).astype(np.float32)
    c["c_maskB"] = np.ascontiguousarray(np.broadcast_to(mB[:, None, :], (128, 4, 128)))
    same = (jj // 16 == ii // 16)
    c["c_triS"] = np.where((jj <= ii) & same, -1.0 / 16, 0.0).astype(np.float32)
    mS = ((jj <= ii) & same).astype(np.float32)
    c["c_maskS"] = np.ascontiguousarray(np.broadcast_to(mS[:, None, :], (128, 4, 128)))
    c["c_mrevS"] = np.where((jj > ii) & same, -1.0 / 16, 0.0).astype(np.float32)
    c["c_bsumS"] = np.where(p[:, None] // 16 == np.arange(4)[None, :], -1.0 / 16, 0.0).astype(np.float32)
    cm = (p[None, :] // 16 == np.arange(4)[:, None]).astype(np.float32)
    c["c_cmaskS"] = np.ascontiguousarray(np.broadcast_to(cm[None], (128, 4, 128)))
    c["c_rmaskS"] = (p[:, None] // 16 == np.arange(4)[None, :]).astype(np.float32)
    c["c_idrep"] = np.ascontiguousarray(np.tile(np.eye(128, dtype=np.float32), (1, 4)))
    ids = np.zeros((128, 4, 64), np.float32)
    sel = np.zeros((128, 4, 128), np.float32)
    for b in range(4):
        for t in range(16):
            for r in range(4):
                ids[16 * b + t, b, r * 16 + t] = 1.0
            sel[16 * b + t, b, t] = 1.0
    c["c_idrepS"] = ids
    c["c_selS"] = sel
    c["c_ctab"] = np.ascontiguousarray(np.broadcast_to((0.5 ** np.arange(1, KIT + 2))[None, :], (128, KIT + 1))).astype(np.float32)
    tb = np.zeros((128, 2, 640), np.float32)
    kk = np.arange(640)
    for r in range(128):
        lim = 128 * j + (16 if r < 16 else (80 if r < 80 else 144))
        tb[r, 0, :] = np.where(kk < lim, 0.0, -1e30)
    tb[:, 1, :] = np.where(kk < 16, 0.0, -1e30)[None, :]
    c["c_tbias"] = tb
    oh = np.zeros((128, 4), np.float32)
    oh[:, j] = 1.0
    c["c_onehot"] = oh
    c["c_tailmask"] = (p < 16).astype(np.float32)[:, None]
    return c


def _dq_perm():
    perm = np.zeros(1024, np.int64)
    n = 0
    for m in range(2):
        for r in range(4):
            for half in range(2):
                g = 2 * m + half
                for d in range(64):
                    perm[n] = (g * 4 + r) * 64 + d
                    n += 1
    return perm


def prep(inp, SEQ):
    T, NB, G = geometry(SEQ)
    f = lambda a: np.ascontiguousarray(np.asarray(a, dtype=np.float32))
    w_in = f(inp["w_in"])[0].copy()
    w_in[:, C_DQ:C_DQ + 1024] = w_in[:, C_DQ:C_DQ + 1024][:, _dq_perm()]
    w3 = np.ascontiguousarray(np.stack([f(inp["w_gla"])[0], f(inp["w_dsa"])[0], f(inp["w_out"])[0]], 0))
    shared = dict(
        w_in=np.ascontiguousarray(w_in), w3=w3,
        ln_in_g=f(inp["ln_in_g"]).reshape(1, D), ln_in_b=f(inp["ln_in_b"]).reshape(1, D),
        ln_g=f(inp["ln_g"]).reshape(1, D), ln_b=f(inp["ln_b"]).reshape(1, D),
        gate_b=f(inp["gate_b"]).reshape(1, 2048), gla_gate_b=f(inp["gla_gate_b"]).reshape(1, 512),
        gla_norm_g=f(inp["gla_norm_g"]).reshape(1, 256),
        idx_kn_g=f(inp["idx_kn_g"]).reshape(1, 64), idx_kn_b=f(inp["idx_kn_b"]).reshape(1, 64),
        gla_w2=f(inp["gla_w2"]).reshape(16, 512))
    xp = f(inp["x_prompt"]); meta = f(inp["meta"]); xsm = f(inp["x_sample"])
    ck = f(inp["cache_k"])[0]; cv = f(inp["cache_v"])[0]; cik = f(inp["cache_idx_k"])[0]; stt = f(inp["state_gla"])[0]
    maps = []
    for c in range(8):
        b, j = c // 4, c % 4
        xall = np.zeros((4 * G * 128, D), np.float32)
        xall[:16] = meta
        xall[16:T] = xp[b]
        xown = np.ascontiguousarray(xall.reshape(G, 4, 128, D)[:, j].reshape(G * 128, D))
        xs_ = np.zeros((128, D), np.float32)
        xs_[:64] = xsm[4 * c:4 * c + 4].reshape(64, D)
        m = dict(shared)
        m.update(_consts(j, G))
        m.update(xall=np.ascontiguousarray(xall[:NB * 128]), xown=xown, xs=xs_,
                 cache_k=np.ascontiguousarray(ck[4 * c:4 * c + 4].reshape(4, 2048, 256)),
                 cache_v=np.ascontiguousarray(cv[4 * c:4 * c + 4].reshape(4, 2048, 256)),
                 cache_ik=np.ascontiguousarray(cik[4 * c:4 * c + 4]),
                 state=np.ascontiguousarray(stt[4 * c:4 * c + 4]))
        maps.append(m)
    return maps


def gather(res, SEQ):
    T, NB, G = geometry(SEQ)
    R = res.results
    yp = np.zeros((2, 4 * G * 128, D), np.float32)
    for c in range(8):
        b, j = c // 4, c % 4
        yp[b].reshape(G, 4, 128, D)[:, j] = R[c]["y_own"].reshape(G, 128, D)
    y_prompt = np.ascontiguousarray(yp[:, 16:T])
    y_sample = np.concatenate([R[c]["ys"][:64].reshape(4, 16, D) for c in range(8)], 0)
    k_prompt = np.stack([R[4 * b]["kp"][:T].reshape(T, 4, 64) for b in range(2)], 0)[None]
    v_prompt = np.stack([R[4 * b]["vp"][:T].reshape(T, 4, 64) for b in range(2)], 0)[None]
    ik_prompt = np.stack([R[4 * b]["ikp"][:T] for b in range(2)], 0)[None]
    gla_prompt = np.stack([R[4 * b]["gla_p"].reshape(128, 4, 256).transpose(1, 0, 2) for b in range(2)], 0)[None]
    k_sample = np.concatenate([R[c]["ks"][:64].reshape(4, 16, 4, 64) for c in range(8)], 0)[None]
    v_sample = np.concatenate([R[c]["vs"][:64].reshape(4, 16, 4, 64) for c in range(8)], 0)[None]
    ik_sample = np.concatenate([R[c]["iks"][:64].reshape(4, 16, 64) for c in range(8)], 0)[None]
    gla_sample = np.concatenate([R[c]["gla_s"].reshape(4, 128, 4, 256).transpose(0, 2, 1, 3) for c in range(8)], 0)[None]
    outs = (y_prompt, y_sample, k_prompt, v_prompt, ik_prompt, gla_prompt, k_sample, v_sample, ik_sample, gla_sample)
    return tuple(np.ascontiguousarray(o, dtype=np.float32) for o in outs)


_NC_CACHE = {}


def run(inputs, SEQ, phases="0ABS"):
    key = (SEQ, phases)
    if key not in _NC_CACHE:
        _NC_CACHE[key] = build(SEQ, phases)
    nc = _NC_CACHE[key]
    maps = prep(inputs, SEQ)
    res = run_bass_kernel_spmd(nc, maps, core_ids=list(range(8)))
    return gather(res, SEQ)


def kernel(**inputs):
    SEQ = int(np.asarray(inputs["x_prompt"]).shape[1])
    return run(inputs, SEQ)
```

```python
from contextlib import ExitStack
import numpy as np
import concourse.bass as bass
import concourse.mybir as mybir
from concourse.bass_utils import run_bass_kernel_spmd

F32 = mybir.dt.float32
BF16 = mybir.dt.bfloat16
AF = mybir.ActivationFunctionType
ALU = mybir.AluOpType
AX = mybir.AxisListType

D = 1024
NEG = -30000.0
KIT = 22
IDX_W_SCALE = (8 ** -0.5) * (64 ** -0.5)
ALPHA = 2.0 ** 0.25
EPS = 1e-5
C_GQ, C_GK, C_GV, C_GLOW, C_GR, C_DQ, C_DK, C_DV, C_IQ, C_IK, C_IW, C_DZ, C_MA, C_MB = (
    0, 512, 1024, 2048, 2064, 3088, 4112, 4368, 4624, 5136, 5200, 5208, 6232, 7256)
IN_COLS = 8280


class Prog:
    def __init__(self):
        self.ops = []

    def add(self, eng, fn, r=(), w=(), dsem=None):
        r = list(r)
        w = list(w)
        for b in list(r):
            if isinstance(b, str) and b.startswith('ps'):
                r.remove(b)
                if b not in w:
                    w.append(b)
        self.ops.append(dict(eng=eng, fn=fn, r=tuple(r), w=tuple(w), dsem=dsem))

    def barrier(self):
        self.ops.append(dict(eng='barrier', fn=None, r=(), w=(), dsem=None))

    def analyze(self):
        ops = self.ops
        last_w, readers = {}, {}
        last_of = {}
        for i, op in enumerate(ops):
            if op['eng'] == 'barrier':
                for e_, j in last_of.items():
                    ops[j]['needed'] = True
                last_w, readers = {}, {}
                op['deps'] = set()
                continue
            deps = set()
            for b in op['r']:
                if b in last_w:
                    deps.add(('raw', last_w[b]))
            for b in op['w']:
                if b in last_w:
                    deps.add(('waw', last_w[b]))
                for rr in readers.get(b, ()):
                    deps.add(('war', rr))
            for b in op['r']:
                readers.setdefault(b, []).append(i)
            for b in op['w']:
                last_w[b] = i
                readers[b] = []
            keep = set()
            for kind, j in deps:
                if j == i:
                    continue
                pj = ops[j]
                if pj['dsem'] is None and op['dsem'] is None and pj['eng'] == op['eng']:
                    if op['eng'] == 'pe' or kind == 'war':
                        continue
                keep.add(j)
            op['deps'] = keep
            for j in keep:
                ops[j]['needed'] = True
            if op['dsem'] is None:
                last_of[op['eng']] = i
        for e_, j in last_of.items():
            ops[j]['needed'] = True
        cnt = {}
        for op in ops:
            if op['eng'] == 'barrier':
                continue
            if op['dsem'] is not None:
                k = 'D:' + op['dsem']
                cnt[k] = cnt.get(k, 0) + 16
                op['sem'] = k
                op['val'] = cnt[k]
            elif op.get('needed'):
                k = 'E:' + op['eng']
                cnt[k] = cnt.get(k, 0) + 1
                op['sem'] = k
                op['val'] = cnt[k]
        waited = {}
        running = {}
        pending = {}
        for op in ops:
            if op['eng'] == 'barrier':
                for e_ in ('pe', 'act', 'dve', 'pool', 'sp'):
                    pending[e_] = dict(running)
                continue
            ws = {}
            if pending.get(op['eng']):
                ws.update(pending[op['eng']])
                pending[op['eng']] = None
            for j in op['deps']:
                pj = ops[j]
                ws[pj['sem']] = max(ws.get(pj['sem'], 0), pj['val'])
            wl = []
            we = waited.setdefault(op['eng'], {})
            for k, v in ws.items():
                if we.get(k, 0) >= v:
                    continue
                we[k] = v
                wl.append((k, v))
            op['waits'] = wl
            if op.get('sem') is not None:
                running[op['sem']] = op['val']
        self.totals = cnt
        return cnt

    def emit(self, nc, es):
        cnt = self.analyze()
        sems = {}
        for k in cnt:
            sems[k] = es.enter_context(nc.semaphore(k.replace(':', '_')))
        block = es.enter_context(nc.Block())
        ops = self.ops

        def run(engname):
            def f(e):
                for op in ops:
                    if op['eng'] != engname:
                        continue
                    for k, v in op['waits']:
                        e.wait_ge(sems[k], v)
                    ins = op['fn'](e)
                    if op.get('sem') is not None:
                        ins.then_inc(sems[op['sem']], 16 if op['dsem'] is not None else 1)
                for k, v in cnt.items():
                    e.wait_ge(sems[k], v)
            return f

        block.sync(run('sp'))
        block.scalar(run('act'))
        block.vector(run('dve'))
        block.gpsimd(run('pool'))
        block.tensor(run('pe'))


class KB:
    def __init__(self, nc):
        self.nc = nc
        self.P = Prog()
        self.rot = 0

    def capture(self):
        self._saved = self.P.ops
        self.P.ops = []

    def end_capture(self):
        l = self.P.ops
        self.P.ops = self._saved
        return l

    def merge(self, A, B):
        out = []
        ia = ib = 0
        na, nb_ = max(len(A), 1), max(len(B), 1)
        while ia < len(A) or ib < len(B):
            if ib >= len(B) or (ia < len(A) and ia * nb_ <= ib * na):
                out.append(A[ia]); ia += 1
            else:
                out.append(B[ib]); ib += 1
        self.P.ops.extend(out)

    def act(self, out, in_, func, r, w, **kw):
        self.P.add('act', lambda e: e.activation(out=out, in_=in_, func=func, **kw), r, w)

    def ts(self, out, in0, s1, s2, op0, op1, r, w, eng='dve', accum=None):
        if accum is None:
            if op1 is None:
                self.P.add(eng, lambda e: e.tensor_scalar(out=out, in0=in0, scalar1=s1, scalar2=None, op0=op0), r, w)
            else:
                self.P.add(eng, lambda e: e.tensor_scalar(out=out, in0=in0, scalar1=s1, scalar2=s2, op0=op0, op1=op1), r, w)
        else:
            self.P.add(eng, lambda e: e.tensor_scalar(out=out, in0=in0, scalar1=s1, scalar2=s2, op0=op0, op1=op1,
                                                      accum_out=accum), r, w)

    def tt(self, out, in0, in1, op, r, w, eng='dve'):
        self.P.add(eng, lambda e: e.tensor_tensor(out=out, in0=in0, in1=in1, op=op), r, w)

    def stt(self, out, in0, scalar, in1, op0, op1, r, w):
        self.P.add('dve', lambda e: e.scalar_tensor_tensor(out=out, in0=in0, scalar=scalar, in1=in1, op0=op0, op1=op1), r, w)

    def cp(self, eng, out, in_, r, w):
        if eng == 'act':
            self.P.add('act', lambda e: e.copy(out=out, in_=in_), r, w)
        else:
            self.P.add(eng, lambda e: e.tensor_copy(out=out, in_=in_), r, w)

    def memset(self, eng, ap, val, w):
        self.P.add(eng, lambda e: e.memset(ap, val), (), w)

    def mm(self, out, lhsT, rhs, start, stop, r, w):
        self.P.add('pe', lambda e: e.matmul(out, lhsT=lhsT, rhs=rhs, start=start, stop=stop), r, w)

    def tr(self, out, in_, ident, r, w):
        self.P.add('pe', lambda e: e.transpose(out=out, in_=in_, identity=ident), r, w)

    def dma(self, q, out, in_, r, w, dsem):
        self.P.add(q, lambda e: e.dma_start(out=out, in_=in_), r, w, dsem=dsem)

    def red(self, out, in_, op, r, w):
        self.P.add('dve', lambda e: e.tensor_reduce(out=out, in_=in_, axis=AX.X, op=op), r, w)

    def recip(self, out, in_, r, w):
        self.P.add('dve', lambda e: e.reciprocal(out=out, in_=in_), r, w)

    def bn_stats(self, out, in_, r, w):
        self.P.add('dve', lambda e: e.bn_stats(out=out, in_=in_), r, w)

    def bn_aggr(self, out, in_, r, w):
        self.P.add('dve', lambda e: e.bn_aggr(out=out, in_=in_), r, w)


def geometry(SEQ):
    T = SEQ + 16
    NB = T // 128 + 1
    assert T == 128 * (NB - 1) + 16 and NB % 4 == 1
    G = (NB + 3) // 4
    return T, NB, G


def build(SEQ, phases="0ABS"):
    T, NB, G = geometry(SEQ)
    NKMAX = NB * 128
    nc = bass.Bass("TRN2", target_bir_lowering=False)

    def din(name, shape, dt=F32):
        return nc.dram_tensor(name, list(shape), dt, kind="ExternalInput").ap()

    def dout(name, shape, dt=F32):
        return nc.dram_tensor(name, list(shape), dt, kind="ExternalOutput").ap()

    def dscr(name, shape, dt):
        return nc.dram_tensor(name, list(shape), dt, kind="Internal").ap()

    xall = din("xall", [NB * 128, D])
    xown = din("xown", [G * 128, D])
    xs = din("xs", [128, D])
    w_in = din("w_in", [D, IN_COLS])
    w3 = din("w3", [3, D, D])
    ln_in_g = din("ln_in_g", [1, D]); ln_in_b = din("ln_in_b", [1, D])
    ln_g = din("ln_g", [1, D]); ln_b = din("ln_b", [1, D])
    gate_b = din("gate_b", [1, 2048])
    gla_gate_b = din("gla_gate_b", [1, 512])
    gla_norm_g = din("gla_norm_g", [1, 256])
    idx_kn_g = din("idx_kn_g", [1, 64]); idx_kn_b = din("idx_kn_b", [1, 64])
    gla_w2 = din("gla_w2", [16, 512])
    cache_k = din("cache_k", [4, 2048, 256]); cache_v = din("cache_v", [4, 2048, 256])
    cache_ik = din("cache_ik", [4, 2048, 64])
    state = din("state", [4, 4, 128, 256])
    c_ident = din("c_ident", [128, 128])
    c_triA = din("c_triA", [128, 128]); c_blkA = din("c_blkA", [128, 2])
    c_triB = din("c_triB", [128, 128]); c_maskB = din("c_maskB", [128, 4, 128])
    c_triS = din("c_triS", [128, 128]); c_maskS = din("c_maskS", [128, 4, 128])
    c_mrevS = din("c_mrevS", [128, 128]); c_bsumS = din("c_bsumS", [128, 4])
    c_cmaskS = din("c_cmaskS", [128, 4, 128]); c_rmaskS = din("c_rmaskS", [128, 4])
    c_idrep = din("c_idrep", [128, 512]); c_idrepS = din("c_idrepS", [128, 4, 64])
    c_selS = din("c_selS", [128, 4, 128])
    c_ctab = din("c_ctab", [128, KIT + 1])
    c_tbias = din("c_tbias", [128, 2, 640])
    c_onehot = din("c_onehot", [128, 4])
    c_tailmask = din("c_tailmask", [128, 1])

    y_own = dout("y_own", [G * 128, D])
    kp = dout("kp", [NB * 128, 256]); vp = dout("vp", [NB * 128, 256]); ikp = dout("ikp", [NB * 128, 64])
    gla_p = dout("gla_p", [128, 1024])
    ys = dout("ys", [128, D]); ks = dout("ks", [128, 256]); vs = dout("vs", [128, 256]); iks = dout("iks", [128, 64])
    gla_s = dout("gla_s", [4, 128, 1024])

    wbf = dscr("wbf", [D, IN_COLS], BF16)
    wbf3 = dscr("wbf3", [3, D, D], BF16)
    kT_d = dscr("kT_d", [128, 2, NKMAX], BF16)
    v_d = dscr("v_d", [128, NB, 260], BF16)
    ki_d = dscr("ki_d", [128, NKMAX], BF16)
    snap = dscr("snap", [NB, 128, 1024], F32)

    k = KB(nc)
    P = k.P
    top = ExitStack()
    with top:
        pst = [top.enter_context(nc.psum_tensor("ps%d" % i, [128, 512], F32)) for i in range(8)]

        def PSF(i):
            return pst[i][:]

        def PSB(i):
            return pst[i][:].bitcast(BF16)

        if "0" in phases:
            with ExitStack() as es:
                def sb(name, shape, dt):
                    return es.enter_context(nc.sbuf_tensor(name, shape, dt))
                wst = [sb("wst%d" % s, [128, 8, 512], F32) for s in range(2)]
                wcb = [sb("wcb%d" % s, [128, 8, 512], BF16) for s in range(2)]
                jobs = []
                for c in range(17):
                    c0 = c * 512
                    n = min(512, IN_COLS - c0)
                    jobs.append((w_in[:, c0:c0 + n], wbf[:, c0:c0 + n], n))
                for m in range(3):
                    for c in range(2):
                        jobs.append((w3[m, :, c * 512:(c + 1) * 512], wbf3[m, :, c * 512:(c + 1) * 512], 512))
                engs = ['dve', 'act', 'pool']
                for idx, (src, dst, n) in enumerate(jobs):
                    s = idx % 2
                    k.dma('sp', wst[s][:, :, :n], src.rearrange("(k p) n -> p k n", p=128), [], ['wst%d' % s], 'wst%d' % s)
                    k.cp(engs[idx % 3], wcb[s][:, :, :n], wst[s][:, :, :n], ['wst%d' % s], ['wcb%d' % s])
                    k.dma('act', dst.rearrange("(k p) n -> p k n", p=128), wcb[s][:, :, :n], ['wcb%d' % s], [], 'wcb%d' % s)
            P.barrier()

        def layernorm(src, srckey, gB, bB, tl, tag, out_f32=None, out_f32_key=None, out_bf=None, out_bf_key=None):
            tk = lambda n: tag + n
            for c in range(2):
                k.bn_stats(tl['st'][:, c, :], src[:, c * 512:(c + 1) * 512], [srckey], [tk('st%d' % c)])
            k.bn_aggr(tl['mv'][:], tl['st'][:].rearrange("p a b -> p (a b)"), [tk('st0'), tk('st1')], [tk('mv')])
            k.act(tl['sd'][:], tl['mv'][:, 1:2], AF.Ln, [tk('mv'), 'eps'], [tk('sd')], bias=tl['eps'][:, 0:1], scale=1.0)
            k.act(tl['rstd'][:], tl['sd'][:], AF.Exp, [tk('sd')], [tk('rstd')], scale=-0.5)
            k.ts(tl['nmr'][:], tl['mv'][:, 0:1], tl['rstd'][:, 0:1], -1.0, ALU.mult, ALU.mult, [tk('mv'), tk('rstd')], [tk('nmr')])
            k.act(tl['xn'][:], src, AF.Identity, [srckey, tk('nmr'), tk('rstd')], [tl['xnkey']],
                  bias=tl['nmr'][:, 0:1], scale=tl['rstd'][:, 0:1])
            k.tt(tl['xn'][:], tl['xn'][:], gB, ALU.mult, [tl['xnkey'], 'lnconst'], [tl['xnkey']])
            if out_f32 is not None:
                k.tt(out_f32, tl['xn'][:], bB, ALU.add, [tl['xnkey'], 'lnconst'], [out_f32_key])
                if out_bf is not None:
                    k.cp('pool', out_bf, out_f32, [out_f32_key], [out_bf_key])
            else:
                k.tt(out_bf, tl['xn'][:], bB, ALU.add, [tl['xnkey'], 'lnconst'], [out_bf_key])

        if "A" in phases:
            with ExitStack() as es:
                def sb(name, shape, dt):
                    return es.enter_context(nc.sbuf_tensor(name, shape, dt))
                gB = sb("a_gB", [128, D], F32); bB = sb("a_bB", [128, D], F32)
                identf = sb("a_idf", [128, 128], F32); identb = sb("a_idb", [128, 128], BF16)
                triA = sb("a_triA", [128, 128], F32); blkA = sb("a_blkA", [128, 2], F32)
                w2 = sb("a_w2", [16, 512], F32); gbias = sb("a_gbias", [1, 512], F32); ones1 = sb("a_ones1", [1, 128], F32)
                gkiB = sb("a_gkiB", [128, 64], F32); bkiB = sb("a_bkiB", [128, 64], F32)
                eps = sb("a_eps", [128, 1], F32); one = sb("a_one", [128, 1], F32)
                tailm = sb("a_tailm", [128, 1], F32)
                wA = sb("a_wA", [128, 8, 2128], BF16)
                SS = [sb("a_S%d" % s, [128, 4, 256], F32) for s in range(3)]
                xa = [sb("a_xa%d" % s, [128, D], F32) for s in range(3)]
                xn = sb("a_xn", [128, D], F32)
                hb = [sb("a_hb%d" % s, [128, D], BF16) for s in range(3)]
                hT = [sb("a_hT%d" % s, [128, 8, 128], BF16) for s in range(2)]
                st = sb("a_st", [128, 2, 6], F32); mv = sb("a_mv", [128, 2], F32)
                sd = sb("a_sd", [128, 1], F32); rstd = sb("a_rstd", [128, 1], F32); nmr = sb("a_nmr", [128, 1], F32)
                st2 = sb("a_st2", [128, 6], F32); mv2 = sb("a_mv2", [128, 2], F32)
                sd2 = sb("a_sd2", [128, 1], F32); rstd2 = sb("a_rstd2", [128, 1], F32); nmr2 = sb("a_nmr2", [128, 1], F32)
                Vt = [sb("a_V%d" % s, [128, 1024], BF16) for s in range(4)]
                kdv = [sb("a_kdv%d" % s, [128, 512], F32) for s in range(3)]
                kdb = [sb("a_kdb%d" % s, [128, 256], BF16) for s in range(2)]
                vext = [sb("a_vext%d" % s, [128, 4, 65], BF16) for s in range(2)]
                kTt = [sb("a_kT%d" % s, [128, 2, 128], BF16) for s in range(2)]
                kin = [sb("a_kin%d" % s, [128, 64], F32) for s in range(3)]
                ksb = [sb("a_ksb%d" % s, [128, 512], F32) for s in range(3)]
                kif = [sb("a_kif%d" % s, [128, 64], F32) for s in range(2)]
                kib = [sb("a_kib%d" % s, [128, 128], BF16) for s in range(2)]
                kiT = [sb("a_kiT%d" % s, [128, 128], BF16) for s in range(2)]
                glb = sb("a_glb", [16, 128], BF16); w2b = sb("a_w2b", [16, 512], BF16); gbB = sb("a_gbB", [128, 512], F32)
                el = [sb("a_el%d" % s, [128, 512], F32) for s in range(3)]
                er = sb("a_er", [128, 512], F32)
                Kt = [sb("a_Kt%d" % s, [128, 512], BF16) for s in range(2)]
                dec = [sb("a_dec%d" % s, [128, 8], F32) for s in range(2)]

                k.dma('sp', gB[:], ln_in_g.partition_broadcast(128), [], ['lnconst0'], 'ca')
                k.dma('sp', bB[:], ln_in_b.partition_broadcast(128), [], ['lnconst1'], 'ca')
                k.dma('sp', identf[:], c_ident, [], ['identf'], 'ca')
                k.dma('sp', triA[:], c_triA, [], ['triA'], 'ca')
                k.dma('sp', blkA[:], c_blkA, [], ['blkA'], 'ca')
                k.dma('sp', w2[:], gla_w2, [], ['w2'], 'ca')
                k.dma('sp', gbias[:], gla_gate_b, [], ['gbias'], 'ca')
                k.dma('sp', gbB[:], gla_gate_b.partition_broadcast(128), [], ['gbB'], 'ca')
                k.dma('sp', gkiB[:], idx_kn_g.partition_broadcast(128), [], ['gkiB'], 'ca')
                k.dma('sp', bkiB[:], idx_kn_b.partition_broadcast(128), [], ['bkiB'], 'ca')
                k.dma('sp', tailm[:], c_tailmask, [], ['tailm'], 'ca')
                wmap = [(C_GK, 512, 0), (C_GV, 1024, 512), (C_DK, 512, 1536), (C_IK, 64, 2048), (C_GLOW, 16, 2112)]
                for (c0, n, o) in wmap:
                    k.dma('sp', wA[:, :, o:o + n], wbf[:, c0:c0 + n].rearrange("(k p) n -> p k n", p=128), [], ['wA%d' % o], 'ca')
                P.barrier()
                k.memset('dve', eps[:], EPS, ['eps'])
                k.memset('dve', one[:], 1.0, ['one'])
                k.memset('dve', ones1[:], 1.0, ['ones1'])
                k.memset('dve', SS[0][:], 0.0, ['S0'])
                for s in range(2):
                    k.memset('pool', vext[s][:], 1.0, ['vext%d' % s])
                k.cp('dve', identb[:], identf[:], [], ['identb'])
                k.cp('dve', w2b[:], w2[:], [], ['w2b'])
                P.barrier()
                tl = dict(st=st, mv=mv, sd=sd, rstd=rstd, nmr=nmr, xn=xn, eps=eps, xnkey='xn')

                def loadx(i):
                    s = i % 3
                    k.dma('sp', xa[s][:], xall[i * 128:(i + 1) * 128, :], [], ['xa%d' % s], 'xa%d' % s)

                loadx(0)

                def fa(i):
                    s = i % 3
                    if i + 1 < NB:
                        loadx(i + 1)
                    layernorm(xa[s][:], 'xa%d' % s, gB[:], bB[:], tl, 'a', out_bf=hb[s][:], out_bf_key='hb%d' % s)

                def fb_a(i):
                    s = i % 3
                    s2 = i % 2
                    for kc in range(8):
                        k.tr(PSB(0)[:, kc * 128:(kc + 1) * 128], hb[s][:, kc * 128:(kc + 1) * 128], identb[:], ['hb%d' % s], ['ps0'])
                    k.cp('act', hT[s2][:].rearrange("p a b -> p (a b)"), PSB(0), ['ps0'], ['hT%d' % s2])

                def fb_b(i):
                    s = i % 3
                    s2 = i % 2
                    last = (i == NB - 1)
                    hk = 'hT%d' % s2
                    for kc in range(8):
                        k.mm(PSF(1), hT[s2][:, kc, :], wA[:, kc, 0:512], kc == 0, kc == 7, [hk], ['ps1'])
                    k.cp('act', ksb[s][:], PSF(1), ['ps1'], ['ksb%d' % s])
                    for half in range(2):
                        for kc in range(8):
                            k.mm(PSF(2 + half), hT[s2][:, kc, :], wA[:, kc, 512 + half * 512:1024 + half * 512], kc == 0, kc == 7, [hk], ['ps%d' % (2 + half)])
                        k.cp('act' if half == 0 else 'dve', Vt[i % 4][:, half * 512:(half + 1) * 512], PSF(2 + half), ['ps%d' % (2 + half)], ['V%d' % (i % 4)])
                    for kc in range(8):
                        k.mm(PSF(4), hT[s2][:, kc, :], wA[:, kc, 1536:2048], kc == 0, kc == 7, [hk], ['ps4'])
                    k.cp('act', kdv[s][:], PSF(4), ['ps4'], ['kdv%d' % s])
                    for kc in range(8):
                        k.mm(PSF(5)[:, 0:64], hT[s2][:, kc, :], wA[:, kc, 2048:2112], kc == 0, kc == 7, [hk], ['ps5'])
                    k.cp('dve', kin[s][:], PSF(5)[:, 0:64], ['ps5'], ['kin%d' % s])

                    for kc in range(8):
                        k.mm(PSF(5)[0:16, 64:192], wA[:, kc, 2112:2128], hT[s2][:, kc, :], kc == 0, kc == 7, [hk], ['ps5'])
                    k.cp('dve', glb[:], PSF(5)[0:16, 64:192], ['ps5'], ['gl'])
                    k.mm(PSF(4), glb[:], w2b[:], True, True, ['gl'], ['ps4'])
                    k.tt(el[s][:], PSF(4), gbB[:], ALU.add, ['ps4'], ['el%d' % s])
                    k.act(el[s][:], el[s][:], AF.Exp, ['el%d' % s], ['el%d' % s], scale=-1.0)
                    k.act(el[s][:], el[s][:], AF.Ln, ['el%d' % s], ['el%d' % s], bias=one[:, 0:1], scale=1.0)
                    if last:
                        k.ts(el[s][:], el[s][:], tailm[:, 0:1], None, ALU.mult, None, ['el%d' % s], ['el%d' % s])
                def back1(i):
                    s3 = i % 3
                    s = i % 2
                    last = (i == NB - 1)
                    k.mm(PSF(6), triA[:], el[s3][:], True, True, ['el%d' % s3], ['ps6'])
                    for h in range(4):
                        k.mm(PSF(7)[:, 448 + 2 * h:450 + 2 * h], el[s3][:, h * 128:(h + 1) * 128], blkA[:], True, True, ['el%d' % s3], ['ps7'])
                    k.act(er[:], PSF(6), AF.Exp, ['ps6'], ['er'])
                    k.act(dec[s][:], PSF(7)[:, 448:456], AF.Exp, ['ps7'], ['dec%d' % s])
                    if last:
                        k.stt(Kt[s][:], ksb[s3][:], tailm[:, 0:1], er[:], ALU.mult, ALU.mult, ['ksb%d' % s3, 'er'], ['Kt%d' % s])
                    else:
                        k.tt(Kt[s][:], ksb[s3][:], er[:], ALU.mult, ['ksb%d' % s3, 'er'], ['Kt%d' % s])
                    k.dma('act', kp[i * 128:(i + 1) * 128, :], kdv[s3][:, 0:256], ['kdv%d' % s3], [], 'kdvo%d' % s3)
                    k.dma('act', vp[i * 128:(i + 1) * 128, :], kdv[s3][:, 256:512], ['kdv%d' % s3], [], 'kdvo%d' % s3)
                    k.cp('pool', kdb[s][:], kdv[s3][:, 0:256], ['kdv%d' % s3], ['kdb%d' % s])
                    k.cp('pool', vext[s][:, :, 0:64], kdv[s3][:, 256:512].rearrange("p (g d) -> p g d", g=4), ['kdv%d' % s3], ['vext%d' % s])
                    for c in range(2):
                        k.tr(PSB(7)[:, c * 128:(c + 1) * 128], kdb[s][:, c * 128:(c + 1) * 128], identb[:], ['kdb%d' % s], ['ps7'])
                    k.cp('dve', kTt[s][:].rearrange("p a b -> p (a b)"), PSB(7)[:, 0:256], ['ps7'], ['kT%d' % s])
                    k.dma('pool', kT_d[:, :, i * 128:(i + 1) * 128], kTt[s][:], ['kT%d' % s], [], 'kTo%d' % s)
                    k.dma('pool', v_d[:, i, :], vext[s][:].rearrange("p g d -> p (g d)"), ['vext%d' % s], [], 'vexto%d' % s)
                    kn = kin[s3]
                    knk = 'kin%d' % s3
                    k.bn_stats(st2[:], kn[:], [knk], ['st2'])
                    k.bn_aggr(mv2[:], st2[:], ['st2'], ['mv2'])
                    k.act(sd2[:], mv2[:, 1:2], AF.Ln, ['mv2'], ['sd2'], bias=eps[:, 0:1], scale=1.0)
                    k.act(rstd2[:], sd2[:], AF.Exp, ['sd2'], ['rstd2'], scale=-0.5)
                    k.ts(nmr2[:], mv2[:, 0:1], rstd2[:, 0:1], -1.0, ALU.mult, ALU.mult, ['mv2', 'rstd2'], ['nmr2'])
                    k.act(kn[:], kn[:], AF.Identity, [knk, 'nmr2', 'rstd2'], [knk], bias=nmr2[:, 0:1], scale=rstd2[:, 0:1])
                    k.tt(kn[:], kn[:], gkiB[:], ALU.mult, [knk], [knk])
                    k.tt(kif[s][:], kn[:], bkiB[:], ALU.add, [knk], ['kif%d' % s])
                    k.dma('act', ikp[i * 128:(i + 1) * 128, :], kif[s][:], ['kif%d' % s], [], 'kifo%d' % s)
                    k.cp('pool', kib[s][:, 0:64], kif[s][:], ['kif%d' % s], ['kib%d' % s])
                    k.cp('pool', kib[s][:, 64:128], kif[s][:], ['kif%d' % s], ['kib%d' % s])

                def back1b(i):
                    s = i % 2
                    k.tr(PSB(7)[:, 256:384], kib[s][:], identb[:], ['kib%d' % s], ['ps7'])
                    k.cp('dve', kiT[s][:], PSB(7)[:, 256:384], ['ps7'], ['kiT%d' % s])
                    k.dma('pool', ki_d[:, i * 128:(i + 1) * 128], kiT[s][:], ['kiT%d' % s], [], 'kiTo%d' % s)

                def back2(i):
                    s3 = i % 3
                    s = i % 2
                    cur = (2 * i) % 3
                    k.dma('sp', snap[i], SS[cur][:].rearrange("p h e -> p (h e)"), ['S%d' % cur], [], 'Ssto%d' % cur)
                    sbanks = [[6, 7], [2, 3]]
                    for c in range(2):
                        for hp in range(2):
                            bk = sbanks[c][hp]
                            for hh in range(2):
                                h = hp * 2 + hh
                                k.mm(PSF(bk)[:, hh * 256:(hh + 1) * 256], Kt[s][c * 64:(c + 1) * 64, h * 128:(h + 1) * 128],
                                     Vt[i % 4][c * 64:(c + 1) * 64, h * 256:(h + 1) * 256], True, True, ['Kt%d' % s, 'V%d' % (i % 4)], ['ps%d' % bk])
                            src_, dst_ = (2 * i + c) % 3, (2 * i + c + 1) % 3
                            for hh in range(2):
                                h = hp * 2 + hh
                                k.stt(SS[dst_][:, h, :], SS[src_][:, h, :], dec[s][:, 2 * h + c:2 * h + c + 1], PSF(bk)[:, hh * 256:(hh + 1) * 256],
                                      ALU.mult, ALU.add, ['S%d' % src_, 'dec%d' % s, 'ps%d' % bk], ['S%d' % dst_])

                for i0 in range(min(3, NB)):
                    fa(i0)
                for i0 in range(3):
                    if i0 < NB:
                        fb_a(i0)
                        fb_b(i0)
                    if i0 + 3 < NB and i0 < 2:
                        fa(i0 + 3)
                back1(0)
                back1b(0)
                for i in range(NB):
                    if i + 3 < NB:
                        fb_a(i + 3)
                    back2(i)
                    if i + 1 < NB:
                        back1(i + 1)
                    if i + 5 < NB:
                        fa(i + 5)
                    if i + 3 < NB:
                        fb_b(i + 3)
                    if i + 1 < NB:
                        back1b(i + 1)
                fin = (2 * NB) % 3
                k.dma('sp', gla_p, SS[fin][:].rearrange("p h e -> p (h e)"), ['S%d' % fin], [], 'glap')
            P.barrier()


        def phase_own(mode):
            PR = (mode == 'P')
            NK = NKMAX if PR else 2176
            with ExitStack() as es:
                def sb(name, shape, dt):
                    return es.enter_context(nc.sbuf_tensor(mode + name, shape, dt))
                gB = sb("gB", [128, D], F32); bB = sb("bB", [128, D], F32)
                g2B = sb("g2B", [128, D], F32); b2B = sb("b2B", [128, D], F32)
                gtb = [sb("gtb%d" % s_, [128, 512], F32) for s_ in range(2)]
                gnB = sb("gnB", [128, 256], F32)
                identf = sb("idf", [128, 128], F32); identb = sb("idb", [128, 128], BF16)
                tri = sb("tri", [128, 128], F32)
                maskf = sb("maskf", [128, 512], F32)
                cst = sb("cst", [128, 512], F32); idrepb = sb("idrepb", [128, 512], BF16)
                w2 = sb("w2", [16, 512], F32); gbias = sb("gbias", [1, 512], F32); ones1 = sb("ones1", [1, 128], F32)
                eps = sb("eps", [128, 1], F32); one = sb("one", [128, 1], F32)
                ctab = sb("ctab", [128, KIT + 1], F32)
                tbias = sb("tbias", [128, 2, 640 if PR else 128], F32)
                onehot = sb("onehot", [128, 4], F32)
                xo = sb("xo", [128, D], F32); h = sb("h", [128, D], F32); hb = sb("hb", [128, D], BF16)
                hT = sb("hT", [128, 8, 128], BF16)
                tmp = sb("tmp", [128, D], F32)
                st = sb("st", [128, 2, 6], F32); mv = sb("mv", [128, 2], F32)
                sd = sb("sd", [128, 1], F32); rstd = sb("rstd", [128, 1], F32); nmr = sb("nmr", [128, 1], F32)
                wch = [sb("wch%d" % s_, [128, 8, 512], BF16) for s_ in range(2)]
                gl = sb("gl", [16, 128], F32)
                el = sb("el", [128, 512], F32); eb = sb("eb", [128, 512], F32); enb = sb("enb", [128, 512], F32)
                qT = sb("qT", [128, 4, 128], BF16); kTh = sb("kTh", [128, 4, 128], BF16)
                V = sb("V", [128, 1024], BF16)
                sg = [sb("sg%d" % s_, [128, 512], F32) for s_ in range(2)]
                QTz = sb("QTz", [128, 4, 512], BF16); qiTz = sb("qiTz", [128, 8, 128], BF16)
                wabs = sb("wabs", [128, 8], F32); wsgn = sb("wsgn", [128, 8], F32)
                AT = sb("AT", [128, 4, 128], BF16)
                ss = sb("ss", [128, 4], F32); rs = sb("rs", [128, 4], F32)
                yain = sb("yain", [128, D], BF16); yT = sb("yT", [128, 8, 128], BF16)
                mrg = sb("mrg", [128, D], F32)
                sc = sb("sc", [128, NK], F32)
                junk = None if PR else sb("junk", [128, 2176], BF16)
                rlw = sb("rlw", [128, 1024], F32)
                rl = [rlw[:, 0:512], rlw[:, 512:1024]]
                rd = sb("rd", [128, 16], F32)
                yout = sb("yout", [128, D], F32) if PR else xo
                rmax = sb("rmax", [128, 1], F32); rmin = sb("rmin", [128, 1], F32); Wd = sb("Wd", [128, 1], F32)
                wtab = sb("wtab", [128, KIT + 1], F32); mids = sb("mids", [128, KIT + 1], F32)
                cnts = sb("cnts", [128, KIT], F32); us = sb("us", [128, KIT], F32); thr = sb("thr", [128, 1], F32)
                sAs = sb("sAs", [128, KIT], F32); vvs = sb("vvs", [128, KIT], F32)
                jd = sb("jd", [128, 8], BF16); ja = sb("ja", [128, 8], BF16); jq = sb("jq", [128, 8], BF16)
                if PR:
                    Sc = [sb("Sc%d" % s_, [128, 1024], F32) for s_ in range(2)]
                    Sown = sb("Sown", [128, 1024], F32); Sb = sb("Sb", [128, 4, 256], BF16)
                    kich = [sb("kich%d" % s_, [128, 1024], BF16) for s_ in range(2)]
                    kTch = [sb("kTch%d" % s_, [128, 2, 512], BF16) for s_ in range(2)]
                    vch = [sb("vch%d" % s_, [128, 4, 260], BF16) for s_ in range(2)]
                    mbt = [sb("mbt%d" % s_, [128, 128], BF16) for s_ in range(3)]
                    pT = [sb("pT%d" % s_, [128, 512], BF16) for s_ in range(3)]
                    oT = [sb("oT%d" % s_, [65, 512], F32) for s_ in range(2)]
                else:
                    cmaskS = sb("cmaskS", [128, 4, 128], F32); rmaskS = sb("rmaskS", [128, 4], F32)
                    mrevS = sb("mrevS", [128, 128], F32); bsumS = sb("bsumS", [128, 4], F32)
                    idrepSb = sb("idrepSb", [128, 4, 64], BF16)
                    selSb = sb("selSb", [128, 4, 128], BF16)
                    gkiB = sb("gkiB", [128, 64], F32); bkiB = sb("bkiB", [128, 64], F32)
                    S0f = [sb("S0f%d" % b_, [128, 4, 256], F32) for b_ in range(2)]
                    S0b = [sb("S0b%d" % b_, [128, 4, 256], BF16) for b_ in range(4)]
                    qTb = [sb("qTb%d" % b_, [128, 4, 128], BF16) for b_ in range(4)]
                    wabsb = sb("wabsb", [128, 4, 8], F32)
                    QTs = sb("QTs", [128, 4, 4, 64], BF16)
                    kTs1 = sb("kTs", [128, 2, 2176], BF16)
                    kiTs = [sb("kiTs%d" % b_, [128, 2176], BF16) for b_ in range(4)]
                    vexts1 = sb("vexts", [128, 17, 260], BF16)
                    ckf = sb("ckf", [128, 8, 256], F32); ckb = sb("ckb", [128, 8, 256], BF16)
                    cif = sb("cif", [128, 8, 64], F32); cib = sb("cib", [128, 8, 128], BF16)
                    kdv = sb("kdv", [128, 512], F32); kdb = sb("kdb", [128, 256], BF16); vnb = sb("vnb", [128, 256], BF16)
                    kin = sb("kin", [128, 64], F32); kif = sb("kif", [128, 64], F32); kib = sb("kib", [128, 128], BF16)
                    st2 = sb("st2", [128, 6], F32); mv2 = sb("mv2", [128, 2], F32)
                    sd2 = sb("sd2", [128, 1], F32); rstd2 = sb("rstd2", [128, 1], F32); nmr2 = sb("nmr2", [128, 1], F32)
                    kTnew = sb("kTnew", [128, 2, 128], BF16); kiTnew = sb("kiTnew", [128, 128], BF16)
                    Kt = sb("Kt", [128, 512], BF16); Ktb = [sb("Ktb%d" % s_, [128, 512], BF16) for s_ in range(2)]
                    decs = sb("decs", [128, 16], F32)
                    pTs = [sb("pTs%d" % s_, [128, 64], BF16) for s_ in range(3)]

                cl = 'c' + mode
                k.dma('sp', gB[:], ln_in_g.partition_broadcast(128), [], [], cl)
                k.dma('sp', bB[:], ln_in_b.partition_broadcast(128), [], [], cl)
                k.dma('sp', g2B[:], ln_g.partition_broadcast(128), [], [], cl)
                k.dma('sp', b2B[:], ln_b.partition_broadcast(128), [], [], cl)
                k.dma('sp', gnB[:], gla_norm_g.partition_broadcast(128), [], [], cl)
                k.dma('sp', identf[:], c_ident, [], [], cl)
                k.dma('sp', tri[:], c_triB if PR else c_triS, [], [], cl)
                k.dma('sp', maskf[:], (c_maskB if PR else c_maskS).rearrange("p a b -> p (a b)"), [], [], cl)
                k.dma('sp', w2[:], gla_w2, [], [], cl)
                k.dma('sp', gbias[:], gla_gate_b, [], [], cl)
                k.dma('sp', ctab[:], c_ctab, [], [], cl)
                k.dma('sp', tbias[:], c_tbias if PR else c_tbias[:, :, 0:128], [], [], cl)
                k.dma('sp', onehot[:], c_onehot, [], [], cl)
                if not PR:
                    k.dma('sp', cmaskS[:], c_cmaskS, [], [], cl)
                    k.dma('sp', rmaskS[:], c_rmaskS, [], [], cl)
                    k.dma('sp', mrevS[:], c_mrevS, [], [], cl)
                    k.dma('sp', bsumS[:], c_bsumS, [], [], cl)
                    k.dma('sp', gkiB[:], idx_kn_g.partition_broadcast(128), [], [], cl)
                    k.dma('sp', bkiB[:], idx_kn_b.partition_broadcast(128), [], [], cl)
                P.barrier()
                k.memset('dve', eps[:], EPS, [])
                k.memset('dve', one[:], 1.0, [])
                k.memset('dve', ones1[:], 1.0, [])
                k.memset('pool', QTz[:], 0.0, [])
                k.memset('pool', qiTz[:], 0.0, [])
                k.memset('pool', yain[:], 0.0, [])
                k.memset('pool', tmp[:], 0.0, [])
                k.cp('dve', identb[:], identf[:], [], [])
                k.dma('sp', cst[:], c_idrep, [], ['cst'], 'cst')
                k.cp('dve', idrepb[:], cst[:], ['cst'], ['idrepb'])
                if not PR:
                    k.dma('sp', cst[:, 0:256], c_idrepS.rearrange("p a b -> p (a b)"), ['idrepb'], ['cst'], 'cst')
                    k.cp('dve', idrepSb[:].rearrange("p a b -> p (a b)"), cst[:, 0:256], ['cst'], ['idrepSb'])
                    k.dma('sp', cst[:], c_selS.rearrange("p a b -> p (a b)"), ['idrepSb'], ['cst'], 'cst')
                    k.cp('dve', selSb[:].rearrange("p a b -> p (a b)"), cst[:], ['cst'], ['selSb'])
                    for b_ in range(4):
                        s_ = b_ % 2
                        k.dma('sp', S0f[s_][:], state[b_].rearrange("h p e -> p h e"), [], ['S0f%d' % s_], 'S0f%d' % s_)
                        k.cp('pool', S0b[b_][:], S0f[s_][:], ['S0f%d' % s_], ['S0b%d' % b_])
                        k.memset('pool', kiTs[b_][:, 2048:2176], 0.0, [])
                    k.memset('pool', vexts1[:], 1.0, [])
                    k.memset('pool', kTs1[:, :, 2048:2176], 0.0, [])
                P.barrier()
                tl = dict(st=st, mv=mv, sd=sd, rstd=rstd, nmr=nmr, xn=tmp, eps=eps, xnkey='tmp')
                bank = [0]

                def nb():
                    bank[0] = (bank[0] + 1) % 8
                    return bank[0]

                def own_block(g):
                    jobs = []

                    def J(src, n):
                        jobs.append((src, n))
                    J(wbf[:, C_IW:C_IW + 8], 8)
                    J(wbf[:, C_IQ:C_IQ + 512], 512)
                    J(wbf[:, C_DQ:C_DQ + 512], 512); J(wbf[:, C_DQ + 512:C_DQ + 1024], 512)
                    if not PR:
                        J(wbf[:, C_DK:C_DK + 512], 512)
                        J(wbf[:, C_IK:C_IK + 64], 64)
                    J(wbf[:, C_GLOW:C_GLOW + 16], 16)
                    J(wbf[:, C_GQ:C_GQ + 512], 512)
                    J(wbf[:, C_GK:C_GK + 512], 512)
                    J(wbf[:, C_GV:C_GV + 512], 512); J(wbf[:, C_GV + 512:C_GV + 1024], 512)
                    J(wbf[:, C_GR:C_GR + 512], 512); J(wbf[:, C_GR + 512:C_GR + 1024], 512)
                    for cc in range(2):
                        J(wbf3[0, :, cc * 512:(cc + 1) * 512], 512)
                        J(wbf[:, C_MA + cc * 512:C_MA + (cc + 1) * 512], 512)
                    J(wbf[:, C_DZ:C_DZ + 512], 512); J(wbf[:, C_DZ + 512:C_DZ + 1024], 512)
                    for cc in range(2):
                        J(wbf3[1, :, cc * 512:(cc + 1) * 512], 512)
                        J(wbf[:, C_MB + cc * 512:C_MB + (cc + 1) * 512], 512)
                    for cc in range(2):
                        J(wbf3[2, :, cc * 512:(cc + 1) * 512], 512)
                    jpos = [0]

                    def wissue(idx):
                        src, n = jobs[idx]
                        s_ = idx % 2
                        k.dma('sp', wch[s_][:, :, :n], src.rearrange("(k p) n -> p k n", p=128), [], ['wch%d' % s_], 'wch%d' % s_)

                    def next_w():
                        idx = jpos[0]
                        if idx == 0:
                            wissue(0)
                        if idx + 1 < len(jobs):
                            wissue(idx + 1)
                        jpos[0] += 1
                        return wch[idx % 2], 'wch%d' % (idx % 2)

                    def projT(wt, wk, n, psap, pskey):
                        for kc in range(8):
                            k.mm(psap, hT[:, kc, :], wt[:, kc, :n], kc == 0, kc == 7, ['hT', wk], [pskey])

                    def projF(wt, wk, bk):
                        for sub in range(4):
                            for kc in range(8):
                                k.mm(PSF(bk)[:, sub * 128:(sub + 1) * 128], wt[:, kc, sub * 128:(sub + 1) * 128], hT[:, kc, :],
                                     kc == 0, kc == 7, ['hT', wk], ['ps%d' % bk])

                    def transpose8(src, srckey, dst, dstkey):
                        bk = nb()
                        for kc in range(8):
                            k.tr(PSB(bk)[:, kc * 128:(kc + 1) * 128], src[:, kc * 128:(kc + 1) * 128], identb[:], [srckey], ['ps%d' % bk])
                        k.cp('act', dst[:].rearrange("p a b -> p (a b)"), PSB(bk), ['ps%d' % bk], [dstkey])

                    if not PR:
                        k.dma('act', xo[:], xs, [], ['xo'], 'xo')
                    elif g == 0:
                        k.dma('act', xo[:], xown[0:128, :], [], ['xo'], 'xo')
                    layernorm(xo[:], 'xo', gB[:], bB[:], tl, 'b', out_f32=h[:], out_f32_key='h', out_bf=hb[:], out_bf_key='hb')
                    transpose8(hb, 'hb', hT, 'hT')
                    wt, wk = next_w()
                    bw = nb()
                    projT(wt, wk, 8, PSF(bw)[:, 0:8], 'ps%d' % bw)
                    k.ts(wabs[:], PSF(bw)[:, 0:8], -IDX_W_SCALE, None, ALU.mult, None, ['ps%d' % bw], ['wabs'])
                    k.stt(wabs[:], PSF(bw)[:, 0:8], IDX_W_SCALE, wabs[:], ALU.mult, ALU.max, ['ps%d' % bw, 'wabs'], ['wabs'])
                    k.ts(wsgn[:], PSF(bw)[:, 0:8], 0.0, 2.0, ALU.is_ge, ALU.mult, ['ps%d' % bw], ['wsgn'])
                    k.ts(wsgn[:], wsgn[:], -1.0, None, ALU.add, None, ['wsgn'], ['wsgn'])
                    wt, wk = next_w()
                    bi = nb()
                    projF(wt, wk, bi)
                    qv = qiTz[:].rearrange("p (s two) t -> p s two t", two=2)
                    pv = PSF(bi).rearrange("p (s t) -> p s t", s=4)
                    k.cp('act', qv[0:64, :, 0, :], pv[0:64, :, :], ['ps%d' % bi], ['qiTz'])
                    k.cp('act', qv[64:128, :, 1, :], pv[64:128, :, :], ['ps%d' % bi], ['qiTz'])
                    for m in range(2):
                        wt, wk = next_w()
                        bdq = nb()
                        projF(wt, wk, bdq)
                        k.cp('act', QTz[0:64, 2 * m, :], PSF(bdq)[0:64, :], ['ps%d' % bdq], ['QTz'])
                        k.cp('act', QTz[64:128, 2 * m + 1, :], PSF(bdq)[64:128, :], ['ps%d' % bdq], ['QTz'])
                    if not PR:
                        wt, wk = next_w()
                        bkv = nb()
                        projT(wt, wk, 512, PSF(bkv), 'ps%d' % bkv)
                        k.cp('act', kdv[:], PSF(bkv), ['ps%d' % bkv], ['kdv'])
                        k.dma('act', ks, kdv[:, 0:256], ['kdv'], [], 'so1')
                        k.dma('act', vs, kdv[:, 256:512], ['kdv'], [], 'so1')
                        k.cp('pool', kdb[:], kdv[:, 0:256], ['kdv'], ['kdb'])
                        k.cp('pool', vnb[:], kdv[:, 256:512], ['kdv'], ['vnb'])
                        wt, wk = next_w()
                        bik = nb()
                        projT(wt, wk, 64, PSF(bik)[:, 0:64], 'ps%d' % bik)
                        k.cp('dve', kin[:], PSF(bik)[:, 0:64], ['ps%d' % bik], ['kin'])
                        k.bn_stats(st2[:], kin[:], ['kin'], ['st2'])
                        k.bn_aggr(mv2[:], st2[:], ['st2'], ['mv2'])
                        k.act(sd2[:], mv2[:, 1:2], AF.Ln, ['mv2'], ['sd2'], bias=eps[:, 0:1], scale=1.0)
                        k.act(rstd2[:], sd2[:], AF.Exp, ['sd2'], ['rstd2'], scale=-0.5)
                        k.ts(nmr2[:], mv2[:, 0:1], rstd2[:, 0:1], -1.0, ALU.mult, ALU.mult, ['mv2', 'rstd2'], ['nmr2'])
                        k.act(kin[:], kin[:], AF.Identity, ['kin', 'nmr2', 'rstd2'], ['kin'], bias=nmr2[:, 0:1], scale=rstd2[:, 0:1])
                        k.tt(kin[:], kin[:], gkiB[:], ALU.mult, ['kin'], ['kin'])
                        k.tt(kif[:], kin[:], bkiB[:], ALU.add, ['kin'], ['kif'])
                        k.dma('act', iks, kif[:], ['kif'], [], 'so1')
                        k.cp('pool', kib[:, 0:64], kif[:], ['kif'], ['kib'])
                        k.cp('pool', kib[:, 64:128], kif[:], ['kif'], ['kib'])
                        bt_ = nb()
                        for c_ in range(2):
                            k.tr(PSB(bt_)[:, c_ * 128:(c_ + 1) * 128], kdb[:, c_ * 128:(c_ + 1) * 128], identb[:], ['kdb'], ['ps%d' % bt_])
                        k.tr(PSB(bt_)[:, 256:384], kib[:], identb[:], ['kib'], ['ps%d' % bt_])
                        k.cp('dve', kTnew[:].rearrange("p a b -> p (a b)"), PSB(bt_)[:, 0:256], ['ps%d' % bt_], ['kTnew'])
                        k.cp('dve', kiTnew[:], PSB(bt_)[:, 256:384], ['ps%d' % bt_], ['kiTnew'])
                        for b_ in range(4):
                            k.cp('pool', kiTs[b_][:, 2048:2064], kiTnew[:, 16 * b_:16 * b_ + 16], ['kiTnew'], ['kiTs%d' % b_])
                            for t8 in range(2):
                                k.dma('sp', cif[:], cache_ik[b_, t8 * 1024:(t8 + 1) * 1024, :].rearrange("(t p) c -> p t c", p=128), [], ['cif'], 'cif')
                                k.cp('pool', cib[:, :, 0:64], cif[:], ['cif'], ['cib'])
                                k.cp('pool', cib[:, :, 64:128], cif[:], ['cif'], ['cib'])
                                bt_ = nb()
                                for tt_ in range(8):
                                    k.tr(PSB(bt_)[:, tt_ * 128:(tt_ + 1) * 128], cib[:, tt_, :], identb[:], ['cib'], ['ps%d' % bt_])
                                k.cp('dve', kiTs[b_][:, t8 * 1024:(t8 + 1) * 1024], PSB(bt_), ['ps%d' % bt_], ['kiTs%d' % b_])
                    if PR:
                        n_tiles = min(4 * g + 5, NB)
                        tail0 = 4 * g
                        tidx = 1 if g == G - 1 else 0
                    else:
                        n_tiles = 17
                        tail0 = 16
                        tidx = 1
                    n_keys = n_tiles * 128
                    nch = (n_tiles + 3) // 4

                    def kiload(ci):
                        k0 = ci * 1024
                        w_ = min(1024, n_keys - k0)
                        s_ = ci % 2
                        k.dma('sp', kich[s_][:, :w_], ki_d[:, k0:k0 + w_], [], ['kich%d' % s_], 'kich%d' % s_)

                    if PR:
                        kiload(0)
                    if not PR:
                        for b_ in range(4):
                            k.ts(wabsb[:, b_, :], wabs[:], rmaskS[:, b_:b_ + 1], None, ALU.mult, None, ['wabs'], ['wabsb'])
                    rli = 0
                    if PR:
                        npair = (n_keys + 1023) // 1024
                        wslots = [(rlw[:, :], 'rlw'), (mrg[:, :], 'mrgW'), (tmp[:, :], 'tmpW')]
                        for cp in range(npair):
                            k0 = cp * 1024
                            wtot = min(1024, n_keys - k0)
                            w0 = min(512, wtot)
                            w1 = wtot - w0
                            if cp + 1 < npair:
                                kiload(cp + 1)
                            for hh in range(8):
                                r_, rk = wslots[rli % 3]
                                rli += 1
                                bx = nb()
                                k.mm(PSF(bx)[:, :w0], qiTz[:, hh, :], kich[cp % 2][:, 0:w0], True, True, ['qiTz', 'kich%d' % (cp % 2)], ['ps%d' % bx])
                                k.act(r_[:, 0:w0], PSF(bx)[:, :w0], AF.Relu, ['ps%d' % bx, 'wabs'], [rk], scale=wabs[:, hh:hh + 1])
                                if w1 > 0:
                                    bx = nb()
                                    k.mm(PSF(bx)[:, :w1], qiTz[:, hh, :], kich[cp % 2][:, 512:512 + w1], True, True, ['qiTz', 'kich%d' % (cp % 2)], ['ps%d' % bx])
                                    k.act(r_[:, 512:512 + w1], PSF(bx)[:, :w1], AF.Relu, ['ps%d' % bx, 'wabs'], [rk], scale=wabs[:, hh:hh + 1])
                                if hh == 0:
                                    k.ts(sc[:, k0:k0 + wtot], r_[:, :wtot], wsgn[:, 0:1], None, ALU.mult, None, [rk, 'wsgn'], ['sc'])
                                else:
                                    k.stt(sc[:, k0:k0 + wtot], r_[:, :wtot], wsgn[:, hh:hh + 1], sc[:, k0:k0 + wtot], ALU.mult, ALU.add, [rk, 'wsgn', 'sc'], ['sc'])
                    for ci in (range(0) if PR else range(nch)):
                        k0 = ci * 512
                        w_ = min(512, n_keys - k0)
                        if PR and ci + 1 < nch:
                            kiload(ci + 1)
                        first = True
                        for hh in range(8):
                            for b_ in (range(1) if PR else range(4)):
                                bx = nb()
                                if PR:
                                    k.mm(PSF(bx)[:, :w_], qiTz[:, hh, :], kich[ci % 2][:, :w_], True, True, ['qiTz', 'kich%d' % (ci % 2)], ['ps%d' % bx])
                                    scl = wabs[:, hh:hh + 1]
                                    sck = 'wabs'
                                else:
                                    k.mm(PSF(bx)[:, :w_], qiTz[:, hh, :], kiTs[b_][:, k0:k0 + w_], True, True, ['qiTz', 'kiTs%d' % b_], ['ps%d' % bx])
                                    scl = wabsb[:, b_, hh:hh + 1]
                                    sck = 'wabsb'
                                rslots = [(rl[0], 'rl0'), (rl[1], 'rl1'), (mrg[:, 0:512], 'mrgA'), (mrg[:, 512:1024], 'mrgB'),
                                          (tmp[:, 0:512], 'tmpA'), (tmp[:, 512:1024], 'tmpB')]
                                r_, rk = rslots[rli % 6]
                                rli += 1
                                k.act(r_[:, :w_], PSF(bx)[:, :w_], AF.Relu, ['ps%d' % bx, sck], [rk], scale=scl)
                                if first:
                                    k.ts(sc[:, k0:k0 + w_], r_[:, :w_], wsgn[:, hh:hh + 1], None, ALU.mult, None, [rk, 'wsgn'], ['sc'])
                                    first = False
                                else:
                                    k.stt(sc[:, k0:k0 + w_], r_[:, :w_], wsgn[:, hh:hh + 1], sc[:, k0:k0 + w_], ALU.mult, ALU.add, [rk, 'wsgn', 'sc'], ['sc'])
                    k.capture()
                    k.red(rmax[:], sc[:, 0:n_keys], ALU.max, ['sc'], ['rmax'])
                    k.red(rmin[:], sc[:, 0:n_keys], ALU.min, ['sc'], ['rmin'])
                    tw = (n_tiles - tail0) * 128
                    k.tt(sc[:, tail0 * 128:tail0 * 128 + tw], sc[:, tail0 * 128:tail0 * 128 + tw], tbias[:, tidx, 0:tw], ALU.add, ['sc'], ['sc'])
                    k.tt(Wd[:], rmax[:], rmin[:], ALU.subtract, ['rmax', 'rmin'], ['Wd'])
                    k.ts(wtab[:], ctab[:], Wd[:, 0:1], None, ALU.mult, None, ['Wd'], ['wtab'])
                    k.tt(mids[:, 0:1], rmin[:], wtab[:, 0:1], ALU.add, ['rmin', 'wtab'], ['mid0'])
                    nD = (int(n_keys * 0.46) // 128) * 128
                    if nD < 256:
                        nD = n_keys
                    nA = n_keys - nD
                    for it in range(1, KIT + 1):
                        mid = mids[:, it - 1:it]
                        mk = 'mid%d' % (it - 1)
                        cn = cnts[:, it - 1:it]
                        ck_ = 'cnt%d' % it
                        k.ts(jd[:, 0:1].to_broadcast([128, nD]), sc[:, 0:nD], mid, None, ALU.is_ge, ALU.add, ['sc', mk], ['jd', ck_], accum=cn)
                        if nA > 0:
                            k.act(ja[:, 0:1].to_broadcast([128, nA]), sc[:, nD:n_keys], AF.Sign, ['sc', mk], ['ja', 'sa%d' % it],
                                  bias=mid, scale=-1.0, accum_out=sAs[:, it - 1:it])
                            k.stt(vvs[:, it - 1:it], cn, 2.0, sAs[:, it - 1:it], ALU.mult, ALU.subtract, [ck_, 'sa%d' % it], ['vv%d' % it])
                            vsrc, vkey, vthr = vvs[:, it - 1:it], 'vv%d' % it, 511.5 - nA
                        else:
                            vsrc, vkey, vthr = cn, ck_, 255.5
                        u_ = us[:, it - 1:it]
                        k.ts(u_, vsrc, vthr, wtab[:, it - 1:it], ALU.is_ge, ALU.mult, [vkey, 'wtab'], ['u%d' % it])
                        if it < KIT:
                            k.stt(mids[:, it:it + 1], u_, wtab[:, it:it + 1], mid, ALU.subtract, ALU.add, ['u%d' % it, 'wtab', mk], ['mid%d' % it])
                        else:
                            k.stt(thr[:], u_, wtab[:, it - 1:it], mid, ALU.subtract, ALU.add, ['u%d' % it, 'wtab', mk], ['thr'])
                    bisA = k.end_capture()
                    k.capture()
                    wt, wk = next_w()
                    b1 = nb()
                    for kc in range(8):
                        k.mm(PSF(b1)[0:16, 0:128], wt[:, kc, 0:16], hT[:, kc, :], kc == 0, kc == 7, ['hT', wk], ['ps%d' % b1])
                    k.cp('dve', gl[:], PSF(b1)[0:16, 0:128], ['ps%d' % b1], ['gl'])
                    bz = nb()
                    k.mm(PSF(bz), gl[:], w2[:], True, False, ['gl'], ['ps%d' % bz])
                    k.mm(PSF(bz), ones1[:], gbias[:], False, True, [], ['ps%d' % bz])
                    k.act(el[:], PSF(bz), AF.Exp, ['ps%d' % bz], ['el'], scale=-1.0)
                    k.act(el[:], el[:], AF.Ln, ['el'], ['el'], bias=one[:, 0:1], scale=1.0)
                    bb = nb()
                    for hh in range(4):
                        k.mm(PSF(bb)[:, hh * 128:(hh + 1) * 128], el[:, hh * 128:(hh + 1) * 128], tri[:], True, True, ['el'], ['ps%d' % bb])
                    k.act(eb[:], PSF(bb), AF.Exp, ['ps%d' % bb], ['eb'])
                    k.act(enb[:], PSF(bb), AF.Exp, ['ps%d' % bb], ['enb'], scale=-1.0)
                    wt, wk = next_w()
                    bq = nb()
                    projF(wt, wk, bq)
                    k.stt(qT[:].rearrange("p a b -> p (a b)"), PSF(bq), 128.0 ** -0.5, eb[:], ALU.mult, ALU.mult, ['ps%d' % bq, 'eb'], ['qT'])
                    wt, wk = next_w()
                    bk_ = nb()
                    projF(wt, wk, bk_)
                    k.tt(kTh[:].rearrange("p a b -> p (a b)"), PSF(bk_), enb[:], ALU.mult, ['ps%d' % bk_, 'enb'], ['kTh'])
                    if not PR:
                        bkt = nb()
                        projT(wt, wk, 512, PSF(bkt), 'ps%d' % bkt)
                        br_ = nb()
                        k.mm(PSF(br_), mrevS[:], el[:], True, True, ['el'], ['ps%d' % br_])
                        k.act(sg[1][:], PSF(br_), AF.Exp, ['ps%d' % br_], ['sg1'])
                        k.tt(Kt[:], PSF(bkt), sg[1][:], ALU.mult, ['ps%d' % bkt, 'sg1'], ['Kt'])
                        bd_ = nb()
                        for hh in range(4):
                            k.mm(PSF(bd_)[:, hh * 4:(hh + 1) * 4], el[:, hh * 128:(hh + 1) * 128], bsumS[:], True, True, ['el'], ['ps%d' % bd_])
                        k.act(decs[:], PSF(bd_)[:, 0:16], AF.Exp, ['ps%d' % bd_], ['decs'])
                    for half in range(2):
                        wt, wk = next_w()
                        bv = nb()
                        projT(wt, wk, 512, PSF(bv), 'ps%d' % bv)
                        k.cp('act', V[:, half * 512:(half + 1) * 512], PSF(bv), ['ps%d' % bv], ['V'])
                    if PR:
                        for m in range(4):
                            sidx = min(4 * g + m, NB - 1)
                            s_ = m % 2
                            k.dma('act', Sc[s_][:], snap[sidx], [], ['Sc%d' % s_], 'Sc%d' % s_)
                            if m == 0:
                                k.ts(Sown[:], Sc[s_][:], onehot[:, 0:1], None, ALU.mult, None, ['Sc%d' % s_], ['Sown'])
                            else:
                                k.stt(Sown[:], Sc[s_][:], onehot[:, m:m + 1], Sown[:], ALU.mult, ALU.add, ['Sc%d' % s_, 'Sown'], ['Sown'])
                        k.cp('pool', Sb[:].rearrange("p a b -> p (a b)"), Sown[:], ['Sown'], ['Sb'])
                    else:
                        for b_ in range(4):
                            for hh in range(4):
                                k.tt(qTb[b_][:, hh, :], qT[:, hh, :], cmaskS[:, b_, :], ALU.mult, ['qT'], ['qTb%d' % b_])
                    ba = nb()
                    for hh in range(4):
                        k.mm(PSF(ba)[:, hh * 128:(hh + 1) * 128], kTh[:, hh, :], qT[:, hh, :], True, True, ['kTh', 'qT'], ['ps%d' % ba])
                    k.tt(AT[:].rearrange("p a b -> p (a b)"), PSF(ba), maskf[:], ALU.mult, ['ps%d' % ba], ['AT'])
                    bo = [nb(), nb()]
                    for hh in range(4):
                        oap = PSF(bo[hh // 2])[:, (hh % 2) * 256:(hh % 2 + 1) * 256]
                        okey = 'ps%d' % bo[hh // 2]
                        if PR:
                            k.mm(oap, qT[:, hh, :], Sb[:, hh, :], True, False, ['qT', 'Sb'], [okey])
                        else:
                            for b_ in range(4):
                                k.mm(oap, qTb[b_][:, hh, :], S0b[b_][:, hh, :], b_ == 0, False, ['qTb%d' % b_], [okey])
                        k.mm(oap, AT[:, hh, :], V[:, hh * 256:(hh + 1) * 256], False, True, ['AT', 'V'], [okey])
                    for hh in range(4):
                        oap = PSF(bo[hh // 2])[:, (hh % 2) * 256:(hh % 2 + 1) * 256]
                        k.act(jq[:, 0:1].to_broadcast([128, 256]), oap, AF.Square, ['ps%d' % bo[hh // 2]], ['jq', 'ss%d' % hh], accum_out=ss[:, hh:hh + 1])
                    k.act(rs[:], ss[:], AF.Ln, ['ss0', 'ss1', 'ss2', 'ss3'], ['rs'], bias=eps[:, 0:1], scale=1.0 / 256)
                    k.act(rs[:], rs[:], AF.Exp, ['rs'], ['rs'], scale=-0.5)
                    for cc in range(2):
                        wt, wk = next_w()
                        bg = nb()
                        projT(wt, wk, 512, PSF(bg), 'ps%d' % bg)
                        k.act(sg[cc][:], PSF(bg), AF.Silu, ['ps%d' % bg], ['sg%d' % cc])
                        for hh in range(2):
                            hd = 2 * cc + hh
                            oap = PSF(bo[hd // 2])[:, (hd % 2) * 256:(hd % 2 + 1) * 256]
                            k.stt(tmp[:, hd * 256:(hd + 1) * 256], oap, rs[:, hd:hd + 1], gnB[:], ALU.mult, ALU.mult,
                                  ['ps%d' % bo[hd // 2], 'rs'], ['tmp'])
                        k.tt(yain[:, cc * 512:(cc + 1) * 512], tmp[:, cc * 512:(cc + 1) * 512], sg[cc][:], ALU.mult, ['tmp', 'sg%d' % cc], ['yain'])
                    transpose8(yain, 'yain', yT, 'yT')
                    for cc in range(2):
                        wt, wk = next_w()
                        by = nb()
                        for kc in range(8):
                            k.mm(PSF(by), yT[:, kc, :], wt[:, kc, :], kc == 0, kc == 7, ['yT', wk], ['ps%d' % by])
                        wt, wk = next_w()
                        bm = nb()
                        projT(wt, wk, 512, PSF(bm), 'ps%d' % bm)
                        k.dma('act', gtb[cc][:], gate_b[0:1, cc * 512:(cc + 1) * 512].partition_broadcast(128), [], ['gtb%d' % cc], 'gtb%d' % cc)
                        k.tt(sg[cc][:], PSF(bm), gtb[cc][:], ALU.add, ['ps%d' % bm, 'gtb%d' % cc], ['sg%d' % cc])
                        k.act(sg[cc][:], sg[cc][:], AF.Sigmoid, ['sg%d' % cc], ['sg%d' % cc])
                        k.tt(mrg[:, cc * 512:(cc + 1) * 512], PSF(by), sg[cc][:], ALU.mult, ['ps%d' % by, 'sg%d' % cc], ['mrg'])
                    if not PR:
                        for b_ in range(4):
                            s_ = b_ % 2
                            k.dma('sp', S0f[s_][:], state[b_].rearrange("h p e -> p h e"), [], ['S0f%d' % s_], 'S0f%d' % s_)
                            k.ts(Ktb[s_][:], Kt[:], rmaskS[:, b_:b_ + 1], None, ALU.mult, None, ['Kt'], ['Ktb%d' % s_])
                            for hp in range(2):
                                bs_ = nb()
                                for hh in range(2):
                                    hd = hp * 2 + hh
                                    k.mm(PSF(bs_)[:, hh * 256:(hh + 1) * 256], Ktb[s_][:, hd * 128:(hd + 1) * 128], V[:, hd * 256:(hd + 1) * 256],
                                         True, True, ['Ktb%d' % s_, 'V'], ['ps%d' % bs_])
                                for hh in range(2):
                                    hd = hp * 2 + hh
                                    k.stt(S0f[s_][:, hd, :], S0f[s_][:, hd, :], decs[:, hd * 4 + b_:hd * 4 + b_ + 1], PSF(bs_)[:, hh * 256:(hh + 1) * 256],
                                          ALU.mult, ALU.add, ['decs', 'ps%d' % bs_, 'S0f%d' % s_], ['S0f%d' % s_])
                            k.dma('act', gla_s[b_], S0f[s_][:].rearrange("p a b -> p (a b)"), ['S0f%d' % s_], [], 'Sno%d' % s_)

                    glaB = k.end_capture()
                    k.merge(bisA, glaB)

                    if PR:
                        def kvload(ci):
                            k0 = ci * 512
                            nt = min(4, n_tiles - ci * 4)
                            s_ = ci % 2
                            k.dma('sp', kTch[s_][:, :, :nt * 128], kT_d[:, :, k0:k0 + nt * 128], [], ['kTch%d' % s_], 'kTch%d' % s_)
                            k.dma('sp', vch[s_][:, :nt, :], v_d[:, ci * 4:ci * 4 + nt, :], [], ['vch%d' % s_], 'vch%d' % s_)
                        kvload(0)
                        if g + 1 < G:
                            k.dma('act', xo[:], xown[(g + 1) * 128:(g + 2) * 128, :], [], ['xo'], 'xo')
                        li = 0
                        groups = []
                        for kt in range(n_tiles):
                            ci, tl_ = kt // 4, kt % 4
                            mb_ = mbt[kt % 3]
                            mbk = 'mbt%d' % (kt % 3)
                            for gg in range(4):
                                bl = 4 + (li % 4)
                                p_ = pT[li % 3]
                                pk = 'pT%d' % (li % 3)
                                li += 1
                                k.capture()
                                if gg == 0:
                                    if tl_ == 1 and ci + 1 < nch:
                                        kvload(ci + 1)
                                    k.ts(mb_[:], sc[:, kt * 128:(kt + 1) * 128], thr[:, 0:1], NEG, ALU.is_lt, ALU.mult, ['sc', 'thr'], [mbk])
                                k.mm(PSF(bl), kTch[ci % 2][:, gg // 2, tl_ * 128:(tl_ + 1) * 128], QTz[:, gg, :], True, False,
                                     ['kTch%d' % (ci % 2), 'QTz'], ['ps%d' % bl])
                                k.mm(PSF(bl), mb_[:], idrepb[:], False, True, [mbk], ['ps%d' % bl])
                                k.act(p_[:], PSF(bl), AF.Exp, ['ps%d' % bl], [pk], scale=0.125)
                                s1 = k.end_capture()
                                k.capture()
                                k.mm(PSF(gg)[0:65, :], vch[ci % 2][:, tl_, gg * 65:(gg + 1) * 65], p_[:], kt == 0, kt == n_tiles - 1,
                                     ['vch%d' % (ci % 2), pk], ['ps%d' % gg])
                                s2 = k.end_capture()
                                groups.append((s1, s2))
                        SK = 2
                        for idx in range(len(groups) + SK):
                            if idx < len(groups):
                                P.ops.extend(groups[idx][0])
                            if idx >= SK:
                                P.ops.extend(groups[idx - SK][1])
                        for gg in range(4):
                            o_ = oT[gg % 2]
                            ok_ = 'oT%d' % (gg % 2)
                            k.cp('act', o_[:], PSF(gg)[0:65, :], ['ps%d' % gg], [ok_])
                            for r_ in range(4):
                                k.tr(PSF(4 + gg)[:, r_ * 65:(r_ + 1) * 65], o_[0:65, r_ * 128:(r_ + 1) * 128], identf[0:65, 0:65], [ok_], ['ps%d' % (4 + gg)])
                        NPT = 128
                    else:
                        mball = junk
                        for b_ in range(4):
                            k.cp('pool', QTs[:, b_, :, :].rearrange("p g (r t) -> p g r t", r=4),
                                 QTz[:].rearrange("p g (r t) -> p g r t", r=4)[:, :, :, 16 * b_:16 * b_ + 16], ['QTz'], ['QTs'])
                        k.ts(mball[:], sc[:, 0:2176], thr[:, 0:1], NEG, ALU.is_lt, ALU.mult, ['sc', 'thr'], ['junk'])
                        li = 0
                        for b_ in range(4):
                            k.cp('pool', kTs1[:, :, 2048:2064], kTnew[:, :, 16 * b_:16 * b_ + 16], ['kTnew'], ['kTs'])
                            bs_ = nb() % 4 + 4
                            k.mm(PSF(bs_)[:, 0:256], selSb[:, b_, :], vnb[:], True, True, ['vnb'], ['ps%d' % bs_])
                            k.cp('act', vexts1[:, 16, :].rearrange("p (g d) -> p g d", g=4)[:, :, 0:64],
                                 PSF(bs_)[:, 0:256].rearrange("p (g d) -> p g d", g=4), ['ps%d' % bs_], ['vexts'])
                            for t8 in range(2):
                                k.dma('sp', ckf[:], cache_k[b_, t8 * 1024:(t8 + 1) * 1024, :].rearrange("(t p) c -> p t c", p=128), [], ['ckf'], 'ckf')
                                k.cp('pool', ckb[:], ckf[:], ['ckf'], ['ckb'])
                                for t4 in range(2):
                                    bt_ = nb() % 4 + 4
                                    for tt_ in range(4):
                                        for c_ in range(2):
                                            col = (tt_ * 2 + c_) * 128
                                            k.tr(PSB(bt_)[:, col:col + 128], ckb[:, t4 * 4 + tt_, c_ * 128:(c_ + 1) * 128], identb[:], ['ckb'], ['ps%d' % bt_])
                                    c0_ = t8 * 1024 + t4 * 512
                                    k.cp('dve', kTs1[:, :, c0_:c0_ + 512].rearrange("p c (t s) -> p c t s", t=4),
                                         PSB(bt_).rearrange("p (t c s) -> p c t s", t=4, c=2), ['ps%d' % bt_], ['kTs'])
                                k.dma('sp', ckf[:], cache_v[b_, t8 * 1024:(t8 + 1) * 1024, :].rearrange("(t p) c -> p t c", p=128), [], ['ckf'], 'ckf')
                                k.cp('pool', vexts1[:, t8 * 8:(t8 + 1) * 8, :].rearrange("p t (g d) -> p t g d", g=4)[:, :, :, 0:64],
                                     ckf[:].rearrange("p t (g d) -> p t g d", g=4), ['ckf'], ['vexts'])
                            sgroups = []
                            for gg in range(4):
                                ob = b_ // 2
                                ocol = ((b_ % 2) * 4 + gg) * 64
                                qsel = QTs[:, b_, gg, :]
                                for kt in range(17):
                                    bl = 4 + (li % 4)
                                    p_ = pTs[li % 3]
                                    pk = 'pTs%d' % (li % 3)
                                    li += 1
                                    k.capture()
                                    k.mm(PSF(bl)[:, 0:64], kTs1[:, gg // 2, kt * 128:(kt + 1) * 128], qsel, True, False,
                                         ['kTs', 'QTs'], ['ps%d' % bl])
                                    k.mm(PSF(bl)[:, 0:64], mball[:, kt * 128:(kt + 1) * 128], idrepSb[:, b_, :], False, True, ['junk'], ['ps%d' % bl])
                                    k.act(p_[:], PSF(bl)[:, 0:64], AF.Exp, ['ps%d' % bl], [pk], scale=0.125)
                                    s1_ = k.end_capture()
                                    k.capture()
                                    k.mm(PSF(ob)[0:65, ocol:ocol + 64], vexts1[:, kt, gg * 65:(gg + 1) * 65], p_[:], kt == 0, kt == 16,
                                         ['vexts', pk], ['ps%d' % ob])
                                    s2_ = k.end_capture()
                                    sgroups.append((s1_, s2_))
                            SKS = 2
                            for idx in range(len(sgroups) + SKS):
                                if idx < len(sgroups):
                                    P.ops.extend(sgroups[idx][0])
                                if idx >= SKS:
                                    P.ops.extend(sgroups[idx - SKS][1])
                        oTs = sc[0:65, 0:1024]
                        ov = oTs.rearrange("p (g r b t) -> p b g r t", g=4, r=4, b=4)
                        for ob in range(2):
                            k.cp('act', ov[:, 2 * ob:2 * ob + 2], PSF(ob)[0:65, :].rearrange("p (b g r t) -> p b g r t", b=2, g=4, r=4),
                                 ['ps%d' % ob, 'junk'], ['sc'])
                        for gg in range(4):
                            for r_ in range(4):
                                c0_ = (gg * 4 + r_) * 64
                                k.tr(PSF(4 + gg)[0:64, r_ * 65:(r_ + 1) * 65], oTs[:, c0_:c0_ + 64], identf[0:65, 0:65], ['sc'], ['ps%d' % (4 + gg)])
                        NPT = 64
                    for gg in range(4):
                        pv4 = PSF(4 + gg)[0:NPT, 0:260].rearrange("p (r c) -> p r c", c=65)
                        k.recip(rd[0:NPT, 4 * gg:4 * gg + 4], pv4[:, :, 64], ['ps%d' % (4 + gg)], ['rd'])
                        for r_ in range(4):
                            hd = 4 * gg + r_
                            k.ts(tmp[0:NPT, hd * 64:(hd + 1) * 64], pv4[:, r_, 0:64], rd[0:NPT, hd:hd + 1], None, ALU.mult, None,
                                 ['ps%d' % (4 + gg), 'rd'], ['tmp'])
                    for cc in range(2):
                        wt, wk = next_w()
                        bg = nb()
                        projT(wt, wk, 512, PSF(bg), 'ps%d' % bg)
                        k.act(sg[cc][:], PSF(bg), AF.Silu, ['ps%d' % bg], ['sg%d' % cc])
                        k.tt(yain[0:NPT, cc * 512:(cc + 1) * 512], tmp[0:NPT, cc * 512:(cc + 1) * 512], sg[cc][0:NPT, :], ALU.mult,
                             ['tmp', 'sg%d' % cc], ['yain'])
                    transpose8(yain, 'yain', yT, 'yT')
                    for cc in range(2):
                        wt, wk = next_w()
                        by = nb()
                        for kc in range(8):
                            k.mm(PSF(by), yT[:, kc, :], wt[:, kc, :], kc == 0, kc == 7, ['yT', wk], ['ps%d' % by])
                        wt, wk = next_w()
                        bm = nb()
                        projT(wt, wk, 512, PSF(bm), 'ps%d' % bm)
                        k.dma('act', gtb[cc][:], gate_b[0:1, 1024 + cc * 512:1024 + (cc + 1) * 512].partition_broadcast(128), [], ['gtb%d' % cc], 'gtb%d' % cc)
                        k.tt(sg[cc][:], PSF(bm), gtb[cc][:], ALU.add, ['ps%d' % bm, 'gtb%d' % cc], ['sg%d' % cc])
                        k.act(sg[cc][:], sg[cc][:], AF.Sigmoid, ['sg%d' % cc], ['sg%d' % cc])
                        k.tt(sg[cc][:], PSF(by), sg[cc][:], ALU.mult, ['ps%d' % by, 'sg%d' % cc], ['sg%d' % cc])
                        k.tt(mrg[:, cc * 512:(cc + 1) * 512], mrg[:, cc * 512:(cc + 1) * 512], sg[cc][:], ALU.add, ['mrg', 'sg%d' % cc], ['mrg'])
                    k.cp('pool', yain[:], mrg[:], ['mrg'], ['yain'])
                    transpose8(yain, 'yain', yT, 'yT')
                    for cc in range(2):
                        wt, wk = next_w()
                        by = nb()
                        for kc in range(8):
                            k.mm(PSF(by), yT[:, kc, :], wt[:, kc, :], kc == 0, kc == 7, ['yT', wk], ['ps%d' % by])
                        k.stt(mrg[:, cc * 512:(cc + 1) * 512], h[:, cc * 512:(cc + 1) * 512], ALPHA, PSF(by), ALU.mult, ALU.add, ['h', 'ps%d' % by], ['mrg'])
                    yk = 'yout' if PR else 'xo'
                    layernorm(mrg[:], 'mrg', g2B[:], b2B[:], tl, 'c', out_f32=yout[:], out_f32_key=yk)
                    ydst = y_own[g * 128:(g + 1) * 128, :] if PR else ys
                    k.dma('act', ydst, yout[:], [yk], [], 'youto')

                for g in (range(G) if PR else [0]):
                    own_block(g)
            P.barrier()

        if "B" in phases:
            phase_own('P')
        if "S" in phases:
            phase_own('S')

        P.emit(nc, top)
    return nc


def _consts(j, G):
    c = {}
    p = np.arange(128)
    c["c_ident"] = np.eye(128, dtype=np.float32)
    jj, ii = np.meshgrid(p, p, indexing="ij")
    c["c_triA"] = np.where((jj > ii) & (jj // 64 == ii // 64), -1.0 / 16, 0.0).astype(np.float32)
    c["c_blkA"] = np.where(p[:, None] // 64 == np.arange(2)[None, :], -1.0 / 16, 0.0).astype(np.float32)
    c["c_triB"] = np.where(jj <= ii, -1.0 / 16, 0.0).astype(np.float32)
    mB = (jj <= ii).astype(np.float32)
    c["c_maskB"] = np.ascontiguousarray(np.broadcast_to(mB[:, None, :], (128, 4, 128)))
    same = (jj // 16 == ii // 16)
    c["c_triS"] = np.where((jj <= ii) & same, -1.0 / 16, 0.0).astype(np.float32)
    mS = ((jj <= ii) & same).astype(np.float32)
    c["c_maskS"] = np.ascontiguousarray(np.broadcast_to(mS[:, None, :], (128, 4, 128)))
    c["c_mrevS"] = np.where((jj > ii) & same, -1.0 / 16, 0.0).astype(np.float32)
    c["c_bsumS"] = np.where(p[:, None] // 16 == np.arange(4)[None, :], -1.0 / 16, 0.0).astype(np.float32)
    cm = (p[None, :] // 16 == np.arange(4)[:, None]).astype(np.float32)
    c["c_cmaskS"] = np.ascontiguousarray(np.broadcast_to(cm[None], (128, 4, 128)))
    c["c_rmaskS"] = (p[:, None] // 16 == np.arange(4)[None, :]).astype(np.float32)
    c["c_idrep"] = np.ascontiguousarray(np.tile(np.eye(128, dtype=np.float32), (1, 4)))
    ids = np.zeros((128, 4, 64), np.float32)
    sel = np.zeros((128, 4, 128), np.float32)
    for b in range(4):
        for t in range(16):
            for r in range(4):
                ids[16 * b + t, b, r * 16 + t] = 1.0
            sel[16 * b + t, b, t] = 1.0
    c["c_idrepS"] = ids
    c["c_selS"] = sel
    c["c_ctab"] = np.ascontiguousarray(np.broadcast_to((0.5 ** np.arange(1, KIT + 2))[None, :], (128, KIT + 1))).astype(np.float32)
    tb = np.zeros((128, 2, 640), np.float32)
    kk = np.arange(640)
    for r in range(128):
        lim = 128 * j + (16 if r < 16 else (80 if r < 80 else 144))
        tb[r, 0, :] = np.where(kk < lim, 0.0, -1e30)
    tb[:, 1, :] = np.where(kk < 16, 0.0, -1e30)[None, :]
    c["c_tbias"] = tb
    oh = np.zeros((128, 4), np.float32)
    oh[:, j] = 1.0
    c["c_onehot"] = oh
    c["c_tailmask"] = (p < 16).astype(np.float32)[:, None]
    return c


def _dq_perm():
    perm = np.zeros(1024, np.int64)
    n = 0
    for m in range(2):
        for r in range(4):
            for half in range(2):
                g = 2 * m + half
                for d in range(64):
                    perm[n] = (g * 4 + r) * 64 + d
                    n += 1
    return perm


def prep(inp, SEQ):
    T, NB, G = geometry(SEQ)
    f = lambda a: np.ascontiguousarray(np.asarray(a, dtype=np.float32))
    w_in = f(inp["w_in"])[0].copy()
    w_in[:, C_DQ:C_DQ + 1024] = w_in[:, C_DQ:C_DQ + 1024][:, _dq_perm()]
    w3 = np.ascontiguousarray(np.stack([f(inp["w_gla"])[0], f(inp["w_dsa"])[0], f(inp["w_out"])[0]], 0))
    shared = dict(
        w_in=np.ascontiguousarray(w_in), w3=w3,
        ln_in_g=f(inp["ln_in_g"]).reshape(1, D), ln_in_b=f(inp["ln_in_b"]).reshape(1, D),
        ln_g=f(inp["ln_g"]).reshape(1, D), ln_b=f(inp["ln_b"]).reshape(1, D),
        gate_b=f(inp["gate_b"]).reshape(1, 2048), gla_gate_b=f(inp["gla_gate_b"]).reshape(1, 512),
        gla_norm_g=f(inp["gla_norm_g"]).reshape(1, 256),
        idx_kn_g=f(inp["idx_kn_g"]).reshape(1, 64), idx_kn_b=f(inp["idx_kn_b"]).reshape(1, 64),
        gla_w2=f(inp["gla_w2"]).reshape(16, 512))
    xp = f(inp["x_prompt"]); meta = f(inp["meta"]); xsm = f(inp["x_sample"])
    ck = f(inp["cache_k"])[0]; cv = f(inp["cache_v"])[0]; cik = f(inp["cache_idx_k"])[0]; stt = f(inp["state_gla"])[0]
    maps = []
    for c in range(8):
        b, j = c // 4, c % 4
        xall = np.zeros((4 * G * 128, D), np.float32)
        xall[:16] = meta
        xall[16:T] = xp[b]
        xown = np.ascontiguousarray(xall.reshape(G, 4, 128, D)[:, j].reshape(G * 128, D))
        xs_ = np.zeros((128, D), np.float32)
        xs_[:64] = xsm[4 * c:4 * c + 4].reshape(64, D)
        m = dict(shared)
        m.update(_consts(j, G))
        m.update(xall=np.ascontiguousarray(xall[:NB * 128]), xown=xown, xs=xs_,
                 cache_k=np.ascontiguousarray(ck[4 * c:4 * c + 4].reshape(4, 2048, 256)),
                 cache_v=np.ascontiguousarray(cv[4 * c:4 * c + 4].reshape(4, 2048, 256)),
                 cache_ik=np.ascontiguousarray(cik[4 * c:4 * c + 4]),
                 state=np.ascontiguousarray(stt[4 * c:4 * c + 4]))
        maps.append(m)
    return maps


def gather(res, SEQ):
    T, NB, G = geometry(SEQ)
    R = res.results
    yp = np.zeros((2, 4 * G * 128, D), np.float32)
    for c in range(8):
        b, j = c // 4, c % 4
        yp[b].reshape(G, 4, 128, D)[:, j] = R[c]["y_own"].reshape(G, 128, D)
    y_prompt = np.ascontiguousarray(yp[:, 16:T])
    y_sample = np.concatenate([R[c]["ys"][:64].reshape(4, 16, D) for c in range(8)], 0)
    k_prompt = np.stack([R[4 * b]["kp"][:T].reshape(T, 4, 64) for b in range(2)], 0)[None]
    v_prompt = np.stack([R[4 * b]["vp"][:T].reshape(T, 4, 64) for b in range(2)], 0)[None]
    ik_prompt = np.stack([R[4 * b]["ikp"][:T] for b in range(2)], 0)[None]
    gla_prompt = np.stack([R[4 * b]["gla_p"].reshape(128, 4, 256).transpose(1, 0, 2) for b in range(2)], 0)[None]
    k_sample = np.concatenate([R[c]["ks"][:64].reshape(4, 16, 4, 64) for c in range(8)], 0)[None]
    v_sample = np.concatenate([R[c]["vs"][:64].reshape(4, 16, 4, 64) for c in range(8)], 0)[None]
    ik_sample = np.concatenate([R[c]["iks"][:64].reshape(4, 16, 64) for c in range(8)], 0)[None]
    gla_sample = np.concatenate([R[c]["gla_s"].reshape(4, 128, 4, 256).transpose(0, 2, 1, 3) for c in range(8)], 0)[None]
    outs = (y_prompt, y_sample, k_prompt, v_prompt, ik_prompt, gla_prompt, k_sample, v_sample, ik_sample, gla_sample)
    return tuple(np.ascontiguousarray(o, dtype=np.float32) for o in outs)


_NC_CACHE = {}


def run(inputs, SEQ, phases="0ABS"):
    key = (SEQ, phases)
    if key not in _NC_CACHE:
        _NC_CACHE[key] = build(SEQ, phases)
    nc = _NC_CACHE[key]
    maps = prep(inputs, SEQ)
    res = run_bass_kernel_spmd(nc, maps, core_ids=list(range(8)))
    return gather(res, SEQ)


def kernel(**inputs):
    SEQ = int(np.asarray(inputs["x_prompt"]).shape[1])
    return run(inputs, SEQ)
```

```python
from contextlib import ExitStack
import numpy as np
import concourse.bass as bass
import concourse.mybir as mybir
from concourse.bass_utils import run_bass_kernel_spmd

F32 = mybir.dt.float32
BF16 = mybir.dt.bfloat16
AF = mybir.ActivationFunctionType
ALU = mybir.AluOpType
AX = mybir.AxisListType

D = 1024
NEG = -30000.0
KIT = 22
IDX_W_SCALE = (8 ** -0.5) * (64 ** -0.5)
ALPHA = 2.0 ** 0.25
EPS = 1e-5
C_GQ, C_GK, C_GV, C_GLOW, C_GR, C_DQ, C_DK, C_DV, C_IQ, C_IK, C_IW, C_DZ, C_MA, C_MB = (
    0, 512, 1024, 2048, 2064, 3088, 4112, 4368, 4624, 5136, 5200, 5208, 6232, 7256)
IN_COLS = 8280


class Prog:
    def __init__(self):
        self.ops = []

    def add(self, eng, fn, r=(), w=(), dsem=None):
        r = list(r)
        w = list(w)
        for b in list(r):
            if isinstance(b, str) and b.startswith('ps'):
                r.remove(b)
                if b not in w:
                    w.append(b)
        self.ops.append(dict(eng=eng, fn=fn, r=tuple(r), w=tuple(w), dsem=dsem))

    def barrier(self):
        self.ops.append(dict(eng='barrier', fn=None, r=(), w=(), dsem=None))

    def analyze(self):
        ops = self.ops
        last_w, readers = {}, {}
        last_of = {}
        for i, op in enumerate(ops):
            if op['eng'] == 'barrier':
                for e_, j in last_of.items():
                    ops[j]['needed'] = True
                last_w, readers = {}, {}
                op['deps'] = set()
                continue
            deps = set()
            for b in op['r']:
                if b in last_w:
                    deps.add(('raw', last_w[b]))
            for b in op['w']:
                if b in last_w:
                    deps.add(('waw', last_w[b]))
                for rr in readers.get(b, ()):
                    deps.add(('war', rr))
            for b in op['r']:
                readers.setdefault(b, []).append(i)
            for b in op['w']:
                last_w[b] = i
                readers[b] = []
            keep = set()
            for kind, j in deps:
                if j == i:
                    continue
                pj = ops[j]
                if pj['dsem'] is None and op['dsem'] is None and pj['eng'] == op['eng']:
                    if op['eng'] == 'pe' or kind == 'war':
                        continue
                keep.add(j)
            op['deps'] = keep
            for j in keep:
                ops[j]['needed'] = True
            if op['dsem'] is None:
                last_of[op['eng']] = i
        for e_, j in last_of.items():
            ops[j]['needed'] = True
        cnt = {}
        for op in ops:
            if op['eng'] == 'barrier':
                continue
            if op['dsem'] is not None:
                k = 'D:' + op['dsem']
                cnt[k] = cnt.get(k, 0) + 16
                op['sem'] = k
                op['val'] = cnt[k]
            elif op.get('needed'):
                k = 'E:' + op['eng']
                cnt[k] = cnt.get(k, 0) + 1
                op['sem'] = k
                op['val'] = cnt[k]
        waited = {}
        running = {}
        pending = {}
        for op in ops:
            if op['eng'] == 'barrier':
                for e_ in ('pe', 'act', 'dve', 'pool', 'sp'):
                    pending[e_] = dict(running)
                continue
            ws = {}
            if pending.get(op['eng']):
                ws.update(pending[op['eng']])
                pending[op['eng']] = None
            for j in op['deps']:
                pj = ops[j]
                ws[pj['sem']] = max(ws.get(pj['sem'], 0), pj['val'])
            wl = []
            we = waited.setdefault(op['eng'], {})
            for k, v in ws.items():
                if we.get(k, 0) >= v:
                    continue
                we[k] = v
                wl.append((k, v))
            op['waits'] = wl
            if op.get('sem') is not None:
                running[op['sem']] = op['val']
        self.totals = cnt
        return cnt

    def emit(self, nc, es):
        cnt = self.analyze()
        sems = {}
        for k in cnt:
            sems[k] = es.enter_context(nc.semaphore(k.replace(':', '_')))
        block = es.enter_context(nc.Block())
        ops = self.ops

        def run(engname):
            def f(e):
                for op in ops:
                    if op['eng'] != engname:
                        continue
                    for k, v in op['waits']:
                        e.wait_ge(sems[k], v)
                    ins = op['fn'](e)
                    if op.get('sem') is not None:
                        ins.then_inc(sems[op['sem']], 16 if op['dsem'] is not None else 1)
                for k, v in cnt.items():
                    e.wait_ge(sems[k], v)
            return f

        block.sync(run('sp'))
        block.scalar(run('act'))
        block.vector(run('dve'))
        block.gpsimd(run('pool'))
        block.tensor(run('pe'))


class KB:
    def __init__(self, nc):
        self.nc = nc
        self.P = Prog()
        self.rot = 0

    def capture(self):
        self._saved = self.P.ops
        self.P.ops = []

    def end_capture(self):
        l = self.P.ops
        self.P.ops = self._saved
        return l

    def merge(self, A, B):
        out = []
        ia = ib = 0
        na, nb_ = max(len(A), 1), max(len(B), 1)
        while ia < len(A) or ib < len(B):
            if ib >= len(B) or (ia < len(A) and ia * nb_ <= ib * na):
                out.append(A[ia]); ia += 1
            else:
                out.append(B[ib]); ib += 1
        self.P.ops.extend(out)

    def act(self, out, in_, func, r, w, **kw):
        self.P.add('act', lambda e: e.activation(out=out, in_=in_, func=func, **kw), r, w)

    def ts(self, out, in0, s1, s2, op0, op1, r, w, eng='dve', accum=None):
        if accum is None:
            if op1 is None:
                self.P.add(eng, lambda e: e.tensor_scalar(out=out, in0=in0, scalar1=s1, scalar2=None, op0=op0), r, w)
            else:
                self.P.add(eng, lambda e: e.tensor_scalar(out=out, in0=in0, scalar1=s1, scalar2=s2, op0=op0, op1=op1), r, w)
        else:
            self.P.add(eng, lambda e: e.tensor_scalar(out=out, in0=in0, scalar1=s1, scalar2=s2, op0=op0, op1=op1,
                                                      accum_out=accum), r, w)

    def tt(self, out, in0, in1, op, r, w, eng='dve'):
        self.P.add(eng, lambda e: e.tensor_tensor(out=out, in0=in0, in1=in1, op=op), r, w)

    def stt(self, out, in0, scalar, in1, op0, op1, r, w):
        self.P.add('dve', lambda e: e.scalar_tensor_tensor(out=out, in0=in0, scalar=scalar, in1=in1, op0=op0, op1=op1), r, w)

    def cp(self, eng, out, in_, r, w):
        if eng == 'act':
            self.P.add('act', lambda e: e.copy(out=out, in_=in_), r, w)
        else:
            self.P.add(eng, lambda e: e.tensor_copy(out=out, in_=in_), r, w)

    def memset(self, eng, ap, val, w):
        self.P.add(eng, lambda e: e.memset(ap, val), (), w)

    def mm(self, out, lhsT, rhs, start, stop, r, w):
        self.P.add('pe', lambda e: e.matmul(out, lhsT=lhsT, rhs=rhs, start=start, stop=stop), r, w)

    def tr(self, out, in_, ident, r, w):
        self.P.add('pe', lambda e: e.transpose(out=out, in_=in_, identity=ident), r, w)

    def dma(self, q, out, in_, r, w, dsem):
        self.P.add(q, lambda e: e.dma_start(out=out, in_=in_), r, w, dsem=dsem)

    def red(self, out, in_, op, r, w):
        self.P.add('dve', lambda e: e.tensor_reduce(out=out, in_=in_, axis=AX.X, op=op), r, w)

    def recip(self, out, in_, r, w):
        self.P.add('dve', lambda e: e.reciprocal(out=out, in_=in_), r, w)

    def bn_stats(self, out, in_, r, w):
        self.P.add('dve', lambda e: e.bn_stats(out=out, in_=in_), r, w)

    def bn_aggr(self, out, in_, r, w):
        self.P.add('dve', lambda e: e.bn_aggr(out=out, in_=in_), r, w)


def geometry(SEQ):
    T = SEQ + 16
    NB = T // 128 + 1
    assert T == 128 * (NB - 1) + 16 and NB % 4 == 1
    G = (NB + 3) // 4
    return T, NB, G


def build(SEQ, phases="0ABS"):
    T, NB, G = geometry(SEQ)
    NKMAX = NB * 128
    nc = bass.Bass("TRN2", target_bir_lowering=False)

    def din(name, shape, dt=F32):
        return nc.dram_tensor(name, list(shape), dt, kind="ExternalInput").ap()

    def dout(name, shape, dt=F32):
        return nc.dram_tensor(name, list(shape), dt, kind="ExternalOutput").ap()

    def dscr(name, shape, dt):
        return nc.dram_tensor(name, list(shape), dt, kind="Internal").ap()

    xall = din("xall", [NB * 128, D])
    xown = din("xown", [G * 128, D])
    xs = din("xs", [128, D])
    w_in = din("w_in", [D, IN_COLS])
    w3 = din("w3", [3, D, D])
    ln_in_g = din("ln_in_g", [1, D]); ln_in_b = din("ln_in_b", [1, D])
    ln_g = din("ln_g", [1, D]); ln_b = din("ln_b", [1, D])
    gate_b = din("gate_b", [1, 2048])
    gla_gate_b = din("gla_gate_b", [1, 512])
    gla_norm_g = din("gla_norm_g", [1, 256])
    idx_kn_g = din("idx_kn_g", [1, 64]); idx_kn_b = din("idx_kn_b", [1, 64])
    gla_w2 = din("gla_w2", [16, 512])
    cache_k = din("cache_k", [4, 2048, 256]); cache_v = din("cache_v", [4, 2048, 256])
    cache_ik = din("cache_ik", [4, 2048, 64])
    state = din("state", [4, 4, 128, 256])
    c_ident = din("c_ident", [128, 128])
    c_triA = din("c_triA", [128, 128]); c_blkA = din("c_blkA", [128, 2])
    c_triB = din("c_triB", [128, 128]); c_maskB = din("c_maskB", [128, 4, 128])
    c_triS = din("c_triS", [128, 128]); c_maskS = din("c_maskS", [128, 4, 128])
    c_mrevS = din("c_mrevS", [128, 128]); c_bsumS = din("c_bsumS", [128, 4])
    c_cmaskS = din("c_cmaskS", [128, 4, 128]); c_rmaskS = din("c_rmaskS", [128, 4])
    c_idrep = din("c_idrep", [128, 512]); c_idrepS = din("c_idrepS", [128, 4, 64])
    c_selS = din("c_selS", [128, 4, 128])
    c_ctab = din("c_ctab", [128, KIT + 1])
    c_tbias = din("c_tbias", [128, 2, 640])
    c_onehot = din("c_onehot", [128, 4])
    c_tailmask = din("c_tailmask", [128, 1])

    y_own = dout("y_own", [G * 128, D])
    kp = dout("kp", [NB * 128, 256]); vp = dout("vp", [NB * 128, 256]); ikp = dout("ikp", [NB * 128, 64])
    gla_p = dout("gla_p", [128, 1024])
    ys = dout("ys", [128, D]); ks = dout("ks", [128, 256]); vs = dout("vs", [128, 256]); iks = dout("iks", [128, 64])
    gla_s = dout("gla_s", [4, 128, 1024])

    wbf = dscr("wbf", [D, IN_COLS], BF16)
    wbf3 = dscr("wbf3", [3, D, D], BF16)
    kT_d = dscr("kT_d", [128, 2, NKMAX], BF16)
    v_d = dscr("v_d", [128, NB, 260], BF16)
    ki_d = dscr("ki_d", [128, NKMAX], BF16)
    snap = dscr("snap", [NB, 128, 1024], F32)

    k = KB(nc)
    P = k.P
    top = ExitStack()
    with top:
        pst = [top.enter_context(nc.psum_tensor("ps%d" % i, [128, 512], F32)) for i in range(8)]

        def PSF(i):
            return pst[i][:]

        def PSB(i):
            return pst[i][:].bitcast(BF16)

        if "0" in phases:
            with ExitStack() as es:
                def sb(name, shape, dt):
                    return es.enter_context(nc.sbuf_tensor(name, shape, dt))
                wst = [sb("wst%d" % s, [128, 8, 512], F32) for s in range(2)]
                wcb = [sb("wcb%d" % s, [128, 8, 512], BF16) for s in range(2)]
                jobs = []
                for c in range(17):
                    c0 = c * 512
                    n = min(512, IN_COLS - c0)
                    jobs.append((w_in[:, c0:c0 + n], wbf[:, c0:c0 + n], n))
                for m in range(3):
                    for c in range(2):
                        jobs.append((w3[m, :, c * 512:(c + 1) * 512], wbf3[m, :, c * 512:(c + 1) * 512], 512))
                engs = ['dve', 'act', 'pool']
                for idx, (src, dst, n) in enumerate(jobs):
                    s = idx % 2
                    k.dma('sp', wst[s][:, :, :n], src.rearrange("(k p) n -> p k n", p=128), [], ['wst%d' % s], 'wst%d' % s)
                    k.cp(engs[idx % 3], wcb[s][:, :, :n], wst[s][:, :, :n], ['wst%d' % s], ['wcb%d' % s])
                    k.dma('act', dst.rearrange("(k p) n -> p k n", p=128), wcb[s][:, :, :n], ['wcb%d' % s], [], 'wcb%d' % s)
            P.barrier()

        def layernorm(src, srckey, gB, bB, tl, tag, out_f32=None, out_f32_key=None, out_bf=None, out_bf_key=None):
            tk = lambda n: tag + n
            for c in range(2):
                k.bn_stats(tl['st'][:, c, :], src[:, c * 512:(c + 1) * 512], [srckey], [tk('st%d' % c)])
            k.bn_aggr(tl['mv'][:], tl['st'][:].rearrange("p a b -> p (a b)"), [tk('st0'), tk('st1')], [tk('mv')])
            k.act(tl['sd'][:], tl['mv'][:, 1:2], AF.Ln, [tk('mv'), 'eps'], [tk('sd')], bias=tl['eps'][:, 0:1], scale=1.0)
            k.act(tl['rstd'][:], tl['sd'][:], AF.Exp, [tk('sd')], [tk('rstd')], scale=-0.5)
            k.ts(tl['nmr'][:], tl['mv'][:, 0:1], tl['rstd'][:, 0:1], -1.0, ALU.mult, ALU.mult, [tk('mv'), tk('rstd')], [tk('nmr')])
            k.act(tl['xn'][:], src, AF.Identity, [srckey, tk('nmr'), tk('rstd')], [tl['xnkey']],
                  bias=tl['nmr'][:, 0:1], scale=tl['rstd'][:, 0:1])
            k.tt(tl['xn'][:], tl['xn'][:], gB, ALU.mult, [tl['xnkey'], 'lnconst'], [tl['xnkey']])
            if out_f32 is not None:
                k.tt(out_f32, tl['xn'][:], bB, ALU.add, [tl['xnkey'], 'lnconst'], [out_f32_key])
                if out_bf is not None:
                    k.cp('pool', out_bf, out_f32, [out_f32_key], [out_bf_key])
            else:
                k.tt(out_bf, tl['xn'][:], bB, ALU.add, [tl['xnkey'], 'lnconst'], [out_bf_key])

        if "A" in phases:
            with ExitStack() as es:
                def sb(name, shape, dt):
                    return es.enter_context(nc.sbuf_tensor(name, shape, dt))
                gB = sb("a_gB", [128, D], F32); bB = sb("a_bB", [128, D], F32)
                identf = sb("a_idf", [128, 128], F32); identb = sb("a_idb", [128, 128], BF16)
                triA = sb("a_triA", [128, 128], F32); blkA = sb("a_blkA", [128, 2], F32)
                w2 = sb("a_w2", [16, 512], F32); gbias = sb("a_gbias", [1, 512], F32); ones1 = sb("a_ones1", [1, 128], F32)
                gkiB = sb("a_gkiB", [128, 64], F32); bkiB = sb("a_bkiB", [128, 64], F32)
                eps = sb("a_eps", [128, 1], F32); one = sb("a_one", [128, 1], F32)
                tailm = sb("a_tailm", [128, 1], F32)
                wA = sb("a_wA", [128, 8, 2128], BF16)
                SS = [sb("a_S%d" % s, [128, 4, 256], F32) for s in range(3)]
                xa = [sb("a_xa%d" % s, [128, D], F32) for s in range(3)]
                xn = sb("a_xn", [128, D], F32)
                hb = [sb("a_hb%d" % s, [128, D], BF16) for s in range(3)]
                hT = [sb("a_hT%d" % s, [128, 8, 128], BF16) for s in range(2)]
                st = sb("a_st", [128, 2, 6], F32); mv = sb("a_mv", [128, 2], F32)
                sd = sb("a_sd", [128, 1], F32); rstd = sb("a_rstd", [128, 1], F32); nmr = sb("a_nmr", [128, 1], F32)
                st2 = sb("a_st2", [128, 6], F32); mv2 = sb("a_mv2", [128, 2], F32)
                sd2 = sb("a_sd2", [128, 1], F32); rstd2 = sb("a_rstd2", [128, 1], F32); nmr2 = sb("a_nmr2", [128, 1], F32)
                Vt = [sb("a_V%d" % s, [128, 1024], BF16) for s in range(4)]
                kdv = [sb("a_kdv%d" % s, [128, 512], F32) for s in range(3)]
                kdb = [sb("a_kdb%d" % s, [128, 256], BF16) for s in range(2)]
                vext = [sb("a_vext%d" % s, [128, 4, 65], BF16) for s in range(2)]
                kTt = [sb("a_kT%d" % s, [128, 2, 128], BF16) for s in range(2)]
                kin = [sb("a_kin%d" % s, [128, 64], F32) for s in range(3)]
                ksb = [sb("a_ksb%d" % s, [128, 512], F32) for s in range(3)]
                kif = [sb("a_kif%d" % s, [128, 64], F32) for s in range(2)]
                kib = [sb("a_kib%d" % s, [128, 128], BF16) for s in range(2)]
                kiT = [sb("a_kiT%d" % s, [128, 128], BF16) for s in range(2)]
                glb = sb("a_glb", [16, 128], BF16); w2b = sb("a_w2b", [16, 512], BF16)
                ones1b = sb("a_ones1b", [1, 128], BF16); gbh = sb("a_gbh", [1, 512], BF16); gbl = sb("a_gbl", [1, 512], BF16)
                gbt = sb("a_gbt", [1, 512], F32)
                el = [sb("a_el%d" % s, [128, 512], F32) for s in range(3)]
                er = sb("a_er", [128, 512], F32)
                Kt = [sb("a_Kt%d" % s, [128, 512], BF16) for s in range(2)]
                dec = [sb("a_dec%d" % s, [128, 8], F32) for s in range(2)]

                k.dma('sp', gB[:], ln_in_g.partition_broadcast(128), [], ['lnconst0'], 'ca')
                k.dma('sp', bB[:], ln_in_b.partition_broadcast(128), [], ['lnconst1'], 'ca')
                k.dma('sp', identf[:], c_ident, [], ['identf'], 'ca')
                k.dma('sp', triA[:], c_triA, [], ['triA'], 'ca')
                k.dma('sp', blkA[:], c_blkA, [], ['blkA'], 'ca')
                k.dma('sp', w2[:], gla_w2, [], ['w2'], 'ca')
                k.dma('sp', gbias[:], gla_gate_b, [], ['gbias'], 'ca')
                k.dma('sp', gkiB[:], idx_kn_g.partition_broadcast(128), [], ['gkiB'], 'ca')
                k.dma('sp', bkiB[:], idx_kn_b.partition_broadcast(128), [], ['bkiB'], 'ca')
                k.dma('sp', tailm[:], c_tailmask, [], ['tailm'], 'ca')
                wmap = [(C_GK, 512, 0), (C_GV, 1024, 512), (C_DK, 512, 1536), (C_IK, 64, 2048), (C_GLOW, 16, 2112)]
                for (c0, n, o) in wmap:
                    k.dma('sp', wA[:, :, o:o + n], wbf[:, c0:c0 + n].rearrange("(k p) n -> p k n", p=128), [], ['wA%d' % o], 'ca')
                P.barrier()
                k.memset('dve', eps[:], EPS, ['eps'])
                k.memset('dve', one[:], 1.0, ['one'])
                k.memset('dve', ones1[:], 1.0, ['ones1'])
                k.memset('dve', SS[0][:], 0.0, ['S0'])
                for s in range(2):
                    k.memset('pool', vext[s][:], 1.0, ['vext%d' % s])
                k.cp('dve', identb[:], identf[:], [], ['identb'])
                k.cp('dve', w2b[:], w2[:], [], ['w2b'])
                k.memset('dve', ones1b[:], 1.0, ['ones1b'])
                k.cp('dve', gbh[:], gbias[:], [], ['gbh'])
                k.cp('dve', gbt[:], gbh[:], ['gbh'], ['gbt'])
                k.tt(gbt[:], gbias[:], gbt[:], ALU.subtract, ['gbt'], ['gbt'])
                k.cp('dve', gbl[:], gbt[:], ['gbt'], ['gbl'])
                P.barrier()
                tl = dict(st=st, mv=mv, sd=sd, rstd=rstd, nmr=nmr, xn=xn, eps=eps, xnkey='xn')

                def loadx(i):
                    s = i % 3
                    k.dma('sp', xa[s][:], xall[i * 128:(i + 1) * 128, :], [], ['xa%d' % s], 'xa%d' % s)

                loadx(0)

                def fa(i):
                    s = i % 3
                    if i + 1 < NB:
                        loadx(i + 1)
                    layernorm(xa[s][:], 'xa%d' % s, gB[:], bB[:], tl, 'a', out_bf=hb[s][:], out_bf_key='hb%d' % s)

                def fb_a(i):
                    s = i % 3
                    s2 = i % 2
                    for kc in range(8):
                        k.tr(PSB(0)[:, kc * 128:(kc + 1) * 128], hb[s][:, kc * 128:(kc + 1) * 128], identb[:], ['hb%d' % s], ['ps0'])
                    k.cp('act', hT[s2][:].rearrange("p a b -> p (a b)"), PSB(0), ['ps0'], ['hT%d' % s2])

                def fb_b(i):
                    s = i % 3
                    s2 = i % 2
                    last = (i == NB - 1)
                    hk = 'hT%d' % s2
                    for kc in range(8):
                        k.mm(PSF(1), hT[s2][:, kc, :], wA[:, kc, 0:512], kc == 0, kc == 7, [hk], ['ps1'])
                    k.cp('act', ksb[s][:], PSF(1), ['ps1'], ['ksb%d' % s])
                    for half in range(2):
                        for kc in range(8):
                            k.mm(PSF(2 + half), hT[s2][:, kc, :], wA[:, kc, 512 + half * 512:1024 + half * 512], kc == 0, kc == 7, [hk], ['ps%d' % (2 + half)])
                        k.cp('act' if half == 0 else 'dve', Vt[i % 4][:, half * 512:(half + 1) * 512], PSF(2 + half), ['ps%d' % (2 + half)], ['V%d' % (i % 4)])
                    for kc in range(8):
                        k.mm(PSF(4), hT[s2][:, kc, :], wA[:, kc, 1536:2048], kc == 0, kc == 7, [hk], ['ps4'])
                    k.cp('act', kdv[s][:], PSF(4), ['ps4'], ['kdv%d' % s])
                    for kc in range(8):
                        k.mm(PSF(5)[:, 0:64], hT[s2][:, kc, :], wA[:, kc, 2048:2112], kc == 0, kc == 7, [hk], ['ps5'])
                    k.cp('dve', kin[s][:], PSF(5)[:, 0:64], ['ps5'], ['kin%d' % s])

                    for kc in range(8):
                        k.mm(PSF(0)[0:16, 0:128], wA[:, kc, 2112:2128], hT[s2][:, kc, :], kc == 0, kc == 7, [hk], ['ps0'])
                    k.cp('act', glb[:], PSF(0)[0:16, 0:128], ['ps0'], ['gl'])
                    k.mm(PSF(4), glb[:], w2b[:], True, False, ['gl'], ['ps4'])
                    k.mm(PSF(4), ones1b[:], gbh[:], False, False, [], ['ps4'])
                    k.mm(PSF(4), ones1b[:], gbl[:], False, True, [], ['ps4'])
                    k.act(el[s][:], PSF(4), AF.Exp, ['ps4'], ['el%d' % s], scale=-1.0)
                    k.act(el[s][:], el[s][:], AF.Ln, ['el%d' % s], ['el%d' % s], bias=one[:, 0:1], scale=1.0)
                    if last:
                        k.ts(el[s][:], el[s][:], tailm[:, 0:1], None, ALU.mult, None, ['el%d' % s], ['el%d' % s])
                def back1(i):
                    s3 = i % 3
                    s = i % 2
                    last = (i == NB - 1)
                    k.mm(PSF(6), triA[:], el[s3][:], True, True, ['el%d' % s3], ['ps6'])
                    for h in range(4):
                        k.mm(PSF(7)[:, 448 + 2 * h:450 + 2 * h], el[s3][:, h * 128:(h + 1) * 128], blkA[:], True, True, ['el%d' % s3], ['ps7'])
                    k.act(er[:], PSF(6), AF.Exp, ['ps6'], ['er'])
                    k.act(dec[s][:], PSF(7)[:, 448:456], AF.Exp, ['ps7'], ['dec%d' % s])
                    if last:
                        k.stt(Kt[s][:], ksb[s3][:], tailm[:, 0:1], er[:], ALU.mult, ALU.mult, ['ksb%d' % s3, 'er'], ['Kt%d' % s])
                    else:
                        k.tt(Kt[s][:], ksb[s3][:], er[:], ALU.mult, ['ksb%d' % s3, 'er'], ['Kt%d' % s])
                    k.dma('act', kp[i * 128:(i + 1) * 128, :], kdv[s3][:, 0:256], ['kdv%d' % s3], [], 'kdvo%d' % s3)
                    k.dma('act', vp[i * 128:(i + 1) * 128, :], kdv[s3][:, 256:512], ['kdv%d' % s3], [], 'kdvo%d' % s3)
                    k.cp('pool', kdb[s][:], kdv[s3][:, 0:256], ['kdv%d' % s3], ['kdb%d' % s])
                    k.cp('pool', vext[s][:, :, 0:64], kdv[s3][:, 256:512].rearrange("p (g d) -> p g d", g=4), ['kdv%d' % s3], ['vext%d' % s])
                    for c in range(2):
                        k.tr(PSB(7)[:, c * 128:(c + 1) * 128], kdb[s][:, c * 128:(c + 1) * 128], identb[:], ['kdb%d' % s], ['ps7'])
                    k.cp('dve', kTt[s][:].rearrange("p a b -> p (a b)"), PSB(7)[:, 0:256], ['ps7'], ['kT%d' % s])
                    k.dma('pool', kT_d[:, :, i * 128:(i + 1) * 128], kTt[s][:], ['kT%d' % s], [], 'kTo%d' % s)
                    k.dma('pool', v_d[:, i, :], vext[s][:].rearrange("p g d -> p (g d)"), ['vext%d' % s], [], 'vexto%d' % s)
                    kn = kin[s3]
                    knk = 'kin%d' % s3
                    k.bn_stats(st2[:], kn[:], [knk], ['st2'])
                    k.bn_aggr(mv2[:], st2[:], ['st2'], ['mv2'])
                    k.act(sd2[:], mv2[:, 1:2], AF.Ln, ['mv2'], ['sd2'], bias=eps[:, 0:1], scale=1.0)
                    k.act(rstd2[:], sd2[:], AF.Exp, ['sd2'], ['rstd2'], scale=-0.5)
                    k.ts(nmr2[:], mv2[:, 0:1], rstd2[:, 0:1], -1.0, ALU.mult, ALU.mult, ['mv2', 'rstd2'], ['nmr2'])
                    k.act(kn[:], kn[:], AF.Identity, [knk, 'nmr2', 'rstd2'], [knk], bias=nmr2[:, 0:1], scale=rstd2[:, 0:1])
                    k.tt(kn[:], kn[:], gkiB[:], ALU.mult, [knk], [knk])
                    k.tt(kif[s][:], kn[:], bkiB[:], ALU.add, [knk], ['kif%d' % s])
                    k.dma('act', ikp[i * 128:(i + 1) * 128, :], kif[s][:], ['kif%d' % s], [], 'kifo%d' % s)
                    k.cp('pool', kib[s][:, 0:64], kif[s][:], ['kif%d' % s], ['kib%d' % s])
                    k.cp('pool', kib[s][:, 64:128], kif[s][:], ['kif%d' % s], ['kib%d' % s])

                def back1b(i):
                    s = i % 2
                    k.tr(PSB(7)[:, 256:384], kib[s][:], identb[:], ['kib%d' % s], ['ps7'])
                    k.cp('dve', kiT[s][:], PSB(7)[:, 256:384], ['ps7'], ['kiT%d' % s])
                    k.dma('pool', ki_d[:, i * 128:(i + 1) * 128], kiT[s][:], ['kiT%d' % s], [], 'kiTo%d' % s)

                def back2(i):
                    s3 = i % 3
                    s = i % 2
                    cur = (2 * i) % 3
                    k.dma('sp', snap[i], SS[cur][:].rearrange("p h e -> p (h e)"), ['S%d' % cur], [], 'Ssto%d' % cur)
                    sbanks = [[6, 7], [2, 3]]
                    for c in range(2):
                        for hp in range(2):
                            bk = sbanks[c][hp]
                            for hh in range(2):
                                h = hp * 2 + hh
                                k.mm(PSF(bk)[:, hh * 256:(hh + 1) * 256], Kt[s][c * 64:(c + 1) * 64, h * 128:(h + 1) * 128],
                                     Vt[i % 4][c * 64:(c + 1) * 64, h * 256:(h + 1) * 256], True, True, ['Kt%d' % s, 'V%d' % (i % 4)], ['ps%d' % bk])
                            src_, dst_ = (2 * i + c) % 3, (2 * i + c + 1) % 3
                            for hh in range(2):
                                h = hp * 2 + hh
                                k.stt(SS[dst_][:, h, :], SS[src_][:, h, :], dec[s][:, 2 * h + c:2 * h + c + 1], PSF(bk)[:, hh * 256:(hh + 1) * 256],
                                      ALU.mult, ALU.add, ['S%d' % src_, 'dec%d' % s, 'ps%d' % bk], ['S%d' % dst_])

                for i0 in range(min(3, NB)):
                    fa(i0)
                for i0 in range(3):
                    if i0 < NB:
                        fb_a(i0)
                        fb_b(i0)
                    if i0 + 3 < NB and i0 < 2:
                        fa(i0 + 3)
                back1(0)
                back1b(0)
                for i in range(NB):
                    if i + 3 < NB:
                        fb_a(i + 3)
                    back2(i)
                    if i + 1 < NB:
                        back1(i + 1)
                    if i + 5 < NB:
                        fa(i + 5)
                    if i + 3 < NB:
                        fb_b(i + 3)
                    if i + 1 < NB:
                        back1b(i + 1)
                fin = (2 * NB) % 3
                k.dma('sp', gla_p, SS[fin][:].rearrange("p h e -> p (h e)"), ['S%d' % fin], [], 'glap')
            P.barrier()


        def phase_own(mode):
            PR = (mode == 'P')
            NK = NKMAX if PR else 2176
            with ExitStack() as es:
                def sb(name, shape, dt):
                    return es.enter_context(nc.sbuf_tensor(mode + name, shape, dt))
                gB = sb("gB", [128, D], F32); bB = sb("bB", [128, D], F32)
                g2B = sb("g2B", [128, D], F32); b2B = sb("b2B", [128, D], F32)
                gtb = [sb("gtb%d" % s_, [128, 512], F32) for s_ in range(2)]
                gnB = sb("gnB", [128, 256], F32)
                identf = sb("idf", [128, 128], F32); identb = sb("idb", [128, 128], BF16)
                tri = sb("tri", [128, 128], F32)
                maskf = sb("maskf", [128, 512], F32)
                cst = sb("cst", [128, 512], F32); idrepb = sb("idrepb", [128, 512], BF16)
                w2 = sb("w2", [16, 512], F32); gbias = sb("gbias", [1, 512], F32); ones1 = sb("ones1", [1, 128], F32)
                eps = sb("eps", [128, 1], F32); one = sb("one", [128, 1], F32)
                ctab = sb("ctab", [128, KIT + 1], F32)
                tbias = sb("tbias", [128, 2, 640 if PR else 128], F32)
                onehot = sb("onehot", [128, 4], F32)
                xo = sb("xo", [128, D], F32); h = sb("h", [128, D], F32); hb = sb("hb", [128, D], BF16)
                hT = sb("hT", [128, 8, 128], BF16)
                tmp = sb("tmp", [128, D], F32)
                st = sb("st", [128, 2, 6], F32); mv = sb("mv", [128, 2], F32)
                sd = sb("sd", [128, 1], F32); rstd = sb("rstd", [128, 1], F32); nmr = sb("nmr", [128, 1], F32)
                wch = [sb("wch%d" % s_, [128, 8, 512], BF16) for s_ in range(2)]
                gl = sb("gl", [16, 128], F32)
                el = sb("el", [128, 512], F32); eb = sb("eb", [128, 512], F32); enb = sb("enb", [128, 512], F32)
                qT = sb("qT", [128, 4, 128], BF16); kTh = sb("kTh", [128, 4, 128], BF16)
                V = sb("V", [128, 1024], BF16)
                sg = [sb("sg%d" % s_, [128, 512], F32) for s_ in range(2)]
                QTz = sb("QTz", [128, 4, 512], BF16); qiTz = sb("qiTz", [128, 8, 128], BF16)
                wabs = sb("wabs", [128, 8], F32); wsgn = sb("wsgn", [128, 8], F32)
                AT = sb("AT", [128, 4, 128], BF16)
                ss = sb("ss", [128, 4], F32); rs = sb("rs", [128, 4], F32)
                yain = sb("yain", [128, D], BF16); yT = sb("yT", [128, 8, 128], BF16)
                mrg = sb("mrg", [128, D], F32)
                sc = sb("sc", [128, NK], F32)
                junk = None if PR else sb("junk", [128, 2176], BF16)
                rlw = sb("rlw", [128, 1024], F32)
                rl = [rlw[:, 0:512], rlw[:, 512:1024]]
                rd = sb("rd", [128, 16], F32)
                yout = sb("yout", [128, D], F32) if PR else xo
                rmax = sb("rmax", [128, 1], F32); rmin = sb("rmin", [128, 1], F32); Wd = sb("Wd", [128, 1], F32)
                wtab = sb("wtab", [128, KIT + 1], F32); mids = sb("mids", [128, KIT + 1], F32)
                cnts = sb("cnts", [128, KIT], F32); us = sb("us", [128, KIT], F32); thr = sb("thr", [128, 1], F32)
                sAs = sb("sAs", [128, KIT], F32); vvs = sb("vvs", [128, KIT], F32)
                jd = sb("jd", [128, 8], BF16); ja = sb("ja", [128, 8], BF16); jq = sb("jq", [128, 8], BF16)
                if PR:
                    Sc = [sb("Sc%d" % s_, [128, 1024], F32) for s_ in range(2)]
                    Sown = sb("Sown", [128, 1024], F32); Sb = sb("Sb", [128, 4, 256], BF16)
                    kich = [sb("kich%d" % s_, [128, 1024], BF16) for s_ in range(2)]
                    kTch = [sb("kTch%d" % s_, [128, 2, 512], BF16) for s_ in range(2)]
                    vch = [sb("vch%d" % s_, [128, 4, 260], BF16) for s_ in range(2)]
                    mbt = [sb("mbt%d" % s_, [128, 128], BF16) for s_ in range(3)]
                    pT = [sb("pT%d" % s_, [128, 512], BF16) for s_ in range(3)]
                    oT = [sb("oT%d" % s_, [65, 512], F32) for s_ in range(2)]
                else:
                    cmaskS = sb("cmaskS", [128, 4, 128], F32); rmaskS = sb("rmaskS", [128, 4], F32)
                    mrevS = sb("mrevS", [128, 128], F32); bsumS = sb("bsumS", [128, 4], F32)
                    idrepSb = sb("idrepSb", [128, 4, 64], BF16)
                    selSb = sb("selSb", [128, 4, 128], BF16)
                    gkiB = sb("gkiB", [128, 64], F32); bkiB = sb("bkiB", [128, 64], F32)
                    S0f = [sb("S0f%d" % b_, [128, 4, 256], F32) for b_ in range(2)]
                    S0b = [sb("S0b%d" % b_, [128, 4, 256], BF16) for b_ in range(4)]
                    qTb = [sb("qTb%d" % b_, [128, 4, 128], BF16) for b_ in range(4)]
                    wabsb = sb("wabsb", [128, 4, 8], F32)
                    QTs = sb("QTs", [128, 4, 4, 64], BF16)
                    kTs1 = sb("kTs", [128, 2, 2176], BF16)
                    kiTs = [sb("kiTs%d" % b_, [128, 2176], BF16) for b_ in range(4)]
                    vexts1 = sb("vexts", [128, 17, 260], BF16)
                    ckf = sb("ckf", [128, 8, 256], F32); ckb = sb("ckb", [128, 8, 256], BF16)
                    cif = sb("cif", [128, 8, 64], F32); cib = sb("cib", [128, 8, 128], BF16)
                    kdv = sb("kdv", [128, 512], F32); kdb = sb("kdb", [128, 256], BF16); vnb = sb("vnb", [128, 256], BF16)
                    kin = sb("kin", [128, 64], F32); kif = sb("kif", [128, 64], F32); kib = sb("kib", [128, 128], BF16)
                    st2 = sb("st2", [128, 6], F32); mv2 = sb("mv2", [128, 2], F32)
                    sd2 = sb("sd2", [128, 1], F32); rstd2 = sb("rstd2", [128, 1], F32); nmr2 = sb("nmr2", [128, 1], F32)
                    kTnew = sb("kTnew", [128, 2, 128], BF16); kiTnew = sb("kiTnew", [128, 128], BF16)
                    Kt = sb("Kt", [128, 512], BF16); Ktb = [sb("Ktb%d" % s_, [128, 512], BF16) for s_ in range(2)]
                    decs = sb("decs", [128, 16], F32)
                    pTs = [sb("pTs%d" % s_, [128, 64], BF16) for s_ in range(3)]

                cl = 'c' + mode
                k.dma('sp', gB[:], ln_in_g.partition_broadcast(128), [], [], cl)
                k.dma('sp', bB[:], ln_in_b.partition_broadcast(128), [], [], cl)
                k.dma('sp', g2B[:], ln_g.partition_broadcast(128), [], [], cl)
                k.dma('sp', b2B[:], ln_b.partition_broadcast(128), [], [], cl)
                k.dma('sp', gnB[:], gla_norm_g.partition_broadcast(128), [], [], cl)
                k.dma('sp', identf[:], c_ident, [], [], cl)
                k.dma('sp', tri[:], c_triB if PR else c_triS, [], [], cl)
                k.dma('sp', maskf[:], (c_maskB if PR else c_maskS).rearrange("p a b -> p (a b)"), [], [], cl)
                k.dma('sp', w2[:], gla_w2, [], [], cl)
                k.dma('sp', gbias[:], gla_gate_b, [], [], cl)
                k.dma('sp', ctab[:], c_ctab, [], [], cl)
                k.dma('sp', tbias[:], c_tbias if PR else c_tbias[:, :, 0:128], [], [], cl)
                k.dma('sp', onehot[:], c_onehot, [], [], cl)
                if not PR:
                    k.dma('sp', cmaskS[:], c_cmaskS, [], [], cl)
                    k.dma('sp', rmaskS[:], c_rmaskS, [], [], cl)
                    k.dma('sp', mrevS[:], c_mrevS, [], [], cl)
                    k.dma('sp', bsumS[:], c_bsumS, [], [], cl)
                    k.dma('sp', gkiB[:], idx_kn_g.partition_broadcast(128), [], [], cl)
                    k.dma('sp', bkiB[:], idx_kn_b.partition_broadcast(128), [], [], cl)
                P.barrier()
                k.memset('dve', eps[:], EPS, [])
                k.memset('dve', one[:], 1.0, [])
                k.memset('dve', ones1[:], 1.0, [])
                k.memset('pool', QTz[:], 0.0, [])
                k.memset('pool', qiTz[:], 0.0, [])
                k.memset('pool', yain[:], 0.0, [])
                k.memset('pool', tmp[:], 0.0, [])
                k.cp('dve', identb[:], identf[:], [], [])
                k.dma('sp', cst[:], c_idrep, [], ['cst'], 'cst')
                k.cp('dve', idrepb[:], cst[:], ['cst'], ['idrepb'])
                if not PR:
                    k.dma('sp', cst[:, 0:256], c_idrepS.rearrange("p a b -> p (a b)"), ['idrepb'], ['cst'], 'cst')
                    k.cp('dve', idrepSb[:].rearrange("p a b -> p (a b)"), cst[:, 0:256], ['cst'], ['idrepSb'])
                    k.dma('sp', cst[:], c_selS.rearrange("p a b -> p (a b)"), ['idrepSb'], ['cst'], 'cst')
                    k.cp('dve', selSb[:].rearrange("p a b -> p (a b)"), cst[:], ['cst'], ['selSb'])
                    for b_ in range(4):
                        s_ = b_ % 2
                        k.dma('sp', S0f[s_][:], state[b_].rearrange("h p e -> p h e"), [], ['S0f%d' % s_], 'S0f%d' % s_)
                        k.cp('pool', S0b[b_][:], S0f[s_][:], ['S0f%d' % s_], ['S0b%d' % b_])
                        k.memset('pool', kiTs[b_][:, 2048:2176], 0.0, [])
                    k.memset('pool', vexts1[:], 1.0, [])
                    k.memset('pool', kTs1[:, :, 2048:2176], 0.0, [])
                P.barrier()
                tl = dict(st=st, mv=mv, sd=sd, rstd=rstd, nmr=nmr, xn=tmp, eps=eps, xnkey='tmp')
                bank = [0]

                def nb():
                    bank[0] = (bank[0] + 1) % 8
                    return bank[0]

                def own_block(g):
                    jobs = []

                    def J(src, n):
                        jobs.append((src, n))
                    J(wbf[:, C_IW:C_IW + 8], 8)
                    J(wbf[:, C_IQ:C_IQ + 512], 512)
                    J(wbf[:, C_DQ:C_DQ + 512], 512); J(wbf[:, C_DQ + 512:C_DQ + 1024], 512)
                    if not PR:
                        J(wbf[:, C_DK:C_DK + 512], 512)
                        J(wbf[:, C_IK:C_IK + 64], 64)
                    J(wbf[:, C_GLOW:C_GLOW + 16], 16)
                    J(wbf[:, C_GQ:C_GQ + 512], 512)
                    J(wbf[:, C_GK:C_GK + 512], 512)
                    J(wbf[:, C_GV:C_GV + 512], 512); J(wbf[:, C_GV + 512:C_GV + 1024], 512)
                    J(wbf[:, C_GR:C_GR + 512], 512); J(wbf[:, C_GR + 512:C_GR + 1024], 512)
                    for cc in range(2):
                        J(wbf3[0, :, cc * 512:(cc + 1) * 512], 512)
                        J(wbf[:, C_MA + cc * 512:C_MA + (cc + 1) * 512], 512)
                    J(wbf[:, C_DZ:C_DZ + 512], 512); J(wbf[:, C_DZ + 512:C_DZ + 1024], 512)
                    for cc in range(2):
                        J(wbf3[1, :, cc * 512:(cc + 1) * 512], 512)
                        J(wbf[:, C_MB + cc * 512:C_MB + (cc + 1) * 512], 512)
                    for cc in range(2):
                        J(wbf3[2, :, cc * 512:(cc + 1) * 512], 512)
                    jpos = [0]

                    def wissue(idx):
                        src, n = jobs[idx]
                        s_ = idx % 2
                        k.dma('sp', wch[s_][:, :, :n], src.rearrange("(k p) n -> p k n", p=128), [], ['wch%d' % s_], 'wch%d' % s_)

                    def next_w():
                        idx = jpos[0]
                        if idx == 0:
                            wissue(0)
                        if idx + 1 < len(jobs):
                            wissue(idx + 1)
                        jpos[0] += 1
                        return wch[idx % 2], 'wch%d' % (idx % 2)

                    def projT(wt, wk, n, psap, pskey):
                        for kc in range(8):
                            k.mm(psap, hT[:, kc, :], wt[:, kc, :n], kc == 0, kc == 7, ['hT', wk], [pskey])

                    def projF(wt, wk, bk):
                        for sub in range(4):
                            for kc in range(8):
                                k.mm(PSF(bk)[:, sub * 128:(sub + 1) * 128], wt[:, kc, sub * 128:(sub + 1) * 128], hT[:, kc, :],
                                     kc == 0, kc == 7, ['hT', wk], ['ps%d' % bk])

                    def transpose8(src, srckey, dst, dstkey):
                        bk = nb()
                        for kc in range(8):
                            k.tr(PSB(bk)[:, kc * 128:(kc + 1) * 128], src[:, kc * 128:(kc + 1) * 128], identb[:], [srckey], ['ps%d' % bk])
                        k.cp('act', dst[:].rearrange("p a b -> p (a b)"), PSB(bk), ['ps%d' % bk], [dstkey])

                    if not PR:
                        k.dma('act', xo[:], xs, [], ['xo'], 'xo')
                    elif g == 0:
                        k.dma('act', xo[:], xown[0:128, :], [], ['xo'], 'xo')
                    layernorm(xo[:], 'xo', gB[:], bB[:], tl, 'b', out_f32=h[:], out_f32_key='h', out_bf=hb[:], out_bf_key='hb')
                    transpose8(hb, 'hb', hT, 'hT')
                    wt, wk = next_w()
                    bw = nb()
                    projT(wt, wk, 8, PSF(bw)[:, 0:8], 'ps%d' % bw)
                    k.ts(wabs[:], PSF(bw)[:, 0:8], -IDX_W_SCALE, None, ALU.mult, None, ['ps%d' % bw], ['wabs'])
                    k.stt(wabs[:], PSF(bw)[:, 0:8], IDX_W_SCALE, wabs[:], ALU.mult, ALU.max, ['ps%d' % bw, 'wabs'], ['wabs'])
                    k.ts(wsgn[:], PSF(bw)[:, 0:8], 0.0, 2.0, ALU.is_ge, ALU.mult, ['ps%d' % bw], ['wsgn'])
                    k.ts(wsgn[:], wsgn[:], -1.0, None, ALU.add, None, ['wsgn'], ['wsgn'])
                    wt, wk = next_w()
                    bi = nb()
                    projF(wt, wk, bi)
                    qv = qiTz[:].rearrange("p (s two) t -> p s two t", two=2)
                    pv = PSF(bi).rearrange("p (s t) -> p s t", s=4)
                    k.cp('act', qv[0:64, :, 0, :], pv[0:64, :, :], ['ps%d' % bi], ['qiTz'])
                    k.cp('act', qv[64:128, :, 1, :], pv[64:128, :, :], ['ps%d' % bi], ['qiTz'])
                    for m in range(2):
                        wt, wk = next_w()
                        bdq = nb()
                        projF(wt, wk, bdq)
                        k.cp('act', QTz[0:64, 2 * m, :], PSF(bdq)[0:64, :], ['ps%d' % bdq], ['QTz'])
                        k.cp('act', QTz[64:128, 2 * m + 1, :], PSF(bdq)[64:128, :], ['ps%d' % bdq], ['QTz'])
                    if not PR:
                        wt, wk = next_w()
                        bkv = nb()
                        projT(wt, wk, 512, PSF(bkv), 'ps%d' % bkv)
                        k.cp('act', kdv[:], PSF(bkv), ['ps%d' % bkv], ['kdv'])
                        k.dma('act', ks, kdv[:, 0:256], ['kdv'], [], 'so1')
                        k.dma('act', vs, kdv[:, 256:512], ['kdv'], [], 'so1')
                        k.cp('pool', kdb[:], kdv[:, 0:256], ['kdv'], ['kdb'])
                        k.cp('pool', vnb[:], kdv[:, 256:512], ['kdv'], ['vnb'])
                        wt, wk = next_w()
                        bik = nb()
                        projT(wt, wk, 64, PSF(bik)[:, 0:64], 'ps%d' % bik)
                        k.cp('dve', kin[:], PSF(bik)[:, 0:64], ['ps%d' % bik], ['kin'])
                        k.bn_stats(st2[:], kin[:], ['kin'], ['st2'])
                        k.bn_aggr(mv2[:], st2[:], ['st2'], ['mv2'])
                        k.act(sd2[:], mv2[:, 1:2], AF.Ln, ['mv2'], ['sd2'], bias=eps[:, 0:1], scale=1.0)
                        k.act(rstd2[:], sd2[:], AF.Exp, ['sd2'], ['rstd2'], scale=-0.5)
                        k.ts(nmr2[:], mv2[:, 0:1], rstd2[:, 0:1], -1.0, ALU.mult, ALU.mult, ['mv2', 'rstd2'], ['nmr2'])
                        k.act(kin[:], kin[:], AF.Identity, ['kin', 'nmr2', 'rstd2'], ['kin'], bias=nmr2[:, 0:1], scale=rstd2[:, 0:1])
                        k.tt(kin[:], kin[:], gkiB[:], ALU.mult, ['kin'], ['kin'])
                        k.tt(kif[:], kin[:], bkiB[:], ALU.add, ['kin'], ['kif'])
                        k.dma('act', iks, kif[:], ['kif'], [], 'so1')
                        k.cp('pool', kib[:, 0:64], kif[:], ['kif'], ['kib'])
                        k.cp('pool', kib[:, 64:128], kif[:], ['kif'], ['kib'])
                        bt_ = nb()
                        for c_ in range(2):
                            k.tr(PSB(bt_)[:, c_ * 128:(c_ + 1) * 128], kdb[:, c_ * 128:(c_ + 1) * 128], identb[:], ['kdb'], ['ps%d' % bt_])
                        k.tr(PSB(bt_)[:, 256:384], kib[:], identb[:], ['kib'], ['ps%d' % bt_])
                        k.cp('dve', kTnew[:].rearrange("p a b -> p (a b)"), PSB(bt_)[:, 0:256], ['ps%d' % bt_], ['kTnew'])
                        k.cp('dve', kiTnew[:], PSB(bt_)[:, 256:384], ['ps%d' % bt_], ['kiTnew'])
                        for b_ in range(4):
                            k.cp('pool', kiTs[b_][:, 2048:2064], kiTnew[:, 16 * b_:16 * b_ + 16], ['kiTnew'], ['kiTs%d' % b_])
                            for t8 in range(2):
                                k.dma('sp', cif[:], cache_ik[b_, t8 * 1024:(t8 + 1) * 1024, :].rearrange("(t p) c -> p t c", p=128), [], ['cif'], 'cif')
                                k.cp('pool', cib[:, :, 0:64], cif[:], ['cif'], ['cib'])
                                k.cp('pool', cib[:, :, 64:128], cif[:], ['cif'], ['cib'])
                                bt_ = nb()
                                for tt_ in range(8):
                                    k.tr(PSB(bt_)[:, tt_ * 128:(tt_ + 1) * 128], cib[:, tt_, :], identb[:], ['cib'], ['ps%d' % bt_])
                                k.cp('dve', kiTs[b_][:, t8 * 1024:(t8 + 1) * 1024], PSB(bt_), ['ps%d' % bt_], ['kiTs%d' % b_])
                    if PR:
                        n_tiles = min(4 * g + 5, NB)
                        tail0 = 4 * g
                        tidx = 1 if g == G - 1 else 0
                    else:
                        n_tiles = 17
                        tail0 = 16
                        tidx = 1
                    n_keys = n_tiles * 128
                    nch = (n_tiles + 3) // 4

                    def kiload(ci):
                        k0 = ci * 1024
                        w_ = min(1024, n_keys - k0)
                        s_ = ci % 2
                        k.dma('sp', kich[s_][:, :w_], ki_d[:, k0:k0 + w_], [], ['kich%d' % s_], 'kich%d' % s_)

                    if PR:
                        kiload(0)
                    if not PR:
                        for b_ in range(4):
                            k.ts(wabsb[:, b_, :], wabs[:], rmaskS[:, b_:b_ + 1], None, ALU.mult, None, ['wabs'], ['wabsb'])
                    rli = 0
                    if PR:
                        npair = (n_keys + 1023) // 1024
                        wslots = [(rlw[:, :], 'rlw'), (mrg[:, :], 'mrgW'), (tmp[:, :], 'tmpW')]
                        for cp in range(npair):
                            k0 = cp * 1024
                            wtot = min(1024, n_keys - k0)
                            w0 = min(512, wtot)
                            w1 = wtot - w0
                            if cp + 1 < npair:
                                kiload(cp + 1)
                            for hh in range(8):
                                r_, rk = wslots[rli % 3]
                                rli += 1
                                bx = nb()
                                k.mm(PSF(bx)[:, :w0], qiTz[:, hh, :], kich[cp % 2][:, 0:w0], True, True, ['qiTz', 'kich%d' % (cp % 2)], ['ps%d' % bx])
                                k.act(r_[:, 0:w0], PSF(bx)[:, :w0], AF.Relu, ['ps%d' % bx, 'wabs'], [rk], scale=wabs[:, hh:hh + 1])
                                if w1 > 0:
                                    bx = nb()
                                    k.mm(PSF(bx)[:, :w1], qiTz[:, hh, :], kich[cp % 2][:, 512:512 + w1], True, True, ['qiTz', 'kich%d' % (cp % 2)], ['ps%d' % bx])
                                    k.act(r_[:, 512:512 + w1], PSF(bx)[:, :w1], AF.Relu, ['ps%d' % bx, 'wabs'], [rk], scale=wabs[:, hh:hh + 1])
                                if hh == 0:
                                    k.ts(sc[:, k0:k0 + wtot], r_[:, :wtot], wsgn[:, 0:1], None, ALU.mult, None, [rk, 'wsgn'], ['sc'])
                                else:
                                    k.stt(sc[:, k0:k0 + wtot], r_[:, :wtot], wsgn[:, hh:hh + 1], sc[:, k0:k0 + wtot], ALU.mult, ALU.add, [rk, 'wsgn', 'sc'], ['sc'])
                    for ci in (range(0) if PR else range(nch)):
                        k0 = ci * 512
                        w_ = min(512, n_keys - k0)
                        if PR and ci + 1 < nch:
                            kiload(ci + 1)
                        first = True
                        for hh in range(8):
                            for b_ in (range(1) if PR else range(4)):
                                bx = nb()
                                if PR:
                                    k.mm(PSF(bx)[:, :w_], qiTz[:, hh, :], kich[ci % 2][:, :w_], True, True, ['qiTz', 'kich%d' % (ci % 2)], ['ps%d' % bx])
                                    scl = wabs[:, hh:hh + 1]
                                    sck = 'wabs'
                                else:
                                    k.mm(PSF(bx)[:, :w_], qiTz[:, hh, :], kiTs[b_][:, k0:k0 + w_], True, True, ['qiTz', 'kiTs%d' % b_], ['ps%d' % bx])
                                    scl = wabsb[:, b_, hh:hh + 1]
                                    sck = 'wabsb'
                                rslots = [(rl[0], 'rl0'), (rl[1], 'rl1'), (mrg[:, 0:512], 'mrgA'), (mrg[:, 512:1024], 'mrgB'),
                                          (tmp[:, 0:512], 'tmpA'), (tmp[:, 512:1024], 'tmpB')]
                                r_, rk = rslots[rli % 6]
                                rli += 1
                                k.act(r_[:, :w_], PSF(bx)[:, :w_], AF.Relu, ['ps%d' % bx, sck], [rk], scale=scl)
                                if first:
                                    k.ts(sc[:, k0:k0 + w_], r_[:, :w_], wsgn[:, hh:hh + 1], None, ALU.mult, None, [rk, 'wsgn'], ['sc'])
                                    first = False
                                else:
                                    k.stt(sc[:, k0:k0 + w_], r_[:, :w_], wsgn[:, hh:hh + 1], sc[:, k0:k0 + w_], ALU.mult, ALU.add, [rk, 'wsgn', 'sc'], ['sc'])
                    k.capture()
                    k.red(rmax[:], sc[:, 0:n_keys], ALU.max, ['sc'], ['rmax'])
                    k.red(rmin[:], sc[:, 0:n_keys], ALU.min, ['sc'], ['rmin'])
                    tw = (n_tiles - tail0) * 128
                    k.tt(sc[:, tail0 * 128:tail0 * 128 + tw], sc[:, tail0 * 128:tail0 * 128 + tw], tbias[:, tidx, 0:tw], ALU.add, ['sc'], ['sc'])
                    k.tt(Wd[:], rmax[:], rmin[:], ALU.subtract, ['rmax', 'rmin'], ['Wd'])
                    k.ts(wtab[:], ctab[:], Wd[:, 0:1], None, ALU.mult, None, ['Wd'], ['wtab'])
                    k.tt(mids[:, 0:1], rmin[:], wtab[:, 0:1], ALU.add, ['rmin', 'wtab'], ['mid0'])
                    nD = (int(n_keys * 0.46) // 128) * 128
                    if nD < 256:
                        nD = n_keys
                    nA = n_keys - nD
                    for it in range(1, KIT + 1):
                        mid = mids[:, it - 1:it]
                        mk = 'mid%d' % (it - 1)
                        cn = cnts[:, it - 1:it]
                        ck_ = 'cnt%d' % it
                        k.ts(jd[:, 0:1].to_broadcast([128, nD]), sc[:, 0:nD], mid, None, ALU.is_ge, ALU.add, ['sc', mk], ['jd', ck_], accum=cn)
                        if nA > 0:
                            k.act(ja[:, 0:1].to_broadcast([128, nA]), sc[:, nD:n_keys], AF.Sign, ['sc', mk], ['ja', 'sa%d' % it],
                                  bias=mid, scale=-1.0, accum_out=sAs[:, it - 1:it])
                            k.stt(vvs[:, it - 1:it], cn, 2.0, sAs[:, it - 1:it], ALU.mult, ALU.subtract, [ck_, 'sa%d' % it], ['vv%d' % it])
                            vsrc, vkey, vthr = vvs[:, it - 1:it], 'vv%d' % it, 511.5 - nA
                        else:
                            vsrc, vkey, vthr = cn, ck_, 255.5
                        u_ = us[:, it - 1:it]
                        k.ts(u_, vsrc, vthr, wtab[:, it - 1:it], ALU.is_ge, ALU.mult, [vkey, 'wtab'], ['u%d' % it])
                        if it < KIT:
                            k.stt(mids[:, it:it + 1], u_, wtab[:, it:it + 1], mid, ALU.subtract, ALU.add, ['u%d' % it, 'wtab', mk], ['mid%d' % it])
                        else:
                            k.stt(thr[:], u_, wtab[:, it - 1:it], mid, ALU.subtract, ALU.add, ['u%d' % it, 'wtab', mk], ['thr'])
                    bisA = k.end_capture()
                    k.capture()
                    wt, wk = next_w()
                    b1 = nb()
                    for kc in range(8):
                        k.mm(PSF(b1)[0:16, 0:128], wt[:, kc, 0:16], hT[:, kc, :], kc == 0, kc == 7, ['hT', wk], ['ps%d' % b1])
                    k.cp('dve', gl[:], PSF(b1)[0:16, 0:128], ['ps%d' % b1], ['gl'])
                    bz = nb()
                    k.mm(PSF(bz), gl[:], w2[:], True, False, ['gl'], ['ps%d' % bz])
                    k.mm(PSF(bz), ones1[:], gbias[:], False, True, [], ['ps%d' % bz])
                    k.act(el[:], PSF(bz), AF.Exp, ['ps%d' % bz], ['el'], scale=-1.0)
                    k.act(el[:], el[:], AF.Ln, ['el'], ['el'], bias=one[:, 0:1], scale=1.0)
                    bb = nb()
                    for hh in range(4):
                        k.mm(PSF(bb)[:, hh * 128:(hh + 1) * 128], el[:, hh * 128:(hh + 1) * 128], tri[:], True, True, ['el'], ['ps%d' % bb])
                    k.act(eb[:], PSF(bb), AF.Exp, ['ps%d' % bb], ['eb'])
                    k.act(enb[:], PSF(bb), AF.Exp, ['ps%d' % bb], ['enb'], scale=-1.0)
                    wt, wk = next_w()
                    bq = nb()
                    projF(wt, wk, bq)
                    k.stt(qT[:].rearrange("p a b -> p (a b)"), PSF(bq), 128.0 ** -0.5, eb[:], ALU.mult, ALU.mult, ['ps%d' % bq, 'eb'], ['qT'])
                    wt, wk = next_w()
                    bk_ = nb()
                    projF(wt, wk, bk_)
                    k.tt(kTh[:].rearrange("p a b -> p (a b)"), PSF(bk_), enb[:], ALU.mult, ['ps%d' % bk_, 'enb'], ['kTh'])
                    if not PR:
                        bkt = nb()
                        projT(wt, wk, 512, PSF(bkt), 'ps%d' % bkt)
                        br_ = nb()
                        k.mm(PSF(br_), mrevS[:], el[:], True, True, ['el'], ['ps%d' % br_])
                        k.act(sg[1][:], PSF(br_), AF.Exp, ['ps%d' % br_], ['sg1'])
                        k.tt(Kt[:], PSF(bkt), sg[1][:], ALU.mult, ['ps%d' % bkt, 'sg1'], ['Kt'])
                        bd_ = nb()
                        for hh in range(4):
                            k.mm(PSF(bd_)[:, hh * 4:(hh + 1) * 4], el[:, hh * 128:(hh + 1) * 128], bsumS[:], True, True, ['el'], ['ps%d' % bd_])
                        k.act(decs[:], PSF(bd_)[:, 0:16], AF.Exp, ['ps%d' % bd_], ['decs'])
                    for half in range(2):
                        wt, wk = next_w()
                        bv = nb()
                        projT(wt, wk, 512, PSF(bv), 'ps%d' % bv)
                        k.cp('act', V[:, half * 512:(half + 1) * 512], PSF(bv), ['ps%d' % bv], ['V'])
                    if PR:
                        for m in range(4):
                            sidx = min(4 * g + m, NB - 1)
                            s_ = m % 2
                            k.dma('act', Sc[s_][:], snap[sidx], [], ['Sc%d' % s_], 'Sc%d' % s_)
                            if m == 0:
                                k.ts(Sown[:], Sc[s_][:], onehot[:, 0:1], None, ALU.mult, None, ['Sc%d' % s_], ['Sown'])
                            else:
                                k.stt(Sown[:], Sc[s_][:], onehot[:, m:m + 1], Sown[:], ALU.mult, ALU.add, ['Sc%d' % s_, 'Sown'], ['Sown'])
                        k.cp('pool', Sb[:].rearrange("p a b -> p (a b)"), Sown[:], ['Sown'], ['Sb'])
                    else:
                        for b_ in range(4):
                            for hh in range(4):
                                k.tt(qTb[b_][:, hh, :], qT[:, hh, :], cmaskS[:, b_, :], ALU.mult, ['qT'], ['qTb%d' % b_])
                    ba = nb()
                    for hh in range(4):
                        k.mm(PSF(ba)[:, hh * 128:(hh + 1) * 128], kTh[:, hh, :], qT[:, hh, :], True, True, ['kTh', 'qT'], ['ps%d' % ba])
                    k.tt(AT[:].rearrange("p a b -> p (a b)"), PSF(ba), maskf[:], ALU.mult, ['ps%d' % ba], ['AT'])
                    bo = [nb(), nb()]
                    for hh in range(4):
                        oap = PSF(bo[hh // 2])[:, (hh % 2) * 256:(hh % 2 + 1) * 256]
                        okey = 'ps%d' % bo[hh // 2]
                        if PR:
                            k.mm(oap, qT[:, hh, :], Sb[:, hh, :], True, False, ['qT', 'Sb'], [okey])
                        else:
                            for b_ in range(4):
                                k.mm(oap, qTb[b_][:, hh, :], S0b[b_][:, hh, :], b_ == 0, False, ['qTb%d' % b_], [okey])
                        k.mm(oap, AT[:, hh, :], V[:, hh * 256:(hh + 1) * 256], False, True, ['AT', 'V'], [okey])
                    for hh in range(4):
                        oap = PSF(bo[hh // 2])[:, (hh % 2) * 256:(hh % 2 + 1) * 256]
                        k.act(jq[:, 0:1].to_broadcast([128, 256]), oap, AF.Square, ['ps%d' % bo[hh // 2]], ['jq', 'ss%d' % hh], accum_out=ss[:, hh:hh + 1])
                    k.act(rs[:], ss[:], AF.Ln, ['ss0', 'ss1', 'ss2', 'ss3'], ['rs'], bias=eps[:, 0:1], scale=1.0 / 256)
                    k.act(rs[:], rs[:], AF.Exp, ['rs'], ['rs'], scale=-0.5)
                    for cc in range(2):
                        wt, wk = next_w()
                        bg = nb()
                        projT(wt, wk, 512, PSF(bg), 'ps%d' % bg)
                        k.act(sg[cc][:], PSF(bg), AF.Silu, ['ps%d' % bg], ['sg%d' % cc])
                        for hh in range(2):
                            hd = 2 * cc + hh
                            oap = PSF(bo[hd // 2])[:, (hd % 2) * 256:(hd % 2 + 1) * 256]
                            k.stt(tmp[:, hd * 256:(hd + 1) * 256], oap, rs[:, hd:hd + 1], gnB[:], ALU.mult, ALU.mult,
                                  ['ps%d' % bo[hd // 2], 'rs'], ['tmp'])
                        k.tt(yain[:, cc * 512:(cc + 1) * 512], tmp[:, cc * 512:(cc + 1) * 512], sg[cc][:], ALU.mult, ['tmp', 'sg%d' % cc], ['yain'])
                    transpose8(yain, 'yain', yT, 'yT')
                    for cc in range(2):
                        wt, wk = next_w()
                        by = nb()
                        for kc in range(8):
                            k.mm(PSF(by), yT[:, kc, :], wt[:, kc, :], kc == 0, kc == 7, ['yT', wk], ['ps%d' % by])
                        wt, wk = next_w()
                        bm = nb()
                        projT(wt, wk, 512, PSF(bm), 'ps%d' % bm)
                        k.dma('act', gtb[cc][:], gate_b[0:1, cc * 512:(cc + 1) * 512].partition_broadcast(128), [], ['gtb%d' % cc], 'gtb%d' % cc)
                        k.tt(sg[cc][:], PSF(bm), gtb[cc][:], ALU.add, ['ps%d' % bm, 'gtb%d' % cc], ['sg%d' % cc])
                        k.act(sg[cc][:], sg[cc][:], AF.Sigmoid, ['sg%d' % cc], ['sg%d' % cc])
                        k.tt(mrg[:, cc * 512:(cc + 1) * 512], PSF(by), sg[cc][:], ALU.mult, ['ps%d' % by, 'sg%d' % cc], ['mrg'])
                    if not PR:
                        for b_ in range(4):
                            s_ = b_ % 2
                            k.dma('sp', S0f[s_][:], state[b_].rearrange("h p e -> p h e"), [], ['S0f%d' % s_], 'S0f%d' % s_)
                            k.ts(Ktb[s_][:], Kt[:], rmaskS[:, b_:b_ + 1], None, ALU.mult, None, ['Kt'], ['Ktb%d' % s_])
                            for hp in range(2):
                                bs_ = nb()
                                for hh in range(2):
                                    hd = hp * 2 + hh
                                    k.mm(PSF(bs_)[:, hh * 256:(hh + 1) * 256], Ktb[s_][:, hd * 128:(hd + 1) * 128], V[:, hd * 256:(hd + 1) * 256],
                                         True, True, ['Ktb%d' % s_, 'V'], ['ps%d' % bs_])
                                for hh in range(2):
                                    hd = hp * 2 + hh
                                    k.stt(S0f[s_][:, hd, :], S0f[s_][:, hd, :], decs[:, hd * 4 + b_:hd * 4 + b_ + 1], PSF(bs_)[:, hh * 256:(hh + 1) * 256],
                                          ALU.mult, ALU.add, ['decs', 'ps%d' % bs_, 'S0f%d' % s_], ['S0f%d' % s_])
                            k.dma('act', gla_s[b_], S0f[s_][:].rearrange("p a b -> p (a b)"), ['S0f%d' % s_], [], 'Sno%d' % s_)

                    glaB = k.end_capture()
                    k.merge(bisA, glaB)

                    if PR:
                        def kvload(ci):
                            k0 = ci * 512
                            nt = min(4, n_tiles - ci * 4)
                            s_ = ci % 2
                            k.dma('sp', kTch[s_][:, :, :nt * 128], kT_d[:, :, k0:k0 + nt * 128], [], ['kTch%d' % s_], 'kTch%d' % s_)
                            k.dma('sp', vch[s_][:, :nt, :], v_d[:, ci * 4:ci * 4 + nt, :], [], ['vch%d' % s_], 'vch%d' % s_)
                        kvload(0)
                        if g + 1 < G:
                            k.dma('act', xo[:], xown[(g + 1) * 128:(g + 2) * 128, :], [], ['xo'], 'xo')
                        li = 0
                        groups = []
                        for kt in range(n_tiles):
                            ci, tl_ = kt // 4, kt % 4
                            mb_ = mbt[kt % 3]
                            mbk = 'mbt%d' % (kt % 3)
                            for gg in range(4):
                                bl = 4 + (li % 4)
                                p_ = pT[li % 3]
                                pk = 'pT%d' % (li % 3)
                                li += 1
                                k.capture()
                                if gg == 0:
                                    if tl_ == 1 and ci + 1 < nch:
                                        kvload(ci + 1)
                                    k.ts(mb_[:], sc[:, kt * 128:(kt + 1) * 128], thr[:, 0:1], NEG, ALU.is_lt, ALU.mult, ['sc', 'thr'], [mbk])
                                k.mm(PSF(bl), kTch[ci % 2][:, gg // 2, tl_ * 128:(tl_ + 1) * 128], QTz[:, gg, :], True, False,
                                     ['kTch%d' % (ci % 2), 'QTz'], ['ps%d' % bl])
                                k.mm(PSF(bl), mb_[:], idrepb[:], False, True, [mbk], ['ps%d' % bl])
                                k.act(p_[:], PSF(bl), AF.Exp, ['ps%d' % bl], [pk], scale=0.125)
                                s1 = k.end_capture()
                                k.capture()
                                k.mm(PSF(gg)[0:65, :], vch[ci % 2][:, tl_, gg * 65:(gg + 1) * 65], p_[:], kt == 0, kt == n_tiles - 1,
                                     ['vch%d' % (ci % 2), pk], ['ps%d' % gg])
                                s2 = k.end_capture()
                                groups.append((s1, s2))
                        SK = 2
                        for idx in range(len(groups) + SK):
                            if idx < len(groups):
                                P.ops.extend(groups[idx][0])
                            if idx >= SK:
                                P.ops.extend(groups[idx - SK][1])
                        for gg in range(4):
                            o_ = oT[gg % 2]
                            ok_ = 'oT%d' % (gg % 2)
                            k.cp('act', o_[:], PSF(gg)[0:65, :], ['ps%d' % gg], [ok_])
                            for r_ in range(4):
                                k.tr(PSF(4 + gg)[:, r_ * 65:(r_ + 1) * 65], o_[0:65, r_ * 128:(r_ + 1) * 128], identf[0:65, 0:65], [ok_], ['ps%d' % (4 + gg)])
                        NPT = 128
                    else:
                        mball = junk
                        for b_ in range(4):
                            k.cp('pool', QTs[:, b_, :, :].rearrange("p g (r t) -> p g r t", r=4),
                                 QTz[:].rearrange("p g (r t) -> p g r t", r=4)[:, :, :, 16 * b_:16 * b_ + 16], ['QTz'], ['QTs'])
                        k.ts(mball[:], sc[:, 0:2176], thr[:, 0:1], NEG, ALU.is_lt, ALU.mult, ['sc', 'thr'], ['junk'])
                        li = 0
                        for b_ in range(4):
                            k.cp('pool', kTs1[:, :, 2048:2064], kTnew[:, :, 16 * b_:16 * b_ + 16], ['kTnew'], ['kTs'])
                            bs_ = nb() % 4 + 4
                            k.mm(PSF(bs_)[:, 0:256], selSb[:, b_, :], vnb[:], True, True, ['vnb'], ['ps%d' % bs_])
                            k.cp('act', vexts1[:, 16, :].rearrange("p (g d) -> p g d", g=4)[:, :, 0:64],
                                 PSF(bs_)[:, 0:256].rearrange("p (g d) -> p g d", g=4), ['ps%d' % bs_], ['vexts'])
                            for t8 in range(2):
                                k.dma('sp', ckf[:], cache_k[b_, t8 * 1024:(t8 + 1) * 1024, :].rearrange("(t p) c -> p t c", p=128), [], ['ckf'], 'ckf')
                                k.cp('pool', ckb[:], ckf[:], ['ckf'], ['ckb'])
                                for t4 in range(2):
                                    bt_ = nb() % 4 + 4
                                    for tt_ in range(4):
                                        for c_ in range(2):
                                            col = (tt_ * 2 + c_) * 128
                                            k.tr(PSB(bt_)[:, col:col + 128], ckb[:, t4 * 4 + tt_, c_ * 128:(c_ + 1) * 128], identb[:], ['ckb'], ['ps%d' % bt_])
                                    c0_ = t8 * 1024 + t4 * 512
                                    k.cp('dve', kTs1[:, :, c0_:c0_ + 512].rearrange("p c (t s) -> p c t s", t=4),
                                         PSB(bt_).rearrange("p (t c s) -> p c t s", t=4, c=2), ['ps%d' % bt_], ['kTs'])
                                k.dma('sp', ckf[:], cache_v[b_, t8 * 1024:(t8 + 1) * 1024, :].rearrange("(t p) c -> p t c", p=128), [], ['ckf'], 'ckf')
                                k.cp('pool', vexts1[:, t8 * 8:(t8 + 1) * 8, :].rearrange("p t (g d) -> p t g d", g=4)[:, :, :, 0:64],
                                     ckf[:].rearrange("p t (g d) -> p t g d", g=4), ['ckf'], ['vexts'])
                            sgroups = []
                            for gg in range(4):
                                ob = b_ // 2
                                ocol = ((b_ % 2) * 4 + gg) * 64
                                qsel = QTs[:, b_, gg, :]
                                for kt in range(17):
                                    bl = 4 + (li % 4)
                                    p_ = pTs[li % 3]
                                    pk = 'pTs%d' % (li % 3)
                                    li += 1
                                    k.capture()
                                    k.mm(PSF(bl)[:, 0:64], kTs1[:, gg // 2, kt * 128:(kt + 1) * 128], qsel, True, False,
                                         ['kTs', 'QTs'], ['ps%d' % bl])
                                    k.mm(PSF(bl)[:, 0:64], mball[:, kt * 128:(kt + 1) * 128], idrepSb[:, b_, :], False, True, ['junk'], ['ps%d' % bl])
                                    k.act(p_[:], PSF(bl)[:, 0:64], AF.Exp, ['ps%d' % bl], [pk], scale=0.125)
                                    s1_ = k.end_capture()
                                    k.capture()
                                    k.mm(PSF(ob)[0:65, ocol:ocol + 64], vexts1[:, kt, gg * 65:(gg + 1) * 65], p_[:], kt == 0, kt == 16,
                                         ['vexts', pk], ['ps%d' % ob])
                                    s2_ = k.end_capture()
                                    sgroups.append((s1_, s2_))
                            SKS = 2
                            for idx in range(len(sgroups) + SKS):
                                if idx < len(sgroups):
                                    P.ops.extend(sgroups[idx][0])
                                if idx >= SKS:
                                    P.ops.extend(sgroups[idx - SKS][1])
                        oTs = sc[0:65, 0:1024]
                        ov = oTs.rearrange("p (g r b t) -> p b g r t", g=4, r=4, b=4)
                        for ob in range(2):
                            k.cp('act', ov[:, 2 * ob:2 * ob + 2], PSF(ob)[0:65, :].rearrange("p (b g r t) -> p b g r t", b=2, g=4, r=4),
                                 ['ps%d' % ob, 'junk'], ['sc'])
                        for gg in range(4):
                            for r_ in range(4):
                                c0_ = (gg * 4 + r_) * 64
                                k.tr(PSF(4 + gg)[0:64, r_ * 65:(r_ + 1) * 65], oTs[:, c0_:c0_ + 64], identf[0:65, 0:65], ['sc'], ['ps%d' % (4 + gg)])
                        NPT = 64
                    for gg in range(4):
                        pv4 = PSF(4 + gg)[0:NPT, 0:260].rearrange("p (r c) -> p r c", c=65)
                        k.recip(rd[0:NPT, 4 * gg:4 * gg + 4], pv4[:, :, 64], ['ps%d' % (4 + gg)], ['rd'])
                        for r_ in range(4):
                            hd = 4 * gg + r_
                            k.ts(tmp[0:NPT, hd * 64:(hd + 1) * 64], pv4[:, r_, 0:64], rd[0:NPT, hd:hd + 1], None, ALU.mult, None,
                                 ['ps%d' % (4 + gg), 'rd'], ['tmp'])
                    for cc in range(2):
                        wt, wk = next_w()
                        bg = nb()
                        projT(wt, wk, 512, PSF(bg), 'ps%d' % bg)
                        k.act(sg[cc][:], PSF(bg), AF.Silu, ['ps%d' % bg], ['sg%d' % cc])
                        k.tt(yain[0:NPT, cc * 512:(cc + 1) * 512], tmp[0:NPT, cc * 512:(cc + 1) * 512], sg[cc][0:NPT, :], ALU.mult,
                             ['tmp', 'sg%d' % cc], ['yain'])
                    transpose8(yain, 'yain', yT, 'yT')
                    for cc in range(2):
                        wt, wk = next_w()
                        by = nb()
                        for kc in range(8):
                            k.mm(PSF(by), yT[:, kc, :], wt[:, kc, :], kc == 0, kc == 7, ['yT', wk], ['ps%d' % by])
                        wt, wk = next_w()
                        bm = nb()
                        projT(wt, wk, 512, PSF(bm), 'ps%d' % bm)
                        k.dma('act', gtb[cc][:], gate_b[0:1, 1024 + cc * 512:1024 + (cc + 1) * 512].partition_broadcast(128), [], ['gtb%d' % cc], 'gtb%d' % cc)
                        k.tt(sg[cc][:], PSF(bm), gtb[cc][:], ALU.add, ['ps%d' % bm, 'gtb%d' % cc], ['sg%d' % cc])
                        k.act(sg[cc][:], sg[cc][:], AF.Sigmoid, ['sg%d' % cc], ['sg%d' % cc])
                        k.tt(sg[cc][:], PSF(by), sg[cc][:], ALU.mult, ['ps%d' % by, 'sg%d' % cc], ['sg%d' % cc])
                        k.tt(mrg[:, cc * 512:(cc + 1) * 512], mrg[:, cc * 512:(cc + 1) * 512], sg[cc][:], ALU.add, ['mrg', 'sg%d' % cc], ['mrg'])
                    k.cp('pool', yain[:], mrg[:], ['mrg'], ['yain'])
                    transpose8(yain, 'yain', yT, 'yT')
                    for cc in range(2):
                        wt, wk = next_w()
                        by = nb()
                        for kc in range(8):
                            k.mm(PSF(by), yT[:, kc, :], wt[:, kc, :], kc == 0, kc == 7, ['yT', wk], ['ps%d' % by])
                        k.stt(mrg[:, cc * 512:(cc + 1) * 512], h[:, cc * 512:(cc + 1) * 512], ALPHA, PSF(by), ALU.mult, ALU.add, ['h', 'ps%d' % by], ['mrg'])
                    yk = 'yout' if PR else 'xo'
                    layernorm(mrg[:], 'mrg', g2B[:], b2B[:], tl, 'c', out_f32=yout[:], out_f32_key=yk)
                    ydst = y_own[g * 128:(g + 1) * 128, :] if PR else ys
                    k.dma('act', ydst, yout[:], [yk], [], 'youto')

                for g in (range(G) if PR else [0]):
                    own_block(g)
            P.barrier()

        if "B" in phases:
            phase_own('P')
        if "S" in phases:
            phase_own('S')

        P.emit(nc, top)
    return nc


def _consts(j, G):
    c = {}
    p = np.arange(128)
    c["c_ident"] = np.eye(128, dtype=np.float32)
    jj, ii = np.meshgrid(p, p, indexing="ij")
    c["c_triA"] = np.where((jj > ii) & (jj // 64 == ii // 64), -1.0 / 16, 0.0).astype(np.float32)
    c["c_blkA"] = np.where(p[:, None] // 64 == np.arange(2)[None, :], -1.0 / 16, 0.0).astype(np.float32)
    c["c_triB"] = np.where(jj <= ii, -1.0 / 16, 0.0).astype(np.float32)
    mB = (jj <= ii).astype(np.float32)
    c["c_maskB"] = np.ascontiguousarray(np.broadcast_to(mB[:, None, :], (128, 4, 128)))
    same = (jj // 16 == ii // 16)
    c["c_triS"] = np.where((jj <= ii) & same, -1.0 / 16, 0.0).astype(np.float32)
    mS = ((jj <= ii) & same).astype(np.float32)
    c["c_maskS"] = np.ascontiguousarray(np.broadcast_to(mS[:, None, :], (128, 4, 128)))
    c["c_mrevS"] = np.where((jj > ii) & same, -1.0 / 16, 0.0).astype(np.float32)
    c["c_bsumS"] = np.where(p[:, None] // 16 == np.arange(4)[None, :], -1.0 / 16, 0.0).astype(np.float32)
    cm = (p[None, :] // 16 == np.arange(4)[:, None]).astype(np.float32)
    c["c_cmaskS"] = np.ascontiguousarray(np.broadcast_to(cm[None], (128, 4, 128)))
    c["c_rmaskS"] = (p[:, None] // 16 == np.arange(4)[None, :]).astype(np.float32)
    c["c_idrep"] = np.ascontiguousarray(np.tile(np.eye(128, dtype=np.float32), (1, 4)))
    ids = np.zeros((128, 4, 64), np.float32)
    sel = np.zeros((128, 4, 128), np.float32)
    for b in range(4):
        for t in range(16):
            for r in range(4):
                ids[16 * b + t, b, r * 16 + t] = 1.0
            sel[16 * b + t, b, t] = 1.0
    c["c_idrepS"] = ids
    c["c_selS"] = sel
    c["c_ctab"] = np.ascontiguousarray(np.broadcast_to((0.5 ** np.arange(1, KIT + 2))[None, :], (128, KIT + 1))).astype(np.float32)
    tb = np.zeros((128, 2, 640), np.float32)
    kk = np.arange(640)
    for r in range(128):
        lim = 128 * j + (16 if r < 16 else (80 if r < 80 else 144))
        tb[r, 0, :] = np.where(kk < lim, 0.0, -1e30)
    tb[:, 1, :] = np.where(kk < 16, 0.0, -1e30)[None, :]
    c["c_tbias"] = tb
    oh = np.zeros((128, 4), np.float32)
    oh[:, j] = 1.0
    c["c_onehot"] = oh
    c["c_tailmask"] = (p < 16).astype(np.float32)[:, None]
    return c


def _dq_perm():
    perm = np.zeros(1024, np.int64)
    n = 0
    for m in range(2):
        for r in range(4):
            for half in range(2):
                g = 2 * m + half
                for d in range(64):
                    perm[n] = (g * 4 + r) * 64 + d
                    n += 1
    return perm


def prep(inp, SEQ):
    T, NB, G = geometry(SEQ)
    f = lambda a: np.ascontiguousarray(np.asarray(a, dtype=np.float32))
    w_in = f(inp["w_in"])[0].copy()
    w_in[:, C_DQ:C_DQ + 1024] = w_in[:, C_DQ:C_DQ + 1024][:, _dq_perm()]
    w3 = np.ascontiguousarray(np.stack([f(inp["w_gla"])[0], f(inp["w_dsa"])[0], f(inp["w_out"])[0]], 0))
    shared = dict(
        w_in=np.ascontiguousarray(w_in), w3=w3,
        ln_in_g=f(inp["ln_in_g"]).reshape(1, D), ln_in_b=f(inp["ln_in_b"]).reshape(1, D),
        ln_g=f(inp["ln_g"]).reshape(1, D), ln_b=f(inp["ln_b"]).reshape(1, D),
        gate_b=f(inp["gate_b"]).reshape(1, 2048), gla_gate_b=f(inp["gla_gate_b"]).reshape(1, 512),
        gla_norm_g=f(inp["gla_norm_g"]).reshape(1, 256),
        idx_kn_g=f(inp["idx_kn_g"]).reshape(1, 64), idx_kn_b=f(inp["idx_kn_b"]).reshape(1, 64),
        gla_w2=f(inp["gla_w2"]).reshape(16, 512))
    xp = f(inp["x_prompt"]); meta = f(inp["meta"]); xsm = f(inp["x_sample"])
    ck = f(inp["cache_k"])[0]; cv = f(inp["cache_v"])[0]; cik = f(inp["cache_idx_k"])[0]; stt = f(inp["state_gla"])[0]
    maps = []
    for c in range(8):
        b, j = c // 4, c % 4
        xall = np.zeros((4 * G * 128, D), np.float32)
        xall[:16] = meta
        xall[16:T] = xp[b]
        xown = np.ascontiguousarray(xall.reshape(G, 4, 128, D)[:, j].reshape(G * 128, D))
        xs_ = np.zeros((128, D), np.float32)
        xs_[:64] = xsm[4 * c:4 * c + 4].reshape(64, D)
        m = dict(shared)
        m.update(_consts(j, G))
        m.update(xall=np.ascontiguousarray(xall[:NB * 128]), xown=xown, xs=xs_,
                 cache_k=np.ascontiguousarray(ck[4 * c:4 * c + 4].reshape(4, 2048, 256)),
                 cache_v=np.ascontiguousarray(cv[4 * c:4 * c + 4].reshape(4, 2048, 256)),
                 cache_ik=np.ascontiguousarray(cik[4 * c:4 * c + 4]),
                 state=np.ascontiguousarray(stt[4 * c:4 * c + 4]))
        maps.append(m)
    return maps


def gather(res, SEQ):
    T, NB, G = geometry(SEQ)
    R = res.results
    yp = np.zeros((2, 4 * G * 128, D), np.float32)
    for c in range(8):
        b, j = c // 4, c % 4
        yp[b].reshape(G, 4, 128, D)[:, j] = R[c]["y_own"].reshape(G, 128, D)
    y_prompt = np.ascontiguousarray(yp[:, 16:T])
    y_sample = np.concatenate([R[c]["ys"][:64].reshape(4, 16, D) for c in range(8)], 0)
    k_prompt = np.stack([R[4 * b]["kp"][:T].reshape(T, 4, 64) for b in range(2)], 0)[None]
    v_prompt = np.stack([R[4 * b]["vp"][:T].reshape(T, 4, 64) for b in range(2)], 0)[None]
    ik_prompt = np.stack([R[4 * b]["ikp"][:T] for b in range(2)], 0)[None]
    gla_prompt = np.stack([R[4 * b]["gla_p"].reshape(128, 4, 256).transpose(1, 0, 2) for b in range(2)], 0)[None]
    k_sample = np.concatenate([R[c]["ks"][:64].reshape(4, 16, 4, 64) for c in range(8)], 0)[None]
    v_sample = np.concatenate([R[c]["vs"][:64].reshape(4, 16, 4, 64) for c in range(8)], 0)[None]
    ik_sample = np.concatenate([R[c]["iks"][:64].reshape(4, 16, 64) for c in range(8)], 0)[None]
    gla_sample = np.concatenate([R[c]["gla_s"].reshape(4, 128, 4, 256).transpose(0, 2, 1, 3) for c in range(8)], 0)[None]
    outs = (y_prompt, y_sample, k_prompt, v_prompt, ik_prompt, gla_prompt, k_sample, v_sample, ik_sample, gla_sample)
    return tuple(np.ascontiguousarray(o, dtype=np.float32) for o in outs)


_NC_CACHE = {}


def run(inputs, SEQ, phases="0ABS"):
    key = (SEQ, phases)
    if key not in _NC_CACHE:
        _NC_CACHE[key] = build(SEQ, phases)
    nc = _NC_CACHE[key]
    maps = prep(inputs, SEQ)
    res = run_bass_kernel_spmd(nc, maps, core_ids=list(range(8)))
    return gather(res, SEQ)


def kernel(**inputs):
    SEQ = int(np.asarray(inputs["x_prompt"]).shape[1])
    return run(inputs, SEQ)
```
